# Optimizing a Trainium2 kernel written in Bass

```python
import math
import jax, jax.numpy as jnp
from jax import lax
import numpy as np

D_MODEL = 1024
BATCH = 8
SEQ = 8192
DEPTH = 1

PLE_DIM = 256
ATTN_HEADS = 8
HEAD_DIM = 64
ATTN_WIDTH = ATTN_HEADS * HEAD_DIM
MOBA_BLOCK = 256
MOBA_TOPK = 3
Q_CHUNK = 128
REL_BUCKETS = 32
REL_MAX_DIST = 128
SSM_WIDTH = 512
SSM_GROUP = 16
SSM_GROUPS = SSM_WIDTH // SSM_GROUP
SSM_STATE = 64
DT_MIN = 1e-3
DT_MAX = 1e-1
IN_SIZES = (ATTN_WIDTH, ATTN_WIDTH, ATTN_WIDTH, ATTN_WIDTH,
            SSM_WIDTH, SSM_WIDTH,
            D_MODEL, D_MODEL)
IN_WIDTH = sum(IN_SIZES)
IN_SPLITS = tuple(int(s) for s in np.cumsum(IN_SIZES)[:-1])
DEEPNORM_ALPHA = (2.0 * DEPTH) ** 0.25
DEEPNORM_BETA = (8.0 * DEPTH) ** -0.25
LN_EPS = 1e-5

kernel_name = "moba_s5_gated_hybrid_deepnorm"


def t5_bucket(dist):
    max_exact = REL_BUCKETS // 2
    is_small = dist < max_exact
    d = jnp.maximum(dist, 1).astype(jnp.float32)
    large = max_exact + (jnp.log(d / max_exact) / math.log(REL_MAX_DIST / max_exact)
                         * (REL_BUCKETS - max_exact)).astype(jnp.int32)
    large = jnp.minimum(large, REL_BUCKETS - 1)
    return jnp.where(is_small, dist, large)


def moba_attention(q, k, v, rel_bias):
    B, H, S, Dh = q.shape
    f32 = jnp.float32
    nb = -(-S // MOBA_BLOCK)
    s_pad = nb * MOBA_BLOCK
    pad = ((0, 0), (0, 0), (0, s_pad - S), (0, 0))
    q = jnp.pad(q, pad)
    k = jnp.pad(k, pad)
    v = jnp.pad(v, pad)
    kb = k.reshape(B, H, nb, MOBA_BLOCK, Dh)
    vb = v.reshape(B, H, nb, MOBA_BLOCK, Dh)
    k_mean = kb.astype(f32).mean(axis=3)
    topk = min(MOBA_TOPK, nb)
    n_chunks = s_pad // Q_CHUNK
    q_chunks = q.reshape(B, H, n_chunks, Q_CHUNK, Dh).transpose(2, 0, 1, 3, 4)
    bias_table = rel_bias.T
    scale = Dh ** -0.5
    offs = jnp.arange(MOBA_BLOCK)
    b_ix = jnp.arange(B)[:, None, None, None]
    h_ix = jnp.arange(H)[None, :, None, None]
    h_ix5 = jnp.arange(H)[None, :, None, None, None]

    def chunk_fn(args):
        qc, ci = args
        q_pos = ci * Q_CHUNK + jnp.arange(Q_CHUNK)
        blk = (ci * Q_CHUNK) // MOBA_BLOCK
        gate = jnp.einsum('bhqd,bhnd->bhqn', qc.astype(f32), k_mean)
        past = jnp.arange(nb) < blk
        gate = jnp.where(past, gate, -jnp.inf)
        _, idx = lax.top_k(gate, topk)
        valid = idx < blk
        k_sel = kb[b_ix, h_ix, idx]
        v_sel = vb[b_ix, h_ix, idx]
        k_pos = idx[..., None] * MOBA_BLOCK + offs
        logit_sel = jnp.einsum('bhqd,bhqnkd->bhqnk', qc, k_sel).astype(f32) * scale
        dist_sel = jnp.maximum(q_pos[None, None, :, None, None] - k_pos, 0)
        bias_sel = bias_table[h_ix5, t5_bucket(dist_sel)].astype(f32)
        logit_sel = jnp.where(valid[..., None], logit_sel + bias_sel, -jnp.inf)
        k_own = lax.dynamic_index_in_dim(kb, blk, axis=2, keepdims=False)
        v_own = lax.dynamic_index_in_dim(vb, blk, axis=2, keepdims=False)
        own_pos = blk * MOBA_BLOCK + offs
        dist_own = q_pos[:, None] - own_pos[None, :]
        logit_own = jnp.einsum('bhqd,bhkd->bhqk', qc, k_own).astype(f32) * scale
        bias_own = bias_table[:, t5_bucket(jnp.maximum(dist_own, 0))].astype(f32)
        logit_own = jnp.where(dist_own >= 0, logit_own + bias_own, -jnp.inf)
        logits = jnp.concatenate(
            [logit_sel.reshape(B, H, Q_CHUNK, topk * MOBA_BLOCK), logit_own], axis=-1)
        probs = jax.nn.softmax(logits, axis=-1)
        p_sel = probs[..., :topk * MOBA_BLOCK].reshape(B, H, Q_CHUNK, topk, MOBA_BLOCK)
        p_own = probs[..., topk * MOBA_BLOCK:]
        out = (jnp.einsum('bhqnk,bhqnkd->bhqd', p_sel, v_sel.astype(f32))
               + jnp.einsum('bhqk,bhkd->bhqd', p_own, v_own.astype(f32)))
        return out.astype(q.dtype)

    out = lax.map(chunk_fn, (q_chunks, jnp.arange(n_chunks)))
    out = out.transpose(1, 0, 3, 2, 4).reshape(B, s_pad, H * Dh)
    return out[:, :S]


def s5_ssm(u, a_re, a_im, log_dt, b_re, b_im, c_re, c_im, d_skip):
    B, S, W = u.shape
    f32 = jnp.float32
    ug = u.reshape(B, S, SSM_GROUPS, SSM_GROUP).astype(f32)
    ar = a_re.astype(f32)
    ai = a_im.astype(f32)
    dt = jnp.exp(log_dt.astype(f32))[:, None]
    mag = jnp.exp(dt * ar)
    ang = dt * ai
    abar_re = mag * jnp.cos(ang)
    abar_im = mag * jnp.sin(ang)
    den = ar * ar + ai * ai
    nr = abar_re - 1.0
    ni = abar_im
    fr = (nr * ar + ni * ai) / den
    fi = (ni * ar - nr * ai) / den
    br = b_re.astype(f32)
    bi = b_im.astype(f32)
    bbar_re = fr[..., None] * br - fi[..., None] * bi
    bbar_im = fr[..., None] * bi + fi[..., None] * br
    bu_re = jnp.einsum('bsgc,gpc->bsgp', ug, bbar_re)
    bu_im = jnp.einsum('bsgc,gpc->bsgp', ug, bbar_im)
    a_re_t = jnp.broadcast_to(abar_re, bu_re.shape)
    a_im_t = jnp.broadcast_to(abar_im, bu_re.shape)

    def combine(lhs, rhs):
        a1r, a1i, b1r, b1i = lhs
        a2r, a2i, b2r, b2i = rhs
        return (a2r * a1r - a2i * a1i,
                a2r * a1i + a2i * a1r,
                a2r * b1r - a2i * b1i + b2r,
                a2r * b1i + a2i * b1r + b2i)

    _, _, h_re, h_im = lax.associative_scan(combine, (a_re_t, a_im_t, bu_re, bu_im), axis=1)
    y = (jnp.einsum('bsgp,gcp->bsgc', h_re, c_re.astype(f32))
         - jnp.einsum('bsgp,gcp->bsgc', h_im, c_im.astype(f32)))
    y = y.reshape(B, S, W) + d_skip.astype(f32) * u.astype(f32)
    return y.astype(u.dtype)


def layer_norm(x, g, b):
    xf = x.astype(jnp.float32)
    mu = jnp.mean(xf, axis=-1, keepdims=True)
    var = jnp.mean(jnp.square(xf - mu), axis=-1, keepdims=True)
    y = (xf - mu) * lax.rsqrt(var + LN_EPS) * g.astype(jnp.float32) + b.astype(jnp.float32)
    return y.astype(x.dtype)


def setup_inputs(seed: int = 0) -> dict:
    key = jax.random.key(seed)
    ks = jax.random.split(key, 20)
    f32 = jnp.float32

    def nrm(k, shape, scale):
        return scale * jax.random.normal(k, shape, f32)

    n = jnp.arange(SSM_STATE, dtype=f32)
    return {
        'x': nrm(ks[0], (BATCH, SEQ, D_MODEL), 1.0),
        'p': nrm(ks[1], (DEPTH, BATCH, SEQ, PLE_DIM), 1.0),
        'w_in': nrm(ks[2], (DEPTH, D_MODEL, IN_WIDTH), D_MODEL ** -0.5),
        'w_attn_proj': nrm(ks[3], (DEPTH, ATTN_WIDTH, D_MODEL), ATTN_WIDTH ** -0.5),
        'w_ssm_proj': nrm(ks[4], (DEPTH, SSM_WIDTH, D_MODEL), SSM_WIDTH ** -0.5),
        'w_out': nrm(ks[5], (DEPTH, D_MODEL, D_MODEL), DEEPNORM_BETA * D_MODEL ** -0.5),
        'ssm_a_re': -0.5 * jnp.exp(nrm(ks[6], (DEPTH, SSM_GROUPS, SSM_STATE), 0.05)),
        'ssm_a_im': math.pi * n + nrm(ks[7], (DEPTH, SSM_GROUPS, SSM_STATE), 0.01),
        'ssm_log_dt': jax.random.uniform(ks[8], (DEPTH, SSM_GROUPS), f32,
                                         math.log(DT_MIN), math.log(DT_MAX)),
        'ssm_b_re': nrm(ks[9], (DEPTH, SSM_GROUPS, SSM_STATE, SSM_GROUP), (2 * SSM_GROUP) ** -0.5),
        'ssm_b_im': nrm(ks[10], (DEPTH, SSM_GROUPS, SSM_STATE, SSM_GROUP), (2 * SSM_GROUP) ** -0.5),
        'ssm_c_re': nrm(ks[11], (DEPTH, SSM_GROUPS, SSM_GROUP, SSM_STATE), SSM_STATE ** -0.5),
        'ssm_c_im': nrm(ks[12], (DEPTH, SSM_GROUPS, SSM_GROUP, SSM_STATE), SSM_STATE ** -0.5),
        'ssm_d': nrm(ks[13], (DEPTH, SSM_WIDTH), 1.0),
        'w_glu': nrm(ks[14], (DEPTH, SSM_WIDTH, 2 * SSM_WIDTH), SSM_WIDTH ** -0.5),
        'w_ple_gate': nrm(ks[15], (DEPTH, D_MODEL, D_MODEL), D_MODEL ** -0.5),
        'w_ple_proj': nrm(ks[16], (DEPTH, PLE_DIM, D_MODEL), PLE_DIM ** -0.5),
        'ln_g': 1.0 + nrm(ks[17], (DEPTH, D_MODEL), 0.02),
        'ln_b': nrm(ks[18], (DEPTH, D_MODEL), 0.02),
        'rel_bias': nrm(ks[19], (REL_BUCKETS, ATTN_HEADS), 0.5),
    }


def reference(x, p, w_in, w_attn_proj, w_ssm_proj, w_out, ssm_a_re, ssm_a_im, ssm_log_dt,
              ssm_b_re, ssm_b_im, ssm_c_re, ssm_c_im, ssm_d, w_glu, w_ple_gate, w_ple_proj,
              ln_g, ln_b, rel_bias):
    B, S, _ = x.shape

    def heads(t):
        return t.reshape(B, S, ATTN_HEADS, HEAD_DIM).transpose(0, 2, 1, 3)

    for i in range(DEPTH):
        proj = x @ w_in[i]
        q, k, v, z_a, u, z_s, g_a, g_s = jnp.split(proj, IN_SPLITS, axis=-1)
        o_a = moba_attention(heads(q), heads(k), heads(v), rel_bias)
        y_a = (o_a * jax.nn.silu(z_a)) @ w_attn_proj[i]
        y_s = s5_ssm(u, ssm_a_re[i], ssm_a_im[i], ssm_log_dt[i], ssm_b_re[i], ssm_b_im[i],
                     ssm_c_re[i], ssm_c_im[i], ssm_d[i])
        glu_a, glu_b = jnp.split(jax.nn.gelu(y_s, approximate=False) @ w_glu[i], 2, axis=-1)
        y_s = glu_a * jax.nn.sigmoid(glu_b)
        y_s = (y_s * jax.nn.silu(z_s)) @ w_ssm_proj[i]
        mix = (jax.nn.sigmoid(g_a) * y_a + jax.nn.sigmoid(g_s) * y_s) @ w_out[i]
        ple = jax.nn.sigmoid(x @ w_ple_gate[i]) * (p[i] @ w_ple_proj[i])
        x = layer_norm(DEEPNORM_ALPHA * x + mix + ple, ln_g[i], ln_b[i])
    return x
```

```python
import math
from contextlib import ExitStack

import numpy as np
import ml_dtypes

import concourse.bass as bass
import concourse.mybir as mybir
from concourse.bass_utils import run_bass_kernel_spmd

F32 = mybir.dt.float32
BF16 = mybir.dt.bfloat16
I32 = mybir.dt.int32
AF = mybir.ActivationFunctionType
ALU = mybir.AluOpType
AX = mybir.AxisListType

D = 1024
NH = 8
HD = 64
BLK = 256
NEG = -30000.0
TWO_PI = 2.0 * math.pi
ALPHA = 2.0 ** 0.25
LN_EPS = 1e-5


class Buf:
    __slots__ = ("w", "r", "name")

    def __init__(self, name=""):
        self.w = None
        self.r = {}
        self.name = name


class Ring:
    def __init__(self, items):
        self.items = items
        self.i = 0

    def next(self):
        it = self.items[self.i % len(self.items)]
        self.i += 1
        return it


class Ctx:
    def __init__(self, nc, es, n_dma_sems=24):
        self.nc = nc
        self.engs = {"pe": nc.tensor, "act": nc.scalar, "dve": nc.vector, "pool": nc.gpsimd, "sp": nc.sync}
        self.sem = {}
        self.cnt = {}
        self.seen = {e: {} for e in self.engs}
        self.pend = {e: ([], []) for e in self.engs}
        for e in self.engs:
            self.sem[e] = es.enter_context(nc.semaphore("s_" + e))
            self.cnt[e] = 0
        self.dq = []
        for i in range(n_dma_sems):
            k = "d%d" % i
            self.sem[k] = es.enter_context(nc.semaphore("s_" + k))
            self.cnt[k] = 0
            self.dq.append(k)
        self.dqi = 0

    def _wait(self, e, tok):
        if tok is None:
            return
        k, v = tok
        if v <= 0 or self.seen[e].get(k, 0) >= v:
            return
        self.engs[e].wait_ge(self.sem[k], v)
        self.seen[e][k] = v

    def _deps(self, e, r, w):
        for b in r:
            if b.w is not None and not (e == "pe" and b.w[0] == "pe"):
                self._wait(e, b.w)
        for b in w:
            if b.w is not None and b.w[0] != e:
                self._wait(e, b.w)
            for k, v in b.r.items():
                if k != e:
                    self._wait(e, (k, v))

    def _reg(self, tok, r, w):
        k, v = tok
        for b in r:
            if b.r.get(k, 0) < v:
                b.r[k] = v
        for b in w:
            b.w = tok
            b.r = {}

    def op(self, e, fn, r=(), w=(), sig=True):
        self._deps(e, r, w)
        inst = fn()
        pr, pw = self.pend[e]
        if not sig:
            pr.extend(r)
            pw.extend(w)
            return None
        self.cnt[e] += 1
        inst.then_inc(self.sem[e], 1)
        tok = (e, self.cnt[e])
        self._reg(tok, list(r) + pr, list(w) + pw)
        self.pend[e] = ([], [])
        return tok

    def dma(self, out, in_, r=(), w=(), q="sp"):
        k = self.dq[self.dqi % len(self.dq)]
        self.dqi += 1
        self._wait(q, (k, self.cnt[k]))
        self._deps(q, r, w)
        inst = self.engs[q].dma_start(out=out, in_=in_)
        self.cnt[k] += 16
        inst.then_inc(self.sem[k], 16)
        tok = (k, self.cnt[k])
        self._reg(tok, r, w)
        return tok

    def barrier(self, engines=None):
        toks = [(k, v) for k, v in self.cnt.items() if v > 0]
        for e in (engines or self.engs):
            for t in toks:
                if t[0] != e:
                    self._wait(e, t)


def t5_bucket_np(dist):
    dist = np.asarray(dist, np.int64)
    d = np.maximum(dist, 1).astype(np.float32)
    large = 16 + (np.log(d / np.float32(16)) / np.float32(math.log(128 / 16)) * np.float32(16)).astype(np.int32)
    large = np.minimum(large, 31)
    return np.where(dist < 16, dist, large)


def build_nc(S, TT=512, TC=256, stop=None, nt_lim=None):
    NT = S // TT
    NTC = S // TC
    NKT = S // 128
    NB = S // BLK
    NCH = S // 128
    assert NB <= 32
    nc = bass.Bass("TRN2", target_bir_lowering=False)
    es = ExitStack()
    ctx = Ctx(nc, es)
    pe, act, dve, pool = nc.tensor, nc.scalar, nc.vector, nc.gpsimd

    def din(name, shape, dt=F32):
        return nc.dram_tensor(name, list(shape), dt, kind="ExternalInput").ap()

    def dscr(name, shape, dt=BF16):
        return nc.dram_tensor(name, list(shape), dt, kind="Internal").ap()

    x_d = din("x", [S, D])
    p_d = din("p", [S, 256])
    w_in_d = din("w_in", [D, 5120])
    w_ap_d = din("w_ap", [512, D])
    w_sp_d = din("w_sp", [512, D])
    w_out_d = din("w_out", [D, D])
    w_glu_d = din("w_glu", [512, D])
    w_pg_d = din("w_pg", [D, D])
    w_pp_d = din("w_pp", [256, D])
    are_rep_d = din("are_rep", [128, 2048])
    aim_rep_d = din("aim_rep", [128, 2048])
    ldt_rep_d = din("ldt_rep", [128, 2048])
    brT_d = din("brT", [128, 2048])
    biT_d = din("biT", [128, 2048])
    cre_d = din("cre_pad", [128, 2048])
    cim_d = din("cim_pad", [128, 2048])
    are_l_d = din("are_l", [128, 16])
    aim_l_d = din("aim_l", [128, 16])
    ldt_l_d = din("ldt_l", [128, 16])
    d_l_d = din("d_l", [128, 4])
    lng_d = din("lng", [128, D])
    lnb_d = din("lnb", [128, D])
    relb_d = din("relb", [32, 8])
    b31_d = din("b31rep", [128, 8])
    ident_d = din("ident", [128, 128], BF16)
    J_d = din("Jm", [128, 128], BF16)
    OH_d = din("OH", [32, 384], BF16)
    NEGM_d = din("NEGM", [8, 384])
    EOH_d = din("EOH", [32, S], BF16)
    PMASK_d = din("PMASK", [128, NCH * 32])
    PAST_d = din("PAST01", [128, NCH * 32])
    jidx_d = din("jidx", [128, 129])
    out_d = nc.dram_tensor("out", [S, D], F32, kind="ExternalOutput").ap()

    QT_d = dscr("QT", [4, 128, S])
    KT_d = dscr("KT", [4, 128, S])
    V_d = dscr("Vs", [S, 512])
    OA_d = dscr("OA", [8, 64, S])
    YS_d = dscr("YS", [8, 128, S])
    F_d = dscr("Fd", [16, 384])

    def sb(stack, name, shape, dt=F32):
        return stack.enter_context(nc.sbuf_tensor("sb_" + name, list(shape), dt))

    def ps(stack, name, shape, dt=F32):
        return stack.enter_context(nc.psum_tensor("ps_" + name, list(shape), dt))

    ident = sb(es, "ident", [128, 128], BF16)
    ident_b = Buf("ident")
    Jm = sb(es, "Jm", [128, 128], BF16)
    Jm_b = Buf("J")
    ctx.dma(ident[:], ident_d, w=[ident_b])
    ctx.dma(Jm[:], J_d, w=[Jm_b])

    cast_rr = Ring(["act", "dve", "pool"])

    def cast_copy(e, out, in_):
        if e == "act":
            return lambda: act.copy(out=out, in_=in_)
        if e == "dve":
            return lambda: dve.tensor_copy(out=out, in_=in_)
        return lambda: pool.tensor_copy(out=out, in_=in_)

    def load_weight(dst, dst_b, src, col0, ncols, KC, stage_ring, dcol0=0):
        cw = 2048 // KC
        for c0 in range(0, ncols, cw):
            w_ = min(cw, ncols - c0)
            st, stb = stage_ring.next()
            stv = st[:, 0:KC * w_].rearrange("p (kc c) -> p kc c", kc=KC)
            ctx.dma(stv, src[:, col0 + c0:col0 + c0 + w_].rearrange("(kc p) c -> p kc c", p=128), w=[stb])
            e = cast_rr.next()
            ctx.op(e, cast_copy(e, dst[:, :, dcol0 + c0:dcol0 + c0 + w_], stv), r=[stb], w=[dst_b])

    def mm_group(out_ap, out_b, pairs, r):
        n = len(pairs)
        for i, (l, rh) in enumerate(pairs):
            ctx.op("pe", lambda: pe.matmul(out_ap, lhsT=l, rhs=rh, start=(i == 0), stop=(i == n - 1)),
                   r=r, w=[out_b], sig=(i == n - 1))

    with ExitStack() as sA:
        _psT = ps(sA, "psT0", [128, 512])[:].bitcast(BF16)
        psT_v = [_psT[:, 0:512], _psT[:, 512:1024]]
        _psT_b = Buf("psT")
        psT_b = [_psT_b, _psT_b]
        mm_ring = Ring([(ps(sA, "mm%d" % i, [128, 512]), Buf("mm%d" % i)) for i in range(2)])
        psS_ring = Ring([[(ps(sA, "psS%d_%d" % (j, i), [128, 512]), Buf("psS%d_%d" % (j, i))) for i in range(2)] for j in range(2)])
        psY = ps(sA, "psY", [128, 512])
        _psY_b = Buf("psY")
        psY_b = [_psY_b] * 4

        Wqk = sb(sA, "Wqk", [128, 8, 1024], BF16); Wqk_b = Buf("Wqk")
        Wv = sb(sA, "Wv", [128, 8, 512], BF16); Wv_b = Buf("Wv")
        Wuz = sb(sA, "Wuz", [128, 8, 1024], BF16); Wuz_b = Buf("Wuz")
        Wglu = sb(sA, "Wglu", [128, 4, 1024], BF16); Wglu_b = Buf("Wglu")
        Wsp = sb(sA, "Wsp", [128, 4, 1024], BF16); Wsp_b = Buf("Wsp")
        Bre = sb(sA, "Bre", [128, 16, 128], BF16); Bim = sb(sA, "Bim", [128, 16, 128], BF16)
        Cre = sb(sA, "Cre", [128, 16, 128], BF16); Cim = sb(sA, "Cim", [128, 16, 128], BF16)
        BC_b = Buf("BC")
        T_b = Buf("T")
        rtab = sb(sA, "rtab", [128, 16, 128])
        Dg = sb(sA, "Dg", [128, 2, 4, 128], BF16)
        Ec_t = sb(sA, "Ec_t", [128, 16]); Es_t = sb(sA, "Es_t", [128, 16])
        Tcb = sb(sA, "Tcb", [128, 16, 128], BF16); Tsb = sb(sA, "Tsb", [128, 16, 128], BF16)
        mag_l = sb(sA, "mag_l", [128, 16]); mag_b = Buf("mag")
        d_l = sb(sA, "d_l", [128, 4]); d_b = Buf("d")
        car_r = sb(sA, "car_r", [128, 16]); car_i = sb(sA, "car_i", [128, 16])
        car_b = [Buf("car%d" % q) for q in range(2)]
        ctx.dma(d_l[:], d_l_d, w=[d_b])

        with ExitStack() as s0:
            Tc = sb(s0, "Tc", [128, 16, 128]); Ts = sb(s0, "Ts", [128, 16, 128])
            dhi = sb(s0, "dhi", [128, 4], BF16); dlo = sb(s0, "dlo", [128, 4])
            A_ = sb(s0, "pA", [128, 2048]); Bm = sb(s0, "pB", [128, 2048]); L_ = sb(s0, "pL", [128, 2048])
            t0 = sb(s0, "pt0", [128, 2048]); t1 = sb(s0, "pt1", [128, 2048]); t2 = sb(s0, "pt2", [128, 2048])
            t3 = sb(s0, "pt3", [128, 2048]); t4 = sb(s0, "pt4", [128, 2048]); t5 = sb(s0, "pt5", [128, 2048])
            ti = sb(s0, "pti", [128, 2048], I32)
            bR = sb(s0, "pbR", [128, 2048]); bI = sb(s0, "pbI", [128, 2048])
            al = sb(s0, "al", [128, 16]); bl = sb(s0, "bl", [128, 16]); ll = sb(s0, "ll", [128, 16])
            th = sb(s0, "th", [128, 16]); jx = sb(s0, "jx", [128, 129])
            th128 = sb(s0, "th128", [128, 16]); sc16 = sb(s0, "sc16", [128, 16])
            P = Buf("setup")

            for t_, d_ in ((A_, are_rep_d), (Bm, aim_rep_d), (L_, ldt_rep_d), (bR, brT_d), (bI, biT_d),
                           (t0, cre_d), (t1, cim_d)):
                ctx.dma(t_[:], d_, w=[P])
            for t_, d_ in ((al, are_l_d), (bl, aim_l_d), (ll, ldt_l_d), (jx, jidx_d)):
                ctx.dma(t_[:], d_, w=[P])

            def V(fn):
                ctx.op("dve", fn, r=[P], w=[P])

            def A(fn):
                ctx.op("act", fn, r=[P], w=[P])

            V(lambda: dve.tensor_copy(out=Cre[:].rearrange("p a b -> p (a b)"), in_=t0[:]))
            V(lambda: dve.tensor_scalar(out=Cim[:].rearrange("p a b -> p (a b)"), in0=t1[:], scalar1=-1.0, scalar2=None, op0=ALU.mult))

            def emit_sin(out, x, n, shift, xs):
                xi = ti[:, 0:n]
                V(lambda: dve.tensor_scalar(out=xs, in0=x, scalar1=shift, scalar2=None, op0=ALU.add))
                V(lambda: dve.tensor_scalar(out=xi, in0=xs, scalar1=1.0 / TWO_PI, scalar2=None, op0=ALU.mult))
                V(lambda: dve.tensor_copy(out=out, in_=xi))
                V(lambda: dve.scalar_tensor_tensor(out=out, in0=out, scalar=-TWO_PI, in1=xs, op0=ALU.mult, op1=ALU.add))
                V(lambda: dve.tensor_scalar(out=xs, in0=out, scalar1=math.pi, scalar2=-TWO_PI, op0=ALU.is_gt, op1=ALU.mult))
                V(lambda: dve.tensor_tensor(out=out, in0=out, in1=xs, op=ALU.add))
                V(lambda: dve.tensor_scalar(out=xs, in0=out, scalar1=-math.pi, scalar2=TWO_PI, op0=ALU.is_lt, op1=ALU.mult))
                V(lambda: dve.tensor_tensor(out=out, in0=out, in1=xs, op=ALU.add))
                V(lambda: dve.tensor_scalar(out=out, in0=out, scalar1=math.pi, scalar2=-math.pi, op0=ALU.min, op1=ALU.max))
                A(lambda: act.activation(out=out, in_=out, func=AF.Sin))

            A(lambda: act.activation(out=L_[:], in_=L_[:], func=AF.Exp))
            V(lambda: dve.tensor_tensor(out=t0[:], in0=L_[:], in1=A_[:], op=ALU.mult))
            A(lambda: act.activation(out=t0[:], in_=t0[:], func=AF.Exp))
            V(lambda: dve.tensor_tensor(out=t1[:], in0=L_[:], in1=Bm[:], op=ALU.mult))
            emit_sin(t2[:], t1[:], 2048, 0.0, t4[:])
            emit_sin(t3[:], t1[:], 2048, math.pi / 2, t4[:])
            V(lambda: dve.tensor_tensor(out=t3[:], in0=t3[:], in1=t0[:], op=ALU.mult))
            V(lambda: dve.tensor_tensor(out=t2[:], in0=t2[:], in1=t0[:], op=ALU.mult))
            V(lambda: dve.tensor_scalar(out=t3[:], in0=t3[:], scalar1=-1.0, scalar2=None, op0=ALU.add))
            V(lambda: dve.tensor_tensor(out=t0[:], in0=A_[:], in1=A_[:], op=ALU.mult))
            V(lambda: dve.tensor_tensor(out=t1[:], in0=Bm[:], in1=Bm[:], op=ALU.mult))
            V(lambda: dve.tensor_tensor(out=t0[:], in0=t0[:], in1=t1[:], op=ALU.add))
            V(lambda: dve.reciprocal(out=t0[:], in_=t0[:]))
            V(lambda: dve.tensor_tensor(out=t4[:], in0=t3[:], in1=A_[:], op=ALU.mult))
            V(lambda: dve.tensor_tensor(out=t1[:], in0=t2[:], in1=Bm[:], op=ALU.mult))
            V(lambda: dve.tensor_tensor(out=t4[:], in0=t4[:], in1=t1[:], op=ALU.add))
            V(lambda: dve.tensor_tensor(out=t4[:], in0=t4[:], in1=t0[:], op=ALU.mult))
            V(lambda: dve.tensor_tensor(out=t5[:], in0=t2[:], in1=A_[:], op=ALU.mult))
            V(lambda: dve.tensor_tensor(out=t1[:], in0=t3[:], in1=Bm[:], op=ALU.mult))
            V(lambda: dve.tensor_tensor(out=t5[:], in0=t5[:], in1=t1[:], op=ALU.subtract))
            V(lambda: dve.tensor_tensor(out=t5[:], in0=t5[:], in1=t0[:], op=ALU.mult))
            V(lambda: dve.tensor_tensor(out=t0[:], in0=t4[:], in1=bR[:], op=ALU.mult))
            V(lambda: dve.tensor_tensor(out=t1[:], in0=t5[:], in1=bI[:], op=ALU.mult))
            V(lambda: dve.tensor_tensor(out=Bre[:].rearrange("p a b -> p (a b)"), in0=t0[:], in1=t1[:], op=ALU.subtract))
            V(lambda: dve.tensor_tensor(out=t0[:], in0=t4[:], in1=bI[:], op=ALU.mult))
            V(lambda: dve.tensor_tensor(out=t1[:], in0=t5[:], in1=bR[:], op=ALU.mult))
            V(lambda: dve.tensor_tensor(out=Bim[:].rearrange("p a b -> p (a b)"), in0=t0[:], in1=t1[:], op=ALU.add))
            A(lambda: act.activation(out=ll[:], in_=ll[:], func=AF.Exp))
            V(lambda: dve.tensor_tensor(out=al[:], in0=ll[:], in1=al[:], op=ALU.mult))
            A(lambda: act.activation(out=mag_l[:], in_=al[:], func=AF.Exp))
            V(lambda: dve.tensor_tensor(out=th[:], in0=ll[:], in1=bl[:], op=ALU.mult))
            V(lambda: dve.tensor_tensor(out=t0[:].rearrange("p (a b) -> p a b", a=16),
                                        in0=th[:].rearrange("p (a o) -> p a o", o=1).to_broadcast([128, 16, 128]),
                                        in1=jx[:, 0:128].rearrange("p (o b) -> p o b", o=1).to_broadcast([128, 16, 128]), op=ALU.mult))
            emit_sin(Ts[:].rearrange("p a b -> p (a b)"), t0[:], 2048, 0.0, t1[:])
            emit_sin(Tc[:].rearrange("p a b -> p (a b)"), t0[:], 2048, math.pi / 2, t1[:])
            V(lambda: dve.tensor_scalar(out=th128[:], in0=th[:], scalar1=128.0, scalar2=None, op0=ALU.mult))
            emit_sin(Es_t[:], th128[:], 16, 0.0, sc16[:])
            emit_sin(Ec_t[:], th128[:], 16, math.pi / 2, sc16[:])
            V(lambda: dve.tensor_copy(out=Tcb[:], in_=Tc[:]))
            V(lambda: dve.tensor_copy(out=Tsb[:], in_=Ts[:]))
            V(lambda: dve.tensor_copy(out=rtab[:], in_=mag_l[:].rearrange("p (a o) -> p a o", o=1).to_broadcast([128, 16, 128])))
            V(lambda: dve.memset(rtab[:, :, 0:1], 0.0))
            ctx.op("dve", lambda: dve.tensor_copy(out=dhi[:], in_=d_l[:]), r=[P, d_b], w=[P])
            V(lambda: dve.tensor_tensor(out=dlo[:], in0=d_l[:], in1=dhi[:], op=ALU.subtract))
            for q_ in range(4):
                ctx.op("dve", lambda: dve.tensor_scalar(out=Dg[:, 0, q_, :], in0=ident[:], scalar1=dhi[:, q_:q_ + 1], scalar2=None, op0=ALU.mult), r=[P, ident_b], w=[P])
                ctx.op("dve", lambda: dve.tensor_scalar(out=Dg[:, 1, q_, :], in0=ident[:], scalar1=dlo[:, q_:q_ + 1], scalar2=None, op0=ALU.mult), r=[P, ident_b], w=[P])
            V(lambda: dve.memset(car_r[:], 0.0))
            V(lambda: dve.memset(car_i[:], 0.0))
            ctx.op("dve", lambda: dve.memset(t0[:, 0:1], 0.0), r=[P], w=[P, BC_b, T_b, mag_b] + car_b)

            ctx.barrier()
        with ExitStack() as s0:
            stage_ring = Ring([(sb(s0, "wst%d" % i, [128, 2048]), Buf("wst%d" % i)) for i in range(2)])
            load_weight(Wqk, Wqk_b, w_in_d, 0, 1024, 8, stage_ring)
            load_weight(Wv, Wv_b, w_in_d, 1024, 512, 8, stage_ring)
            load_weight(Wuz, Wuz_b, w_in_d, 2048, 1024, 8, stage_ring)
            load_weight(Wglu, Wglu_b, w_glu_d, 0, 1024, 4, stage_ring)
            load_weight(Wsp, Wsp_b, w_sp_d, 0, 1024, 4, stage_ring)
            ctx.barrier()

        x_ring = Ring([(sb(sA, "xa%d" % i, [128, D]), Buf("xa%d" % i)) for i in range(3)])
        xb = sb(sA, "xbA", [128, 4, D], BF16)
        xb_b = [Buf("xb%d" % i) for i in range(4)]
        xT_ring = Ring([(sb(sA, "xTA%d" % i, [128, 8, TT], BF16), Buf("xTA%d" % i)) for i in range(2)])
        uT_ring = Ring([(sb(sA, "uT%d" % i, [128, 4, TT], BF16), [Buf("uT%d_%d" % (i, q)) for q in range(4)]) for i in range(2)])
        szs_ring = Ring([(sb(sA, "szs%d" % i, [128, 4, TT], BF16), Buf("szs%d" % i)) for i in range(2)])
        qk_ring = Ring([(sb(sA, "qkst%d" % i, [128, TT], BF16), Buf("qkst%d" % i)) for i in range(2)])
        v_ring = Ring([(sb(sA, "vst%d" % i, [128, 512], BF16), Buf("vst%d" % i)) for i in range(2)])
        ys_ring = Ring([(sb(sA, "ysst%d" % i, [128, TT], BF16), Buf("ysst%d" % i)) for i in range(2)])
        bpr = sb(sA, "bpr", [128, 8, 128], BF16); bpi = sb(sA, "bpi", [128, 8, 128], BF16); tm2 = sb(sA, "tm2", [128, 8, 128], BF16)
        bp_b = Buf("bp")
        Sb_ring = Ring([((sb(sA, "Sbr%d" % i, [128, 8, 128], BF16), sb(sA, "Sbi%d" % i, [128, 8, 128], BF16)), Buf("Sb%d" % i)) for i in range(2)])
        g_ring = Ring([((sb(sA, "gr%d" % i, [128, 8, 128], BF16), sb(sA, "gi%d" % i, [128, 8, 128], BF16)), Buf("g%d" % i)) for i in range(2)])
        tp3 = sb(sA, "tp3", [128, 8, 128], BF16); tp4 = sb(sA, "tp4", [128, 8, 128], BF16)
        tp5 = sb(sA, "tp5", [128, 8, 128], BF16); tp6 = sb(sA, "tp6", [128, 8, 128], BF16); tpd_b = Buf("tpd")
        h_ring = Ring([((sb(sA, "hr%d" % i, [128, 8, 128], BF16), sb(sA, "mhi%d" % i, [128, 8, 128], BF16)), (Buf("hr%d" % i), Buf("hi%d" % i))) for i in range(2)])
        rc_r = sb(sA, "rc_r", [128, 8]); rc_i = sb(sA, "rc_i", [128, 8]); ctm = sb(sA, "ctm", [128, 8]); ctm2 = sb(sA, "ctm2", [128, 8])
        gT_ring = Ring([(sb(sA, "gT%d" % i, [128, 4, TT], BF16), Buf("gT%d" % i)) for i in range(2)])
        sigb_ring = Ring([(sb(sA, "sigb%d" % i, [128, TT]), Buf("sigb%d" % i)) for i in range(2)])
        yg = sb(sA, "yg", [128, 4, TT], BF16); yg_b = Buf("yg")

        def issue_xA(t_):
            for sub in range(4):
                xt, xtb = x_ring.next()
                ctx.dma(xt[:], x_d[t_ * TT + sub * 128:t_ * TT + (sub + 1) * 128, :], w=[xtb])
                ctx.op("act", lambda: act.copy(out=xb[:, sub, :], in_=xt[:]), r=[xtb], w=[xb_b[sub]])

        NTA = 0 if stop == 'setup' else (nt_lim or NT)

        def proj_tasks(t_):
            tk0 = t_ * TT
            xT_, xT_b_ = xT_ring.next()
            uT_, uT_b_ = uT_ring.next()
            szs_, szs_b_ = szs_ring.next()
            tasks_ = []

            def t_tr(dk):
                hf = dk % 2
                tp = psT_v[hf][:, 0:512]
                for sub in range(4):
                    ctx.op("pe", lambda: pe.transpose(tp[:, sub * 128:(sub + 1) * 128], xb[:, sub, dk * 128:(dk + 1) * 128], ident[:]),
                           r=[xb_b[sub], ident_b], w=[psT_b[hf]], sig=(sub == 3))
                ctx.op("act", lambda: act.copy(out=xT_[:, dk, :], in_=tp), r=[psT_b[hf]], w=[xT_b_])
                if dk == 7 and t_ + 1 < NTA:
                    issue_xA(t_ + 1)

            def t_qk(cc):
                pt, pb = mm_ring.next()
                mm_group(pt[:, 0:TT], pb, [(Wqk[:, dk, cc * 128:(cc + 1) * 128], xT_[:, dk, :]) for dk in range(8)], r=[Wqk_b, xT_b_])
                stg, stb = qk_ring.next()
                if cc < 4:
                    ctx.op("act", lambda: act.activation(out=stg[:], in_=pt[:, 0:TT], func=AF.Copy, scale=0.125), r=[pb], w=[stb])
                    ctx.dma(QT_d[cc, :, tk0:tk0 + TT], stg[:], r=[stb])
                else:
                    ctx.op("act", lambda: act.copy(out=stg[:], in_=pt[:, 0:TT]), r=[pb], w=[stb])
                    ctx.dma(KT_d[cc - 4, :, tk0:tk0 + TT], stg[:], r=[stb])

            def t_v(sub):
                pt, pb = mm_ring.next()
                mm_group(pt[:, 0:512], pb, [(xT_[:, dk, sub * 128:(sub + 1) * 128], Wv[:, dk, :]) for dk in range(8)], r=[Wv_b, xT_b_])
                stg, stb = v_ring.next()
                ctx.op("act", lambda: act.copy(out=stg[:], in_=pt[:, 0:512]), r=[pb], w=[stb])
                ctx.dma(V_d[tk0 + sub * 128:tk0 + (sub + 1) * 128, :], stg[:], r=[stb])

            def t_uz(cc):
                pt, pb = mm_ring.next()
                mm_group(pt[:, 0:TT], pb, [(Wuz[:, dk, cc * 128:(cc + 1) * 128], xT_[:, dk, :]) for dk in range(8)], r=[Wuz_b, xT_b_])
                if cc < 4:
                    ctx.op("act", lambda: act.copy(out=uT_[:, cc, :], in_=pt[:, 0:TT]), r=[pb], w=[uT_b_[cc]])
                else:
                    ctx.op("act", lambda: act.activation(out=szs_[:, cc - 4, :], in_=pt[:, 0:TT], func=AF.Silu), r=[pb], w=[szs_b_])

            for dk in range(8):
                tasks_.append(lambda dk=dk: t_tr(dk))
            for cc in range(8):
                tasks_.append(lambda cc=cc: t_uz(cc))
            for cc in range(8):
                tasks_.append(lambda cc=cc: t_qk(cc))
            for sub in range(4):
                tasks_.append(lambda sub=sub: t_v(sub))
            return tasks_, (uT_, uT_b_, szs_, szs_b_)

        def post_tasks(t_, gT_, gT_b_, szs_, szs_b_):
            tk0 = t_ * TT
            tasks_ = []

            def t_glu(oc):
                pa, pab = mm_ring.next()
                mm_group(pa[:, 0:TT], pab, [(Wglu[:, kc, oc * 128:(oc + 1) * 128], gT_[:, kc, :]) for kc in range(4)], r=[Wglu_b, gT_b_])
                pbk, pbb = mm_ring.next()
                mm_group(pbk[:, 0:TT], pbb, [(Wglu[:, kc, (oc + 4) * 128:(oc + 5) * 128], gT_[:, kc, :]) for kc in range(4)], r=[Wglu_b, gT_b_])
                sg, sgb = sigb_ring.next()
                ctx.op("act", lambda: act.activation(out=sg[:], in_=pbk[:, 0:TT], func=AF.Sigmoid), r=[pbb], w=[sgb])
                ctx.op("dve", lambda: dve.tensor_tensor(out=sg[:], in0=sg[:], in1=pa[:, 0:TT], op=ALU.mult), r=[sgb, pab], w=[sgb])
                ctx.op("dve", lambda: dve.tensor_tensor(out=yg[:, oc, :], in0=sg[:], in1=szs_[:, oc, :], op=ALU.mult), r=[sgb, szs_b_], w=[yg_b])

            def t_sp(oc):
                pt, pb = mm_ring.next()
                mm_group(pt[:, 0:TT], pb, [(Wsp[:, kc, oc * 128:(oc + 1) * 128], yg[:, kc, :]) for kc in range(4)], r=[Wsp_b, yg_b])
                stg, stb = ys_ring.next()
                ctx.op("act", lambda: act.copy(out=stg[:], in_=pt[:, 0:TT]), r=[pb], w=[stb])
                ctx.dma(YS_d[oc, :, tk0:tk0 + TT], stg[:], r=[stb])

            for oc in range(4):
                tasks_.append(lambda oc=oc: t_glu(oc))
            for oc in range(8):
                tasks_.append(lambda oc=oc: t_sp(oc))
            return tasks_

        post = []
        cur_tile = None
        if NTA:
            issue_xA(0)
            tasks, cur_tile = proj_tasks(0)
            for tk_ in tasks:
                tk_()
        for t in range(NTA):
            tok0 = t * TT
            uT, uT_b, szs, szs_b = cur_tile
            gT, gT_b = gT_ring.next()
            tasks = list(post)
            post_n = list(post)
            if t + 1 < NTA:
                ptk, cur_tile = proj_tasks(t + 1)
                tasks += ptk
            per_unit = (len(tasks) + 7) // 8
            def ssm_stage1a(ti_, s_, hf):
                tsl = slice(s_ * 128, (s_ + 1) * 128)
                (Sbr, Sbi), Sb_b = Sb_ring.next()
                for qi in range(2):
                    q = 2 * hf + qi
                    (Sre, Sre_b), (Sim, Sim_b) = psS_ring.next()
                    for i in range(4):
                        pr = 4 * q + i
                        ctx.op("pe", lambda: pe.matmul(Sre[:, i * 128:(i + 1) * 128], lhsT=Bre[:, pr, :], rhs=ti_[0][:, q, tsl], start=True, stop=True),
                               r=[BC_b, ti_[1][q]], w=[Sre_b], sig=(i == 3))
                    for i in range(4):
                        pr = 4 * q + i
                        ctx.op("pe", lambda: pe.matmul(Sim[:, i * 128:(i + 1) * 128], lhsT=Bim[:, pr, :], rhs=ti_[0][:, q, tsl], start=True, stop=True),
                               r=[BC_b, ti_[1][q]], w=[Sim_b], sig=(i == 3))
                    ctx.op("act", lambda: act.copy(out=Sbr[:, qi * 4:(qi + 1) * 4, :].rearrange("p a b -> p (a b)"), in_=Sre[:, 0:512]), r=[Sre_b], w=[Sb_b])
                    ctx.op("act", lambda: act.copy(out=Sbi[:, qi * 4:(qi + 1) * 4, :].rearrange("p a b -> p (a b)"), in_=Sim[:, 0:512]), r=[Sim_b], w=[Sb_b])
                return s_, hf, Sbr, Sbi, Sb_b

            def ssm_stage1b(s_, hf, Sbr, Sbi, Sb_b):
                p0 = 8 * hf
                Tcq = Tcb[:, p0:p0 + 8, :]
                Tsq = Tsb[:, p0:p0 + 8, :]
                cb = car_b[hf]
                ctx.op("dve", lambda: dve.tensor_tensor(out=bpr[:], in0=Sbr[:], in1=Tcq, op=ALU.mult), r=[Sb_b, T_b], w=[bp_b])
                ctx.op("dve", lambda: dve.tensor_tensor(out=tm2[:], in0=Sbi[:], in1=Tsq, op=ALU.mult), r=[Sb_b, T_b], w=[bp_b])
                ctx.op("dve", lambda: dve.tensor_tensor(out=bpr[:], in0=bpr[:], in1=tm2[:], op=ALU.add), r=[bp_b], w=[bp_b])
                ctx.op("dve", lambda: dve.tensor_tensor(out=bpi[:], in0=Sbi[:], in1=Tcq, op=ALU.mult), r=[Sb_b, T_b, bp_b], w=[bp_b])
                ctx.op("dve", lambda: dve.tensor_tensor(out=tm2[:], in0=Sbr[:], in1=Tsq, op=ALU.mult), r=[Sb_b, T_b, bp_b], w=[bp_b])
                ctx.op("dve", lambda: dve.tensor_tensor(out=bpi[:], in0=bpi[:], in1=tm2[:], op=ALU.subtract), r=[bp_b], w=[bp_b])
                ctx.op("dve", lambda: dve.tensor_tensor(out=rc_r[:], in0=mag_l[:, p0:p0 + 8], in1=car_r[:, p0:p0 + 8], op=ALU.mult), r=[mag_b, cb], w=[cb])
                ctx.op("dve", lambda: dve.tensor_tensor(out=rc_i[:], in0=mag_l[:, p0:p0 + 8], in1=car_i[:, p0:p0 + 8], op=ALU.mult), r=[mag_b, cb], w=[cb])
                ctx.op("dve", lambda: dve.tensor_tensor(out=bpr[:, :, 0], in0=bpr[:, :, 0], in1=rc_r[:], op=ALU.add), r=[bp_b, cb], w=[bp_b])
                ctx.op("dve", lambda: dve.tensor_tensor(out=bpi[:, :, 0], in0=bpi[:, :, 0], in1=rc_i[:], op=ALU.add), r=[bp_b, cb], w=[bp_b])
                (gr, gi), g_b = g_ring.next()
                rt = rtab[:, p0:p0 + 8, :].rearrange("p a b -> p (a b)")
                ctx.op("dve", lambda: dve.tensor_tensor_scan(out=gr[:].rearrange("p a b -> p (a b)"), data0=rt, data1=bpr[:].rearrange("p a b -> p (a b)"),
                                                             initial=0.0, op0=ALU.mult, op1=ALU.add), r=[bp_b, mag_b], w=[g_b])
                ctx.op("dve", lambda: dve.tensor_tensor_scan(out=gi[:].rearrange("p a b -> p (a b)"), data0=rt, data1=bpi[:].rearrange("p a b -> p (a b)"),
                                                             initial=0.0, op0=ALU.mult, op1=ALU.add), r=[bp_b, mag_b], w=[g_b])
                Ec = Ec_t[:, p0:p0 + 8]
                Es = Es_t[:, p0:p0 + 8]
                ctx.op("dve", lambda: dve.tensor_tensor(out=ctm[:], in0=Ec, in1=gr[:, :, 127], op=ALU.mult), r=[T_b, g_b, cb], w=[cb])
                ctx.op("dve", lambda: dve.tensor_tensor(out=ctm2[:], in0=Es, in1=gi[:, :, 127], op=ALU.mult), r=[T_b, g_b, cb], w=[cb])
                ctx.op("dve", lambda: dve.tensor_tensor(out=car_r[:, p0:p0 + 8], in0=ctm[:], in1=ctm2[:], op=ALU.subtract), r=[cb], w=[cb])
                ctx.op("dve", lambda: dve.tensor_tensor(out=ctm[:], in0=Ec, in1=gi[:, :, 127], op=ALU.mult), r=[T_b, g_b, cb], w=[cb])
                ctx.op("dve", lambda: dve.tensor_tensor(out=ctm2[:], in0=Es, in1=gr[:, :, 127], op=ALU.mult), r=[T_b, g_b, cb], w=[cb])
                ctx.op("dve", lambda: dve.tensor_tensor(out=car_i[:, p0:p0 + 8], in0=ctm[:], in1=ctm2[:], op=ALU.add), r=[cb], w=[cb])
                return s_, hf, gr, gi, g_b

            def ssm_stage2(ti_, s_, hf, gr, gi, g_b):
                tsl = slice(s_ * 128, (s_ + 1) * 128)
                p0 = 8 * hf
                Tcq = Tcb[:, p0:p0 + 8, :]
                Tsq = Tsb[:, p0:p0 + 8, :]
                (hr, mhi), (h_b, hi_b) = h_ring.next()
                ctx.op("dve", lambda: dve.tensor_tensor(out=tp3[:], in0=gr[:], in1=Tcq, op=ALU.mult), r=[g_b, T_b], w=[tpd_b])
                ctx.op("dve", lambda: dve.tensor_tensor(out=tp4[:], in0=gi[:], in1=Tsq, op=ALU.mult), r=[g_b, T_b], w=[tpd_b])
                ctx.op("dve", lambda: dve.tensor_tensor(out=hr[:], in0=tp3[:], in1=tp4[:], op=ALU.subtract), r=[tpd_b], w=[h_b])
                ctx.op("dve", lambda: dve.tensor_tensor(out=tp5[:], in0=gi[:], in1=Tcq, op=ALU.mult), r=[g_b, T_b], w=[tpd_b])
                ctx.op("dve", lambda: dve.tensor_tensor(out=tp6[:], in0=gr[:], in1=Tsq, op=ALU.mult), r=[g_b, T_b], w=[tpd_b])
                ctx.op("dve", lambda: dve.tensor_tensor(out=mhi[:], in0=tp5[:], in1=tp6[:], op=ALU.add), r=[tpd_b], w=[hi_b])
                for qi in range(2):
                    q = 2 * hf + qi
                    yreg = psY[:, qi * 128:(qi + 1) * 128]
                    ctx.op("pe", lambda: pe.matmul(yreg, lhsT=Dg[:, 0, q, :], rhs=ti_[0][:, q, tsl], start=True, stop=False), r=[BC_b, ti_[1][q]], w=[_psY_b], sig=False)
                    ctx.op("pe", lambda: pe.matmul(yreg, lhsT=Dg[:, 1, q, :], rhs=ti_[0][:, q, tsl], start=False, stop=False), r=[BC_b, ti_[1][q]], w=[_psY_b], sig=False)
                    for i in range(4):
                        pr = 4 * q + i
                        ctx.op("pe", lambda: pe.matmul(yreg, lhsT=Cre[:, pr, :], rhs=hr[:, qi * 4 + i, :], start=False, stop=False),
                               r=[BC_b, h_b], w=[_psY_b], sig=False)
                        ctx.op("pe", lambda: pe.matmul(yreg, lhsT=Cim[:, pr, :], rhs=mhi[:, qi * 4 + i, :], start=False, stop=(i == 3)),
                               r=[BC_b, hi_b], w=[_psY_b], sig=(i == 3 and qi == 1))
                ctx.op("act", lambda: act.activation(out=ti_[2][:, 2 * hf:2 * hf + 2, tsl], in_=psY[:, 0:256].rearrange("p (a b) -> p a b", a=2), func=AF.Gelu),
                       r=[_psY_b], w=[ti_[3]])

            units = [(s_, hf) for s_ in range(TT // 128) for hf in range(2)]
            nu = len(units)
            ti_cur = (uT, uT_b, gT, gT_b)
            ti_nxt = (cur_tile[0], cur_tile[1]) if t + 1 < NTA else None
            if t + 1 < NTA:
                assert len(post_n) + 16 <= 6 * per_unit
            if t == 0:
                a_res = {0: ssm_stage1a(ti_cur, *units[0]), 1: ssm_stage1a(ti_cur, *units[1])}
                b_res = {0: ssm_stage1b(*a_res.pop(0))}
            else:
                a_res, b_res = next_a, next_b
            next_a, next_b = {}, {}
            for k in range(nu):
                if k + 2 < nu:
                    a_res[k + 2] = ssm_stage1a(ti_cur, *units[k + 2])
                elif ti_nxt is not None:
                    next_a[k + 2 - nu] = ssm_stage1a(ti_nxt, *units[k + 2 - nu])
                if k + 1 < nu:
                    b_res[k + 1] = ssm_stage1b(*a_res.pop(k + 1))
                elif ti_nxt is not None:
                    next_b[0] = ssm_stage1b(*next_a.pop(0))
                ssm_stage2(ti_cur, *b_res.pop(k))
                for _ in range(per_unit):
                    if tasks:
                        tasks.pop(0)()
            while tasks:
                tasks.pop(0)()
            post = post_tasks(t, gT, gT_b, szs, szs_b)
        for tk_ in post:
            tk_()
        ctx.barrier()

    with ExitStack() as sB:
        S_ring = Ring([(ps(sB, "psSc%d" % i, [128, 512]), Buf("psSc%d" % i)) for i in range(3)])
        O_ring = Ring([(ps(sB, "psO%d" % i, [128, 512]), Buf("psO%d" % i)) for i in range(2)])
        psG = ps(sB, "psG", [128, 512]); psG_b = Buf("psG")
        psM = ps(sB, "psM", [128, 512]); psM_v = psM[:].bitcast(BF16); psM_b = Buf("psM")
        psBc = ps(sB, "psBc", [128, 512]); psBc_b = Buf("psBc")

        PMASK = sb(sB, "PMASK", [128, NCH, 32]); PAST = sb(sB, "PAST", [128, NCH, 32]); cm_b = Buf("cmask")
        ctx.dma(PMASK[:].rearrange("p a b -> p (a b)"), PMASK_d, w=[cm_b])
        ctx.dma(PAST[:].rearrange("p a b -> p (a b)"), PAST_d, w=[cm_b])
        b31 = sb(sB, "b31", [128, 8])
        ctx.dma(b31[:], b31_d, w=[cm_b])
        ones_f = sb(sB, "ones_f", [128, 64], BF16)
        ctx.op("dve", lambda: dve.memset(ones_f[:], 1.0), w=[cm_b])

        Htiles = sb(sB, "Htiles", [128, 8, 3, 128], BF16); H_b = Buf("H")
        KA = [sb(sB, "KA%d" % i, [96, S], BF16) for i in range(2)]
        QA = [sb(sB, "QA%d" % i, [96, S], BF16) for i in range(2)]
        VE = [sb(sB, "VE%d" % i, [128, NKT, 65], BF16) for i in range(2)]
        KA_b = [Buf("KA%d" % i) for i in range(2)]
        QA_b = [Buf("QA%d" % i) for i in range(2)]
        QM_b = [[Buf("QM%d_%d" % (i, t)) for t in range(NT)] for i in range(2)]
        VE_b = [Buf("VE%d" % i) for i in range(2)]

        def load_head(h):
            i = h % 2
            pr, hh = h // 2, h % 2
            ctx.dma(KA[i][0:64, :], KT_d[pr, hh * 64:(hh + 1) * 64, :], w=[KA_b[i]])
            ctx.dma(QA[i][0:64, :], QT_d[pr, hh * 64:(hh + 1) * 64, :], w=[QA_b[i]])
            ctx.dma(VE[i][:, :, 0:64], V_d[:, h * 64:(h + 1) * 64].rearrange("(kt p) d -> p kt d", p=128), w=[VE_b[i]])

        NHB = 0 if stop in ('setup', 'A') else (NH if stop != 'B1' else 1)
        ctx.dma(KA[0][64:96, :], EOH_d, w=[KA_b[0]])
        ctx.op("pool", lambda: pool.memset(VE[0][:, :, 64:65], 1.0), w=[VE_b[0]])
        if NHB:
            load_head(0)
        ctx.dma(KA[1][64:96, :], EOH_d, w=[KA_b[1]])
        ctx.op("pool", lambda: pool.memset(VE[1][:, :, 64:65], 1.0), w=[VE_b[1]])
        if True:
            sb0 = sB
            relb_f = sb(sb0, "relb_f", [32, 8]); relb_h = sb(sb0, "relb_h", [32, 8], BF16)
            OHt = sb(sb0, "OHt", [32, 384], BF16); negm = sb(sb0, "negm", [8, 384])
            Ff = sb(sb0, "Ff", [8, 384]); Fh = sb(sb0, "Fh", [8, 2, 384], BF16)
            Pb = Buf("biasprep")
            ctx.dma(relb_f[:], relb_d, w=[Pb]); ctx.dma(OHt[:], OH_d, w=[Pb]); ctx.dma(negm[:], NEGM_d, w=[Pb])
            ctx.op("dve", lambda: dve.tensor_copy(out=relb_h[:], in_=relb_f[:]), r=[Pb], w=[Pb])
            ctx.op("pe", lambda: pe.matmul(psG[0:8, 0:384], lhsT=relb_h[:, :], rhs=OHt[:, :], start=True, stop=True), r=[Pb], w=[psG_b])
            ctx.op("dve", lambda: dve.tensor_tensor(out=Ff[:], in0=psG[0:8, 0:384], in1=negm[:], op=ALU.add), r=[psG_b, Pb], w=[Pb])
            ctx.op("dve", lambda: dve.tensor_copy(out=Fh[:, 0, :], in_=Ff[:]), r=[Pb], w=[Pb])
            ctx.op("dve", lambda: dve.tensor_scalar(out=Fh[:, 1, :], in0=Ff[:], scalar1=Ff[:, 380:381], scalar2=None, op0=ALU.subtract), r=[Pb], w=[Pb])
            ctx.dma(F_d[0:8, :], Fh[:, 0, :], r=[Pb], w=[Pb])
            ctx.dma(F_d[8:16, :], Fh[:, 1, :], r=[Pb], w=[Pb])
            for h in range(NH):
                for j, (row, off) in enumerate(((h, 0), (h, 128), (8 + h, 128))):
                    src = bass.AP(tensor=F_d.tensor, offset=row * 384 + off, ap=[[1, 128], [1, 128]])
                    ctx.dma(Htiles[:, h, j, :], src, r=[Pb], w=[H_b])
        ksum = sb(sB, "ksum", [64, 32]); kmT = [sb(sB, "kmT%d" % i, [64, 32], BF16) for i in range(2)]
        km_b = [Buf("km%d" % i) for i in range(2)]
        gm = sb(sB, "gm", [128, 4, 32]); ns_ = sb(sB, "ns", [128, 4, 32]); thr8 = sb(sB, "thr8", [128, 4, 8]); gm_b = Buf("gm")
        Mq_ring = Ring([(sb(sB, "Mq%d" % i, [128, 4, 96], BF16), Buf("Mq%d" % i)) for i in range(2)])
        for mq, mqb in Mq_ring.items:
            ctx.op("pool", lambda: pool.memset(mq[:], 0.0), w=[mqb])
        PT_ring = Ring([(sb(sB, "PT%d" % i, [128, 512], BF16), Buf("PT%d" % i)) for i in range(4)])
        osb_ring = Ring([(sb(sB, "osb%d" % i, [65, 512]), Buf("osb%d" % i)) for i in range(2)])
        rec_ring = Ring([(sb(sB, "rec%d" % i, [65, 512], BF16), Buf("rec%d" % i)) for i in range(2)])
        oo_ring = Ring([(sb(sB, "oo%d" % i, [64, 512], BF16), Buf("oo%d" % i)) for i in range(2)])

        def prep_head(h):
            i = h % 2
            Kh = KA[i]
            ctx.op("dve", lambda: dve.tensor_reduce(out=ksum[:, 0:NB], in_=Kh[0:64, :].rearrange("p (a b) -> p a b", b=BLK), axis=AX.X, op=ALU.add),
                   r=[KA_b[i]], w=[km_b[i]])
            if NB < 32:
                ctx.op("dve", lambda: dve.memset(ksum[:, NB:32], 0.0), r=[], w=[km_b[i]])
            ctx.op("dve", lambda: dve.tensor_scalar(out=kmT[i][:], in0=ksum[:], scalar1=1.0 / BLK, scalar2=None, op0=ALU.mult), r=[km_b[i]], w=[km_b[i]])

        def gate1(h, T):
            i = h % 2
            Qh = QA[i]
            c0 = 4 * T
            q0 = T * TT
            for ci in range(4):
                ctx.op("pe", lambda: pe.matmul(psG[:, ci * 32:(ci + 1) * 32], lhsT=Qh[0:64, q0 + ci * 128:q0 + (ci + 1) * 128], rhs=kmT[i][:, :], start=True, stop=True),
                       r=[QA_b[i], km_b[i]], w=[psG_b], sig=(ci == 3))
            ctx.op("dve", lambda: dve.tensor_tensor(out=gm[:], in0=psG[:, 0:128].rearrange("p (a b) -> p a b", a=4), in1=PMASK[:, c0:c0 + 4, :], op=ALU.add),
                   r=[psG_b, cm_b], w=[gm_b])
            for ci in range(4):
                ctx.op("dve", lambda: dve.max(out=thr8[:, ci, :], in_=gm[:, ci, :]), r=[gm_b], w=[gm_b])
            ctx.op("dve", lambda: dve.tensor_tensor(out=ns_[:], in0=gm[:], in1=thr8[:, :, 2:3].to_broadcast([128, 4, 32]), op=ALU.is_lt), r=[gm_b], w=[gm_b])
            ctx.op("dve", lambda: dve.tensor_scalar(out=ns_[:], in0=ns_[:], scalar1=NEG, scalar2=b31[:, h:h + 1], op0=ALU.mult, op1=ALU.add), r=[gm_b, cm_b], w=[gm_b])
            Mq, Mq_b = Mq_ring.next()
            ctx.op("dve", lambda: dve.tensor_tensor(out=Mq[:, :, 64:96], in0=ns_[:], in1=PAST[:, c0:c0 + 4, :], op=ALU.mult), r=[gm_b, cm_b], w=[Mq_b])
            return Mq, Mq_b

        def gate2(h, T, Mq, Mq_b):
            i = h % 2
            q0 = T * TT
            for ci in range(4):
                ctx.op("pe", lambda: pe.transpose(psM_v[0:96, ci * 128:(ci + 1) * 128], Mq[:, ci, :], ident[:]), r=[Mq_b, ident_b], w=[psM_b], sig=(ci == 3))
            ctx.op("dve", lambda: dve.tensor_copy(out=QA[i][64:96, q0:q0 + TT], in_=psM_v[64:96, 0:512]), r=[psM_b], w=[QM_b[i][T]])

        work = [(h, T) for h in range(NHB) for T in range(NT)]
        LA = 2
        if work:
            prep_head(0)
            g = gate1(0, 0)
            gate2(0, 0, *g)
        for wi, (h, T) in enumerate(work):
            i = h % 2
            if T == 0 and h + 1 < NHB:
                load_head(h + 1)
            Kh, Qh, Vh = KA[i], QA[i], VE[i]
            q0 = T * TT
            nxt_w = work[wi + 1] if wi + 1 < len(work) else None
            g = None
            if nxt_w is not None:
                if nxt_w[1] == 0:
                    prep_head(nxt_w[0])
                g = gate1(*nxt_w)
            Ops, Ops_b = O_ring.next()
            nkt = 4 * T + 4

            def emit_S(kt):
                cl = max(0, kt - 4 * T)
                cs = slice(cl * 128, 512)
                Sps, Sps_b = S_ring.next()
                adds = []
                ci0 = kt - 4 * T
                if 0 <= ci0 <= 3:
                    adds.append((ci0, 0))
                ci1 = kt + 1 - 4 * T
                if 0 <= ci1 <= 3:
                    adds.append((ci1, 1 if kt % 2 == 0 else 2))
                ctx.op("pe", lambda: pe.matmul(Sps[:, cs], lhsT=Kh[0:96, kt * 128:(kt + 1) * 128], rhs=Qh[0:96, q0 + cl * 128:q0 + 512], start=True, stop=(len(adds) == 0)),
                       r=[KA_b[i], QA_b[i], QM_b[i][T]], w=[Sps_b], sig=(len(adds) == 0))
                for ai, (cidx, j) in enumerate(adds):
                    last = ai == len(adds) - 1
                    ctx.op("pe", lambda: pe.matmul(Sps[:, cidx * 128:(cidx + 1) * 128], lhsT=Jm[:, :], rhs=Htiles[:, h, j, :], start=False, stop=last),
                           r=[Jm_b, H_b], w=[Sps_b], sig=last)
                return kt, cs, Sps, Sps_b

            def emit_PV(kt, cs, Sps, Sps_b):
                PT, PT_b = PT_ring.next()
                ctx.op("act", lambda: act.activation(out=PT[:, cs], in_=Sps[:, cs], func=AF.Exp), r=[Sps_b], w=[PT_b])
                ctx.op("pe", lambda: pe.matmul(Ops[0:65, cs], lhsT=Vh[:, kt, 0:65], rhs=PT[:, cs], start=(kt == 0), stop=(kt == nkt - 1)),
                       r=[VE_b[i], PT_b], w=[Ops_b], sig=(kt == nkt - 1))

            pend = []
            for kt in range(nkt):
                pend.append(emit_S(kt))
                if len(pend) > LA:
                    emit_PV(*pend.pop(0))
            while pend:
                emit_PV(*pend.pop(0))
            if g is not None:
                gate2(nxt_w[0], nxt_w[1], *g)
            osb, osb_b = osb_ring.next()
            ctx.op("act", lambda: act.copy(out=osb[:], in_=Ops[0:65, :]), r=[Ops_b], w=[osb_b])
            rec, rec_b = rec_ring.next()
            ctx.op("dve", lambda: dve.reciprocal(out=osb[64:65, :], in_=osb[64:65, :]), r=[osb_b], w=[osb_b])
            ctx.op("dve", lambda: dve.tensor_copy(out=rec[64:65, :], in_=osb[64:65, :]), r=[osb_b], w=[rec_b])
            ctx.op("pe", lambda: pe.matmul(psBc[0:64, :], lhsT=ones_f[64:65, 0:64], rhs=rec[64:65, :], start=True, stop=True), r=[rec_b, cm_b], w=[psBc_b])
            oo, oo_b = oo_ring.next()
            ctx.op("dve", lambda: dve.tensor_tensor(out=oo[:], in0=osb[0:64, :], in1=psBc[0:64, :], op=ALU.mult), r=[osb_b, psBc_b], w=[oo_b])
            ctx.dma(OA_d[h, :, q0:q0 + TT], oo[:], r=[oo_b])
        ctx.barrier()

    with ExitStack() as sC:
        psT_v = [ps(sC, "psTc%d" % i, [128, 512])[:].bitcast(BF16) for i in range(2)]
        psT_b = [Buf("psTc0"), Buf("psTc1")]
        mm_ring = Ring([(ps(sC, "mc%d" % i, [128, 512]), Buf("mc%d" % i)) for i in range(6)])
        tk_ring = mm_ring

        Wz = sb(sC, "Wz", [128, 8, 2560], BF16); Wz_b = Buf("Wz")
        Wap = sb(sC, "Wap", [128, 4, 1024], BF16); Wap_b = Buf("Wap")
        Wout = sb(sC, "Wout", [128, 8, 1024], BF16); Wout_b = Buf("Wout")
        Wpg = sb(sC, "Wpg", [128, 8, 1024], BF16); Wpg_b = Buf("Wpg")
        Wpp = sb(sC, "Wpp", [128, 2, 1024], BF16); Wpp_b = Buf("Wpp")
        lng = sb(sC, "lng", [128, D]); lnb = sb(sC, "lnb", [128, D]); ln_b = Buf("ln")
        ctx.dma(lng[:], lng_d, w=[ln_b]); ctx.dma(lnb[:], lnb_d, w=[ln_b])
        stage_ring = Ring([(sb(sC, "wstc%d" % i, [128, 2048]), Buf("wstc%d" % i)) for i in range(2)])
        load_weight(Wz, Wz_b, w_in_d, 1536, 512, 8, stage_ring, dcol0=0)
        load_weight(Wz, Wz_b, w_in_d, 3072, 2048, 8, stage_ring, dcol0=512)
        load_weight(Wap, Wap_b, w_ap_d, 0, 1024, 4, stage_ring)
        load_weight(Wpg, Wpg_b, w_pg_d, 0, 1024, 8, stage_ring)
        load_weight(Wpp, Wpp_b, w_pp_d, 0, 1024, 2, stage_ring)
        load_weight(Wout, Wout_b, w_out_d, 0, 1024, 8, stage_ring)

        NS = TC // 128
        x_ring = Ring([(sb(sC, "xc%d" % i, [128, D]), Buf("xc%d" % i)) for i in range(2 * NS)])
        xb = sb(sC, "xbC", [128, NS, D], BF16); xb_b = [Buf("xbC%d" % i) for i in range(NS)]
        xTc_ring = Ring([(sb(sC, "xTC%d" % i, [128, 8, TC], BF16), Buf("xTC%d" % i)) for i in range(2)])
        p_ring = Ring([(sb(sC, "pc%d" % i, [128, 256]), Buf("pc%d" % i)) for i in range(NS)])
        pbf = sb(sC, "pbf", [128, NS, 256], BF16); pbf_b = [Buf("pbf%d" % i) for i in range(NS)]
        pTc_ring = Ring([(sb(sC, "pT%d" % i, [128, 2, TC], BF16), Buf("pT%d" % i)) for i in range(2)])
        sza = sb(sC, "sza", [128, 4, TC], BF16); sza_b = Buf("sza")
        sga = sb(sC, "sga", [128, 8, TC], BF16); sga_b = Buf("sga")
        sgs = sb(sC, "sgs", [128, 8, TC], BF16); sgs_b = Buf("sgs")
        oa_ring = Ring([(sb(sC, "oat%d" % i, [128, 4, TC], BF16), Buf("oat%d" % i)) for i in range(2)])
        oz = sb(sC, "oz", [128, 4, TC], BF16); oz_b = Buf("oz")
        ys_ring = Ring([(sb(sC, "yst%d" % i, [128, 8, TC], BF16), Buf("yst%d" % i)) for i in range(2)])
        mg_ring = Ring([(sb(sC, "merge%d" % i, [128, 8, TC], BF16), Buf("merge%d" % i)) for i in range(2)])
        ta_ring = Ring([(sb(sC, "tma%d" % i, [128, TC]), Buf("tma%d" % i)) for i in range(2)])
        tb_ring = Ring([(sb(sC, "tmb%d" % i, [128, TC]), Buf("tmb%d" % i)) for i in range(2)])
        s_ring = Ring([(sb(sC, "srow%d" % i, [128, D]), Buf("srow%d" % i)) for i in range(2)])
        o_ring = Ring([(sb(sC, "orow%d" % i, [128, D]), Buf("orow%d" % i)) for i in range(2)])
        st_ring = Ring([((sb(sC, "bst%d" % i, [128, 2, 6]), sb(sC, "bmv%d" % i, [128, 2]), sb(sC, "brs%d" % i, [128, 2])), Buf("bst%d" % i)) for i in range(2)])

        def issue_xC(t_):
            tk0 = t_ * TC
            xts_ = []
            for sub in range(NS):
                xt, xtb = x_ring.next()
                xts_.append((xt, xtb))
                ctx.dma(xt[:], x_d[tk0 + sub * 128:tk0 + (sub + 1) * 128, :], w=[xtb])
                ctx.op("pool", lambda: pool.tensor_copy(out=xb[:, sub, :], in_=xt[:]), r=[xtb], w=[xb_b[sub]])
                pt_, ptb = p_ring.next()
                ctx.dma(pt_[:], p_d[tk0 + sub * 128:tk0 + (sub + 1) * 128, :], w=[ptb])
                ctx.op("pool", lambda: pool.tensor_copy(out=pbf[:, sub, :], in_=pt_[:]), r=[ptb], w=[pbf_b[sub]])
            oat_, oat_b_ = oa_ring.next()
            ctx.dma(oat_[:], OA_d[:, :, tk0:tk0 + TC].rearrange("(pr hh) d t -> (hh d) pr t", hh=2), w=[oat_b_])
            yst_, yst_b_ = ys_ring.next()
            ctx.dma(yst_[:], YS_d[:, :, tk0:tk0 + TC].rearrange("c p t -> p c t"), w=[yst_b_])
            return xts_, (oat_, oat_b_), (yst_, yst_b_)

        NTCC = 0 if stop in ('setup', 'A', 'B', 'B1') else NTC
        if NTCC:
            nxt = issue_xC(0)
        for t in range(NTCC):
            tok0 = t * TC
            xts, (oat, oat_b), (yst, yst_b) = nxt
            xT, xT_b = xTc_ring.next()
            pT, pT_b = pTc_ring.next()
            merge, merge_b = mg_ring.next()
            for dk in range(8):
                hf = dk % 2
                tp = psT_v[hf][:, 0:TC]
                for sub in range(NS):
                    ctx.op("pe", lambda: pe.transpose(tp[:, sub * 128:(sub + 1) * 128], xb[:, sub, dk * 128:(dk + 1) * 128], ident[:]),
                           r=[xb_b[sub], ident_b], w=[psT_b[hf]], sig=(sub == NS - 1))
                ctx.op("act", lambda: act.copy(out=xT[:, dk, :], in_=tp), r=[psT_b[hf]], w=[xT_b])
            for kc in range(2):
                hf = kc % 2
                tp = psT_v[hf][:, 0:TC]
                for sub in range(NS):
                    ctx.op("pe", lambda: pe.transpose(tp[:, sub * 128:(sub + 1) * 128], pbf[:, sub, kc * 128:(kc + 1) * 128], ident[:]),
                           r=[pbf_b[sub], ident_b], w=[psT_b[hf]], sig=(sub == NS - 1))
                ctx.op("act", lambda: act.copy(out=pT[:, kc, :], in_=tp), r=[psT_b[hf]], w=[pT_b])
            if t + 1 < NTCC:
                nxt = issue_xC(t + 1)
            for cc in range(20):
                pt, pb = mm_ring.next()
                mm_group(pt[:, 0:TC], pb, [(Wz[:, dk, cc * 128:(cc + 1) * 128], xT[:, dk, :]) for dk in range(8)], r=[Wz_b, xT_b])
                if cc < 4:
                    ctx.op("act", lambda: act.activation(out=sza[:, cc, :], in_=pt[:, 0:TC], func=AF.Silu), r=[pb], w=[sza_b])
                elif cc < 12:
                    ctx.op("act", lambda: act.activation(out=sga[:, cc - 4, :], in_=pt[:, 0:TC], func=AF.Sigmoid), r=[pb], w=[sga_b])
                else:
                    ctx.op("act", lambda: act.activation(out=sgs[:, cc - 12, :], in_=pt[:, 0:TC], func=AF.Sigmoid), r=[pb], w=[sgs_b])
            ctx.op("pool", lambda: pool.tensor_tensor(out=oz[:], in0=oat[:], in1=sza[:], op=ALU.mult), r=[oat_b, sza_b], w=[oz_b])
            for oc in range(8):
                pt, pb = mm_ring.next()
                mm_group(pt[:, 0:TC], pb, [(Wap[:, kc, oc * 128:(oc + 1) * 128], oz[:, kc, :]) for kc in range(4)], r=[Wap_b, oz_b])
                ta, ta_b = ta_ring.next()
                tb, tb_b = tb_ring.next()
                ctx.op("dve", lambda: dve.tensor_tensor(out=ta[:], in0=pt[:, 0:TC], in1=sga[:, oc, :], op=ALU.mult), r=[pb, sga_b], w=[ta_b])
                ctx.op("pool", lambda: pool.tensor_tensor(out=tb[:], in0=yst[:, oc, :], in1=sgs[:, oc, :], op=ALU.mult), r=[yst_b, sgs_b], w=[tb_b])
                ctx.op("dve", lambda: dve.tensor_tensor(out=merge[:, oc, :], in0=ta[:], in1=tb[:], op=ALU.add), r=[ta_b, tb_b], w=[merge_b])
            for sub in range(NS):
                xt, xtb = xts[sub]
                srow, srow_b = s_ring.next()
                tsl = slice(sub * 128, (sub + 1) * 128)
                (bst, bmv, brs), bst_b = st_ring.next()
                for hc in range(2):
                    csl = slice(hc * 512, (hc + 1) * 512)
                    pg, pg_b = tk_ring.next()
                    mm_group(pg[:, :], pg_b, [(xT[:, dk, tsl], Wpg[:, dk, csl]) for dk in range(8)], r=[Wpg_b, xT_b])
                    pp, pp_b = tk_ring.next()
                    mm_group(pp[:, :], pp_b, [(pT[:, kc, tsl], Wpp[:, kc, csl]) for kc in range(2)], r=[Wpp_b, pT_b])
                    mx, mx_b = tk_ring.next()
                    mm_group(mx[:, :], mx_b, [(merge[:, kc, tsl], Wout[:, kc, csl]) for kc in range(8)], r=[Wout_b, merge_b])
                    ctx.op("act", lambda: act.activation(out=srow[:, csl], in_=pg[:, :], func=AF.Sigmoid), r=[pg_b], w=[srow_b])
                    ctx.op("dve", lambda: dve.tensor_tensor(out=srow[:, csl], in0=srow[:, csl], in1=pp[:, :], op=ALU.mult), r=[srow_b, pp_b], w=[srow_b])
                    ctx.op("dve", lambda: dve.tensor_tensor(out=srow[:, csl], in0=srow[:, csl], in1=mx[:, :], op=ALU.add), r=[srow_b, mx_b], w=[srow_b])
                    ctx.op("dve", lambda: dve.scalar_tensor_tensor(out=srow[:, csl], in0=xt[:, csl], scalar=ALPHA, in1=srow[:, csl], op0=ALU.mult, op1=ALU.add),
                           r=[srow_b, xtb], w=[srow_b])
                    ctx.op("dve", lambda: dve.bn_stats(out=bst[:, hc, :], in_=srow[:, csl]), r=[srow_b], w=[bst_b])
                ctx.op("dve", lambda: dve.bn_aggr(out=bmv[:], in_=bst[:].rearrange("p a b -> p (a b)")), r=[bst_b], w=[bst_b])
                ctx.op("dve", lambda: dve.tensor_scalar(out=brs[:, 0:1], in0=bmv[:, 1:2], scalar1=LN_EPS, scalar2=None, op0=ALU.add), r=[bst_b], w=[bst_b])
                ctx.op("act", lambda: act.activation(out=brs[:, 0:1], in_=brs[:, 0:1], func=AF.Sqrt), r=[bst_b], w=[bst_b])
                ctx.op("dve", lambda: dve.reciprocal(out=brs[:, 0:1], in_=brs[:, 0:1]), r=[bst_b], w=[bst_b])
                ctx.op("dve", lambda: dve.scalar_tensor_tensor(out=brs[:, 1:2], in0=bmv[:, 0:1], scalar=-1.0, in1=brs[:, 0:1], op0=ALU.mult, op1=ALU.mult),
                       r=[bst_b], w=[bst_b])
                orow, orow_b = o_ring.next()
                ctx.op("dve", lambda: dve.tensor_scalar(out=orow[:], in0=srow[:], scalar1=brs[:, 0:1], scalar2=brs[:, 1:2], op0=ALU.mult, op1=ALU.add), r=[srow_b, bst_b], w=[orow_b])
                ctx.op("dve", lambda: dve.tensor_tensor(out=orow[:], in0=orow[:], in1=lng[:], op=ALU.mult), r=[orow_b, ln_b], w=[orow_b])
                ctx.op("pool", lambda: pool.tensor_tensor(out=orow[:], in0=orow[:], in1=lnb[:], op=ALU.add), r=[orow_b, ln_b], w=[orow_b])
                ctx.dma(out_d[tok0 + sub * 128:tok0 + (sub + 1) * 128, :], orow[:], r=[orow_b])
        ctx.barrier(engines=["sp"])
    es.close()
    return nc


def host_consts(S):
    bf = ml_dtypes.bfloat16
    NCH = S // 128
    c = {}
    c["ident"] = np.eye(128, dtype=np.float32).astype(bf)
    c["Jm"] = np.ascontiguousarray(np.eye(128, dtype=np.float32)[::-1]).astype(bf)
    i = np.arange(384)
    dist = i - 127
    bk = t5_bucket_np(np.maximum(dist, 0))
    OH = np.zeros((32, 384), np.float32)
    valid = dist >= 0
    OH[bk[valid], i[valid]] = 1.0
    c["OH"] = OH.astype(bf)
    NEGM = np.zeros((8, 384), np.float32)
    NEGM[:, ~valid] = NEG
    c["NEGM"] = NEGM
    keys = np.arange(S)
    EOH = (keys[None, :] // BLK == np.arange(32)[:, None]).astype(np.float32)
    c["EOH"] = EOH.astype(bf)
    ch = np.arange(NCH)
    blk = ch // 2
    n = np.arange(32)
    past = (n[None, :] < blk[:, None])
    PM = np.where(past, 0.0, -1e30).astype(np.float32)
    c["PMASK"] = np.ascontiguousarray(np.broadcast_to(PM.reshape(1, -1), (128, NCH * 32)))
    c["PAST01"] = np.ascontiguousarray(np.broadcast_to(past.astype(np.float32).reshape(1, -1), (128, NCH * 32)))
    c["jidx"] = np.ascontiguousarray(np.broadcast_to(np.arange(129, dtype=np.float32)[None, :], (128, 129)))
    return c


def host_params(inp):
    o = {}
    a_re = inp["ssm_a_re"][0]; a_im = inp["ssm_a_im"][0]; ldt = inp["ssm_log_dt"][0]
    b_re = inp["ssm_b_re"][0]; b_im = inp["ssm_b_im"][0]; c_re = inp["ssm_c_re"][0]; c_im = inp["ssm_c_im"][0]
    pidx = np.arange(128); g2 = pidx // 64; n = pidx % 64
    pr = np.arange(16)
    G = 2 * pr[None, :] + g2[:, None]
    o["are_l"] = np.ascontiguousarray(a_re[G, n[:, None]])
    o["aim_l"] = np.ascontiguousarray(a_im[G, n[:, None]])
    o["ldt_l"] = np.ascontiguousarray(ldt[G])
    Gc = 2 * pr[:, None] + g2[None, :]
    are_pc = a_re[Gc, n[None, :]]
    aim_pc = a_im[Gc, n[None, :]]
    ldt_pc = ldt[Gc]
    for name, v in (("are_rep", are_pc), ("aim_rep", aim_pc), ("ldt_rep", ldt_pc)):
        o[name] = np.ascontiguousarray(np.broadcast_to(v.reshape(1, 2048), (128, 2048))).astype(np.float32)
    brT = np.zeros((128, 16, 128), np.float32); biT = np.zeros((128, 16, 128), np.float32)
    cre = np.zeros((128, 16, 128), np.float32); cim = np.zeros((128, 16, 128), np.float32)
    for p_ in range(16):
        r0 = (p_ % 4) * 32
        for gg in range(2):
            g = 2 * p_ + gg
            brT[r0 + gg * 16:r0 + gg * 16 + 16, p_, gg * 64:(gg + 1) * 64] = b_re[g].T
            biT[r0 + gg * 16:r0 + gg * 16 + 16, p_, gg * 64:(gg + 1) * 64] = b_im[g].T
            cre[gg * 64:(gg + 1) * 64, p_, r0 + gg * 16:r0 + gg * 16 + 16] = c_re[g].T
            cim[gg * 64:(gg + 1) * 64, p_, r0 + gg * 16:r0 + gg * 16 + 16] = c_im[g].T
    o["brT"] = brT.reshape(128, 2048); o["biT"] = biT.reshape(128, 2048)
    o["cre_pad"] = cre.reshape(128, 2048); o["cim_pad"] = cim.reshape(128, 2048)
    o["d_l"] = np.ascontiguousarray(inp["ssm_d"][0].reshape(4, 128).T)
    o["lng"] = np.ascontiguousarray(np.broadcast_to(inp["ln_g"][0][None, :], (128, D)))
    o["lnb"] = np.ascontiguousarray(np.broadcast_to(inp["ln_b"][0][None, :], (128, D)))
    o["relb"] = np.ascontiguousarray(inp["rel_bias"])
    o["b31rep"] = np.ascontiguousarray(np.broadcast_to(inp["rel_bias"][31][None, :], (128, 8)))
    o["w_in"] = np.ascontiguousarray(inp["w_in"][0]); o["w_ap"] = np.ascontiguousarray(inp["w_attn_proj"][0])
    o["w_sp"] = np.ascontiguousarray(inp["w_ssm_proj"][0]); o["w_out"] = np.ascontiguousarray(inp["w_out"][0])
    o["w_glu"] = np.ascontiguousarray(inp["w_glu"][0]); o["w_pg"] = np.ascontiguousarray(inp["w_ple_gate"][0])
    o["w_pp"] = np.ascontiguousarray(inp["w_ple_proj"][0])
    return {k: np.asarray(v, np.float32) for k, v in o.items()}


_NC_CACHE = {}


def run(inputs, S, n_cores, **bk):
    inp = {k: np.asarray(v) for k, v in inputs.items()}
    if S not in _NC_CACHE:
        _NC_CACHE[S] = build_nc(S, **bk)
    nc = _NC_CACHE[S]
    shared = host_params(inp)
    shared.update(host_consts(S))
    in_maps = []
    for b in range(n_cores):
        m = dict(shared)
        m["x"] = np.ascontiguousarray(inp["x"][b], dtype=np.float32)
        m["p"] = np.ascontiguousarray(inp["p"][0, b], dtype=np.float32)
        in_maps.append(m)
    res = run_bass_kernel_spmd(nc, in_maps, core_ids=list(range(n_cores)))
    return np.stack([np.asarray(r["out"]) for r in res.results], axis=0).astype(np.float32)


def kernel(**inputs):
    return run(inputs, 8192, 8)
```

```python
import math
from contextlib import ExitStack

import numpy as np
import ml_dtypes

import concourse.bass as bass
import concourse.mybir as mybir
from concourse.bass_utils import run_bass_kernel_spmd

F32 = mybir.dt.float32
BF16 = mybir.dt.bfloat16
I32 = mybir.dt.int32
AF = mybir.ActivationFunctionType
ALU = mybir.AluOpType
AX = mybir.AxisListType

D = 1024
NH = 8
HD = 64
BLK = 256
NEG = -30000.0
TWO_PI = 2.0 * math.pi
ALPHA = 2.0 ** 0.25
LN_EPS = 1e-5


class Buf:
    __slots__ = ("w", "r", "name")

    def __init__(self, name=""):
        self.w = None
        self.r = {}
        self.name = name


class Ring:
    def __init__(self, items):
        self.items = items
        self.i = 0

    def next(self):
        it = self.items[self.i % len(self.items)]
        self.i += 1
        return it


class Ctx:
    def __init__(self, nc, es, n_dma_sems=24):
        self.nc = nc
        self.engs = {"pe": nc.tensor, "act": nc.scalar, "dve": nc.vector, "pool": nc.gpsimd, "sp": nc.sync}
        self.sem = {}
        self.cnt = {}
        self.seen = {e: {} for e in self.engs}
        self.pend = {e: ([], []) for e in self.engs}
        for e in self.engs:
            self.sem[e] = es.enter_context(nc.semaphore("s_" + e))
            self.cnt[e] = 0
        self.dq = []
        for i in range(n_dma_sems):
            k = "d%d" % i
            self.sem[k] = es.enter_context(nc.semaphore("s_" + k))
            self.cnt[k] = 0
            self.dq.append(k)
        self.dqi = 0

    def _wait(self, e, tok):
        if tok is None:
            return
        k, v = tok
        if v <= 0 or self.seen[e].get(k, 0) >= v:
            return
        self.engs[e].wait_ge(self.sem[k], v)
        self.seen[e][k] = v

    def _deps(self, e, r, w):
        for b in r:
            if b.w is not None and not (e == "pe" and b.w[0] == "pe"):
                self._wait(e, b.w)
        for b in w:
            if b.w is not None and b.w[0] != e:
                self._wait(e, b.w)
            for k, v in b.r.items():
                if k != e:
                    self._wait(e, (k, v))

    def _reg(self, tok, r, w):
        k, v = tok
        for b in r:
            if b.r.get(k, 0) < v:
                b.r[k] = v
        for b in w:
            b.w = tok
            b.r = {}

    def op(self, e, fn, r=(), w=(), sig=True):
        self._deps(e, r, w)
        inst = fn()
        pr, pw = self.pend[e]
        if not sig:
            pr.extend(r)
            pw.extend(w)
            return None
        self.cnt[e] += 1
        inst.then_inc(self.sem[e], 1)
        tok = (e, self.cnt[e])
        self._reg(tok, list(r) + pr, list(w) + pw)
        self.pend[e] = ([], [])
        return tok

    def dma(self, out, in_, r=(), w=(), q="sp"):
        k = self.dq[self.dqi % len(self.dq)]
        self.dqi += 1
        self._wait(q, (k, self.cnt[k]))
        self._deps(q, r, w)
        inst = self.engs[q].dma_start(out=out, in_=in_)
        self.cnt[k] += 16
        inst.then_inc(self.sem[k], 16)
        tok = (k, self.cnt[k])
        self._reg(tok, r, w)
        return tok

    def barrier(self, engines=None):
        toks = [(k, v) for k, v in self.cnt.items() if v > 0]
        for e in (engines or self.engs):
            for t in toks:
                if t[0] != e:
                    self._wait(e, t)


def t5_bucket_np(dist):
    dist = np.asarray(dist, np.int64)
    d = np.maximum(dist, 1).astype(np.float32)
    large = 16 + (np.log(d / np.float32(16)) / np.float32(math.log(128 / 16)) * np.float32(16)).astype(np.int32)
    large = np.minimum(large, 31)
    return np.where(dist < 16, dist, large)


def build_nc(S, TT=512, TC=256, stop=None, nt_lim=None):
    NT = S // TT
    NTC = S // TC
    NKT = S // 128
    NB = S // BLK
    NCH = S // 128
    assert NB <= 32
    nc = bass.Bass("TRN2", target_bir_lowering=False)
    es = ExitStack()
    ctx = Ctx(nc, es)
    pe, act, dve, pool = nc.tensor, nc.scalar, nc.vector, nc.gpsimd

    def din(name, shape, dt=F32):
        return nc.dram_tensor(name, list(shape), dt, kind="ExternalInput").ap()

    def dscr(name, shape, dt=BF16):
        return nc.dram_tensor(name, list(shape), dt, kind="Internal").ap()

    x_d = din("x", [S, D])
    p_d = din("p", [S, 256])
    w_in_d = din("w_in", [D, 5120])
    w_ap_d = din("w_ap", [512, D])
    w_sp_d = din("w_sp", [512, D])
    w_out_d = din("w_out", [D, D])
    w_glu_d = din("w_glu", [512, D])
    w_pg_d = din("w_pg", [D, D])
    w_pp_d = din("w_pp", [256, D])
    are_rep_d = din("are_rep", [128, 2048])
    aim_rep_d = din("aim_rep", [128, 2048])
    ldt_rep_d = din("ldt_rep", [128, 2048])
    brT_d = din("brT", [128, 2048])
    biT_d = din("biT", [128, 2048])
    cre_d = din("cre_pad", [128, 2048])
    cim_d = din("cim_pad", [128, 2048])
    are_l_d = din("are_l", [128, 16])
    aim_l_d = din("aim_l", [128, 16])
    ldt_l_d = din("ldt_l", [128, 16])
    d_l_d = din("d_l", [128, 4])
    lng_d = din("lng", [128, D])
    lnb_d = din("lnb", [128, D])
    relb_d = din("relb", [32, 8])
    b31_d = din("b31rep", [128, 8])
    ident_d = din("ident", [128, 128], BF16)
    J_d = din("Jm", [128, 128], BF16)
    OH_d = din("OH", [32, 384], BF16)
    NEGM_d = din("NEGM", [8, 384])
    EOH_d = din("EOH", [32, S], BF16)
    PMASK_d = din("PMASK", [128, NCH * 32])
    PAST_d = din("PAST01", [128, NCH * 32])
    jidx_d = din("jidx", [128, 129])
    out_d = nc.dram_tensor("out", [S, D], F32, kind="ExternalOutput").ap()

    QT_d = dscr("QT", [4, 128, S])
    KT_d = dscr("KT", [4, 128, S])
    V_d = dscr("Vs", [8, 128, S // 128, 64])
    OA_d = dscr("OA", [8, 64, S])
    YS_d = dscr("YS", [8, 128, S])
    F_d = dscr("Fd", [16, 384])

    def sb(stack, name, shape, dt=F32):
        return stack.enter_context(nc.sbuf_tensor("sb_" + name, list(shape), dt))

    def ps(stack, name, shape, dt=F32):
        return stack.enter_context(nc.psum_tensor("ps_" + name, list(shape), dt))

    ident = sb(es, "ident", [128, 128], BF16)
    ident_b = Buf("ident")
    Jm = sb(es, "Jm", [128, 128], BF16)
    Jm_b = Buf("J")
    ctx.dma(ident[:], ident_d, w=[ident_b])
    ctx.dma(Jm[:], J_d, w=[Jm_b])

    cast_rr = Ring(["act", "dve", "pool"])

    def cast_copy(e, out, in_):
        if e == "act":
            return lambda: act.copy(out=out, in_=in_)
        if e == "dve":
            return lambda: dve.tensor_copy(out=out, in_=in_)
        return lambda: pool.tensor_copy(out=out, in_=in_)

    def load_weight(dst, dst_b, src, col0, ncols, KC, stage_ring, dcol0=0):
        cw = 2048 // KC
        for c0 in range(0, ncols, cw):
            w_ = min(cw, ncols - c0)
            st, stb = stage_ring.next()
            stv = st[:, 0:KC * w_].rearrange("p (kc c) -> p kc c", kc=KC)
            ctx.dma(stv, src[:, col0 + c0:col0 + c0 + w_].rearrange("(kc p) c -> p kc c", p=128), w=[stb])
            e = cast_rr.next()
            ctx.op(e, cast_copy(e, dst[:, :, dcol0 + c0:dcol0 + c0 + w_], stv), r=[stb], w=[dst_b])

    def mm_group(out_ap, out_b, pairs, r):
        n = len(pairs)
        for i, (l, rh) in enumerate(pairs):
            ctx.op("pe", lambda: pe.matmul(out_ap, lhsT=l, rhs=rh, start=(i == 0), stop=(i == n - 1)),
                   r=r, w=[out_b], sig=(i == n - 1))

    with ExitStack() as sA:
        _psT = ps(sA, "psT0", [128, 512])[:].bitcast(BF16)
        psT_v = [_psT[:, 0:512], _psT[:, 512:1024]]
        _psT_b = Buf("psT")
        psT_b = [_psT_b, _psT_b]
        mm_ring = Ring([(ps(sA, "mm%d" % i, [128, 512]), Buf("mm%d" % i)) for i in range(2)])
        psS_ring = Ring([[(ps(sA, "psS%d_%d" % (j, i), [128, 512]), Buf("psS%d_%d" % (j, i))) for i in range(2)] for j in range(2)])
        psY = ps(sA, "psY", [128, 512])
        _psY_b = Buf("psY")
        psY_b = [_psY_b] * 4

        Wqk = sb(sA, "Wqk", [128, 8, 1024], BF16); Wqk_b = Buf("Wqk")
        Wv = sb(sA, "Wv", [128, 8, 512], BF16); Wv_b = Buf("Wv")
        Wuz = sb(sA, "Wuz", [128, 8, 1024], BF16); Wuz_b = Buf("Wuz")
        Wglu = sb(sA, "Wglu", [128, 4, 1024], BF16); Wglu_b = Buf("Wglu")
        Wsp = sb(sA, "Wsp", [128, 4, 1024], BF16); Wsp_b = Buf("Wsp")
        Bre = sb(sA, "Bre", [128, 16, 128], BF16); Bim = sb(sA, "Bim", [128, 16, 128], BF16)
        Cre = sb(sA, "Cre", [128, 16, 128], BF16); Cim = sb(sA, "Cim", [128, 16, 128], BF16)
        BC_b = Buf("BC")
        T_b = Buf("T")
        rtab = sb(sA, "rtab", [128, 16, 128])
        Dg = sb(sA, "Dg", [128, 2, 4, 128], BF16)
        Ec_t = sb(sA, "Ec_t", [128, 16]); Es_t = sb(sA, "Es_t", [128, 16])
        Tcb = sb(sA, "Tcb", [128, 16, 128], BF16); Tsb = sb(sA, "Tsb", [128, 16, 128], BF16)
        mag_l = sb(sA, "mag_l", [128, 16]); mag_b = Buf("mag")
        d_l = sb(sA, "d_l", [128, 4]); d_b = Buf("d")
        car_r = sb(sA, "car_r", [128, 16]); car_i = sb(sA, "car_i", [128, 16])
        car_b = [Buf("car%d" % q) for q in range(2)]
        ctx.dma(d_l[:], d_l_d, w=[d_b])

        with ExitStack() as s0:
            Tc = sb(s0, "Tc", [128, 16, 128]); Ts = sb(s0, "Ts", [128, 16, 128])
            dhi = sb(s0, "dhi", [128, 4], BF16); dlo = sb(s0, "dlo", [128, 4])
            A_ = sb(s0, "pA", [128, 2048]); Bm = sb(s0, "pB", [128, 2048]); L_ = sb(s0, "pL", [128, 2048])
            t0 = sb(s0, "pt0", [128, 2048]); t1 = sb(s0, "pt1", [128, 2048]); t2 = sb(s0, "pt2", [128, 2048])
            t3 = sb(s0, "pt3", [128, 2048]); t4 = sb(s0, "pt4", [128, 2048]); t5 = sb(s0, "pt5", [128, 2048])
            ti = sb(s0, "pti", [128, 2048], I32)
            bR = sb(s0, "pbR", [128, 2048]); bI = sb(s0, "pbI", [128, 2048])
            al = sb(s0, "al", [128, 16]); bl = sb(s0, "bl", [128, 16]); ll = sb(s0, "ll", [128, 16])
            th = sb(s0, "th", [128, 16]); jx = sb(s0, "jx", [128, 129])
            th128 = sb(s0, "th128", [128, 16]); sc16 = sb(s0, "sc16", [128, 16])
            P = Buf("setup")

            for t_, d_ in ((A_, are_rep_d), (Bm, aim_rep_d), (L_, ldt_rep_d), (bR, brT_d), (bI, biT_d),
                           (t0, cre_d), (t1, cim_d)):
                ctx.dma(t_[:], d_, w=[P])
            for t_, d_ in ((al, are_l_d), (bl, aim_l_d), (ll, ldt_l_d), (jx, jidx_d)):
                ctx.dma(t_[:], d_, w=[P])

            def V(fn):
                ctx.op("dve", fn, r=[P], w=[P])

            def A(fn):
                ctx.op("act", fn, r=[P], w=[P])

            V(lambda: dve.tensor_copy(out=Cre[:].rearrange("p a b -> p (a b)"), in_=t0[:]))
            V(lambda: dve.tensor_scalar(out=Cim[:].rearrange("p a b -> p (a b)"), in0=t1[:], scalar1=-1.0, scalar2=None, op0=ALU.mult))

            def emit_sin(out, x, n, shift, xs):
                xi = ti[:, 0:n]
                V(lambda: dve.tensor_scalar(out=xs, in0=x, scalar1=shift, scalar2=None, op0=ALU.add))
                V(lambda: dve.tensor_scalar(out=xi, in0=xs, scalar1=1.0 / TWO_PI, scalar2=None, op0=ALU.mult))
                V(lambda: dve.tensor_copy(out=out, in_=xi))
                V(lambda: dve.scalar_tensor_tensor(out=out, in0=out, scalar=-TWO_PI, in1=xs, op0=ALU.mult, op1=ALU.add))
                V(lambda: dve.tensor_scalar(out=xs, in0=out, scalar1=math.pi, scalar2=-TWO_PI, op0=ALU.is_gt, op1=ALU.mult))
                V(lambda: dve.tensor_tensor(out=out, in0=out, in1=xs, op=ALU.add))
                V(lambda: dve.tensor_scalar(out=xs, in0=out, scalar1=-math.pi, scalar2=TWO_PI, op0=ALU.is_lt, op1=ALU.mult))
                V(lambda: dve.tensor_tensor(out=out, in0=out, in1=xs, op=ALU.add))
                V(lambda: dve.tensor_scalar(out=out, in0=out, scalar1=math.pi, scalar2=-math.pi, op0=ALU.min, op1=ALU.max))
                A(lambda: act.activation(out=out, in_=out, func=AF.Sin))

            A(lambda: act.activation(out=L_[:], in_=L_[:], func=AF.Exp))
            V(lambda: dve.tensor_tensor(out=t0[:], in0=L_[:], in1=A_[:], op=ALU.mult))
            A(lambda: act.activation(out=t0[:], in_=t0[:], func=AF.Exp))
            V(lambda: dve.tensor_tensor(out=t1[:], in0=L_[:], in1=Bm[:], op=ALU.mult))
            emit_sin(t2[:], t1[:], 2048, 0.0, t4[:])
            emit_sin(t3[:], t1[:], 2048, math.pi / 2, t4[:])
            V(lambda: dve.tensor_tensor(out=t3[:], in0=t3[:], in1=t0[:], op=ALU.mult))
            V(lambda: dve.tensor_tensor(out=t2[:], in0=t2[:], in1=t0[:], op=ALU.mult))
            V(lambda: dve.tensor_scalar(out=t3[:], in0=t3[:], scalar1=-1.0, scalar2=None, op0=ALU.add))
            V(lambda: dve.tensor_tensor(out=t0[:], in0=A_[:], in1=A_[:], op=ALU.mult))
            V(lambda: dve.tensor_tensor(out=t1[:], in0=Bm[:], in1=Bm[:], op=ALU.mult))
            V(lambda: dve.tensor_tensor(out=t0[:], in0=t0[:], in1=t1[:], op=ALU.add))
            V(lambda: dve.reciprocal(out=t0[:], in_=t0[:]))
            V(lambda: dve.tensor_tensor(out=t4[:], in0=t3[:], in1=A_[:], op=ALU.mult))
            V(lambda: dve.tensor_tensor(out=t1[:], in0=t2[:], in1=Bm[:], op=ALU.mult))
            V(lambda: dve.tensor_tensor(out=t4[:], in0=t4[:], in1=t1[:], op=ALU.add))
            V(lambda: dve.tensor_tensor(out=t4[:], in0=t4[:], in1=t0[:], op=ALU.mult))
            V(lambda: dve.tensor_tensor(out=t5[:], in0=t2[:], in1=A_[:], op=ALU.mult))
            V(lambda: dve.tensor_tensor(out=t1[:], in0=t3[:], in1=Bm[:], op=ALU.mult))
            V(lambda: dve.tensor_tensor(out=t5[:], in0=t5[:], in1=t1[:], op=ALU.subtract))
            V(lambda: dve.tensor_tensor(out=t5[:], in0=t5[:], in1=t0[:], op=ALU.mult))
            V(lambda: dve.tensor_tensor(out=t0[:], in0=t4[:], in1=bR[:], op=ALU.mult))
            V(lambda: dve.tensor_tensor(out=t1[:], in0=t5[:], in1=bI[:], op=ALU.mult))
            V(lambda: dve.tensor_tensor(out=Bre[:].rearrange("p a b -> p (a b)"), in0=t0[:], in1=t1[:], op=ALU.subtract))
            V(lambda: dve.tensor_tensor(out=t0[:], in0=t4[:], in1=bI[:], op=ALU.mult))
            V(lambda: dve.tensor_tensor(out=t1[:], in0=t5[:], in1=bR[:], op=ALU.mult))
            V(lambda: dve.tensor_tensor(out=Bim[:].rearrange("p a b -> p (a b)"), in0=t0[:], in1=t1[:], op=ALU.add))
            A(lambda: act.activation(out=ll[:], in_=ll[:], func=AF.Exp))
            V(lambda: dve.tensor_tensor(out=al[:], in0=ll[:], in1=al[:], op=ALU.mult))
            A(lambda: act.activation(out=mag_l[:], in_=al[:], func=AF.Exp))
            V(lambda: dve.tensor_tensor(out=th[:], in0=ll[:], in1=bl[:], op=ALU.mult))
            V(lambda: dve.tensor_tensor(out=t0[:].rearrange("p (a b) -> p a b", a=16),
                                        in0=th[:].rearrange("p (a o) -> p a o", o=1).to_broadcast([128, 16, 128]),
                                        in1=jx[:, 0:128].rearrange("p (o b) -> p o b", o=1).to_broadcast([128, 16, 128]), op=ALU.mult))
            emit_sin(Ts[:].rearrange("p a b -> p (a b)"), t0[:], 2048, 0.0, t1[:])
            emit_sin(Tc[:].rearrange("p a b -> p (a b)"), t0[:], 2048, math.pi / 2, t1[:])
            V(lambda: dve.tensor_scalar(out=th128[:], in0=th[:], scalar1=128.0, scalar2=None, op0=ALU.mult))
            emit_sin(Es_t[:], th128[:], 16, 0.0, sc16[:])
            emit_sin(Ec_t[:], th128[:], 16, math.pi / 2, sc16[:])
            V(lambda: dve.tensor_copy(out=Tcb[:], in_=Tc[:]))
            V(lambda: dve.tensor_copy(out=Tsb[:], in_=Ts[:]))
            V(lambda: dve.tensor_copy(out=rtab[:], in_=mag_l[:].rearrange("p (a o) -> p a o", o=1).to_broadcast([128, 16, 128])))
            V(lambda: dve.memset(rtab[:, :, 0:1], 0.0))
            ctx.op("dve", lambda: dve.tensor_copy(out=dhi[:], in_=d_l[:]), r=[P, d_b], w=[P])
            V(lambda: dve.tensor_tensor(out=dlo[:], in0=d_l[:], in1=dhi[:], op=ALU.subtract))
            for q_ in range(4):
                ctx.op("dve", lambda: dve.tensor_scalar(out=Dg[:, 0, q_, :], in0=ident[:], scalar1=dhi[:, q_:q_ + 1], scalar2=None, op0=ALU.mult), r=[P, ident_b], w=[P])
                ctx.op("dve", lambda: dve.tensor_scalar(out=Dg[:, 1, q_, :], in0=ident[:], scalar1=dlo[:, q_:q_ + 1], scalar2=None, op0=ALU.mult), r=[P, ident_b], w=[P])
            V(lambda: dve.memset(car_r[:], 0.0))
            V(lambda: dve.memset(car_i[:], 0.0))
            ctx.op("dve", lambda: dve.memset(t0[:, 0:1], 0.0), r=[P], w=[P, BC_b, T_b, mag_b] + car_b)

            ctx.barrier()
        with ExitStack() as s0:
            stage_ring = Ring([(sb(s0, "wst%d" % i, [128, 2048]), Buf("wst%d" % i)) for i in range(2)])
            load_weight(Wqk, Wqk_b, w_in_d, 0, 1024, 8, stage_ring)
            load_weight(Wv, Wv_b, w_in_d, 1024, 512, 8, stage_ring)
            load_weight(Wuz, Wuz_b, w_in_d, 2048, 1024, 8, stage_ring)
            load_weight(Wglu, Wglu_b, w_glu_d, 0, 1024, 4, stage_ring)
            load_weight(Wsp, Wsp_b, w_sp_d, 0, 1024, 4, stage_ring)
            ctx.barrier()

        x_ring = Ring([(sb(sA, "xa%d" % i, [128, D]), Buf("xa%d" % i)) for i in range(3)])
        xb = sb(sA, "xbA", [128, 4, D], BF16)
        xb_b = [Buf("xb%d" % i) for i in range(4)]
        xT_ring = Ring([(sb(sA, "xTA%d" % i, [128, 8, TT], BF16), Buf("xTA%d" % i)) for i in range(2)])
        uT_ring = Ring([(sb(sA, "uT%d" % i, [128, 4, TT], BF16), [Buf("uT%d_%d" % (i, q)) for q in range(4)]) for i in range(2)])
        szs_ring = Ring([(sb(sA, "szs%d" % i, [128, 4, TT], BF16), Buf("szs%d" % i)) for i in range(2)])
        qk_ring = Ring([(sb(sA, "qkst%d" % i, [128, TT], BF16), Buf("qkst%d" % i)) for i in range(2)])
        v_ring = Ring([(sb(sA, "vst%d" % i, [128, 512], BF16), Buf("vst%d" % i)) for i in range(2)])
        ys_ring = Ring([(sb(sA, "ysst%d" % i, [128, TT], BF16), Buf("ysst%d" % i)) for i in range(2)])
        bpr = sb(sA, "bpr", [128, 8, 128], BF16); bpi = sb(sA, "bpi", [128, 8, 128], BF16); tm2 = sb(sA, "tm2", [128, 8, 128], BF16)
        bp_b = Buf("bp")
        Sb_ring = Ring([((sb(sA, "Sbr%d" % i, [128, 8, 128], BF16), sb(sA, "Sbi%d" % i, [128, 8, 128], BF16)), Buf("Sb%d" % i)) for i in range(2)])
        g_ring = Ring([((sb(sA, "gr%d" % i, [128, 8, 128], BF16), sb(sA, "gi%d" % i, [128, 8, 128], BF16)), Buf("g%d" % i)) for i in range(2)])
        tp3 = sb(sA, "tp3", [128, 8, 128], BF16); tp4 = sb(sA, "tp4", [128, 8, 128], BF16)
        tp5 = sb(sA, "tp5", [128, 8, 128], BF16); tp6 = sb(sA, "tp6", [128, 8, 128], BF16); tpd_b = Buf("tpd")
        h_ring = Ring([((sb(sA, "hr%d" % i, [128, 8, 128], BF16), sb(sA, "mhi%d" % i, [128, 8, 128], BF16)), (Buf("hr%d" % i), Buf("hi%d" % i))) for i in range(2)])
        rc_r = sb(sA, "rc_r", [128, 8]); rc_i = sb(sA, "rc_i", [128, 8]); ctm = sb(sA, "ctm", [128, 8]); ctm2 = sb(sA, "ctm2", [128, 8])
        gT_ring = Ring([(sb(sA, "gT%d" % i, [128, 4, TT], BF16), Buf("gT%d" % i)) for i in range(2)])
        sigb_ring = Ring([(sb(sA, "sigb%d" % i, [128, TT]), Buf("sigb%d" % i)) for i in range(2)])
        yg = sb(sA, "yg", [128, 4, TT], BF16); yg_b = Buf("yg")

        def issue_xA(t_):
            for sub in range(4):
                xt, xtb = x_ring.next()
                ctx.dma(xt[:], x_d[t_ * TT + sub * 128:t_ * TT + (sub + 1) * 128, :], w=[xtb])
                ctx.op("act", lambda: act.copy(out=xb[:, sub, :], in_=xt[:]), r=[xtb], w=[xb_b[sub]])

        NTA = 0 if stop == 'setup' else (nt_lim or NT)

        def proj_tasks(t_):
            tk0 = t_ * TT
            xT_, xT_b_ = xT_ring.next()
            uT_, uT_b_ = uT_ring.next()
            szs_, szs_b_ = szs_ring.next()
            tasks_ = []

            def t_tr(dk):
                hf = dk % 2
                tp = psT_v[hf][:, 0:512]
                for sub in range(4):
                    ctx.op("pe", lambda: pe.transpose(tp[:, sub * 128:(sub + 1) * 128], xb[:, sub, dk * 128:(dk + 1) * 128], ident[:]),
                           r=[xb_b[sub], ident_b], w=[psT_b[hf]], sig=(sub == 3))
                ctx.op("act", lambda: act.copy(out=xT_[:, dk, :], in_=tp), r=[psT_b[hf]], w=[xT_b_])
                if dk == 7 and t_ + 1 < NTA:
                    issue_xA(t_ + 1)

            def t_qk(cc):
                pt, pb = mm_ring.next()
                mm_group(pt[:, 0:TT], pb, [(Wqk[:, dk, cc * 128:(cc + 1) * 128], xT_[:, dk, :]) for dk in range(8)], r=[Wqk_b, xT_b_])
                stg, stb = qk_ring.next()
                if cc < 4:
                    ctx.op("act", lambda: act.activation(out=stg[:], in_=pt[:, 0:TT], func=AF.Copy, scale=0.125), r=[pb], w=[stb])
                    ctx.dma(QT_d[cc, :, tk0:tk0 + TT], stg[:], r=[stb])
                else:
                    ctx.op("act", lambda: act.copy(out=stg[:], in_=pt[:, 0:TT]), r=[pb], w=[stb])
                    ctx.dma(KT_d[cc - 4, :, tk0:tk0 + TT], stg[:], r=[stb])

            def t_v(sub):
                pt, pb = mm_ring.next()
                mm_group(pt[:, 0:512], pb, [(xT_[:, dk, sub * 128:(sub + 1) * 128], Wv[:, dk, :]) for dk in range(8)], r=[Wv_b, xT_b_])
                stg, stb = v_ring.next()
                ctx.op("act", lambda: act.copy(out=stg[:], in_=pt[:, 0:512]), r=[pb], w=[stb])
                ctx.dma(V_d[:, :, t_ * 4 + sub, :].rearrange("h p d -> p h d"), stg[:].rearrange("p (h d) -> p h d", h=8), r=[stb])

            def t_uz(cc):
                pt, pb = mm_ring.next()
                mm_group(pt[:, 0:TT], pb, [(Wuz[:, dk, cc * 128:(cc + 1) * 128], xT_[:, dk, :]) for dk in range(8)], r=[Wuz_b, xT_b_])
                if cc < 4:
                    ctx.op("act", lambda: act.copy(out=uT_[:, cc, :], in_=pt[:, 0:TT]), r=[pb], w=[uT_b_[cc]])
                else:
                    ctx.op("act", lambda: act.activation(out=szs_[:, cc - 4, :], in_=pt[:, 0:TT], func=AF.Silu), r=[pb], w=[szs_b_])

            for dk in range(8):
                tasks_.append(lambda dk=dk: t_tr(dk))
            for cc in range(8):
                tasks_.append(lambda cc=cc: t_uz(cc))
            for cc in range(8):
                tasks_.append(lambda cc=cc: t_qk(cc))
            for sub in range(4):
                tasks_.append(lambda sub=sub: t_v(sub))
            return tasks_, (uT_, uT_b_, szs_, szs_b_)

        def post_tasks(t_, gT_, gT_b_, szs_, szs_b_):
            tk0 = t_ * TT
            tasks_ = []

            def t_glu(oc):
                pa, pab = mm_ring.next()
                mm_group(pa[:, 0:TT], pab, [(Wglu[:, kc, oc * 128:(oc + 1) * 128], gT_[:, kc, :]) for kc in range(4)], r=[Wglu_b, gT_b_])
                pbk, pbb = mm_ring.next()
                mm_group(pbk[:, 0:TT], pbb, [(Wglu[:, kc, (oc + 4) * 128:(oc + 5) * 128], gT_[:, kc, :]) for kc in range(4)], r=[Wglu_b, gT_b_])
                sg, sgb = sigb_ring.next()
                ctx.op("act", lambda: act.activation(out=sg[:], in_=pbk[:, 0:TT], func=AF.Sigmoid), r=[pbb], w=[sgb])
                ctx.op("dve", lambda: dve.tensor_tensor(out=sg[:], in0=sg[:], in1=pa[:, 0:TT], op=ALU.mult), r=[sgb, pab], w=[sgb])
                ctx.op("dve", lambda: dve.tensor_tensor(out=yg[:, oc, :], in0=sg[:], in1=szs_[:, oc, :], op=ALU.mult), r=[sgb, szs_b_], w=[yg_b])

            def t_sp(oc):
                pt, pb = mm_ring.next()
                mm_group(pt[:, 0:TT], pb, [(Wsp[:, kc, oc * 128:(oc + 1) * 128], yg[:, kc, :]) for kc in range(4)], r=[Wsp_b, yg_b])
                stg, stb = ys_ring.next()
                ctx.op("act", lambda: act.copy(out=stg[:], in_=pt[:, 0:TT]), r=[pb], w=[stb])
                ctx.dma(YS_d[oc, :, tk0:tk0 + TT], stg[:], r=[stb])

            for oc in range(4):
                tasks_.append(lambda oc=oc: t_glu(oc))
            for oc in range(8):
                tasks_.append(lambda oc=oc: t_sp(oc))
            return tasks_

        post = []
        cur_tile = None
        if NTA:
            issue_xA(0)
            tasks, cur_tile = proj_tasks(0)
            for tk_ in tasks:
                tk_()
        for t in range(NTA):
            tok0 = t * TT
            uT, uT_b, szs, szs_b = cur_tile
            gT, gT_b = gT_ring.next()
            tasks = list(post)
            post_n = list(post)
            if t + 1 < NTA:
                ptk, cur_tile = proj_tasks(t + 1)
                tasks += ptk
            per_unit = (len(tasks) + 7) // 8
            def ssm_stage1a(ti_, s_, hf):
                tsl = slice(s_ * 128, (s_ + 1) * 128)
                (Sbr, Sbi), Sb_b = Sb_ring.next()
                for qi in range(2):
                    q = 2 * hf + qi
                    (Sre, Sre_b), (Sim, Sim_b) = psS_ring.next()
                    for i in range(4):
                        pr = 4 * q + i
                        ctx.op("pe", lambda: pe.matmul(Sre[:, i * 128:(i + 1) * 128], lhsT=Bre[:, pr, :], rhs=ti_[0][:, q, tsl], start=True, stop=True),
                               r=[BC_b, ti_[1][q]], w=[Sre_b], sig=(i == 3))
                    for i in range(4):
                        pr = 4 * q + i
                        ctx.op("pe", lambda: pe.matmul(Sim[:, i * 128:(i + 1) * 128], lhsT=Bim[:, pr, :], rhs=ti_[0][:, q, tsl], start=True, stop=True),
                               r=[BC_b, ti_[1][q]], w=[Sim_b], sig=(i == 3))
                    ctx.op("act", lambda: act.copy(out=Sbr[:, qi * 4:(qi + 1) * 4, :].rearrange("p a b -> p (a b)"), in_=Sre[:, 0:512]), r=[Sre_b], w=[Sb_b])
                    ctx.op("act", lambda: act.copy(out=Sbi[:, qi * 4:(qi + 1) * 4, :].rearrange("p a b -> p (a b)"), in_=Sim[:, 0:512]), r=[Sim_b], w=[Sb_b])
                return s_, hf, Sbr, Sbi, Sb_b

            def ssm_stage1b(s_, hf, Sbr, Sbi, Sb_b):
                p0 = 8 * hf
                Tcq = Tcb[:, p0:p0 + 8, :]
                Tsq = Tsb[:, p0:p0 + 8, :]
                cb = car_b[hf]
                ctx.op("dve", lambda: dve.tensor_tensor(out=bpr[:], in0=Sbr[:], in1=Tcq, op=ALU.mult), r=[Sb_b, T_b], w=[bp_b])
                ctx.op("dve", lambda: dve.tensor_tensor(out=tm2[:], in0=Sbi[:], in1=Tsq, op=ALU.mult), r=[Sb_b, T_b], w=[bp_b])
                ctx.op("dve", lambda: dve.tensor_tensor(out=bpr[:], in0=bpr[:], in1=tm2[:], op=ALU.add), r=[bp_b], w=[bp_b])
                ctx.op("dve", lambda: dve.tensor_tensor(out=bpi[:], in0=Sbi[:], in1=Tcq, op=ALU.mult), r=[Sb_b, T_b, bp_b], w=[bp_b])
                ctx.op("dve", lambda: dve.tensor_tensor(out=tm2[:], in0=Sbr[:], in1=Tsq, op=ALU.mult), r=[Sb_b, T_b, bp_b], w=[bp_b])
                ctx.op("dve", lambda: dve.tensor_tensor(out=bpi[:], in0=bpi[:], in1=tm2[:], op=ALU.subtract), r=[bp_b], w=[bp_b])
                ctx.op("dve", lambda: dve.tensor_tensor(out=rc_r[:], in0=mag_l[:, p0:p0 + 8], in1=car_r[:, p0:p0 + 8], op=ALU.mult), r=[mag_b, cb], w=[cb])
                ctx.op("dve", lambda: dve.tensor_tensor(out=rc_i[:], in0=mag_l[:, p0:p0 + 8], in1=car_i[:, p0:p0 + 8], op=ALU.mult), r=[mag_b, cb], w=[cb])
                ctx.op("dve", lambda: dve.tensor_tensor(out=bpr[:, :, 0], in0=bpr[:, :, 0], in1=rc_r[:], op=ALU.add), r=[bp_b, cb], w=[bp_b])
                ctx.op("dve", lambda: dve.tensor_tensor(out=bpi[:, :, 0], in0=bpi[:, :, 0], in1=rc_i[:], op=ALU.add), r=[bp_b, cb], w=[bp_b])
                (gr, gi), g_b = g_ring.next()
                rt = rtab[:, p0:p0 + 8, :].rearrange("p a b -> p (a b)")
                ctx.op("dve", lambda: dve.tensor_tensor_scan(out=gr[:].rearrange("p a b -> p (a b)"), data0=rt, data1=bpr[:].rearrange("p a b -> p (a b)"),
                                                             initial=0.0, op0=ALU.mult, op1=ALU.add), r=[bp_b, mag_b], w=[g_b])
                ctx.op("dve", lambda: dve.tensor_tensor_scan(out=gi[:].rearrange("p a b -> p (a b)"), data0=rt, data1=bpi[:].rearrange("p a b -> p (a b)"),
                                                             initial=0.0, op0=ALU.mult, op1=ALU.add), r=[bp_b, mag_b], w=[g_b])
                Ec = Ec_t[:, p0:p0 + 8]
                Es = Es_t[:, p0:p0 + 8]
                ctx.op("dve", lambda: dve.tensor_tensor(out=ctm[:], in0=Ec, in1=gr[:, :, 127], op=ALU.mult), r=[T_b, g_b, cb], w=[cb])
                ctx.op("dve", lambda: dve.tensor_tensor(out=ctm2[:], in0=Es, in1=gi[:, :, 127], op=ALU.mult), r=[T_b, g_b, cb], w=[cb])
                ctx.op("dve", lambda: dve.tensor_tensor(out=car_r[:, p0:p0 + 8], in0=ctm[:], in1=ctm2[:], op=ALU.subtract), r=[cb], w=[cb])
                ctx.op("dve", lambda: dve.tensor_tensor(out=ctm[:], in0=Ec, in1=gi[:, :, 127], op=ALU.mult), r=[T_b, g_b, cb], w=[cb])
                ctx.op("dve", lambda: dve.tensor_tensor(out=ctm2[:], in0=Es, in1=gr[:, :, 127], op=ALU.mult), r=[T_b, g_b, cb], w=[cb])
                ctx.op("dve", lambda: dve.tensor_tensor(out=car_i[:, p0:p0 + 8], in0=ctm[:], in1=ctm2[:], op=ALU.add), r=[cb], w=[cb])
                return s_, hf, gr, gi, g_b

            def ssm_stage2(ti_, s_, hf, gr, gi, g_b):
                tsl = slice(s_ * 128, (s_ + 1) * 128)
                p0 = 8 * hf
                Tcq = Tcb[:, p0:p0 + 8, :]
                Tsq = Tsb[:, p0:p0 + 8, :]
                (hr, mhi), (h_b, hi_b) = h_ring.next()
                ctx.op("dve", lambda: dve.tensor_tensor(out=tp3[:], in0=gr[:], in1=Tcq, op=ALU.mult), r=[g_b, T_b], w=[tpd_b])
                ctx.op("dve", lambda: dve.tensor_tensor(out=tp4[:], in0=gi[:], in1=Tsq, op=ALU.mult), r=[g_b, T_b], w=[tpd_b])
                ctx.op("dve", lambda: dve.tensor_tensor(out=hr[:], in0=tp3[:], in1=tp4[:], op=ALU.subtract), r=[tpd_b], w=[h_b])
                ctx.op("dve", lambda: dve.tensor_tensor(out=tp5[:], in0=gi[:], in1=Tcq, op=ALU.mult), r=[g_b, T_b], w=[tpd_b])
                ctx.op("dve", lambda: dve.tensor_tensor(out=tp6[:], in0=gr[:], in1=Tsq, op=ALU.mult), r=[g_b, T_b], w=[tpd_b])
                ctx.op("dve", lambda: dve.tensor_tensor(out=mhi[:], in0=tp5[:], in1=tp6[:], op=ALU.add), r=[tpd_b], w=[hi_b])
                for qi in range(2):
                    q = 2 * hf + qi
                    yreg = psY[:, qi * 128:(qi + 1) * 128]
                    ctx.op("pe", lambda: pe.matmul(yreg, lhsT=Dg[:, 0, q, :], rhs=ti_[0][:, q, tsl], start=True, stop=False), r=[BC_b, ti_[1][q]], w=[_psY_b], sig=False)
                    ctx.op("pe", lambda: pe.matmul(yreg, lhsT=Dg[:, 1, q, :], rhs=ti_[0][:, q, tsl], start=False, stop=False), r=[BC_b, ti_[1][q]], w=[_psY_b], sig=False)
                    for i in range(4):
                        pr = 4 * q + i
                        ctx.op("pe", lambda: pe.matmul(yreg, lhsT=Cre[:, pr, :], rhs=hr[:, qi * 4 + i, :], start=False, stop=False),
                               r=[BC_b, h_b], w=[_psY_b], sig=False)
                        ctx.op("pe", lambda: pe.matmul(yreg, lhsT=Cim[:, pr, :], rhs=mhi[:, qi * 4 + i, :], start=False, stop=(i == 3)),
                               r=[BC_b, hi_b], w=[_psY_b], sig=(i == 3 and qi == 1))
                ctx.op("act", lambda: act.activation(out=ti_[2][:, 2 * hf:2 * hf + 2, tsl], in_=psY[:, 0:256].rearrange("p (a b) -> p a b", a=2), func=AF.Gelu),
                       r=[_psY_b], w=[ti_[3]])

            units = [(s_, hf) for s_ in range(TT // 128) for hf in range(2)]
            nu = len(units)
            ti_cur = (uT, uT_b, gT, gT_b)
            ti_nxt = (cur_tile[0], cur_tile[1]) if t + 1 < NTA else None
            if t + 1 < NTA:
                assert len(post_n) + 16 <= 6 * per_unit
            if t == 0:
                a_res = {0: ssm_stage1a(ti_cur, *units[0]), 1: ssm_stage1a(ti_cur, *units[1])}
                b_res = {0: ssm_stage1b(*a_res.pop(0))}
            else:
                a_res, b_res = next_a, next_b
            next_a, next_b = {}, {}
            for k in range(nu):
                if k + 2 < nu:
                    a_res[k + 2] = ssm_stage1a(ti_cur, *units[k + 2])
                elif ti_nxt is not None:
                    next_a[k + 2 - nu] = ssm_stage1a(ti_nxt, *units[k + 2 - nu])
                if k + 1 < nu:
                    b_res[k + 1] = ssm_stage1b(*a_res.pop(k + 1))
                elif ti_nxt is not None:
                    next_b[0] = ssm_stage1b(*next_a.pop(0))
                ssm_stage2(ti_cur, *b_res.pop(k))
                for _ in range(per_unit):
                    if tasks:
                        tasks.pop(0)()
            while tasks:
                tasks.pop(0)()
            post = post_tasks(t, gT, gT_b, szs, szs_b)
        for tk_ in post:
            tk_()
        ctx.barrier()

    with ExitStack() as sB:
        S_ring = Ring([(ps(sB, "psSc%d" % i, [128, 512]), Buf("psSc%d" % i)) for i in range(3)])
        O_ring = Ring([(ps(sB, "psO%d" % i, [128, 512]), Buf("psO%d" % i)) for i in range(2)])
        psG = ps(sB, "psG", [128, 512]); psG_b = Buf("psG")
        psM = ps(sB, "psM", [128, 512]); psM_v = psM[:].bitcast(BF16); psM_b = Buf("psM")
        psBc = ps(sB, "psBc", [128, 512]); psBc_b = Buf("psBc")

        PMASK = sb(sB, "PMASK", [128, NCH, 32]); PAST = sb(sB, "PAST", [128, NCH, 32]); cm_b = Buf("cmask")
        ctx.dma(PMASK[:].rearrange("p a b -> p (a b)"), PMASK_d, w=[cm_b])
        ctx.dma(PAST[:].rearrange("p a b -> p (a b)"), PAST_d, w=[cm_b])
        b31 = sb(sB, "b31", [128, 8])
        ctx.dma(b31[:], b31_d, w=[cm_b])
        ones_f = sb(sB, "ones_f", [128, 64], BF16)
        ctx.op("dve", lambda: dve.memset(ones_f[:], 1.0), w=[cm_b])

        Htiles = sb(sB, "Htiles", [128, 8, 3, 128], BF16); H_b = Buf("H")
        KA = [sb(sB, "KA%d" % i, [96, S], BF16) for i in range(2)]
        QA = [sb(sB, "QA%d" % i, [96, S], BF16) for i in range(2)]
        VE = [sb(sB, "VE%d" % i, [128, NKT, 65], BF16) for i in range(2)]
        KA_b = [Buf("KA%d" % i) for i in range(2)]
        QA_b = [Buf("QA%d" % i) for i in range(2)]
        QM_b = [[Buf("QM%d_%d" % (i, t)) for t in range(NT)] for i in range(2)]
        VE_b = [Buf("VE%d" % i) for i in range(2)]

        def load_head(h):
            i = h % 2
            pr, hh = h // 2, h % 2
            ctx.dma(KA[i][0:64, :], KT_d[pr, hh * 64:(hh + 1) * 64, :], w=[KA_b[i]])
            ctx.dma(QA[i][0:64, :], QT_d[pr, hh * 64:(hh + 1) * 64, :], w=[QA_b[i]])
            ctx.dma(VE[i][:, :, 0:64], V_d[h], w=[VE_b[i]])

        NHB = 0 if stop in ('setup', 'A') else (NH if stop != 'B1' else 1)
        ctx.dma(KA[0][64:96, :], EOH_d, w=[KA_b[0]])
        ctx.op("pool", lambda: pool.memset(VE[0][:, :, 64:65], 1.0), w=[VE_b[0]])
        if NHB:
            load_head(0)
        ctx.dma(KA[1][64:96, :], EOH_d, w=[KA_b[1]])
        ctx.op("pool", lambda: pool.memset(VE[1][:, :, 64:65], 1.0), w=[VE_b[1]])
        if True:
            sb0 = sB
            relb_f = sb(sb0, "relb_f", [32, 8]); relb_h = sb(sb0, "relb_h", [32, 8], BF16)
            OHt = sb(sb0, "OHt", [32, 384], BF16); negm = sb(sb0, "negm", [8, 384])
            Ff = sb(sb0, "Ff", [8, 384]); Fh = sb(sb0, "Fh", [8, 2, 384], BF16)
            Pb = Buf("biasprep")
            ctx.dma(relb_f[:], relb_d, w=[Pb]); ctx.dma(OHt[:], OH_d, w=[Pb]); ctx.dma(negm[:], NEGM_d, w=[Pb])
            ctx.op("dve", lambda: dve.tensor_copy(out=relb_h[:], in_=relb_f[:]), r=[Pb], w=[Pb])
            ctx.op("pe", lambda: pe.matmul(psG[0:8, 0:384], lhsT=relb_h[:, :], rhs=OHt[:, :], start=True, stop=True), r=[Pb], w=[psG_b])
            ctx.op("dve", lambda: dve.tensor_tensor(out=Ff[:], in0=psG[0:8, 0:384], in1=negm[:], op=ALU.add), r=[psG_b, Pb], w=[Pb])
            ctx.op("dve", lambda: dve.tensor_copy(out=Fh[:, 0, :], in_=Ff[:]), r=[Pb], w=[Pb])
            ctx.op("dve", lambda: dve.tensor_scalar(out=Fh[:, 1, :], in0=Ff[:], scalar1=Ff[:, 380:381], scalar2=None, op0=ALU.subtract), r=[Pb], w=[Pb])
            ctx.dma(F_d[0:8, :], Fh[:, 0, :], r=[Pb], w=[Pb])
            ctx.dma(F_d[8:16, :], Fh[:, 1, :], r=[Pb], w=[Pb])
            for h in range(NH):
                for j, (row, off) in enumerate(((h, 0), (h, 128), (8 + h, 128))):
                    src = bass.AP(tensor=F_d.tensor, offset=row * 384 + off, ap=[[1, 128], [1, 128]])
                    ctx.dma(Htiles[:, h, j, :], src, r=[Pb], w=[H_b])
        ksum = sb(sB, "ksum", [64, 32]); kmT = [sb(sB, "kmT%d" % i, [64, 32], BF16) for i in range(2)]
        km_b = [Buf("km%d" % i) for i in range(2)]
        gm = sb(sB, "gm", [128, 4, 32]); ns_ = sb(sB, "ns", [128, 4, 32]); thr8 = sb(sB, "thr8", [128, 4, 8]); gm_b = Buf("gm")
        Mq_ring = Ring([(sb(sB, "Mq%d" % i, [128, 4, 96], BF16), Buf("Mq%d" % i)) for i in range(2)])
        for mq, mqb in Mq_ring.items:
            ctx.op("pool", lambda: pool.memset(mq[:], 0.0), w=[mqb])
        PT_ring = Ring([(sb(sB, "PT%d" % i, [128, 512], BF16), Buf("PT%d" % i)) for i in range(4)])
        osb_ring = Ring([(sb(sB, "osb%d" % i, [65, 512]), Buf("osb%d" % i)) for i in range(2)])
        rec_ring = Ring([(sb(sB, "rec%d" % i, [65, 512], BF16), Buf("rec%d" % i)) for i in range(2)])
        oo_ring = Ring([(sb(sB, "oo%d" % i, [64, 512], BF16), Buf("oo%d" % i)) for i in range(2)])

        def prep_head(h):
            i = h % 2
            Kh = KA[i]
            ctx.op("dve", lambda: dve.tensor_reduce(out=ksum[:, 0:NB], in_=Kh[0:64, :].rearrange("p (a b) -> p a b", b=BLK), axis=AX.X, op=ALU.add),
                   r=[KA_b[i]], w=[km_b[i]])
            if NB < 32:
                ctx.op("dve", lambda: dve.memset(ksum[:, NB:32], 0.0), r=[], w=[km_b[i]])
            ctx.op("dve", lambda: dve.tensor_scalar(out=kmT[i][:], in0=ksum[:], scalar1=1.0 / BLK, scalar2=None, op0=ALU.mult), r=[km_b[i]], w=[km_b[i]])

        def gate1(h, T):
            i = h % 2
            Qh = QA[i]
            c0 = 4 * T
            q0 = T * TT
            for ci in range(4):
                ctx.op("pe", lambda: pe.matmul(psG[:, ci * 32:(ci + 1) * 32], lhsT=Qh[0:64, q0 + ci * 128:q0 + (ci + 1) * 128], rhs=kmT[i][:, :], start=True, stop=True),
                       r=[QA_b[i], km_b[i]], w=[psG_b], sig=(ci == 3))
            ctx.op("dve", lambda: dve.tensor_tensor(out=gm[:], in0=psG[:, 0:128].rearrange("p (a b) -> p a b", a=4), in1=PMASK[:, c0:c0 + 4, :], op=ALU.add),
                   r=[psG_b, cm_b], w=[gm_b])
            for ci in range(4):
                ctx.op("dve", lambda: dve.max(out=thr8[:, ci, :], in_=gm[:, ci, :]), r=[gm_b], w=[gm_b])
            ctx.op("dve", lambda: dve.tensor_tensor(out=ns_[:], in0=gm[:], in1=thr8[:, :, 2:3].to_broadcast([128, 4, 32]), op=ALU.is_lt), r=[gm_b], w=[gm_b])
            ctx.op("dve", lambda: dve.tensor_scalar(out=ns_[:], in0=ns_[:], scalar1=NEG, scalar2=b31[:, h:h + 1], op0=ALU.mult, op1=ALU.add), r=[gm_b, cm_b], w=[gm_b])
            Mq, Mq_b = Mq_ring.next()
            ctx.op("dve", lambda: dve.tensor_tensor(out=Mq[:, :, 64:96], in0=ns_[:], in1=PAST[:, c0:c0 + 4, :], op=ALU.mult), r=[gm_b, cm_b], w=[Mq_b])
            return Mq, Mq_b

        def gate2(h, T, Mq, Mq_b):
            i = h % 2
            q0 = T * TT
            for ci in range(4):
                ctx.op("pe", lambda: pe.transpose(psM_v[0:96, ci * 128:(ci + 1) * 128], Mq[:, ci, :], ident[:]), r=[Mq_b, ident_b], w=[psM_b], sig=(ci == 3))
            ctx.op("dve", lambda: dve.tensor_copy(out=QA[i][64:96, q0:q0 + TT], in_=psM_v[64:96, 0:512]), r=[psM_b], w=[QM_b[i][T]])

        work = [(h, T) for h in range(NHB) for T in range(NT)]
        LA = 2
        if work:
            prep_head(0)
            g = gate1(0, 0)
            gate2(0, 0, *g)
        for wi, (h, T) in enumerate(work):
            i = h % 2
            if T == 0 and h + 1 < NHB:
                load_head(h + 1)
            Kh, Qh, Vh = KA[i], QA[i], VE[i]
            q0 = T * TT
            nxt_w = work[wi + 1] if wi + 1 < len(work) else None
            g = None
            if nxt_w is not None:
                if nxt_w[1] == 0:
                    prep_head(nxt_w[0])
                g = gate1(*nxt_w)
            Ops, Ops_b = O_ring.next()
            nkt = 4 * T + 4

            def emit_S(kt):
                cl = max(0, kt - 4 * T)
                cs = slice(cl * 128, 512)
                Sps, Sps_b = S_ring.next()
                adds = []
                ci0 = kt - 4 * T
                if 0 <= ci0 <= 3:
                    adds.append((ci0, 0))
                ci1 = kt + 1 - 4 * T
                if 0 <= ci1 <= 3:
                    adds.append((ci1, 1 if kt % 2 == 0 else 2))
                ctx.op("pe", lambda: pe.matmul(Sps[:, cs], lhsT=Kh[0:96, kt * 128:(kt + 1) * 128], rhs=Qh[0:96, q0 + cl * 128:q0 + 512], start=True, stop=(len(adds) == 0)),
                       r=[KA_b[i], QA_b[i], QM_b[i][T]], w=[Sps_b], sig=(len(adds) == 0))
                for ai, (cidx, j) in enumerate(adds):
                    last = ai == len(adds) - 1
                    ctx.op("pe", lambda: pe.matmul(Sps[:, cidx * 128:(cidx + 1) * 128], lhsT=Jm[:, :], rhs=Htiles[:, h, j, :], start=False, stop=last),
                           r=[Jm_b, H_b], w=[Sps_b], sig=last)
                return kt, cs, Sps, Sps_b

            def emit_PV(kt, cs, Sps, Sps_b):
                PT, PT_b = PT_ring.next()
                ctx.op("act", lambda: act.activation(out=PT[:, cs], in_=Sps[:, cs], func=AF.Exp), r=[Sps_b], w=[PT_b])
                ctx.op("pe", lambda: pe.matmul(Ops[0:65, cs], lhsT=Vh[:, kt, 0:65], rhs=PT[:, cs], start=(kt == 0), stop=(kt == nkt - 1)),
                       r=[VE_b[i], PT_b], w=[Ops_b], sig=(kt == nkt - 1))

            pend = []
            for kt in range(nkt):
                pend.append(emit_S(kt))
                if len(pend) > LA:
                    emit_PV(*pend.pop(0))
            while pend:
                emit_PV(*pend.pop(0))
            if g is not None:
                gate2(nxt_w[0], nxt_w[1], *g)
            osb, osb_b = osb_ring.next()
            ctx.op("act", lambda: act.copy(out=osb[:], in_=Ops[0:65, :]), r=[Ops_b], w=[osb_b])
            rec, rec_b = rec_ring.next()
            ctx.op("dve", lambda: dve.reciprocal(out=osb[64:65, :], in_=osb[64:65, :]), r=[osb_b], w=[osb_b])
            ctx.op("dve", lambda: dve.tensor_copy(out=rec[64:65, :], in_=osb[64:65, :]), r=[osb_b], w=[rec_b])
            ctx.op("pe", lambda: pe.matmul(psBc[0:64, :], lhsT=ones_f[64:65, 0:64], rhs=rec[64:65, :], start=True, stop=True), r=[rec_b, cm_b], w=[psBc_b])
            oo, oo_b = oo_ring.next()
            ctx.op("dve", lambda: dve.tensor_tensor(out=oo[:], in0=osb[0:64, :], in1=psBc[0:64, :], op=ALU.mult), r=[osb_b, psBc_b], w=[oo_b])
            ctx.dma(OA_d[h, :, q0:q0 + TT], oo[:], r=[oo_b])
        ctx.barrier()

    with ExitStack() as sC:
        psT_v = [ps(sC, "psTc%d" % i, [128, 512])[:].bitcast(BF16) for i in range(2)]
        psT_b = [Buf("psTc0"), Buf("psTc1")]
        mm_ring = Ring([(ps(sC, "mc%d" % i, [128, 512]), Buf("mc%d" % i)) for i in range(6)])
        tk_ring = mm_ring

        Wz = sb(sC, "Wz", [128, 8, 2560], BF16); Wz_b = Buf("Wz")
        Wap = sb(sC, "Wap", [128, 4, 1024], BF16); Wap_b = Buf("Wap")
        Wout = sb(sC, "Wout", [128, 8, 1024], BF16); Wout_b = Buf("Wout")
        Wpg = sb(sC, "Wpg", [128, 8, 1024], BF16); Wpg_b = Buf("Wpg")
        Wpp = sb(sC, "Wpp", [128, 2, 1024], BF16); Wpp_b = Buf("Wpp")
        lng = sb(sC, "lng", [128, D]); lnb = sb(sC, "lnb", [128, D]); ln_b = Buf("ln")
        ctx.dma(lng[:], lng_d, w=[ln_b]); ctx.dma(lnb[:], lnb_d, w=[ln_b])
        NS = TC // 128
        x_ring = Ring([(sb(sC, "xc%d" % i, [128, D]), Buf("xc%d" % i)) for i in range(2 * NS)])
        xb = sb(sC, "xbC", [128, NS, D], BF16); xb_b = [Buf("xbC%d" % i) for i in range(NS)]
        xTc_ring = Ring([(sb(sC, "xTC%d" % i, [128, 8, TC], BF16), Buf("xTC%d" % i)) for i in range(2)])
        p_ring = Ring([(sb(sC, "pc%d" % i, [128, 256]), Buf("pc%d" % i)) for i in range(NS)])
        pbf = sb(sC, "pbf", [128, NS, 256], BF16); pbf_b = [Buf("pbf%d" % i) for i in range(NS)]
        pTc_ring = Ring([(sb(sC, "pT%d" % i, [128, 2, TC], BF16), Buf("pT%d" % i)) for i in range(2)])
        sza = sb(sC, "sza", [128, 4, TC], BF16); sza_b = Buf("sza")
        sga = sb(sC, "sga", [128, 8, TC], BF16); sga_b = Buf("sga")
        sgs = sb(sC, "sgs", [128, 8, TC], BF16); sgs_b = Buf("sgs")
        oa_ring = Ring([(sb(sC, "oat%d" % i, [128, 4, TC], BF16), Buf("oat%d" % i)) for i in range(2)])
        oz = sb(sC, "oz", [128, 4, TC], BF16); oz_b = Buf("oz")
        ys_ring = Ring([(sb(sC, "yst%d" % i, [128, 8, TC], BF16), Buf("yst%d" % i)) for i in range(2)])
        mg_ring = Ring([(sb(sC, "merge%d" % i, [128, 8, TC], BF16), Buf("merge%d" % i)) for i in range(2)])
        ta_ring = Ring([(sb(sC, "tma%d" % i, [128, TC]), Buf("tma%d" % i)) for i in range(2)])
        tb_ring = Ring([(sb(sC, "tmb%d" % i, [128, TC]), Buf("tmb%d" % i)) for i in range(2)])
        s_ring = Ring([(sb(sC, "srow%d" % i, [128, D]), Buf("srow%d" % i)) for i in range(2)])
        o_ring = Ring([(sb(sC, "orow%d" % i, [128, D]), Buf("orow%d" % i)) for i in range(2)])
        st_ring = Ring([((sb(sC, "bst%d" % i, [128, 2, 6]), sb(sC, "bmv%d" % i, [128, 2]), sb(sC, "brs%d" % i, [128, 2])), Buf("bst%d" % i)) for i in range(2)])

        def issue_xC(t_):
            tk0 = t_ * TC
            xts_ = []
            for sub in range(NS):
                xt, xtb = x_ring.next()
                xts_.append((xt, xtb))
                ctx.dma(xt[:], x_d[tk0 + sub * 128:tk0 + (sub + 1) * 128, :], w=[xtb])
                ctx.op("pool", lambda: pool.tensor_copy(out=xb[:, sub, :], in_=xt[:]), r=[xtb], w=[xb_b[sub]])
                pt_, ptb = p_ring.next()
                ctx.dma(pt_[:], p_d[tk0 + sub * 128:tk0 + (sub + 1) * 128, :], w=[ptb])
                ctx.op("pool", lambda: pool.tensor_copy(out=pbf[:, sub, :], in_=pt_[:]), r=[ptb], w=[pbf_b[sub]])
            oat_, oat_b_ = oa_ring.next()
            ctx.dma(oat_[:], OA_d[:, :, tk0:tk0 + TC].rearrange("(pr hh) d t -> (hh d) pr t", hh=2), w=[oat_b_])
            yst_, yst_b_ = ys_ring.next()
            ctx.dma(yst_[:], YS_d[:, :, tk0:tk0 + TC].rearrange("c p t -> p c t"), w=[yst_b_])
            return xts_, (oat_, oat_b_), (yst_, yst_b_)

        NTCC = 0 if stop in ('setup', 'A', 'B', 'B1') else NTC
        if NTCC:
            nxt = issue_xC(0)
        stage_ring = Ring([(sb(sC, "wstc%d" % i, [128, 2048]), Buf("wstc%d" % i)) for i in range(2)])
        load_weight(Wz, Wz_b, w_in_d, 1536, 512, 8, stage_ring, dcol0=0)
        load_weight(Wz, Wz_b, w_in_d, 3072, 2048, 8, stage_ring, dcol0=512)
        load_weight(Wap, Wap_b, w_ap_d, 0, 1024, 4, stage_ring)
        load_weight(Wpg, Wpg_b, w_pg_d, 0, 1024, 8, stage_ring)
        load_weight(Wpp, Wpp_b, w_pp_d, 0, 1024, 2, stage_ring)
        load_weight(Wout, Wout_b, w_out_d, 0, 1024, 8, stage_ring)

        for t in range(NTCC):
            tok0 = t * TC
            xts, (oat, oat_b), (yst, yst_b) = nxt
            xT, xT_b = xTc_ring.next()
            pT, pT_b = pTc_ring.next()
            merge, merge_b = mg_ring.next()
            for dk in range(8):
                hf = dk % 2
                tp = psT_v[hf][:, 0:TC]
                for sub in range(NS):
                    ctx.op("pe", lambda: pe.transpose(tp[:, sub * 128:(sub + 1) * 128], xb[:, sub, dk * 128:(dk + 1) * 128], ident[:]),
                           r=[xb_b[sub], ident_b], w=[psT_b[hf]], sig=(sub == NS - 1))
                ctx.op("act", lambda: act.copy(out=xT[:, dk, :], in_=tp), r=[psT_b[hf]], w=[xT_b])
            for kc in range(2):
                hf = kc % 2
                tp = psT_v[hf][:, 0:TC]
                for sub in range(NS):
                    ctx.op("pe", lambda: pe.transpose(tp[:, sub * 128:(sub + 1) * 128], pbf[:, sub, kc * 128:(kc + 1) * 128], ident[:]),
                           r=[pbf_b[sub], ident_b], w=[psT_b[hf]], sig=(sub == NS - 1))
                ctx.op("act", lambda: act.copy(out=pT[:, kc, :], in_=tp), r=[psT_b[hf]], w=[pT_b])
            if t + 1 < NTCC:
                nxt = issue_xC(t + 1)
            for cc in range(20):
                pt, pb = mm_ring.next()
                mm_group(pt[:, 0:TC], pb, [(Wz[:, dk, cc * 128:(cc + 1) * 128], xT[:, dk, :]) for dk in range(8)], r=[Wz_b, xT_b])
                if cc < 4:
                    ctx.op("act", lambda: act.activation(out=sza[:, cc, :], in_=pt[:, 0:TC], func=AF.Silu), r=[pb], w=[sza_b])
                elif cc < 12:
                    ctx.op("act", lambda: act.activation(out=sga[:, cc - 4, :], in_=pt[:, 0:TC], func=AF.Sigmoid), r=[pb], w=[sga_b])
                else:
                    ctx.op("act", lambda: act.activation(out=sgs[:, cc - 12, :], in_=pt[:, 0:TC], func=AF.Sigmoid), r=[pb], w=[sgs_b])
            ctx.op("pool", lambda: pool.tensor_tensor(out=oz[:], in0=oat[:], in1=sza[:], op=ALU.mult), r=[oat_b, sza_b], w=[oz_b])
            for oc in range(8):
                pt, pb = mm_ring.next()
                mm_group(pt[:, 0:TC], pb, [(Wap[:, kc, oc * 128:(oc + 1) * 128], oz[:, kc, :]) for kc in range(4)], r=[Wap_b, oz_b])
                ta, ta_b = ta_ring.next()
                tb, tb_b = tb_ring.next()
                ctx.op("dve", lambda: dve.tensor_tensor(out=ta[:], in0=pt[:, 0:TC], in1=sga[:, oc, :], op=ALU.mult), r=[pb, sga_b], w=[ta_b])
                ctx.op("pool", lambda: pool.tensor_tensor(out=tb[:], in0=yst[:, oc, :], in1=sgs[:, oc, :], op=ALU.mult), r=[yst_b, sgs_b], w=[tb_b])
                ctx.op("dve", lambda: dve.tensor_tensor(out=merge[:, oc, :], in0=ta[:], in1=tb[:], op=ALU.add), r=[ta_b, tb_b], w=[merge_b])
            for sub in range(NS):
                xt, xtb = xts[sub]
                srow, srow_b = s_ring.next()
                tsl = slice(sub * 128, (sub + 1) * 128)
                (bst, bmv, brs), bst_b = st_ring.next()
                for hc in range(2):
                    csl = slice(hc * 512, (hc + 1) * 512)
                    pg, pg_b = tk_ring.next()
                    mm_group(pg[:, :], pg_b, [(xT[:, dk, tsl], Wpg[:, dk, csl]) for dk in range(8)], r=[Wpg_b, xT_b])
                    pp, pp_b = tk_ring.next()
                    mm_group(pp[:, :], pp_b, [(pT[:, kc, tsl], Wpp[:, kc, csl]) for kc in range(2)], r=[Wpp_b, pT_b])
                    mx, mx_b = tk_ring.next()
                    mm_group(mx[:, :], mx_b, [(merge[:, kc, tsl], Wout[:, kc, csl]) for kc in range(8)], r=[Wout_b, merge_b])
                    ctx.op("act", lambda: act.activation(out=srow[:, csl], in_=pg[:, :], func=AF.Sigmoid), r=[pg_b], w=[srow_b])
                    ctx.op("dve", lambda: dve.tensor_tensor(out=srow[:, csl], in0=srow[:, csl], in1=pp[:, :], op=ALU.mult), r=[srow_b, pp_b], w=[srow_b])
                    ctx.op("dve", lambda: dve.tensor_tensor(out=srow[:, csl], in0=srow[:, csl], in1=mx[:, :], op=ALU.add), r=[srow_b, mx_b], w=[srow_b])
                    ctx.op("dve", lambda: dve.scalar_tensor_tensor(out=srow[:, csl], in0=xt[:, csl], scalar=ALPHA, in1=srow[:, csl], op0=ALU.mult, op1=ALU.add),
                           r=[srow_b, xtb], w=[srow_b])
                    ctx.op("dve", lambda: dve.bn_stats(out=bst[:, hc, :], in_=srow[:, csl]), r=[srow_b], w=[bst_b])
                ctx.op("dve", lambda: dve.bn_aggr(out=bmv[:], in_=bst[:].rearrange("p a b -> p (a b)")), r=[bst_b], w=[bst_b])
                ctx.op("dve", lambda: dve.tensor_scalar(out=brs[:, 0:1], in0=bmv[:, 1:2], scalar1=LN_EPS, scalar2=None, op0=ALU.add), r=[bst_b], w=[bst_b])
                ctx.op("act", lambda: act.activation(out=brs[:, 0:1], in_=brs[:, 0:1], func=AF.Sqrt), r=[bst_b], w=[bst_b])
                ctx.op("dve", lambda: dve.reciprocal(out=brs[:, 0:1], in_=brs[:, 0:1]), r=[bst_b], w=[bst_b])
                ctx.op("dve", lambda: dve.scalar_tensor_tensor(out=brs[:, 1:2], in0=bmv[:, 0:1], scalar=-1.0, in1=brs[:, 0:1], op0=ALU.mult, op1=ALU.mult),
                       r=[bst_b], w=[bst_b])
                orow, orow_b = o_ring.next()
                ctx.op("dve", lambda: dve.tensor_scalar(out=orow[:], in0=srow[:], scalar1=brs[:, 0:1], scalar2=brs[:, 1:2], op0=ALU.mult, op1=ALU.add), r=[srow_b, bst_b], w=[orow_b])
                ctx.op("dve", lambda: dve.tensor_tensor(out=orow[:], in0=orow[:], in1=lng[:], op=ALU.mult), r=[orow_b, ln_b], w=[orow_b])
                ctx.op("pool", lambda: pool.tensor_tensor(out=orow[:], in0=orow[:], in1=lnb[:], op=ALU.add), r=[orow_b, ln_b], w=[orow_b])
                ctx.dma(out_d[tok0 + sub * 128:tok0 + (sub + 1) * 128, :], orow[:], r=[orow_b])
        ctx.barrier(engines=["sp"])
    es.close()
    return nc


def host_consts(S):
    bf = ml_dtypes.bfloat16
    NCH = S // 128
    c = {}
    c["ident"] = np.eye(128, dtype=np.float32).astype(bf)
    c["Jm"] = np.ascontiguousarray(np.eye(128, dtype=np.float32)[::-1]).astype(bf)
    i = np.arange(384)
    dist = i - 127
    bk = t5_bucket_np(np.maximum(dist, 0))
    OH = np.zeros((32, 384), np.float32)
    valid = dist >= 0
    OH[bk[valid], i[valid]] = 1.0
    c["OH"] = OH.astype(bf)
    NEGM = np.zeros((8, 384), np.float32)
    NEGM[:, ~valid] = NEG
    c["NEGM"] = NEGM
    keys = np.arange(S)
    EOH = (keys[None, :] // BLK == np.arange(32)[:, None]).astype(np.float32)
    c["EOH"] = EOH.astype(bf)
    ch = np.arange(NCH)
    blk = ch // 2
    n = np.arange(32)
    past = (n[None, :] < blk[:, None])
    PM = np.where(past, 0.0, -1e30).astype(np.float32)
    c["PMASK"] = np.ascontiguousarray(np.broadcast_to(PM.reshape(1, -1), (128, NCH * 32)))
    c["PAST01"] = np.ascontiguousarray(np.broadcast_to(past.astype(np.float32).reshape(1, -1), (128, NCH * 32)))
    c["jidx"] = np.ascontiguousarray(np.broadcast_to(np.arange(129, dtype=np.float32)[None, :], (128, 129)))
    return c


def host_params(inp):
    o = {}
    a_re = inp["ssm_a_re"][0]; a_im = inp["ssm_a_im"][0]; ldt = inp["ssm_log_dt"][0]
    b_re = inp["ssm_b_re"][0]; b_im = inp["ssm_b_im"][0]; c_re = inp["ssm_c_re"][0]; c_im = inp["ssm_c_im"][0]
    pidx = np.arange(128); g2 = pidx // 64; n = pidx % 64
    pr = np.arange(16)
    G = 2 * pr[None, :] + g2[:, None]
    o["are_l"] = np.ascontiguousarray(a_re[G, n[:, None]])
    o["aim_l"] = np.ascontiguousarray(a_im[G, n[:, None]])
    o["ldt_l"] = np.ascontiguousarray(ldt[G])
    Gc = 2 * pr[:, None] + g2[None, :]
    are_pc = a_re[Gc, n[None, :]]
    aim_pc = a_im[Gc, n[None, :]]
    ldt_pc = ldt[Gc]
    for name, v in (("are_rep", are_pc), ("aim_rep", aim_pc), ("ldt_rep", ldt_pc)):
        o[name] = np.ascontiguousarray(np.broadcast_to(v.reshape(1, 2048), (128, 2048))).astype(np.float32)
    brT = np.zeros((128, 16, 128), np.float32); biT = np.zeros((128, 16, 128), np.float32)
    cre = np.zeros((128, 16, 128), np.float32); cim = np.zeros((128, 16, 128), np.float32)
    for p_ in range(16):
        r0 = (p_ % 4) * 32
        for gg in range(2):
            g = 2 * p_ + gg
            brT[r0 + gg * 16:r0 + gg * 16 + 16, p_, gg * 64:(gg + 1) * 64] = b_re[g].T
            biT[r0 + gg * 16:r0 + gg * 16 + 16, p_, gg * 64:(gg + 1) * 64] = b_im[g].T
            cre[gg * 64:(gg + 1) * 64, p_, r0 + gg * 16:r0 + gg * 16 + 16] = c_re[g].T
            cim[gg * 64:(gg + 1) * 64, p_, r0 + gg * 16:r0 + gg * 16 + 16] = c_im[g].T
    o["brT"] = brT.reshape(128, 2048); o["biT"] = biT.reshape(128, 2048)
    o["cre_pad"] = cre.reshape(128, 2048); o["cim_pad"] = cim.reshape(128, 2048)
    o["d_l"] = np.ascontiguousarray(inp["ssm_d"][0].reshape(4, 128).T)
    o["lng"] = np.ascontiguousarray(np.broadcast_to(inp["ln_g"][0][None, :], (128, D)))
    o["lnb"] = np.ascontiguousarray(np.broadcast_to(inp["ln_b"][0][None, :], (128, D)))
    o["relb"] = np.ascontiguousarray(inp["rel_bias"])
    o["b31rep"] = np.ascontiguousarray(np.broadcast_to(inp["rel_bias"][31][None, :], (128, 8)))
    o["w_in"] = np.ascontiguousarray(inp["w_in"][0]); o["w_ap"] = np.ascontiguousarray(inp["w_attn_proj"][0])
    o["w_sp"] = np.ascontiguousarray(inp["w_ssm_proj"][0]); o["w_out"] = np.ascontiguousarray(inp["w_out"][0])
    o["w_glu"] = np.ascontiguousarray(inp["w_glu"][0]); o["w_pg"] = np.ascontiguousarray(inp["w_ple_gate"][0])
    o["w_pp"] = np.ascontiguousarray(inp["w_ple_proj"][0])
    return {k: np.asarray(v, np.float32) for k, v in o.items()}


_NC_CACHE = {}


def run(inputs, S, n_cores, **bk):
    inp = {k: np.asarray(v) for k, v in inputs.items()}
    if S not in _NC_CACHE:
        _NC_CACHE[S] = build_nc(S, **bk)
    nc = _NC_CACHE[S]
    shared = host_params(inp)
    shared.update(host_consts(S))
    in_maps = []
    for b in range(n_cores):
        m = dict(shared)
        m["x"] = np.ascontiguousarray(inp["x"][b], dtype=np.float32)
        m["p"] = np.ascontiguousarray(inp["p"][0, b], dtype=np.float32)
        in_maps.append(m)
    res = run_bass_kernel_spmd(nc, in_maps, core_ids=list(range(n_cores)))
    return np.stack([np.asarray(r["out"]) for r in res.results], axis=0).astype(np.float32)


def kernel(**inputs):
    return run(inputs, 8192, 8)
```

```python
import math
from contextlib import ExitStack

import numpy as np
import ml_dtypes

import concourse.bass as bass
import concourse.mybir as mybir
from concourse.bass_utils import run_bass_kernel_spmd

F32 = mybir.dt.float32
BF16 = mybir.dt.bfloat16
I32 = mybir.dt.int32
AF = mybir.ActivationFunctionType
ALU = mybir.AluOpType
AX = mybir.AxisListType

D = 1024
NH = 8
HD = 64
BLK = 256
NEG = -30000.0
TWO_PI = 2.0 * math.pi
ALPHA = 2.0 ** 0.25
LN_EPS = 1e-5


class Buf:
    __slots__ = ("w", "r", "name")

    def __init__(self, name=""):
        self.w = None
        self.r = {}
        self.name = name


class Ring:
    def __init__(self, items):
        self.items = items
        self.i = 0

    def next(self):
        it = self.items[self.i % len(self.items)]
        self.i += 1
        return it


class Ctx:
    def __init__(self, nc, es, n_dma_sems=24):
        self.nc = nc
        self.engs = {"pe": nc.tensor, "act": nc.scalar, "dve": nc.vector, "pool": nc.gpsimd, "sp": nc.sync}
        self.sem = {}
        self.cnt = {}
        self.seen = {e: {} for e in self.engs}
        self.pend = {e: ([], []) for e in self.engs}
        for e in self.engs:
            self.sem[e] = es.enter_context(nc.semaphore("s_" + e))
            self.cnt[e] = 0
        self.dq = []
        for i in range(n_dma_sems):
            k = "d%d" % i
            self.sem[k] = es.enter_context(nc.semaphore("s_" + k))
            self.cnt[k] = 0
            self.dq.append(k)
        self.dqi = 0

    def _wait(self, e, tok):
        if tok is None:
            return
        k, v = tok
        if v <= 0 or self.seen[e].get(k, 0) >= v:
            return
        self.engs[e].wait_ge(self.sem[k], v)
        self.seen[e][k] = v

    def _deps(self, e, r, w):
        for b in r:
            if b.w is not None and not (e == "pe" and b.w[0] == "pe"):
                self._wait(e, b.w)
        for b in w:
            if b.w is not None and b.w[0] != e:
                self._wait(e, b.w)
            for k, v in b.r.items():
                if k != e:
                    self._wait(e, (k, v))

    def _reg(self, tok, r, w):
        k, v = tok
        for b in r:
            if b.r.get(k, 0) < v:
                b.r[k] = v
        for b in w:
            b.w = tok
            b.r = {}

    def op(self, e, fn, r=(), w=(), sig=True):
        self._deps(e, r, w)
        inst = fn()
        pr, pw = self.pend[e]
        if not sig:
            pr.extend(r)
            pw.extend(w)
            return None
        self.cnt[e] += 1
        inst.then_inc(self.sem[e], 1)
        tok = (e, self.cnt[e])
        self._reg(tok, list(r) + pr, list(w) + pw)
        self.pend[e] = ([], [])
        return tok

    def dma(self, out, in_, r=(), w=(), q="sp"):
        k = self.dq[self.dqi % len(self.dq)]
        self.dqi += 1
        self._wait(q, (k, self.cnt[k]))
        self._deps(q, r, w)
        inst = self.engs[q].dma_start(out=out, in_=in_)
        self.cnt[k] += 16
        inst.then_inc(self.sem[k], 16)
        tok = (k, self.cnt[k])
        self._reg(tok, r, w)
        return tok

    def barrier(self, engines=None):
        toks = [(k, v) for k, v in self.cnt.items() if v > 0]
        for e in (engines or self.engs):
            for t in toks:
                if t[0] != e:
                    self._wait(e, t)


def t5_bucket_np(dist):
    dist = np.asarray(dist, np.int64)
    d = np.maximum(dist, 1).astype(np.float32)
    large = 16 + (np.log(d / np.float32(16)) / np.float32(math.log(128 / 16)) * np.float32(16)).astype(np.int32)
    large = np.minimum(large, 31)
    return np.where(dist < 16, dist, large)


def build_nc(S, TT=512, TC=256, stop=None, nt_lim=None):
    NT = S // TT
    NTC = S // TC
    NKT = S // 128
    NB = S // BLK
    NCH = S // 128
    assert NB <= 32
    nc = bass.Bass("TRN2", target_bir_lowering=False)
    es = ExitStack()
    ctx = Ctx(nc, es)
    pe, act, dve, pool = nc.tensor, nc.scalar, nc.vector, nc.gpsimd

    def din(name, shape, dt=F32):
        return nc.dram_tensor(name, list(shape), dt, kind="ExternalInput").ap()

    def dscr(name, shape, dt=BF16):
        return nc.dram_tensor(name, list(shape), dt, kind="Internal").ap()

    x_d = din("x", [S, D])
    p_d = din("p", [S, 256])
    w_in_d = din("w_in", [D, 5120])
    w_ap_d = din("w_ap", [512, D])
    w_sp_d = din("w_sp", [512, D])
    w_out_d = din("w_out", [D, D])
    w_glu_d = din("w_glu", [512, D])
    w_pg_d = din("w_pg", [D, D])
    w_pp_d = din("w_pp", [256, D])
    are_rep_d = din("are_rep", [128, 2048])
    aim_rep_d = din("aim_rep", [128, 2048])
    ldt_rep_d = din("ldt_rep", [128, 2048])
    brT_d = din("brT", [128, 2048])
    biT_d = din("biT", [128, 2048])
    cre_d = din("cre_pad", [128, 2048])
    cim_d = din("cim_pad", [128, 2048])
    are_l_d = din("are_l", [128, 16])
    aim_l_d = din("aim_l", [128, 16])
    ldt_l_d = din("ldt_l", [128, 16])
    d_l_d = din("d_l", [128, 4])
    lng_d = din("lng", [128, D])
    lnb_d = din("lnb", [128, D])
    relb_d = din("relb", [32, 8])
    b31_d = din("b31rep", [128, 8])
    ident_d = din("ident", [128, 128], BF16)
    J_d = din("Jm", [128, 128], BF16)
    OH_d = din("OH", [32, 384], BF16)
    NEGM_d = din("NEGM", [8, 384])
    EOH_d = din("EOH", [32, S], BF16)
    PMASK_d = din("PMASK", [128, NCH * 32])
    PAST_d = din("PAST01", [128, NCH * 32])
    jidx_d = din("jidx", [128, 129])
    out_d = nc.dram_tensor("out", [S, D], F32, kind="ExternalOutput").ap()

    QT_d = dscr("QT", [4, 128, S])
    KT_d = dscr("KT", [4, 128, S])
    V_d = dscr("Vs", [8, 128, S // 128, 64])
    OA_d = dscr("OA", [8, 64, S])
    YS_d = dscr("YS", [8, 128, S])
    F_d = dscr("Fd", [16, 384])

    def sb(stack, name, shape, dt=F32):
        return stack.enter_context(nc.sbuf_tensor("sb_" + name, list(shape), dt))

    def ps(stack, name, shape, dt=F32):
        return stack.enter_context(nc.psum_tensor("ps_" + name, list(shape), dt))

    ident = sb(es, "ident", [128, 128], BF16)
    ident_b = Buf("ident")
    Jm = sb(es, "Jm", [128, 128], BF16)
    Jm_b = Buf("J")
    ctx.dma(ident[:], ident_d, w=[ident_b])
    ctx.dma(Jm[:], J_d, w=[Jm_b])

    cast_rr = Ring(["act", "dve", "pool"])

    def cast_copy(e, out, in_):
        if e == "act":
            return lambda: act.copy(out=out, in_=in_)
        if e == "dve":
            return lambda: dve.tensor_copy(out=out, in_=in_)
        return lambda: pool.tensor_copy(out=out, in_=in_)

    def load_weight(dst, dst_b, src, col0, ncols, KC, stage_ring, dcol0=0):
        cw = 2048 // KC
        for c0 in range(0, ncols, cw):
            w_ = min(cw, ncols - c0)
            st, stb = stage_ring.next()
            stv = st[:, 0:KC * w_].rearrange("p (kc c) -> p kc c", kc=KC)
            ctx.dma(stv, src[:, col0 + c0:col0 + c0 + w_].rearrange("(kc p) c -> p kc c", p=128), w=[stb])
            e = cast_rr.next()
            ctx.op(e, cast_copy(e, dst[:, :, dcol0 + c0:dcol0 + c0 + w_], stv), r=[stb], w=[dst_b])

    def mm_group(out_ap, out_b, pairs, r):
        n = len(pairs)
        for i, (l, rh) in enumerate(pairs):
            ctx.op("pe", lambda: pe.matmul(out_ap, lhsT=l, rhs=rh, start=(i == 0), stop=(i == n - 1)),
                   r=r, w=[out_b], sig=(i == n - 1))

    with ExitStack() as sA:
        _psT = ps(sA, "psT0", [128, 512])[:].bitcast(BF16)
        psT_v = [_psT[:, 0:512], _psT[:, 512:1024]]
        _psT_b = Buf("psT")
        psT_b = [_psT_b, _psT_b]
        mm_ring = Ring([(ps(sA, "mm%d" % i, [128, 512]), Buf("mm%d" % i)) for i in range(2)])
        psS_ring = Ring([[(ps(sA, "psS%d_%d" % (j, i), [128, 512]), Buf("psS%d_%d" % (j, i))) for i in range(2)] for j in range(2)])
        psY = ps(sA, "psY", [128, 512])
        _psY_b = Buf("psY")
        psY_b = [_psY_b] * 4

        Wqk = sb(sA, "Wqk", [128, 8, 1024], BF16); Wqk_b = Buf("Wqk")
        Wv = sb(sA, "Wv", [128, 8, 512], BF16); Wv_b = Buf("Wv")
        Wuz = sb(sA, "Wuz", [128, 8, 1024], BF16); Wuz_b = Buf("Wuz")
        Wglu = sb(sA, "Wglu", [128, 4, 1024], BF16); Wglu_b = Buf("Wglu")
        Wsp = sb(sA, "Wsp", [128, 4, 1024], BF16); Wsp_b = Buf("Wsp")
        Bre = sb(sA, "Bre", [128, 16, 128], BF16); Bim = sb(sA, "Bim", [128, 16, 128], BF16)
        Cre = sb(sA, "Cre", [128, 16, 128], BF16); Cim = sb(sA, "Cim", [128, 16, 128], BF16)
        BC_b = Buf("BC")
        T_b = Buf("T")
        rtab = sb(sA, "rtab", [128, 16, 128])
        Dg = sb(sA, "Dg", [128, 2, 4, 128], BF16)
        Ec_t = sb(sA, "Ec_t", [128, 16]); Es_t = sb(sA, "Es_t", [128, 16])
        Tcb = sb(sA, "Tcb", [128, 16, 128], BF16); Tsb = sb(sA, "Tsb", [128, 16, 128], BF16)
        mag_l = sb(sA, "mag_l", [128, 16]); mag_b = Buf("mag")
        d_l = sb(sA, "d_l", [128, 4]); d_b = Buf("d")
        car_r = sb(sA, "car_r", [128, 16]); car_i = sb(sA, "car_i", [128, 16])
        car_b = [Buf("car%d" % q) for q in range(2)]
        ctx.dma(d_l[:], d_l_d, w=[d_b])

        with ExitStack() as s0:
            Tc = sb(s0, "Tc", [128, 16, 128]); Ts = sb(s0, "Ts", [128, 16, 128])
            dhi = sb(s0, "dhi", [128, 4], BF16); dlo = sb(s0, "dlo", [128, 4])
            A_ = sb(s0, "pA", [128, 2048]); Bm = sb(s0, "pB", [128, 2048]); L_ = sb(s0, "pL", [128, 2048])
            t0 = sb(s0, "pt0", [128, 2048]); t1 = sb(s0, "pt1", [128, 2048]); t2 = sb(s0, "pt2", [128, 2048])
            t3 = sb(s0, "pt3", [128, 2048]); t4 = sb(s0, "pt4", [128, 2048]); t5 = sb(s0, "pt5", [128, 2048])
            ti = sb(s0, "pti", [128, 2048], I32)
            bR = sb(s0, "pbR", [128, 2048]); bI = sb(s0, "pbI", [128, 2048])
            al = sb(s0, "al", [128, 16]); bl = sb(s0, "bl", [128, 16]); ll = sb(s0, "ll", [128, 16])
            th = sb(s0, "th", [128, 16]); jx = sb(s0, "jx", [128, 129])
            th128 = sb(s0, "th128", [128, 16]); sc16 = sb(s0, "sc16", [128, 16])
            P = Buf("setup")

            Pl = []
            for t_, d_ in ((A_, are_rep_d), (Bm, aim_rep_d), (L_, ldt_rep_d), (bR, brT_d), (bI, biT_d),
                           (t0, cre_d), (t1, cim_d), (al, are_l_d), (bl, aim_l_d), (ll, ldt_l_d), (jx, jidx_d)):
                pb_ = Buf("pl")
                Pl.append(pb_)
                ctx.dma(t_[:], d_, w=[pb_])
            ctx.op("dve", lambda: dve.memset(th[:], 0.0), r=Pl, w=[P])

            def V(fn):
                ctx.op("dve", fn, r=[P], w=[P])

            def A(fn):
                ctx.op("act", fn, r=[P], w=[P])

            V(lambda: dve.tensor_copy(out=Cre[:].rearrange("p a b -> p (a b)"), in_=t0[:]))
            V(lambda: dve.tensor_scalar(out=Cim[:].rearrange("p a b -> p (a b)"), in0=t1[:], scalar1=-1.0, scalar2=None, op0=ALU.mult))

            def emit_sin(out, x, n, shift, xs):
                xi = ti[:, 0:n]
                V(lambda: dve.tensor_scalar(out=xs, in0=x, scalar1=shift, scalar2=None, op0=ALU.add))
                V(lambda: dve.tensor_scalar(out=xi, in0=xs, scalar1=1.0 / TWO_PI, scalar2=None, op0=ALU.mult))
                V(lambda: dve.tensor_copy(out=out, in_=xi))
                V(lambda: dve.scalar_tensor_tensor(out=out, in0=out, scalar=-TWO_PI, in1=xs, op0=ALU.mult, op1=ALU.add))
                V(lambda: dve.tensor_scalar(out=xs, in0=out, scalar1=math.pi, scalar2=-TWO_PI, op0=ALU.is_gt, op1=ALU.mult))
                V(lambda: dve.tensor_tensor(out=out, in0=out, in1=xs, op=ALU.add))
                V(lambda: dve.tensor_scalar(out=xs, in0=out, scalar1=-math.pi, scalar2=TWO_PI, op0=ALU.is_lt, op1=ALU.mult))
                V(lambda: dve.tensor_tensor(out=out, in0=out, in1=xs, op=ALU.add))
                V(lambda: dve.tensor_scalar(out=out, in0=out, scalar1=math.pi, scalar2=-math.pi, op0=ALU.min, op1=ALU.max))
                A(lambda: act.activation(out=out, in_=out, func=AF.Sin))

            A(lambda: act.activation(out=L_[:], in_=L_[:], func=AF.Exp))
            V(lambda: dve.tensor_tensor(out=t0[:], in0=L_[:], in1=A_[:], op=ALU.mult))
            A(lambda: act.activation(out=t0[:], in_=t0[:], func=AF.Exp))
            V(lambda: dve.tensor_tensor(out=t1[:], in0=L_[:], in1=Bm[:], op=ALU.mult))
            emit_sin(t2[:], t1[:], 2048, 0.0, t4[:])
            emit_sin(t3[:], t1[:], 2048, math.pi / 2, t4[:])
            V(lambda: dve.tensor_tensor(out=t3[:], in0=t3[:], in1=t0[:], op=ALU.mult))
            V(lambda: dve.tensor_tensor(out=t2[:], in0=t2[:], in1=t0[:], op=ALU.mult))
            V(lambda: dve.tensor_scalar(out=t3[:], in0=t3[:], scalar1=-1.0, scalar2=None, op0=ALU.add))
            V(lambda: dve.tensor_tensor(out=t0[:], in0=A_[:], in1=A_[:], op=ALU.mult))
            V(lambda: dve.tensor_tensor(out=t1[:], in0=Bm[:], in1=Bm[:], op=ALU.mult))
            V(lambda: dve.tensor_tensor(out=t0[:], in0=t0[:], in1=t1[:], op=ALU.add))
            V(lambda: dve.reciprocal(out=t0[:], in_=t0[:]))
            V(lambda: dve.tensor_tensor(out=t4[:], in0=t3[:], in1=A_[:], op=ALU.mult))
            V(lambda: dve.tensor_tensor(out=t1[:], in0=t2[:], in1=Bm[:], op=ALU.mult))
            V(lambda: dve.tensor_tensor(out=t4[:], in0=t4[:], in1=t1[:], op=ALU.add))
            V(lambda: dve.tensor_tensor(out=t4[:], in0=t4[:], in1=t0[:], op=ALU.mult))
            V(lambda: dve.tensor_tensor(out=t5[:], in0=t2[:], in1=A_[:], op=ALU.mult))
            V(lambda: dve.tensor_tensor(out=t1[:], in0=t3[:], in1=Bm[:], op=ALU.mult))
            V(lambda: dve.tensor_tensor(out=t5[:], in0=t5[:], in1=t1[:], op=ALU.subtract))
            V(lambda: dve.tensor_tensor(out=t5[:], in0=t5[:], in1=t0[:], op=ALU.mult))
            V(lambda: dve.tensor_tensor(out=t0[:], in0=t4[:], in1=bR[:], op=ALU.mult))
            V(lambda: dve.tensor_tensor(out=t1[:], in0=t5[:], in1=bI[:], op=ALU.mult))
            V(lambda: dve.tensor_tensor(out=Bre[:].rearrange("p a b -> p (a b)"), in0=t0[:], in1=t1[:], op=ALU.subtract))
            V(lambda: dve.tensor_tensor(out=t0[:], in0=t4[:], in1=bI[:], op=ALU.mult))
            V(lambda: dve.tensor_tensor(out=t1[:], in0=t5[:], in1=bR[:], op=ALU.mult))
            V(lambda: dve.tensor_tensor(out=Bim[:].rearrange("p a b -> p (a b)"), in0=t0[:], in1=t1[:], op=ALU.add))
            A(lambda: act.activation(out=ll[:], in_=ll[:], func=AF.Exp))
            V(lambda: dve.tensor_tensor(out=al[:], in0=ll[:], in1=al[:], op=ALU.mult))
            A(lambda: act.activation(out=mag_l[:], in_=al[:], func=AF.Exp))
            V(lambda: dve.tensor_tensor(out=th[:], in0=ll[:], in1=bl[:], op=ALU.mult))
            V(lambda: dve.tensor_tensor(out=t0[:].rearrange("p (a b) -> p a b", a=16),
                                        in0=th[:].rearrange("p (a o) -> p a o", o=1).to_broadcast([128, 16, 128]),
                                        in1=jx[:, 0:128].rearrange("p (o b) -> p o b", o=1).to_broadcast([128, 16, 128]), op=ALU.mult))
            emit_sin(Ts[:].rearrange("p a b -> p (a b)"), t0[:], 2048, 0.0, t1[:])
            emit_sin(Tc[:].rearrange("p a b -> p (a b)"), t0[:], 2048, math.pi / 2, t1[:])
            V(lambda: dve.tensor_scalar(out=th128[:], in0=th[:], scalar1=128.0, scalar2=None, op0=ALU.mult))
            emit_sin(Es_t[:], th128[:], 16, 0.0, sc16[:])
            emit_sin(Ec_t[:], th128[:], 16, math.pi / 2, sc16[:])
            V(lambda: dve.tensor_copy(out=Tcb[:], in_=Tc[:]))
            V(lambda: dve.tensor_copy(out=Tsb[:], in_=Ts[:]))
            V(lambda: dve.tensor_copy(out=rtab[:], in_=mag_l[:].rearrange("p (a o) -> p a o", o=1).to_broadcast([128, 16, 128])))
            V(lambda: dve.memset(rtab[:, :, 0:1], 0.0))
            ctx.op("dve", lambda: dve.tensor_copy(out=dhi[:], in_=d_l[:]), r=[P, d_b], w=[P])
            V(lambda: dve.tensor_tensor(out=dlo[:], in0=d_l[:], in1=dhi[:], op=ALU.subtract))
            for q_ in range(4):
                ctx.op("dve", lambda: dve.tensor_scalar(out=Dg[:, 0, q_, :], in0=ident[:], scalar1=dhi[:, q_:q_ + 1], scalar2=None, op0=ALU.mult), r=[P, ident_b], w=[P])
                ctx.op("dve", lambda: dve.tensor_scalar(out=Dg[:, 1, q_, :], in0=ident[:], scalar1=dlo[:, q_:q_ + 1], scalar2=None, op0=ALU.mult), r=[P, ident_b], w=[P])
            V(lambda: dve.memset(car_r[:], 0.0))
            V(lambda: dve.memset(car_i[:], 0.0))
            ctx.op("dve", lambda: dve.memset(t0[:, 0:1], 0.0), r=[P], w=[P, BC_b, T_b, mag_b] + car_b)

            ctx.barrier()
        with ExitStack() as s0:
            stage_ring = Ring([(sb(s0, "wst%d" % i, [128, 2048]), Buf("wst%d" % i)) for i in range(2)])
            load_weight(Wqk, Wqk_b, w_in_d, 0, 1024, 8, stage_ring)
            load_weight(Wv, Wv_b, w_in_d, 1024, 512, 8, stage_ring)
            load_weight(Wuz, Wuz_b, w_in_d, 2048, 1024, 8, stage_ring)
            load_weight(Wglu, Wglu_b, w_glu_d, 0, 1024, 4, stage_ring)
            load_weight(Wsp, Wsp_b, w_sp_d, 0, 1024, 4, stage_ring)
            ctx.barrier()

        x_ring = Ring([(sb(sA, "xa%d" % i, [128, D]), Buf("xa%d" % i)) for i in range(3)])
        xb = sb(sA, "xbA", [128, 4, D], BF16)
        xb_b = [Buf("xb%d" % i) for i in range(4)]
        xT_ring = Ring([(sb(sA, "xTA%d" % i, [128, 8, TT], BF16), Buf("xTA%d" % i)) for i in range(2)])
        uT_ring = Ring([(sb(sA, "uT%d" % i, [128, 4, TT], BF16), [Buf("uT%d_%d" % (i, q)) for q in range(4)]) for i in range(2)])
        szs_ring = Ring([(sb(sA, "szs%d" % i, [128, 4, TT], BF16), Buf("szs%d" % i)) for i in range(2)])
        qk_ring = Ring([(sb(sA, "qkst%d" % i, [128, TT], BF16), Buf("qkst%d" % i)) for i in range(2)])
        v_ring = Ring([(sb(sA, "vst%d" % i, [128, 512], BF16), Buf("vst%d" % i)) for i in range(2)])
        ys_ring = Ring([(sb(sA, "ysst%d" % i, [128, TT], BF16), Buf("ysst%d" % i)) for i in range(2)])
        bpr = sb(sA, "bpr", [128, 8, 128], BF16); bpi = sb(sA, "bpi", [128, 8, 128], BF16); tm2 = sb(sA, "tm2", [128, 8, 128], BF16)
        bp_b = Buf("bp")
        Sb_ring = Ring([((sb(sA, "Sbr%d" % i, [128, 8, 128], BF16), sb(sA, "Sbi%d" % i, [128, 8, 128], BF16)), Buf("Sb%d" % i)) for i in range(2)])
        g_ring = Ring([((sb(sA, "gr%d" % i, [128, 8, 128], BF16), sb(sA, "gi%d" % i, [128, 8, 128], BF16)), Buf("g%d" % i)) for i in range(2)])
        tp3 = sb(sA, "tp3", [128, 8, 128], BF16); tp4 = sb(sA, "tp4", [128, 8, 128], BF16)
        tp5 = sb(sA, "tp5", [128, 8, 128], BF16); tp6 = sb(sA, "tp6", [128, 8, 128], BF16); tpd_b = Buf("tpd")
        h_ring = Ring([((sb(sA, "hr%d" % i, [128, 8, 128], BF16), sb(sA, "mhi%d" % i, [128, 8, 128], BF16)), (Buf("hr%d" % i), Buf("hi%d" % i))) for i in range(2)])
        rc_r = sb(sA, "rc_r", [128, 8]); rc_i = sb(sA, "rc_i", [128, 8]); ctm = sb(sA, "ctm", [128, 8]); ctm2 = sb(sA, "ctm2", [128, 8])
        gT_ring = Ring([(sb(sA, "gT%d" % i, [128, 4, TT], BF16), Buf("gT%d" % i)) for i in range(2)])
        sigb_ring = Ring([(sb(sA, "sigb%d" % i, [128, TT]), Buf("sigb%d" % i)) for i in range(2)])
        yg = sb(sA, "yg", [128, 4, TT], BF16); yg_b = Buf("yg")

        def issue_xA(t_):
            for sub in range(4):
                xt, xtb = x_ring.next()
                ctx.dma(xt[:], x_d[t_ * TT + sub * 128:t_ * TT + (sub + 1) * 128, :], w=[xtb])
                ctx.op("act", lambda: act.copy(out=xb[:, sub, :], in_=xt[:]), r=[xtb], w=[xb_b[sub]])

        NTA = 0 if stop == 'setup' else (nt_lim or NT)

        def proj_tasks(t_):
            tk0 = t_ * TT
            xT_, xT_b_ = xT_ring.next()
            uT_, uT_b_ = uT_ring.next()
            szs_, szs_b_ = szs_ring.next()
            tasks_ = []

            def t_tr(dk):
                hf = dk % 2
                tp = psT_v[hf][:, 0:512]
                for sub in range(4):
                    ctx.op("pe", lambda: pe.transpose(tp[:, sub * 128:(sub + 1) * 128], xb[:, sub, dk * 128:(dk + 1) * 128], ident[:]),
                           r=[xb_b[sub], ident_b], w=[psT_b[hf]], sig=(sub == 3))
                ctx.op("act", lambda: act.copy(out=xT_[:, dk, :], in_=tp), r=[psT_b[hf]], w=[xT_b_])
                if dk == 7 and t_ + 1 < NTA:
                    issue_xA(t_ + 1)

            def t_qk(cc):
                pt, pb = mm_ring.next()
                mm_group(pt[:, 0:TT], pb, [(Wqk[:, dk, cc * 128:(cc + 1) * 128], xT_[:, dk, :]) for dk in range(8)], r=[Wqk_b, xT_b_])
                stg, stb = qk_ring.next()
                if cc < 4:
                    ctx.op("act", lambda: act.activation(out=stg[:], in_=pt[:, 0:TT], func=AF.Copy, scale=0.125), r=[pb], w=[stb])
                    ctx.dma(QT_d[cc, :, tk0:tk0 + TT], stg[:], r=[stb])
                else:
                    ctx.op("act", lambda: act.copy(out=stg[:], in_=pt[:, 0:TT]), r=[pb], w=[stb])
                    ctx.dma(KT_d[cc - 4, :, tk0:tk0 + TT], stg[:], r=[stb])

            def t_v(sub):
                pt, pb = mm_ring.next()
                mm_group(pt[:, 0:512], pb, [(xT_[:, dk, sub * 128:(sub + 1) * 128], Wv[:, dk, :]) for dk in range(8)], r=[Wv_b, xT_b_])
                stg, stb = v_ring.next()
                ctx.op("act", lambda: act.copy(out=stg[:], in_=pt[:, 0:512]), r=[pb], w=[stb])
                ctx.dma(V_d[:, :, t_ * 4 + sub, :].rearrange("h p d -> p h d"), stg[:].rearrange("p (h d) -> p h d", h=8), r=[stb])

            def t_uz(cc):
                pt, pb = mm_ring.next()
                mm_group(pt[:, 0:TT], pb, [(Wuz[:, dk, cc * 128:(cc + 1) * 128], xT_[:, dk, :]) for dk in range(8)], r=[Wuz_b, xT_b_])
                if cc < 4:
                    ctx.op("act", lambda: act.copy(out=uT_[:, cc, :], in_=pt[:, 0:TT]), r=[pb], w=[uT_b_[cc]])
                else:
                    ctx.op("act", lambda: act.activation(out=szs_[:, cc - 4, :], in_=pt[:, 0:TT], func=AF.Silu), r=[pb], w=[szs_b_])

            for dk in range(8):
                tasks_.append(lambda dk=dk: t_tr(dk))
            for cc in range(8):
                tasks_.append(lambda cc=cc: t_uz(cc))
            for cc in range(8):
                tasks_.append(lambda cc=cc: t_qk(cc))
            for sub in range(4):
                tasks_.append(lambda sub=sub: t_v(sub))
            return tasks_, (uT_, uT_b_, szs_, szs_b_)

        def post_tasks(t_, gT_, gT_b_, szs_, szs_b_):
            tk0 = t_ * TT
            tasks_ = []

            def t_glu(oc):
                pa, pab = mm_ring.next()
                mm_group(pa[:, 0:TT], pab, [(Wglu[:, kc, oc * 128:(oc + 1) * 128], gT_[:, kc, :]) for kc in range(4)], r=[Wglu_b, gT_b_])
                pbk, pbb = mm_ring.next()
                mm_group(pbk[:, 0:TT], pbb, [(Wglu[:, kc, (oc + 4) * 128:(oc + 5) * 128], gT_[:, kc, :]) for kc in range(4)], r=[Wglu_b, gT_b_])
                sg, sgb = sigb_ring.next()
                ctx.op("act", lambda: act.activation(out=sg[:], in_=pbk[:, 0:TT], func=AF.Sigmoid), r=[pbb], w=[sgb])
                ctx.op("dve", lambda: dve.tensor_tensor(out=sg[:], in0=sg[:], in1=pa[:, 0:TT], op=ALU.mult), r=[sgb, pab], w=[sgb])
                ctx.op("dve", lambda: dve.tensor_tensor(out=yg[:, oc, :], in0=sg[:], in1=szs_[:, oc, :], op=ALU.mult), r=[sgb, szs_b_], w=[yg_b])

            def t_sp(oc):
                pt, pb = mm_ring.next()
                mm_group(pt[:, 0:TT], pb, [(Wsp[:, kc, oc * 128:(oc + 1) * 128], yg[:, kc, :]) for kc in range(4)], r=[Wsp_b, yg_b])
                stg, stb = ys_ring.next()
                ctx.op("act", lambda: act.copy(out=stg[:], in_=pt[:, 0:TT]), r=[pb], w=[stb])
                ctx.dma(YS_d[oc, :, tk0:tk0 + TT], stg[:], r=[stb])

            for oc in range(4):
                tasks_.append(lambda oc=oc: t_glu(oc))
            for oc in range(8):
                tasks_.append(lambda oc=oc: t_sp(oc))
            return tasks_

        post = []
        cur_tile = None
        if NTA:
            issue_xA(0)
            tasks, cur_tile = proj_tasks(0)
            for tk_ in tasks:
                tk_()
        for t in range(NTA):
            tok0 = t * TT
            uT, uT_b, szs, szs_b = cur_tile
            gT, gT_b = gT_ring.next()
            tasks = list(post)
            post_n = list(post)
            if t + 1 < NTA:
                ptk, cur_tile = proj_tasks(t + 1)
                tasks += ptk
            per_unit = (len(tasks) + 7) // 8
            def ssm_stage1a(ti_, s_, hf):
                tsl = slice(s_ * 128, (s_ + 1) * 128)
                (Sbr, Sbi), Sb_b = Sb_ring.next()
                for qi in range(2):
                    q = 2 * hf + qi
                    (Sre, Sre_b), (Sim, Sim_b) = psS_ring.next()
                    for i in range(4):
                        pr = 4 * q + i
                        ctx.op("pe", lambda: pe.matmul(Sre[:, i * 128:(i + 1) * 128], lhsT=Bre[:, pr, :], rhs=ti_[0][:, q, tsl], start=True, stop=True),
                               r=[BC_b, ti_[1][q]], w=[Sre_b], sig=(i == 3))
                    for i in range(4):
                        pr = 4 * q + i
                        ctx.op("pe", lambda: pe.matmul(Sim[:, i * 128:(i + 1) * 128], lhsT=Bim[:, pr, :], rhs=ti_[0][:, q, tsl], start=True, stop=True),
                               r=[BC_b, ti_[1][q]], w=[Sim_b], sig=(i == 3))
                    ctx.op("act", lambda: act.copy(out=Sbr[:, qi * 4:(qi + 1) * 4, :].rearrange("p a b -> p (a b)"), in_=Sre[:, 0:512]), r=[Sre_b], w=[Sb_b])
                    ctx.op("act", lambda: act.copy(out=Sbi[:, qi * 4:(qi + 1) * 4, :].rearrange("p a b -> p (a b)"), in_=Sim[:, 0:512]), r=[Sim_b], w=[Sb_b])
                return s_, hf, Sbr, Sbi, Sb_b

            def ssm_stage1b(s_, hf, Sbr, Sbi, Sb_b):
                p0 = 8 * hf
                Tcq = Tcb[:, p0:p0 + 8, :]
                Tsq = Tsb[:, p0:p0 + 8, :]
                cb = car_b[hf]
                ctx.op("dve", lambda: dve.tensor_tensor(out=bpr[:], in0=Sbr[:], in1=Tcq, op=ALU.mult), r=[Sb_b, T_b], w=[bp_b])
                ctx.op("dve", lambda: dve.tensor_tensor(out=tm2[:], in0=Sbi[:], in1=Tsq, op=ALU.mult), r=[Sb_b, T_b], w=[bp_b])
                ctx.op("dve", lambda: dve.tensor_tensor(out=bpr[:], in0=bpr[:], in1=tm2[:], op=ALU.add), r=[bp_b], w=[bp_b])
                ctx.op("dve", lambda: dve.tensor_tensor(out=bpi[:], in0=Sbi[:], in1=Tcq, op=ALU.mult), r=[Sb_b, T_b, bp_b], w=[bp_b])
                ctx.op("dve", lambda: dve.tensor_tensor(out=tm2[:], in0=Sbr[:], in1=Tsq, op=ALU.mult), r=[Sb_b, T_b, bp_b], w=[bp_b])
                ctx.op("dve", lambda: dve.tensor_tensor(out=bpi[:], in0=bpi[:], in1=tm2[:], op=ALU.subtract), r=[bp_b], w=[bp_b])
                ctx.op("dve", lambda: dve.tensor_tensor(out=rc_r[:], in0=mag_l[:, p0:p0 + 8], in1=car_r[:, p0:p0 + 8], op=ALU.mult), r=[mag_b, cb], w=[cb])
                ctx.op("dve", lambda: dve.tensor_tensor(out=rc_i[:], in0=mag_l[:, p0:p0 + 8], in1=car_i[:, p0:p0 + 8], op=ALU.mult), r=[mag_b, cb], w=[cb])
                ctx.op("dve", lambda: dve.tensor_tensor(out=bpr[:, :, 0], in0=bpr[:, :, 0], in1=rc_r[:], op=ALU.add), r=[bp_b, cb], w=[bp_b])
                ctx.op("dve", lambda: dve.tensor_tensor(out=bpi[:, :, 0], in0=bpi[:, :, 0], in1=rc_i[:], op=ALU.add), r=[bp_b, cb], w=[bp_b])
                (gr, gi), g_b = g_ring.next()
                rt = rtab[:, p0:p0 + 8, :].rearrange("p a b -> p (a b)")
                ctx.op("dve", lambda: dve.tensor_tensor_scan(out=gr[:].rearrange("p a b -> p (a b)"), data0=rt, data1=bpr[:].rearrange("p a b -> p (a b)"),
                                                             initial=0.0, op0=ALU.mult, op1=ALU.add), r=[bp_b, mag_b], w=[g_b])
                ctx.op("dve", lambda: dve.tensor_tensor_scan(out=gi[:].rearrange("p a b -> p (a b)"), data0=rt, data1=bpi[:].rearrange("p a b -> p (a b)"),
                                                             initial=0.0, op0=ALU.mult, op1=ALU.add), r=[bp_b, mag_b], w=[g_b])
                Ec = Ec_t[:, p0:p0 + 8]
                Es = Es_t[:, p0:p0 + 8]
                ctx.op("dve", lambda: dve.tensor_tensor(out=ctm[:], in0=Ec, in1=gr[:, :, 127], op=ALU.mult), r=[T_b, g_b, cb], w=[cb])
                ctx.op("dve", lambda: dve.tensor_tensor(out=ctm2[:], in0=Es, in1=gi[:, :, 127], op=ALU.mult), r=[T_b, g_b, cb], w=[cb])
                ctx.op("dve", lambda: dve.tensor_tensor(out=car_r[:, p0:p0 + 8], in0=ctm[:], in1=ctm2[:], op=ALU.subtract), r=[cb], w=[cb])
                ctx.op("dve", lambda: dve.tensor_tensor(out=ctm[:], in0=Ec, in1=gi[:, :, 127], op=ALU.mult), r=[T_b, g_b, cb], w=[cb])
                ctx.op("dve", lambda: dve.tensor_tensor(out=ctm2[:], in0=Es, in1=gr[:, :, 127], op=ALU.mult), r=[T_b, g_b, cb], w=[cb])
                ctx.op("dve", lambda: dve.tensor_tensor(out=car_i[:, p0:p0 + 8], in0=ctm[:], in1=ctm2[:], op=ALU.add), r=[cb], w=[cb])
                return s_, hf, gr, gi, g_b

            def ssm_stage2(ti_, s_, hf, gr, gi, g_b):
                tsl = slice(s_ * 128, (s_ + 1) * 128)
                p0 = 8 * hf
                Tcq = Tcb[:, p0:p0 + 8, :]
                Tsq = Tsb[:, p0:p0 + 8, :]
                (hr, mhi), (h_b, hi_b) = h_ring.next()
                ctx.op("dve", lambda: dve.tensor_tensor(out=tp3[:], in0=gr[:], in1=Tcq, op=ALU.mult), r=[g_b, T_b], w=[tpd_b])
                ctx.op("dve", lambda: dve.tensor_tensor(out=tp4[:], in0=gi[:], in1=Tsq, op=ALU.mult), r=[g_b, T_b], w=[tpd_b])
                ctx.op("dve", lambda: dve.tensor_tensor(out=hr[:], in0=tp3[:], in1=tp4[:], op=ALU.subtract), r=[tpd_b], w=[h_b])
                ctx.op("dve", lambda: dve.tensor_tensor(out=tp5[:], in0=gi[:], in1=Tcq, op=ALU.mult), r=[g_b, T_b], w=[tpd_b])
                ctx.op("dve", lambda: dve.tensor_tensor(out=tp6[:], in0=gr[:], in1=Tsq, op=ALU.mult), r=[g_b, T_b], w=[tpd_b])
                ctx.op("dve", lambda: dve.tensor_tensor(out=mhi[:], in0=tp5[:], in1=tp6[:], op=ALU.add), r=[tpd_b], w=[hi_b])
                for qi in range(2):
                    q = 2 * hf + qi
                    yreg = psY[:, qi * 128:(qi + 1) * 128]
                    ctx.op("pe", lambda: pe.matmul(yreg, lhsT=Dg[:, 0, q, :], rhs=ti_[0][:, q, tsl], start=True, stop=False), r=[BC_b, ti_[1][q]], w=[_psY_b], sig=False)
                    ctx.op("pe", lambda: pe.matmul(yreg, lhsT=Dg[:, 1, q, :], rhs=ti_[0][:, q, tsl], start=False, stop=False), r=[BC_b, ti_[1][q]], w=[_psY_b], sig=False)
                    for i in range(4):
                        pr = 4 * q + i
                        ctx.op("pe", lambda: pe.matmul(yreg, lhsT=Cre[:, pr, :], rhs=hr[:, qi * 4 + i, :], start=False, stop=False),
                               r=[BC_b, h_b], w=[_psY_b], sig=False)
                        ctx.op("pe", lambda: pe.matmul(yreg, lhsT=Cim[:, pr, :], rhs=mhi[:, qi * 4 + i, :], start=False, stop=(i == 3)),
                               r=[BC_b, hi_b], w=[_psY_b], sig=(i == 3 and qi == 1))
                ctx.op("act", lambda: act.activation(out=ti_[2][:, 2 * hf:2 * hf + 2, tsl], in_=psY[:, 0:256].rearrange("p (a b) -> p a b", a=2), func=AF.Gelu),
                       r=[_psY_b], w=[ti_[3]])

            units = [(s_, hf) for s_ in range(TT // 128) for hf in range(2)]
            nu = len(units)
            ti_cur = (uT, uT_b, gT, gT_b)
            ti_nxt = (cur_tile[0], cur_tile[1]) if t + 1 < NTA else None
            if t + 1 < NTA:
                assert len(post_n) + 16 <= 6 * per_unit
            if t == 0:
                a_res = {0: ssm_stage1a(ti_cur, *units[0]), 1: ssm_stage1a(ti_cur, *units[1])}
                b_res = {0: ssm_stage1b(*a_res.pop(0))}
            else:
                a_res, b_res = next_a, next_b
            next_a, next_b = {}, {}
            for k in range(nu):
                if k + 2 < nu:
                    a_res[k + 2] = ssm_stage1a(ti_cur, *units[k + 2])
                elif ti_nxt is not None:
                    next_a[k + 2 - nu] = ssm_stage1a(ti_nxt, *units[k + 2 - nu])
                if k + 1 < nu:
                    b_res[k + 1] = ssm_stage1b(*a_res.pop(k + 1))
                elif ti_nxt is not None:
                    next_b[0] = ssm_stage1b(*next_a.pop(0))
                ssm_stage2(ti_cur, *b_res.pop(k))
                for _ in range(per_unit):
                    if tasks:
                        tasks.pop(0)()
            while tasks:
                tasks.pop(0)()
            post = post_tasks(t, gT, gT_b, szs, szs_b)
        for tk_ in post:
            tk_()
        ctx.barrier()

    with ExitStack() as sB:
        S_ring = Ring([(ps(sB, "psSc%d" % i, [128, 512]), Buf("psSc%d" % i)) for i in range(3)])
        O_ring = Ring([(ps(sB, "psO%d" % i, [128, 512]), Buf("psO%d" % i)) for i in range(2)])
        psG = ps(sB, "psG", [128, 512]); psG_b = Buf("psG")
        psM = ps(sB, "psM", [128, 512]); psM_v = psM[:].bitcast(BF16); psM_b = Buf("psM")
        psBc = ps(sB, "psBc", [128, 512]); psBc_b = Buf("psBc")

        PMASK = sb(sB, "PMASK", [128, NCH, 32]); PAST = sb(sB, "PAST", [128, NCH, 32]); cm_b = Buf("cmask")
        ctx.dma(PMASK[:].rearrange("p a b -> p (a b)"), PMASK_d, w=[cm_b])
        ctx.dma(PAST[:].rearrange("p a b -> p (a b)"), PAST_d, w=[cm_b])
        b31 = sb(sB, "b31", [128, 8])
        ctx.dma(b31[:], b31_d, w=[cm_b])
        ones_f = sb(sB, "ones_f", [128, 64], BF16)
        ctx.op("dve", lambda: dve.memset(ones_f[:], 1.0), w=[cm_b])

        Htiles = sb(sB, "Htiles", [128, 8, 3, 128], BF16); H_b = [[Buf("H%d_%d" % (h_, j_)) for j_ in range(3)] for h_ in range(8)]
        KA = [sb(sB, "KA%d" % i, [96, S], BF16) for i in range(2)]
        QA = [sb(sB, "QA%d" % i, [96, S], BF16) for i in range(2)]
        VE = [sb(sB, "VE%d" % i, [128, NKT, 65], BF16) for i in range(2)]
        KA_b = [Buf("KA%d" % i) for i in range(2)]
        QA_b = [Buf("QA%d" % i) for i in range(2)]
        QM_b = [[Buf("QM%d_%d" % (i, t)) for t in range(NT)] for i in range(2)]
        VE_b = [Buf("VE%d" % i) for i in range(2)]

        def load_head(h):
            i = h % 2
            pr, hh = h // 2, h % 2
            ctx.dma(KA[i][0:64, :], KT_d[pr, hh * 64:(hh + 1) * 64, :], w=[KA_b[i]])
            ctx.dma(QA[i][0:64, :], QT_d[pr, hh * 64:(hh + 1) * 64, :], w=[QA_b[i]])
            ctx.dma(VE[i][:, :, 0:64], V_d[h], w=[VE_b[i]])

        NHB = 0 if stop in ('setup', 'A') else (NH if stop != 'B1' else 1)
        ctx.dma(KA[0][64:96, :], EOH_d, w=[KA_b[0]])
        ctx.op("pool", lambda: pool.memset(VE[0][:, :, 64:65], 1.0), w=[VE_b[0]])
        if NHB:
            load_head(0)
        ctx.dma(KA[1][64:96, :], EOH_d, w=[KA_b[1]])
        ctx.op("pool", lambda: pool.memset(VE[1][:, :, 64:65], 1.0), w=[VE_b[1]])
        if True:
            sb0 = sB
            relb_f = sb(sb0, "relb_f", [32, 8]); relb_h = sb(sb0, "relb_h", [32, 8], BF16)
            OHt = sb(sb0, "OHt", [32, 384], BF16); negm = sb(sb0, "negm", [8, 384])
            Ff = sb(sb0, "Ff", [8, 384]); Fh = sb(sb0, "Fh", [8, 2, 384], BF16)
            Pb = Buf("biasprep")
            Pb1, Pb2, Pb3 = Buf("bp1"), Buf("bp2"), Buf("bp3")
            ctx.dma(relb_f[:], relb_d, w=[Pb1]); ctx.dma(OHt[:], OH_d, w=[Pb2]); ctx.dma(negm[:], NEGM_d, w=[Pb3])
            ctx.op("dve", lambda: dve.tensor_copy(out=relb_h[:], in_=relb_f[:]), r=[Pb1, Pb2, Pb3], w=[Pb])
            ctx.op("pe", lambda: pe.matmul(psG[0:8, 0:384], lhsT=relb_h[:, :], rhs=OHt[:, :], start=True, stop=True), r=[Pb], w=[psG_b])
            ctx.op("dve", lambda: dve.tensor_tensor(out=Ff[:], in0=psG[0:8, 0:384], in1=negm[:], op=ALU.add), r=[psG_b, Pb], w=[Pb])
            ctx.op("dve", lambda: dve.tensor_copy(out=Fh[:, 0, :], in_=Ff[:]), r=[Pb], w=[Pb])
            ctx.op("dve", lambda: dve.tensor_scalar(out=Fh[:, 1, :], in0=Ff[:], scalar1=Ff[:, 380:381], scalar2=None, op0=ALU.subtract), r=[Pb], w=[Pb])
            ctx.dma(F_d[0:8, :], Fh[:, 0, :], r=[Pb], w=[Pb])
            ctx.dma(F_d[8:16, :], Fh[:, 1, :], r=[Pb], w=[Pb])
            for h in range(NH):
                for j, (row, off) in enumerate(((h, 0), (h, 128), (8 + h, 128))):
                    src = bass.AP(tensor=F_d.tensor, offset=row * 384 + off, ap=[[1, 128], [1, 128]])
                    ctx.dma(Htiles[:, h, j, :], src, r=[Pb], w=[H_b[h][j]])
        ksum = sb(sB, "ksum", [64, 32]); kmT = [sb(sB, "kmT%d" % i, [64, 32], BF16) for i in range(2)]
        km_b = [Buf("km%d" % i) for i in range(2)]
        gm = sb(sB, "gm", [128, 4, 32]); ns_ = sb(sB, "ns", [128, 4, 32]); thr8 = sb(sB, "thr8", [128, 4, 8]); gm_b = Buf("gm")
        Mq_ring = Ring([(sb(sB, "Mq%d" % i, [128, 4, 96], BF16), Buf("Mq%d" % i)) for i in range(2)])
        for mq, mqb in Mq_ring.items:
            ctx.op("pool", lambda: pool.memset(mq[:], 0.0), w=[mqb])
        PT_ring = Ring([(sb(sB, "PT%d" % i, [128, 512], BF16), Buf("PT%d" % i)) for i in range(4)])
        osb_ring = Ring([(sb(sB, "osb%d" % i, [65, 512]), Buf("osb%d" % i)) for i in range(2)])
        rec_ring = Ring([(sb(sB, "rec%d" % i, [65, 512], BF16), Buf("rec%d" % i)) for i in range(2)])
        oo_ring = Ring([(sb(sB, "oo%d" % i, [64, 512], BF16), Buf("oo%d" % i)) for i in range(2)])

        def prep_head(h):
            i = h % 2
            Kh = KA[i]
            ctx.op("dve", lambda: dve.tensor_reduce(out=ksum[:, 0:NB], in_=Kh[0:64, :].rearrange("p (a b) -> p a b", b=BLK), axis=AX.X, op=ALU.add),
                   r=[KA_b[i]], w=[km_b[i]])
            if NB < 32:
                ctx.op("dve", lambda: dve.memset(ksum[:, NB:32], 0.0), r=[], w=[km_b[i]])
            ctx.op("dve", lambda: dve.tensor_scalar(out=kmT[i][:], in0=ksum[:], scalar1=1.0 / BLK, scalar2=None, op0=ALU.mult), r=[km_b[i]], w=[km_b[i]])

        def gate1(h, T):
            i = h % 2
            Qh = QA[i]
            c0 = 4 * T
            q0 = T * TT
            for ci in range(4):
                ctx.op("pe", lambda: pe.matmul(psG[:, ci * 32:(ci + 1) * 32], lhsT=Qh[0:64, q0 + ci * 128:q0 + (ci + 1) * 128], rhs=kmT[i][:, :], start=True, stop=True),
                       r=[QA_b[i], km_b[i]], w=[psG_b], sig=(ci == 3))
            ctx.op("dve", lambda: dve.tensor_tensor(out=gm[:], in0=psG[:, 0:128].rearrange("p (a b) -> p a b", a=4), in1=PMASK[:, c0:c0 + 4, :], op=ALU.add),
                   r=[psG_b, cm_b], w=[gm_b])
            for ci in range(4):
                ctx.op("dve", lambda: dve.max(out=thr8[:, ci, :], in_=gm[:, ci, :]), r=[gm_b], w=[gm_b])
            ctx.op("dve", lambda: dve.tensor_tensor(out=ns_[:], in0=gm[:], in1=thr8[:, :, 2:3].to_broadcast([128, 4, 32]), op=ALU.is_lt), r=[gm_b], w=[gm_b])
            ctx.op("dve", lambda: dve.tensor_scalar(out=ns_[:], in0=ns_[:], scalar1=NEG, scalar2=b31[:, h:h + 1], op0=ALU.mult, op1=ALU.add), r=[gm_b, cm_b], w=[gm_b])
            Mq, Mq_b = Mq_ring.next()
            ctx.op("dve", lambda: dve.tensor_tensor(out=Mq[:, :, 64:96], in0=ns_[:], in1=PAST[:, c0:c0 + 4, :], op=ALU.mult), r=[gm_b, cm_b], w=[Mq_b])
            return Mq, Mq_b

        def gate2(h, T, Mq, Mq_b):
            i = h % 2
            q0 = T * TT
            for ci in range(4):
                ctx.op("pe", lambda: pe.transpose(psM_v[0:96, ci * 128:(ci + 1) * 128], Mq[:, ci, :], ident[:]), r=[Mq_b, ident_b], w=[psM_b], sig=(ci == 3))
            ctx.op("dve", lambda: dve.tensor_copy(out=QA[i][64:96, q0:q0 + TT], in_=psM_v[64:96, 0:512]), r=[psM_b], w=[QM_b[i][T]])

        work = [(h, T) for h in range(NHB) for T in range(NT)]
        LA = 2
        if work:
            prep_head(0)
            g = gate1(0, 0)
            gate2(0, 0, *g)
        for wi, (h, T) in enumerate(work):
            i = h % 2
            if T == 0 and h + 1 < NHB:
                load_head(h + 1)
            Kh, Qh, Vh = KA[i], QA[i], VE[i]
            q0 = T * TT
            nxt_w = work[wi + 1] if wi + 1 < len(work) else None
            g = None
            if nxt_w is not None:
                if nxt_w[1] == 0:
                    prep_head(nxt_w[0])
                g = gate1(*nxt_w)
            Ops, Ops_b = O_ring.next()
            nkt = 4 * T + 4

            def emit_S(kt):
                cl = max(0, kt - 4 * T)
                cs = slice(cl * 128, 512)
                Sps, Sps_b = S_ring.next()
                adds = []
                ci0 = kt - 4 * T
                if 0 <= ci0 <= 3:
                    adds.append((ci0, 0))
                ci1 = kt + 1 - 4 * T
                if 0 <= ci1 <= 3:
                    adds.append((ci1, 1 if kt % 2 == 0 else 2))
                ctx.op("pe", lambda: pe.matmul(Sps[:, cs], lhsT=Kh[0:96, kt * 128:(kt + 1) * 128], rhs=Qh[0:96, q0 + cl * 128:q0 + 512], start=True, stop=(len(adds) == 0)),
                       r=[KA_b[i], QA_b[i], QM_b[i][T]], w=[Sps_b], sig=(len(adds) == 0))
                for ai, (cidx, j) in enumerate(adds):
                    last = ai == len(adds) - 1
                    ctx.op("pe", lambda: pe.matmul(Sps[:, cidx * 128:(cidx + 1) * 128], lhsT=Jm[:, :], rhs=Htiles[:, h, j, :], start=False, stop=last),
                           r=[Jm_b, H_b[h][j]], w=[Sps_b], sig=last)
                return kt, cs, Sps, Sps_b

            def emit_PV(kt, cs, Sps, Sps_b):
                PT, PT_b = PT_ring.next()
                ctx.op("act", lambda: act.activation(out=PT[:, cs], in_=Sps[:, cs], func=AF.Exp), r=[Sps_b], w=[PT_b])
                ctx.op("pe", lambda: pe.matmul(Ops[0:65, cs], lhsT=Vh[:, kt, 0:65], rhs=PT[:, cs], start=(kt == 0), stop=(kt == nkt - 1)),
                       r=[VE_b[i], PT_b], w=[Ops_b], sig=(kt == nkt - 1))

            pend = []
            for kt in range(nkt):
                pend.append(emit_S(kt))
                if len(pend) > LA:
                    emit_PV(*pend.pop(0))
            while pend:
                emit_PV(*pend.pop(0))
            if g is not None:
                gate2(nxt_w[0], nxt_w[1], *g)
            osb, osb_b = osb_ring.next()
            ctx.op("act", lambda: act.copy(out=osb[:], in_=Ops[0:65, :]), r=[Ops_b], w=[osb_b])
            rec, rec_b = rec_ring.next()
            ctx.op("dve", lambda: dve.reciprocal(out=osb[64:65, :], in_=osb[64:65, :]), r=[osb_b], w=[osb_b])
            ctx.op("dve", lambda: dve.tensor_copy(out=rec[64:65, :], in_=osb[64:65, :]), r=[osb_b], w=[rec_b])
            ctx.op("pe", lambda: pe.matmul(psBc[0:64, :], lhsT=ones_f[64:65, 0:64], rhs=rec[64:65, :], start=True, stop=True), r=[rec_b, cm_b], w=[psBc_b])
            oo, oo_b = oo_ring.next()
            ctx.op("dve", lambda: dve.tensor_tensor(out=oo[:], in0=osb[0:64, :], in1=psBc[0:64, :], op=ALU.mult), r=[osb_b, psBc_b], w=[oo_b])
            ctx.dma(OA_d[h, :, q0:q0 + TT], oo[:], r=[oo_b])
        ctx.barrier()

    with ExitStack() as sC:
        psT_v = [ps(sC, "psTc%d" % i, [128, 512])[:].bitcast(BF16) for i in range(2)]
        psT_b = [Buf("psTc0"), Buf("psTc1")]
        mm_ring = Ring([(ps(sC, "mc%d" % i, [128, 512]), Buf("mc%d" % i)) for i in range(6)])
        tk_ring = mm_ring

        Wz = sb(sC, "Wz", [128, 8, 2560], BF16); Wz_b = Buf("Wz")
        Wap = sb(sC, "Wap", [128, 4, 1024], BF16); Wap_b = Buf("Wap")
        Wout = sb(sC, "Wout", [128, 8, 1024], BF16); Wout_b = Buf("Wout")
        Wpg = sb(sC, "Wpg", [128, 8, 1024], BF16); Wpg_b = Buf("Wpg")
        Wpp = sb(sC, "Wpp", [128, 2, 1024], BF16); Wpp_b = Buf("Wpp")
        lng = sb(sC, "lng", [128, D]); lnb = sb(sC, "lnb", [128, D]); ln_b = Buf("ln")
        ctx.dma(lng[:], lng_d, w=[ln_b]); ctx.dma(lnb[:], lnb_d, w=[ln_b])
        NS = TC // 128
        x_ring = Ring([(sb(sC, "xc%d" % i, [128, D]), Buf("xc%d" % i)) for i in range(2 * NS)])
        xb = sb(sC, "xbC", [128, NS, D], BF16); xb_b = [Buf("xbC%d" % i) for i in range(NS)]
        xTc_ring = Ring([(sb(sC, "xTC%d" % i, [128, 8, TC], BF16), Buf("xTC%d" % i)) for i in range(2)])
        p_ring = Ring([(sb(sC, "pc%d" % i, [128, 256]), Buf("pc%d" % i)) for i in range(NS)])
        pbf = sb(sC, "pbf", [128, NS, 256], BF16); pbf_b = [Buf("pbf%d" % i) for i in range(NS)]
        pTc_ring = Ring([(sb(sC, "pT%d" % i, [128, 2, TC], BF16), Buf("pT%d" % i)) for i in range(2)])
        sza = sb(sC, "sza", [128, 4, TC], BF16); sza_b = Buf("sza")
        sga = sb(sC, "sga", [128, 8, TC], BF16); sga_b = Buf("sga")
        sgs = sb(sC, "sgs", [128, 8, TC], BF16); sgs_b = Buf("sgs")
        oa_ring = Ring([(sb(sC, "oat%d" % i, [128, 4, TC], BF16), Buf("oat%d" % i)) for i in range(2)])
        oz = sb(sC, "oz", [128, 4, TC], BF16); oz_b = Buf("oz")
        ys_ring = Ring([(sb(sC, "yst%d" % i, [128, 8, TC], BF16), Buf("yst%d" % i)) for i in range(2)])
        mg_ring = Ring([(sb(sC, "merge%d" % i, [128, 8, TC], BF16), Buf("merge%d" % i)) for i in range(2)])
        ta_ring = Ring([(sb(sC, "tma%d" % i, [128, TC]), Buf("tma%d" % i)) for i in range(2)])
        tb_ring = Ring([(sb(sC, "tmb%d" % i, [128, TC]), Buf("tmb%d" % i)) for i in range(2)])
        s_ring = Ring([(sb(sC, "srow%d" % i, [128, D]), Buf("srow%d" % i)) for i in range(2)])
        o_ring = Ring([(sb(sC, "orow%d" % i, [128, D]), Buf("orow%d" % i)) for i in range(2)])
        st_ring = Ring([((sb(sC, "bst%d" % i, [128, 2, 6]), sb(sC, "bmv%d" % i, [128, 2]), sb(sC, "brs%d" % i, [128, 2])), Buf("bst%d" % i)) for i in range(2)])

        def issue_xC(t_):
            tk0 = t_ * TC
            xts_ = []
            for sub in range(NS):
                xt, xtb = x_ring.next()
                xts_.append((xt, xtb))
                ctx.dma(xt[:], x_d[tk0 + sub * 128:tk0 + (sub + 1) * 128, :], w=[xtb])
                ctx.op("pool", lambda: pool.tensor_copy(out=xb[:, sub, :], in_=xt[:]), r=[xtb], w=[xb_b[sub]])
                pt_, ptb = p_ring.next()
                ctx.dma(pt_[:], p_d[tk0 + sub * 128:tk0 + (sub + 1) * 128, :], w=[ptb])
                ctx.op("pool", lambda: pool.tensor_copy(out=pbf[:, sub, :], in_=pt_[:]), r=[ptb], w=[pbf_b[sub]])
            oat_, oat_b_ = oa_ring.next()
            ctx.dma(oat_[:], OA_d[:, :, tk0:tk0 + TC].rearrange("(pr hh) d t -> (hh d) pr t", hh=2), w=[oat_b_])
            yst_, yst_b_ = ys_ring.next()
            ctx.dma(yst_[:], YS_d[:, :, tk0:tk0 + TC].rearrange("c p t -> p c t"), w=[yst_b_])
            return xts_, (oat_, oat_b_), (yst_, yst_b_)

        NTCC = 0 if stop in ('setup', 'A', 'B', 'B1') else NTC
        if NTCC:
            nxt = issue_xC(0)
        stage_ring = Ring([(sb(sC, "wstc%d" % i, [128, 2048]), Buf("wstc%d" % i)) for i in range(2)])
        load_weight(Wz, Wz_b, w_in_d, 1536, 512, 8, stage_ring, dcol0=0)
        load_weight(Wz, Wz_b, w_in_d, 3072, 2048, 8, stage_ring, dcol0=512)
        load_weight(Wap, Wap_b, w_ap_d, 0, 1024, 4, stage_ring)
        load_weight(Wpg, Wpg_b, w_pg_d, 0, 1024, 8, stage_ring)
        load_weight(Wpp, Wpp_b, w_pp_d, 0, 1024, 2, stage_ring)
        load_weight(Wout, Wout_b, w_out_d, 0, 1024, 8, stage_ring)

        for t in range(NTCC):
            tok0 = t * TC
            xts, (oat, oat_b), (yst, yst_b) = nxt
            xT, xT_b = xTc_ring.next()
            pT, pT_b = pTc_ring.next()
            merge, merge_b = mg_ring.next()
            for dk in range(8):
                hf = dk % 2
                tp = psT_v[hf][:, 0:TC]
                for sub in range(NS):
                    ctx.op("pe", lambda: pe.transpose(tp[:, sub * 128:(sub + 1) * 128], xb[:, sub, dk * 128:(dk + 1) * 128], ident[:]),
                           r=[xb_b[sub], ident_b], w=[psT_b[hf]], sig=(sub == NS - 1))
                ctx.op("act", lambda: act.copy(out=xT[:, dk, :], in_=tp), r=[psT_b[hf]], w=[xT_b])
            for kc in range(2):
                hf = kc % 2
                tp = psT_v[hf][:, 0:TC]
                for sub in range(NS):
                    ctx.op("pe", lambda: pe.transpose(tp[:, sub * 128:(sub + 1) * 128], pbf[:, sub, kc * 128:(kc + 1) * 128], ident[:]),
                           r=[pbf_b[sub], ident_b], w=[psT_b[hf]], sig=(sub == NS - 1))
                ctx.op("act", lambda: act.copy(out=pT[:, kc, :], in_=tp), r=[psT_b[hf]], w=[pT_b])
            if t + 1 < NTCC:
                nxt = issue_xC(t + 1)
            for cc in range(20):
                pt, pb = mm_ring.next()
                mm_group(pt[:, 0:TC], pb, [(Wz[:, dk, cc * 128:(cc + 1) * 128], xT[:, dk, :]) for dk in range(8)], r=[Wz_b, xT_b])
                if cc < 4:
                    ctx.op("act", lambda: act.activation(out=sza[:, cc, :], in_=pt[:, 0:TC], func=AF.Silu), r=[pb], w=[sza_b])
                elif cc < 12:
                    ctx.op("act", lambda: act.activation(out=sga[:, cc - 4, :], in_=pt[:, 0:TC], func=AF.Sigmoid), r=[pb], w=[sga_b])
                else:
                    ctx.op("act", lambda: act.activation(out=sgs[:, cc - 12, :], in_=pt[:, 0:TC], func=AF.Sigmoid), r=[pb], w=[sgs_b])
            ctx.op("pool", lambda: pool.tensor_tensor(out=oz[:], in0=oat[:], in1=sza[:], op=ALU.mult), r=[oat_b, sza_b], w=[oz_b])
            for oc in range(8):
                pt, pb = mm_ring.next()
                mm_group(pt[:, 0:TC], pb, [(Wap[:, kc, oc * 128:(oc + 1) * 128], oz[:, kc, :]) for kc in range(4)], r=[Wap_b, oz_b])
                ta, ta_b = ta_ring.next()
                tb, tb_b = tb_ring.next()
                ctx.op("dve", lambda: dve.tensor_tensor(out=ta[:], in0=pt[:, 0:TC], in1=sga[:, oc, :], op=ALU.mult), r=[pb, sga_b], w=[ta_b])
                ctx.op("pool", lambda: pool.tensor_tensor(out=tb[:], in0=yst[:, oc, :], in1=sgs[:, oc, :], op=ALU.mult), r=[yst_b, sgs_b], w=[tb_b])
                ctx.op("dve", lambda: dve.tensor_tensor(out=merge[:, oc, :], in0=ta[:], in1=tb[:], op=ALU.add), r=[ta_b, tb_b], w=[merge_b])
            for sub in range(NS):
                xt, xtb = xts[sub]
                srow, srow_b = s_ring.next()
                tsl = slice(sub * 128, (sub + 1) * 128)
                (bst, bmv, brs), bst_b = st_ring.next()
                for hc in range(2):
                    csl = slice(hc * 512, (hc + 1) * 512)
                    pg, pg_b = tk_ring.next()
                    mm_group(pg[:, :], pg_b, [(xT[:, dk, tsl], Wpg[:, dk, csl]) for dk in range(8)], r=[Wpg_b, xT_b])
                    pp, pp_b = tk_ring.next()
                    mm_group(pp[:, :], pp_b, [(pT[:, kc, tsl], Wpp[:, kc, csl]) for kc in range(2)], r=[Wpp_b, pT_b])
                    mx, mx_b = tk_ring.next()
                    mm_group(mx[:, :], mx_b, [(merge[:, kc, tsl], Wout[:, kc, csl]) for kc in range(8)], r=[Wout_b, merge_b])
                    ctx.op("act", lambda: act.activation(out=srow[:, csl], in_=pg[:, :], func=AF.Sigmoid), r=[pg_b], w=[srow_b])
                    ctx.op("dve", lambda: dve.tensor_tensor(out=srow[:, csl], in0=srow[:, csl], in1=pp[:, :], op=ALU.mult), r=[srow_b, pp_b], w=[srow_b])
                    ctx.op("dve", lambda: dve.tensor_tensor(out=srow[:, csl], in0=srow[:, csl], in1=mx[:, :], op=ALU.add), r=[srow_b, mx_b], w=[srow_b])
                    ctx.op("dve", lambda: dve.scalar_tensor_tensor(out=srow[:, csl], in0=xt[:, csl], scalar=ALPHA, in1=srow[:, csl], op0=ALU.mult, op1=ALU.add),
                           r=[srow_b, xtb], w=[srow_b])
                    ctx.op("dve", lambda: dve.bn_stats(out=bst[:, hc, :], in_=srow[:, csl]), r=[srow_b], w=[bst_b])
                ctx.op("dve", lambda: dve.bn_aggr(out=bmv[:], in_=bst[:].rearrange("p a b -> p (a b)")), r=[bst_b], w=[bst_b])
                ctx.op("dve", lambda: dve.tensor_scalar(out=brs[:, 0:1], in0=bmv[:, 1:2], scalar1=LN_EPS, scalar2=None, op0=ALU.add), r=[bst_b], w=[bst_b])
                ctx.op("act", lambda: act.activation(out=brs[:, 0:1], in_=brs[:, 0:1], func=AF.Sqrt), r=[bst_b], w=[bst_b])
                ctx.op("dve", lambda: dve.reciprocal(out=brs[:, 0:1], in_=brs[:, 0:1]), r=[bst_b], w=[bst_b])
                ctx.op("dve", lambda: dve.scalar_tensor_tensor(out=brs[:, 1:2], in0=bmv[:, 0:1], scalar=-1.0, in1=brs[:, 0:1], op0=ALU.mult, op1=ALU.mult),
                       r=[bst_b], w=[bst_b])
                orow, orow_b = o_ring.next()
                ctx.op("dve", lambda: dve.tensor_scalar(out=orow[:], in0=srow[:], scalar1=brs[:, 0:1], scalar2=brs[:, 1:2], op0=ALU.mult, op1=ALU.add), r=[srow_b, bst_b], w=[orow_b])
                ctx.op("dve", lambda: dve.tensor_tensor(out=orow[:], in0=orow[:], in1=lng[:], op=ALU.mult), r=[orow_b, ln_b], w=[orow_b])
                ctx.op("pool", lambda: pool.tensor_tensor(out=orow[:], in0=orow[:], in1=lnb[:], op=ALU.add), r=[orow_b, ln_b], w=[orow_b])
                ctx.dma(out_d[tok0 + sub * 128:tok0 + (sub + 1) * 128, :], orow[:], r=[orow_b])
        ctx.barrier(engines=["sp"])
    es.close()
    return nc


def host_consts(S):
    bf = ml_dtypes.bfloat16
    NCH = S // 128
    c = {}
    c["ident"] = np.eye(128, dtype=np.float32).astype(bf)
    c["Jm"] = np.ascontiguousarray(np.eye(128, dtype=np.float32)[::-1]).astype(bf)
    i = np.arange(384)
    dist = i - 127
    bk = t5_bucket_np(np.maximum(dist, 0))
    OH = np.zeros((32, 384), np.float32)
    valid = dist >= 0
    OH[bk[valid], i[valid]] = 1.0
    c["OH"] = OH.astype(bf)
    NEGM = np.zeros((8, 384), np.float32)
    NEGM[:, ~valid] = NEG
    c["NEGM"] = NEGM
    keys = np.arange(S)
    EOH = (keys[None, :] // BLK == np.arange(32)[:, None]).astype(np.float32)
    c["EOH"] = EOH.astype(bf)
    ch = np.arange(NCH)
    blk = ch // 2
    n = np.arange(32)
    past = (n[None, :] < blk[:, None])
    PM = np.where(past, 0.0, -1e30).astype(np.float32)
    c["PMASK"] = np.ascontiguousarray(np.broadcast_to(PM.reshape(1, -1), (128, NCH * 32)))
    c["PAST01"] = np.ascontiguousarray(np.broadcast_to(past.astype(np.float32).reshape(1, -1), (128, NCH * 32)))
    c["jidx"] = np.ascontiguousarray(np.broadcast_to(np.arange(129, dtype=np.float32)[None, :], (128, 129)))
    return c


def host_params(inp):
    o = {}
    a_re = inp["ssm_a_re"][0]; a_im = inp["ssm_a_im"][0]; ldt = inp["ssm_log_dt"][0]
    b_re = inp["ssm_b_re"][0]; b_im = inp["ssm_b_im"][0]; c_re = inp["ssm_c_re"][0]; c_im = inp["ssm_c_im"][0]
    pidx = np.arange(128); g2 = pidx // 64; n = pidx % 64
    pr = np.arange(16)
    G = 2 * pr[None, :] + g2[:, None]
    o["are_l"] = np.ascontiguousarray(a_re[G, n[:, None]])
    o["aim_l"] = np.ascontiguousarray(a_im[G, n[:, None]])
    o["ldt_l"] = np.ascontiguousarray(ldt[G])
    Gc = 2 * pr[:, None] + g2[None, :]
    are_pc = a_re[Gc, n[None, :]]
    aim_pc = a_im[Gc, n[None, :]]
    ldt_pc = ldt[Gc]
    for name, v in (("are_rep", are_pc), ("aim_rep", aim_pc), ("ldt_rep", ldt_pc)):
        o[name] = np.ascontiguousarray(np.broadcast_to(v.reshape(1, 2048), (128, 2048))).astype(np.float32)
    brT = np.zeros((128, 16, 128), np.float32); biT = np.zeros((128, 16, 128), np.float32)
    cre = np.zeros((128, 16, 128), np.float32); cim = np.zeros((128, 16, 128), np.float32)
    for p_ in range(16):
        r0 = (p_ % 4) * 32
        for gg in range(2):
            g = 2 * p_ + gg
            brT[r0 + gg * 16:r0 + gg * 16 + 16, p_, gg * 64:(gg + 1) * 64] = b_re[g].T
            biT[r0 + gg * 16:r0 + gg * 16 + 16, p_, gg * 64:(gg + 1) * 64] = b_im[g].T
            cre[gg * 64:(gg + 1) * 64, p_, r0 + gg * 16:r0 + gg * 16 + 16] = c_re[g].T
            cim[gg * 64:(gg + 1) * 64, p_, r0 + gg * 16:r0 + gg * 16 + 16] = c_im[g].T
    o["brT"] = brT.reshape(128, 2048); o["biT"] = biT.reshape(128, 2048)
    o["cre_pad"] = cre.reshape(128, 2048); o["cim_pad"] = cim.reshape(128, 2048)
    o["d_l"] = np.ascontiguousarray(inp["ssm_d"][0].reshape(4, 128).T)
    o["lng"] = np.ascontiguousarray(np.broadcast_to(inp["ln_g"][0][None, :], (128, D)))
    o["lnb"] = np.ascontiguousarray(np.broadcast_to(inp["ln_b"][0][None, :], (128, D)))
    o["relb"] = np.ascontiguousarray(inp["rel_bias"])
    o["b31rep"] = np.ascontiguousarray(np.broadcast_to(inp["rel_bias"][31][None, :], (128, 8)))
    o["w_in"] = np.ascontiguousarray(inp["w_in"][0]); o["w_ap"] = np.ascontiguousarray(inp["w_attn_proj"][0])
    o["w_sp"] = np.ascontiguousarray(inp["w_ssm_proj"][0]); o["w_out"] = np.ascontiguousarray(inp["w_out"][0])
    o["w_glu"] = np.ascontiguousarray(inp["w_glu"][0]); o["w_pg"] = np.ascontiguousarray(inp["w_ple_gate"][0])
    o["w_pp"] = np.ascontiguousarray(inp["w_ple_proj"][0])
    return {k: np.asarray(v, np.float32) for k, v in o.items()}


_NC_CACHE = {}


def run(inputs, S, n_cores, **bk):
    inp = {k: np.asarray(v) for k, v in inputs.items()}
    if S not in _NC_CACHE:
        _NC_CACHE[S] = build_nc(S, **bk)
    nc = _NC_CACHE[S]
    shared = host_params(inp)
    shared.update(host_consts(S))
    in_maps = []
    for b in range(n_cores):
        m = dict(shared)
        m["x"] = np.ascontiguousarray(inp["x"][b], dtype=np.float32)
        m["p"] = np.ascontiguousarray(inp["p"][0, b], dtype=np.float32)
        in_maps.append(m)
    res = run_bass_kernel_spmd(nc, in_maps, core_ids=list(range(n_cores)))
    return np.stack([np.asarray(r["out"]) for r in res.results], axis=0).astype(np.float32)


def kernel(**inputs):
    return run(inputs, 8192, 8)
```

```python
import math
from contextlib import ExitStack

import numpy as np
import ml_dtypes

import concourse.bass as bass
import concourse.mybir as mybir
from concourse.bass_utils import run_bass_kernel_spmd

F32 = mybir.dt.float32
BF16 = mybir.dt.bfloat16
I32 = mybir.dt.int32
AF = mybir.ActivationFunctionType
ALU = mybir.AluOpType
AX = mybir.AxisListType

D = 1024
NH = 8
HD = 64
BLK = 256
NEG = -30000.0
TWO_PI = 2.0 * math.pi
ALPHA = 2.0 ** 0.25
LN_EPS = 1e-5


class Buf:
    __slots__ = ("w", "r", "name")

    def __init__(self, name=""):
        self.w = None
        self.r = {}
        self.name = name


class Ring:
    def __init__(self, items):
        self.items = items
        self.i = 0

    def next(self):
        it = self.items[self.i % len(self.items)]
        self.i += 1
        return it


class Ctx:
    def __init__(self, nc, es, n_dma_sems=24):
        self.nc = nc
        self.engs = {"pe": nc.tensor, "act": nc.scalar, "dve": nc.vector, "pool": nc.gpsimd, "sp": nc.sync}
        self.sem = {}
        self.cnt = {}
        self.seen = {e: {} for e in self.engs}
        self.pend = {e: ([], []) for e in self.engs}
        for e in self.engs:
            self.sem[e] = es.enter_context(nc.semaphore("s_" + e))
            self.cnt[e] = 0
        self.dq = []
        for i in range(n_dma_sems):
            k = "d%d" % i
            self.sem[k] = es.enter_context(nc.semaphore("s_" + k))
            self.cnt[k] = 0
            self.dq.append(k)
        self.dqi = 0

    def _wait(self, e, tok):
        if tok is None:
            return
        k, v = tok
        if v <= 0 or self.seen[e].get(k, 0) >= v:
            return
        self.engs[e].wait_ge(self.sem[k], v)
        self.seen[e][k] = v

    def _deps(self, e, r, w):
        for b in r:
            if b.w is not None and not (e == "pe" and b.w[0] == "pe"):
                self._wait(e, b.w)
        for b in w:
            if b.w is not None and b.w[0] != e:
                self._wait(e, b.w)
            for k, v in b.r.items():
                if k != e:
                    self._wait(e, (k, v))

    def _reg(self, tok, r, w):
        k, v = tok
        for b in r:
            if b.r.get(k, 0) < v:
                b.r[k] = v
        for b in w:
            b.w = tok
            b.r = {}

    def op(self, e, fn, r=(), w=(), sig=True):
        self._deps(e, r, w)
        inst = fn()
        pr, pw = self.pend[e]
        if not sig:
            pr.extend(r)
            pw.extend(w)
            return None
        self.cnt[e] += 1
        inst.then_inc(self.sem[e], 1)
        tok = (e, self.cnt[e])
        self._reg(tok, list(r) + pr, list(w) + pw)
        self.pend[e] = ([], [])
        return tok

    def dma(self, out, in_, r=(), w=(), q="sp"):
        k = self.dq[self.dqi % len(self.dq)]
        self.dqi += 1
        self._wait(q, (k, self.cnt[k]))
        self._deps(q, r, w)
        inst = self.engs[q].dma_start(out=out, in_=in_)
        self.cnt[k] += 16
        inst.then_inc(self.sem[k], 16)
        tok = (k, self.cnt[k])
        self._reg(tok, r, w)
        return tok

    def barrier(self, engines=None):
        toks = [(k, v) for k, v in self.cnt.items() if v > 0]
        for e in (engines or self.engs):
            for t in toks:
                if t[0] != e:
                    self._wait(e, t)


def t5_bucket_np(dist):
    dist = np.asarray(dist, np.int64)
    d = np.maximum(dist, 1).astype(np.float32)
    large = 16 + (np.log(d / np.float32(16)) / np.float32(math.log(128 / 16)) * np.float32(16)).astype(np.int32)
    large = np.minimum(large, 31)
    return np.where(dist < 16, dist, large)


def build_nc(S, TT=512, TC=256, stop=None, nt_lim=None):
    NT = S // TT
    NTC = S // TC
    NKT = S // 128
    NB = S // BLK
    NCH = S // 128
    assert NB <= 32
    nc = bass.Bass("TRN2", target_bir_lowering=False)
    es = ExitStack()
    ctx = Ctx(nc, es)
    pe, act, dve, pool = nc.tensor, nc.scalar, nc.vector, nc.gpsimd

    def din(name, shape, dt=F32):
        return nc.dram_tensor(name, list(shape), dt, kind="ExternalInput").ap()

    def dscr(name, shape, dt=BF16):
        return nc.dram_tensor(name, list(shape), dt, kind="Internal").ap()

    x_d = din("x", [S, D])
    p_d = din("p", [S, 256])
    w_in_d = din("w_in", [D, 5120])
    w_ap_d = din("w_ap", [512, D])
    w_sp_d = din("w_sp", [512, D])
    w_out_d = din("w_out", [D, D])
    w_glu_d = din("w_glu", [512, D])
    w_pg_d = din("w_pg", [D, D])
    w_pp_d = din("w_pp", [256, D])
    are_rep_d = din("are_rep", [128, 2048])
    aim_rep_d = din("aim_rep", [128, 2048])
    ldt_rep_d = din("ldt_rep", [128, 2048])
    brT_d = din("brT", [128, 2048])
    biT_d = din("biT", [128, 2048])
    cre_d = din("cre_pad", [128, 2048])
    cim_d = din("cim_pad", [128, 2048])
    are_l_d = din("are_l", [128, 16])
    aim_l_d = din("aim_l", [128, 16])
    ldt_l_d = din("ldt_l", [128, 16])
    d_l_d = din("d_l", [128, 4])
    lng_d = din("lng", [128, D])
    lnb_d = din("lnb", [128, D])
    relb_d = din("relb", [32, 8])
    b31_d = din("b31rep", [128, 8])
    ident_d = din("ident", [128, 128], BF16)
    J_d = din("Jm", [128, 128], BF16)
    OH_d = din("OH", [32, 384], BF16)
    NEGM_d = din("NEGM", [8, 384])
    EOH_d = din("EOH", [32, S], BF16)
    PMASK_d = din("PMASK", [128, NCH * 32])
    PAST_d = din("PAST01", [128, NCH * 32])
    jidx_d = din("jidx", [128, 129])
    out_d = nc.dram_tensor("out", [S, D], F32, kind="ExternalOutput").ap()

    QT_d = dscr("QT", [4, 128, S])
    KT_d = dscr("KT", [4, 128, S])
    V_d = dscr("Vs", [8, 128, S // 128, 64])
    OA_d = dscr("OA", [8, 64, S])
    YS_d = dscr("YS", [8, 128, S])
    F_d = dscr("Fd", [16, 384])

    def sb(stack, name, shape, dt=F32):
        return stack.enter_context(nc.sbuf_tensor("sb_" + name, list(shape), dt))

    def ps(stack, name, shape, dt=F32):
        return stack.enter_context(nc.psum_tensor("ps_" + name, list(shape), dt))

    ident = sb(es, "ident", [128, 128], BF16)
    ident_b = Buf("ident")
    Jm = sb(es, "Jm", [128, 128], BF16)
    Jm_b = Buf("J")
    ctx.dma(ident[:], ident_d, w=[ident_b])
    ctx.dma(Jm[:], J_d, w=[Jm_b])

    cast_rr = Ring(["act", "dve", "pool"])

    def cast_copy(e, out, in_):
        if e == "act":
            return lambda: act.copy(out=out, in_=in_)
        if e == "dve":
            return lambda: dve.tensor_copy(out=out, in_=in_)
        return lambda: pool.tensor_copy(out=out, in_=in_)

    def load_weight(dst, dst_b, src, col0, ncols, KC, stage_ring, dcol0=0):
        cw = 2048 // KC
        for c0 in range(0, ncols, cw):
            w_ = min(cw, ncols - c0)
            st, stb = stage_ring.next()
            stv = st[:, 0:KC * w_].rearrange("p (kc c) -> p kc c", kc=KC)
            ctx.dma(stv, src[:, col0 + c0:col0 + c0 + w_].rearrange("(kc p) c -> p kc c", p=128), w=[stb])
            e = cast_rr.next()
            ctx.op(e, cast_copy(e, dst[:, :, dcol0 + c0:dcol0 + c0 + w_], stv), r=[stb], w=[dst_b])

    def mm_group(out_ap, out_b, pairs, r):
        n = len(pairs)
        for i, (l, rh) in enumerate(pairs):
            ctx.op("pe", lambda: pe.matmul(out_ap, lhsT=l, rhs=rh, start=(i == 0), stop=(i == n - 1)),
                   r=r, w=[out_b], sig=(i == n - 1))

    with ExitStack() as sA:
        _psT = ps(sA, "psT0", [128, 512])[:].bitcast(BF16)
        psT_v = [_psT[:, 0:512], _psT[:, 512:1024]]
        _psT_b = Buf("psT")
        psT_b = [_psT_b, _psT_b]
        mm_ring = Ring([(ps(sA, "mm%d" % i, [128, 512]), Buf("mm%d" % i)) for i in range(2)])
        psS_ring = Ring([[(ps(sA, "psS%d_%d" % (j, i), [128, 512]), Buf("psS%d_%d" % (j, i))) for i in range(2)] for j in range(2)])
        psY = ps(sA, "psY", [128, 512])
        _psY_b = Buf("psY")
        psY_b = [_psY_b] * 4

        Wqk = sb(sA, "Wqk", [128, 8, 1024], BF16); Wqk_b = Buf("Wqk")
        Wv = sb(sA, "Wv", [128, 8, 512], BF16); Wv_b = Buf("Wv")
        Wuz = sb(sA, "Wuz", [128, 8, 1024], BF16); Wuz_b = Buf("Wuz")
        Wglu = sb(sA, "Wglu", [128, 4, 1024], BF16); Wglu_b = Buf("Wglu")
        Wsp = sb(sA, "Wsp", [128, 4, 1024], BF16); Wsp_b = Buf("Wsp")
        Bre = sb(sA, "Bre", [128, 16, 128], BF16); Bim = sb(sA, "Bim", [128, 16, 128], BF16)
        Cre = sb(sA, "Cre", [128, 16, 128], BF16); Cim = sb(sA, "Cim", [128, 16, 128], BF16)
        BC_b = Buf("BC")
        T_b = Buf("T")
        rtab = sb(sA, "rtab", [128, 16, 128])
        Dg = sb(sA, "Dg", [128, 2, 4, 128], BF16)
        Ec_t = sb(sA, "Ec_t", [128, 16]); Es_t = sb(sA, "Es_t", [128, 16])
        Tcb = sb(sA, "Tcb", [128, 16, 128], BF16); Tsb = sb(sA, "Tsb", [128, 16, 128], BF16)
        mag_l = sb(sA, "mag_l", [128, 16]); mag_b = Buf("mag")
        d_l = sb(sA, "d_l", [128, 4]); d_b = Buf("d")
        car_r = sb(sA, "car_r", [128, 16]); car_i = sb(sA, "car_i", [128, 16])
        car_b = [Buf("car%d" % q) for q in range(2)]
        ctx.dma(d_l[:], d_l_d, w=[d_b])

        with ExitStack() as s0:
            Tc = sb(s0, "Tc", [128, 16, 128]); Ts = sb(s0, "Ts", [128, 16, 128])
            dhi = sb(s0, "dhi", [128, 4], BF16); dlo = sb(s0, "dlo", [128, 4])
            A_ = sb(s0, "pA", [128, 2048]); Bm = sb(s0, "pB", [128, 2048]); L_ = sb(s0, "pL", [128, 2048])
            t0 = sb(s0, "pt0", [128, 2048]); t1 = sb(s0, "pt1", [128, 2048]); t2 = sb(s0, "pt2", [128, 2048])
            t3 = sb(s0, "pt3", [128, 2048]); t4 = sb(s0, "pt4", [128, 2048]); t5 = sb(s0, "pt5", [128, 2048])
            ti = sb(s0, "pti", [128, 2048], I32)
            bR = sb(s0, "pbR", [128, 2048]); bI = sb(s0, "pbI", [128, 2048])
            al = sb(s0, "al", [128, 16]); bl = sb(s0, "bl", [128, 16]); ll = sb(s0, "ll", [128, 16])
            th = sb(s0, "th", [128, 16]); jx = sb(s0, "jx", [128, 129])
            th128 = sb(s0, "th128", [128, 16]); sc16 = sb(s0, "sc16", [128, 16])
            P = Buf("setup")

            Pl = []
            for t_, d_ in ((A_, are_rep_d), (Bm, aim_rep_d), (L_, ldt_rep_d), (bR, brT_d), (bI, biT_d),
                           (t0, cre_d), (t1, cim_d), (al, are_l_d), (bl, aim_l_d), (ll, ldt_l_d), (jx, jidx_d)):
                pb_ = Buf("pl")
                Pl.append(pb_)
                ctx.dma(t_[:], d_, w=[pb_])
            ctx.op("dve", lambda: dve.memset(th[:], 0.0), r=Pl, w=[P])

            def V(fn):
                ctx.op("dve", fn, r=[P], w=[P])

            def A(fn):
                ctx.op("act", fn, r=[P], w=[P])

            V(lambda: dve.tensor_copy(out=Cre[:].rearrange("p a b -> p (a b)"), in_=t0[:]))
            V(lambda: dve.tensor_scalar(out=Cim[:].rearrange("p a b -> p (a b)"), in0=t1[:], scalar1=-1.0, scalar2=None, op0=ALU.mult))

            def emit_sin(out, x, n, shift, xs):
                xi = ti[:, 0:n]
                V(lambda: dve.tensor_scalar(out=xs, in0=x, scalar1=shift, scalar2=None, op0=ALU.add))
                V(lambda: dve.tensor_scalar(out=xi, in0=xs, scalar1=1.0 / TWO_PI, scalar2=None, op0=ALU.mult))
                V(lambda: dve.tensor_copy(out=out, in_=xi))
                V(lambda: dve.scalar_tensor_tensor(out=out, in0=out, scalar=-TWO_PI, in1=xs, op0=ALU.mult, op1=ALU.add))
                V(lambda: dve.tensor_scalar(out=xs, in0=out, scalar1=math.pi, scalar2=-TWO_PI, op0=ALU.is_gt, op1=ALU.mult))
                V(lambda: dve.tensor_tensor(out=out, in0=out, in1=xs, op=ALU.add))
                V(lambda: dve.tensor_scalar(out=xs, in0=out, scalar1=-math.pi, scalar2=TWO_PI, op0=ALU.is_lt, op1=ALU.mult))
                V(lambda: dve.tensor_tensor(out=out, in0=out, in1=xs, op=ALU.add))
                V(lambda: dve.tensor_scalar(out=out, in0=out, scalar1=math.pi, scalar2=-math.pi, op0=ALU.min, op1=ALU.max))
                A(lambda: act.activation(out=out, in_=out, func=AF.Sin))

            A(lambda: act.activation(out=L_[:], in_=L_[:], func=AF.Exp))
            V(lambda: dve.tensor_tensor(out=t0[:], in0=L_[:], in1=A_[:], op=ALU.mult))
            A(lambda: act.activation(out=t0[:], in_=t0[:], func=AF.Exp))
            V(lambda: dve.tensor_tensor(out=t1[:], in0=L_[:], in1=Bm[:], op=ALU.mult))
            emit_sin(t2[:], t1[:], 2048, 0.0, t4[:])
            emit_sin(t3[:], t1[:], 2048, math.pi / 2, t4[:])
            V(lambda: dve.tensor_tensor(out=t3[:], in0=t3[:], in1=t0[:], op=ALU.mult))
            V(lambda: dve.tensor_tensor(out=t2[:], in0=t2[:], in1=t0[:], op=ALU.mult))
            V(lambda: dve.tensor_scalar(out=t3[:], in0=t3[:], scalar1=-1.0, scalar2=None, op0=ALU.add))
            V(lambda: dve.tensor_tensor(out=t0[:], in0=A_[:], in1=A_[:], op=ALU.mult))
            V(lambda: dve.tensor_tensor(out=t1[:], in0=Bm[:], in1=Bm[:], op=ALU.mult))
            V(lambda: dve.tensor_tensor(out=t0[:], in0=t0[:], in1=t1[:], op=ALU.add))
            V(lambda: dve.reciprocal(out=t0[:], in_=t0[:]))
            V(lambda: dve.tensor_tensor(out=t4[:], in0=t3[:], in1=A_[:], op=ALU.mult))
            V(lambda: dve.tensor_tensor(out=t1[:], in0=t2[:], in1=Bm[:], op=ALU.mult))
            V(lambda: dve.tensor_tensor(out=t4[:], in0=t4[:], in1=t1[:], op=ALU.add))
            V(lambda: dve.tensor_tensor(out=t4[:], in0=t4[:], in1=t0[:], op=ALU.mult))
            V(lambda: dve.tensor_tensor(out=t5[:], in0=t2[:], in1=A_[:], op=ALU.mult))
            V(lambda: dve.tensor_tensor(out=t1[:], in0=t3[:], in1=Bm[:], op=ALU.mult))
            V(lambda: dve.tensor_tensor(out=t5[:], in0=t5[:], in1=t1[:], op=ALU.subtract))
            V(lambda: dve.tensor_tensor(out=t5[:], in0=t5[:], in1=t0[:], op=ALU.mult))
            V(lambda: dve.tensor_tensor(out=t0[:], in0=t4[:], in1=bR[:], op=ALU.mult))
            V(lambda: dve.tensor_tensor(out=t1[:], in0=t5[:], in1=bI[:], op=ALU.mult))
            V(lambda: dve.tensor_tensor(out=Bre[:].rearrange("p a b -> p (a b)"), in0=t0[:], in1=t1[:], op=ALU.subtract))
            V(lambda: dve.tensor_tensor(out=t0[:], in0=t4[:], in1=bI[:], op=ALU.mult))
            V(lambda: dve.tensor_tensor(out=t1[:], in0=t5[:], in1=bR[:], op=ALU.mult))
            V(lambda: dve.tensor_tensor(out=Bim[:].rearrange("p a b -> p (a b)"), in0=t0[:], in1=t1[:], op=ALU.add))
            A(lambda: act.activation(out=ll[:], in_=ll[:], func=AF.Exp))
            V(lambda: dve.tensor_tensor(out=al[:], in0=ll[:], in1=al[:], op=ALU.mult))
            A(lambda: act.activation(out=mag_l[:], in_=al[:], func=AF.Exp))
            V(lambda: dve.tensor_tensor(out=th[:], in0=ll[:], in1=bl[:], op=ALU.mult))
            V(lambda: dve.tensor_tensor(out=t0[:].rearrange("p (a b) -> p a b", a=16),
                                        in0=th[:].rearrange("p (a o) -> p a o", o=1).to_broadcast([128, 16, 128]),
                                        in1=jx[:, 0:128].rearrange("p (o b) -> p o b", o=1).to_broadcast([128, 16, 128]), op=ALU.mult))
            emit_sin(Ts[:].rearrange("p a b -> p (a b)"), t0[:], 2048, 0.0, t1[:])
            emit_sin(Tc[:].rearrange("p a b -> p (a b)"), t0[:], 2048, math.pi / 2, t1[:])
            V(lambda: dve.tensor_scalar(out=th128[:], in0=th[:], scalar1=128.0, scalar2=None, op0=ALU.mult))
            emit_sin(Es_t[:], th128[:], 16, 0.0, sc16[:])
            emit_sin(Ec_t[:], th128[:], 16, math.pi / 2, sc16[:])
            V(lambda: dve.tensor_copy(out=Tcb[:], in_=Tc[:]))
            V(lambda: dve.tensor_copy(out=Tsb[:], in_=Ts[:]))
            V(lambda: dve.tensor_copy(out=rtab[:], in_=mag_l[:].rearrange("p (a o) -> p a o", o=1).to_broadcast([128, 16, 128])))
            V(lambda: dve.memset(rtab[:, :, 0:1], 0.0))
            ctx.op("dve", lambda: dve.tensor_copy(out=dhi[:], in_=d_l[:]), r=[P, d_b], w=[P])
            V(lambda: dve.tensor_tensor(out=dlo[:], in0=d_l[:], in1=dhi[:], op=ALU.subtract))
            for q_ in range(4):
                ctx.op("dve", lambda: dve.tensor_scalar(out=Dg[:, 0, q_, :], in0=ident[:], scalar1=dhi[:, q_:q_ + 1], scalar2=None, op0=ALU.mult), r=[P, ident_b], w=[P])
                ctx.op("dve", lambda: dve.tensor_scalar(out=Dg[:, 1, q_, :], in0=ident[:], scalar1=dlo[:, q_:q_ + 1], scalar2=None, op0=ALU.mult), r=[P, ident_b], w=[P])
            V(lambda: dve.memset(car_r[:], 0.0))
            V(lambda: dve.memset(car_i[:], 0.0))
            ctx.op("dve", lambda: dve.memset(t0[:, 0:1], 0.0), r=[P], w=[P, BC_b, T_b, mag_b] + car_b)

            ctx.barrier()
        with ExitStack() as s0:
            stage_ring = Ring([(sb(s0, "wst%d" % i, [128, 2048]), Buf("wst%d" % i)) for i in range(2)])
            load_weight(Wqk, Wqk_b, w_in_d, 0, 1024, 8, stage_ring)
            load_weight(Wv, Wv_b, w_in_d, 1024, 512, 8, stage_ring)
            load_weight(Wuz, Wuz_b, w_in_d, 2048, 1024, 8, stage_ring)
            load_weight(Wglu, Wglu_b, w_glu_d, 0, 1024, 4, stage_ring)
            load_weight(Wsp, Wsp_b, w_sp_d, 0, 1024, 4, stage_ring)
            ctx.barrier()

        x_ring = Ring([(sb(sA, "xa%d" % i, [128, D]), Buf("xa%d" % i)) for i in range(3)])
        xb = sb(sA, "xbA", [128, 4, D], BF16)
        xb_b = [Buf("xb%d" % i) for i in range(4)]
        xT_ring = Ring([(sb(sA, "xTA%d" % i, [128, 8, TT], BF16), Buf("xTA%d" % i)) for i in range(2)])
        uT_ring = Ring([(sb(sA, "uT%d" % i, [128, 4, TT], BF16), [Buf("uT%d_%d" % (i, q)) for q in range(4)]) for i in range(2)])
        szs_ring = Ring([(sb(sA, "szs%d" % i, [128, 4, TT], BF16), Buf("szs%d" % i)) for i in range(2)])
        qk_ring = Ring([(sb(sA, "qkst%d" % i, [128, TT], BF16), Buf("qkst%d" % i)) for i in range(2)])
        v_ring = Ring([(sb(sA, "vst%d" % i, [128, 512], BF16), Buf("vst%d" % i)) for i in range(2)])
        ys_ring = Ring([(sb(sA, "ysst%d" % i, [128, TT], BF16), Buf("ysst%d" % i)) for i in range(2)])
        bpr = sb(sA, "bpr", [128, 8, 128], BF16); bpi = sb(sA, "bpi", [128, 8, 128], BF16); tm2 = sb(sA, "tm2", [128, 8, 128], BF16)
        bp_b = Buf("bp")
        Sb_ring = Ring([((sb(sA, "Sbr%d" % i, [128, 8, 128], BF16), sb(sA, "Sbi%d" % i, [128, 8, 128], BF16)), Buf("Sb%d" % i)) for i in range(2)])
        g_ring = Ring([((sb(sA, "gr%d" % i, [128, 8, 128], BF16), sb(sA, "gi%d" % i, [128, 8, 128], BF16)), Buf("g%d" % i)) for i in range(2)])
        tp3 = sb(sA, "tp3", [128, 8, 128], BF16); tp4 = sb(sA, "tp4", [128, 8, 128], BF16)
        tp5 = sb(sA, "tp5", [128, 8, 128], BF16); tp6 = sb(sA, "tp6", [128, 8, 128], BF16); tpd_b = Buf("tpd")
        h_ring = Ring([((sb(sA, "hr%d" % i, [128, 8, 128], BF16), sb(sA, "mhi%d" % i, [128, 8, 128], BF16)), (Buf("hr%d" % i), Buf("hi%d" % i))) for i in range(2)])
        rc_r = sb(sA, "rc_r", [128, 8]); rc_i = sb(sA, "rc_i", [128, 8]); ctm = sb(sA, "ctm", [128, 8]); ctm2 = sb(sA, "ctm2", [128, 8])
        gT_ring = Ring([(sb(sA, "gT%d" % i, [128, 4, TT], BF16), Buf("gT%d" % i)) for i in range(2)])
        sigb_ring = Ring([(sb(sA, "sigb%d" % i, [128, TT]), Buf("sigb%d" % i)) for i in range(2)])
        yg = sb(sA, "yg", [128, 4, TT], BF16); yg_b = Buf("yg")

        def issue_xA(t_):
            for sub in range(4):
                xt, xtb = x_ring.next()
                ctx.dma(xt[:], x_d[t_ * TT + sub * 128:t_ * TT + (sub + 1) * 128, :], w=[xtb])
                ctx.op("act", lambda: act.copy(out=xb[:, sub, :], in_=xt[:]), r=[xtb], w=[xb_b[sub]])

        NTA = 0 if stop == 'setup' else (nt_lim or NT)

        def proj_tasks(t_):
            tk0 = t_ * TT
            xT_, xT_b_ = xT_ring.next()
            uT_, uT_b_ = uT_ring.next()
            szs_, szs_b_ = szs_ring.next()
            tasks_ = []

            def t_tr(dk):
                hf = dk % 2
                tp = psT_v[hf][:, 0:512]
                for sub in range(4):
                    ctx.op("pe", lambda: pe.transpose(tp[:, sub * 128:(sub + 1) * 128], xb[:, sub, dk * 128:(dk + 1) * 128], ident[:]),
                           r=[xb_b[sub], ident_b], w=[psT_b[hf]], sig=(sub == 3))
                ctx.op("act", lambda: act.copy(out=xT_[:, dk, :], in_=tp), r=[psT_b[hf]], w=[xT_b_])
                if dk == 7 and t_ + 1 < NTA:
                    issue_xA(t_ + 1)

            def t_qk(cc):
                pt, pb = mm_ring.next()
                mm_group(pt[:, 0:TT], pb, [(Wqk[:, dk, cc * 128:(cc + 1) * 128], xT_[:, dk, :]) for dk in range(8)], r=[Wqk_b, xT_b_])
                stg, stb = qk_ring.next()
                if cc < 4:
                    ctx.op("act", lambda: act.activation(out=stg[:], in_=pt[:, 0:TT], func=AF.Copy, scale=0.125), r=[pb], w=[stb])
                    ctx.dma(QT_d[cc, :, tk0:tk0 + TT], stg[:], r=[stb])
                else:
                    ctx.op("act", lambda: act.copy(out=stg[:], in_=pt[:, 0:TT]), r=[pb], w=[stb])
                    ctx.dma(KT_d[cc - 4, :, tk0:tk0 + TT], stg[:], r=[stb])

            def t_v(sub):
                pt, pb = mm_ring.next()
                mm_group(pt[:, 0:512], pb, [(xT_[:, dk, sub * 128:(sub + 1) * 128], Wv[:, dk, :]) for dk in range(8)], r=[Wv_b, xT_b_])
                stg, stb = v_ring.next()
                ctx.op("act", lambda: act.copy(out=stg[:], in_=pt[:, 0:512]), r=[pb], w=[stb])
                ctx.dma(V_d[:, :, t_ * 4 + sub, :].rearrange("h p d -> p h d"), stg[:].rearrange("p (h d) -> p h d", h=8), r=[stb])

            def t_uz(cc):
                pt, pb = mm_ring.next()
                mm_group(pt[:, 0:TT], pb, [(Wuz[:, dk, cc * 128:(cc + 1) * 128], xT_[:, dk, :]) for dk in range(8)], r=[Wuz_b, xT_b_])
                if cc < 4:
                    ctx.op("act", lambda: act.copy(out=uT_[:, cc, :], in_=pt[:, 0:TT]), r=[pb], w=[uT_b_[cc]])
                else:
                    ctx.op("act", lambda: act.activation(out=szs_[:, cc - 4, :], in_=pt[:, 0:TT], func=AF.Silu), r=[pb], w=[szs_b_])

            for dk in range(8):
                tasks_.append(lambda dk=dk: t_tr(dk))
            for cc in range(8):
                tasks_.append(lambda cc=cc: t_uz(cc))
            for cc in range(8):
                tasks_.append(lambda cc=cc: t_qk(cc))
            for sub in range(4):
                tasks_.append(lambda sub=sub: t_v(sub))
            return tasks_, (uT_, uT_b_, szs_, szs_b_)

        def post_tasks(t_, gT_, gT_b_, szs_, szs_b_):
            tk0 = t_ * TT
            tasks_ = []

            def t_glu(oc):
                pa, pab = mm_ring.next()
                mm_group(pa[:, 0:TT], pab, [(Wglu[:, kc, oc * 128:(oc + 1) * 128], gT_[:, kc, :]) for kc in range(4)], r=[Wglu_b, gT_b_])
                pbk, pbb = mm_ring.next()
                mm_group(pbk[:, 0:TT], pbb, [(Wglu[:, kc, (oc + 4) * 128:(oc + 5) * 128], gT_[:, kc, :]) for kc in range(4)], r=[Wglu_b, gT_b_])
                sg, sgb = sigb_ring.next()
                ctx.op("act", lambda: act.activation(out=sg[:], in_=pbk[:, 0:TT], func=AF.Sigmoid), r=[pbb], w=[sgb])
                ctx.op("dve", lambda: dve.tensor_tensor(out=sg[:], in0=sg[:], in1=pa[:, 0:TT], op=ALU.mult), r=[sgb, pab], w=[sgb])
                ctx.op("dve", lambda: dve.tensor_tensor(out=yg[:, oc, :], in0=sg[:], in1=szs_[:, oc, :], op=ALU.mult), r=[sgb, szs_b_], w=[yg_b])

            def t_sp(oc):
                pt, pb = mm_ring.next()
                mm_group(pt[:, 0:TT], pb, [(Wsp[:, kc, oc * 128:(oc + 1) * 128], yg[:, kc, :]) for kc in range(4)], r=[Wsp_b, yg_b])
                stg, stb = ys_ring.next()
                ctx.op("act", lambda: act.copy(out=stg[:], in_=pt[:, 0:TT]), r=[pb], w=[stb])
                ctx.dma(YS_d[oc, :, tk0:tk0 + TT], stg[:], r=[stb])

            for oc in range(4):
                tasks_.append(lambda oc=oc: t_glu(oc))
            for oc in range(8):
                tasks_.append(lambda oc=oc: t_sp(oc))
            return tasks_

        post = []
        cur_tile = None
        if NTA:
            issue_xA(0)
            tasks, cur_tile = proj_tasks(0)
            for tk_ in tasks:
                tk_()
        for t in range(NTA):
            tok0 = t * TT
            uT, uT_b, szs, szs_b = cur_tile
            gT, gT_b = gT_ring.next()
            tasks = list(post)
            post_n = list(post)
            if t + 1 < NTA:
                ptk, cur_tile = proj_tasks(t + 1)
                tasks += ptk
            per_unit = (len(tasks) + 7) // 8
            def ssm_stage1a(ti_, s_, hf):
                tsl = slice(s_ * 128, (s_ + 1) * 128)
                (Sbr, Sbi), Sb_b = Sb_ring.next()
                for qi in range(2):
                    q = 2 * hf + qi
                    (Sre, Sre_b), (Sim, Sim_b) = psS_ring.next()
                    for i in range(4):
                        pr = 4 * q + i
                        ctx.op("pe", lambda: pe.matmul(Sre[:, i * 128:(i + 1) * 128], lhsT=Bre[:, pr, :], rhs=ti_[0][:, q, tsl], start=True, stop=True),
                               r=[BC_b, ti_[1][q]], w=[Sre_b], sig=(i == 3))
                    for i in range(4):
                        pr = 4 * q + i
                        ctx.op("pe", lambda: pe.matmul(Sim[:, i * 128:(i + 1) * 128], lhsT=Bim[:, pr, :], rhs=ti_[0][:, q, tsl], start=True, stop=True),
                               r=[BC_b, ti_[1][q]], w=[Sim_b], sig=(i == 3))
                    ctx.op("act", lambda: act.copy(out=Sbr[:, qi * 4:(qi + 1) * 4, :].rearrange("p a b -> p (a b)"), in_=Sre[:, 0:512]), r=[Sre_b], w=[Sb_b])
                    ctx.op("act", lambda: act.copy(out=Sbi[:, qi * 4:(qi + 1) * 4, :].rearrange("p a b -> p (a b)"), in_=Sim[:, 0:512]), r=[Sim_b], w=[Sb_b])
                return s_, hf, Sbr, Sbi, Sb_b

            def ssm_stage1b(s_, hf, Sbr, Sbi, Sb_b):
                p0 = 8 * hf
                Tcq = Tcb[:, p0:p0 + 8, :]
                Tsq = Tsb[:, p0:p0 + 8, :]
                cb = car_b[hf]
                ctx.op("dve", lambda: dve.tensor_tensor(out=bpr[:], in0=Sbr[:], in1=Tcq, op=ALU.mult), r=[Sb_b, T_b], w=[bp_b])
                ctx.op("dve", lambda: dve.tensor_tensor(out=tm2[:], in0=Sbi[:], in1=Tsq, op=ALU.mult), r=[Sb_b, T_b], w=[bp_b])
                ctx.op("dve", lambda: dve.tensor_tensor(out=bpr[:], in0=bpr[:], in1=tm2[:], op=ALU.add), r=[bp_b], w=[bp_b])
                ctx.op("dve", lambda: dve.tensor_tensor(out=bpi[:], in0=Sbi[:], in1=Tcq, op=ALU.mult), r=[Sb_b, T_b, bp_b], w=[bp_b])
                ctx.op("dve", lambda: dve.tensor_tensor(out=tm2[:], in0=Sbr[:], in1=Tsq, op=ALU.mult), r=[Sb_b, T_b, bp_b], w=[bp_b])
                ctx.op("dve", lambda: dve.tensor_tensor(out=bpi[:], in0=bpi[:], in1=tm2[:], op=ALU.subtract), r=[bp_b], w=[bp_b])
                ctx.op("dve", lambda: dve.tensor_tensor(out=rc_r[:], in0=mag_l[:, p0:p0 + 8], in1=car_r[:, p0:p0 + 8], op=ALU.mult), r=[mag_b, cb], w=[cb])
                ctx.op("dve", lambda: dve.tensor_tensor(out=rc_i[:], in0=mag_l[:, p0:p0 + 8], in1=car_i[:, p0:p0 + 8], op=ALU.mult), r=[mag_b, cb], w=[cb])
                ctx.op("dve", lambda: dve.tensor_tensor(out=bpr[:, :, 0], in0=bpr[:, :, 0], in1=rc_r[:], op=ALU.add), r=[bp_b, cb], w=[bp_b])
                ctx.op("dve", lambda: dve.tensor_tensor(out=bpi[:, :, 0], in0=bpi[:, :, 0], in1=rc_i[:], op=ALU.add), r=[bp_b, cb], w=[bp_b])
                (gr, gi), g_b = g_ring.next()
                rt = rtab[:, p0:p0 + 8, :].rearrange("p a b -> p (a b)")
                ctx.op("dve", lambda: dve.tensor_tensor_scan(out=gr[:].rearrange("p a b -> p (a b)"), data0=rt, data1=bpr[:].rearrange("p a b -> p (a b)"),
                                                             initial=0.0, op0=ALU.mult, op1=ALU.add), r=[bp_b, mag_b], w=[g_b])
                ctx.op("dve", lambda: dve.tensor_tensor_scan(out=gi[:].rearrange("p a b -> p (a b)"), data0=rt, data1=bpi[:].rearrange("p a b -> p (a b)"),
                                                             initial=0.0, op0=ALU.mult, op1=ALU.add), r=[bp_b, mag_b], w=[g_b])
                Ec = Ec_t[:, p0:p0 + 8]
                Es = Es_t[:, p0:p0 + 8]
                ctx.op("dve", lambda: dve.tensor_tensor(out=ctm[:], in0=Ec, in1=gr[:, :, 127], op=ALU.mult), r=[T_b, g_b, cb], w=[cb])
                ctx.op("dve", lambda: dve.tensor_tensor(out=ctm2[:], in0=Es, in1=gi[:, :, 127], op=ALU.mult), r=[T_b, g_b, cb], w=[cb])
                ctx.op("dve", lambda: dve.tensor_tensor(out=car_r[:, p0:p0 + 8], in0=ctm[:], in1=ctm2[:], op=ALU.subtract), r=[cb], w=[cb])
                ctx.op("dve", lambda: dve.tensor_tensor(out=ctm[:], in0=Ec, in1=gi[:, :, 127], op=ALU.mult), r=[T_b, g_b, cb], w=[cb])
                ctx.op("dve", lambda: dve.tensor_tensor(out=ctm2[:], in0=Es, in1=gr[:, :, 127], op=ALU.mult), r=[T_b, g_b, cb], w=[cb])
                ctx.op("dve", lambda: dve.tensor_tensor(out=car_i[:, p0:p0 + 8], in0=ctm[:], in1=ctm2[:], op=ALU.add), r=[cb], w=[cb])
                return s_, hf, gr, gi, g_b

            def ssm_stage2(ti_, s_, hf, gr, gi, g_b):
                tsl = slice(s_ * 128, (s_ + 1) * 128)
                p0 = 8 * hf
                Tcq = Tcb[:, p0:p0 + 8, :]
                Tsq = Tsb[:, p0:p0 + 8, :]
                (hr, mhi), (h_b, hi_b) = h_ring.next()
                ctx.op("dve", lambda: dve.tensor_tensor(out=tp3[:], in0=gr[:], in1=Tcq, op=ALU.mult), r=[g_b, T_b], w=[tpd_b])
                ctx.op("dve", lambda: dve.tensor_tensor(out=tp4[:], in0=gi[:], in1=Tsq, op=ALU.mult), r=[g_b, T_b], w=[tpd_b])
                ctx.op("dve", lambda: dve.tensor_tensor(out=hr[:], in0=tp3[:], in1=tp4[:], op=ALU.subtract), r=[tpd_b], w=[h_b])
                ctx.op("dve", lambda: dve.tensor_tensor(out=tp5[:], in0=gi[:], in1=Tcq, op=ALU.mult), r=[g_b, T_b], w=[tpd_b])
                ctx.op("dve", lambda: dve.tensor_tensor(out=tp6[:], in0=gr[:], in1=Tsq, op=ALU.mult), r=[g_b, T_b], w=[tpd_b])
                ctx.op("dve", lambda: dve.tensor_tensor(out=mhi[:], in0=tp5[:], in1=tp6[:], op=ALU.add), r=[tpd_b], w=[hi_b])
                for qi in range(2):
                    q = 2 * hf + qi
                    yreg = psY[:, qi * 128:(qi + 1) * 128]
                    ctx.op("pe", lambda: pe.matmul(yreg, lhsT=Dg[:, 0, q, :], rhs=ti_[0][:, q, tsl], start=True, stop=False), r=[BC_b, ti_[1][q]], w=[_psY_b], sig=False)
                    ctx.op("pe", lambda: pe.matmul(yreg, lhsT=Dg[:, 1, q, :], rhs=ti_[0][:, q, tsl], start=False, stop=False), r=[BC_b, ti_[1][q]], w=[_psY_b], sig=False)
                    for i in range(4):
                        pr = 4 * q + i
                        ctx.op("pe", lambda: pe.matmul(yreg, lhsT=Cre[:, pr, :], rhs=hr[:, qi * 4 + i, :], start=False, stop=False),
                               r=[BC_b, h_b], w=[_psY_b], sig=False)
                        ctx.op("pe", lambda: pe.matmul(yreg, lhsT=Cim[:, pr, :], rhs=mhi[:, qi * 4 + i, :], start=False, stop=(i == 3)),
                               r=[BC_b, hi_b], w=[_psY_b], sig=(i == 3 and qi == 1))
                ctx.op("act", lambda: act.activation(out=ti_[2][:, 2 * hf:2 * hf + 2, tsl], in_=psY[:, 0:256].rearrange("p (a b) -> p a b", a=2), func=AF.Gelu),
                       r=[_psY_b], w=[ti_[3]])

            units = [(s_, hf) for s_ in range(TT // 128) for hf in range(2)]
            nu = len(units)
            ti_cur = (uT, uT_b, gT, gT_b)
            ti_nxt = (cur_tile[0], cur_tile[1]) if t + 1 < NTA else None
            if t + 1 < NTA:
                assert len(post_n) + 16 <= 6 * per_unit
            if t == 0:
                a_res = {0: ssm_stage1a(ti_cur, *units[0]), 1: ssm_stage1a(ti_cur, *units[1])}
                b_res = {0: ssm_stage1b(*a_res.pop(0))}
            else:
                a_res, b_res = next_a, next_b
            next_a, next_b = {}, {}
            for k in range(nu):
                if k + 2 < nu:
                    a_res[k + 2] = ssm_stage1a(ti_cur, *units[k + 2])
                elif ti_nxt is not None:
                    next_a[k + 2 - nu] = ssm_stage1a(ti_nxt, *units[k + 2 - nu])
                if k + 1 < nu:
                    b_res[k + 1] = ssm_stage1b(*a_res.pop(k + 1))
                elif ti_nxt is not None:
                    next_b[0] = ssm_stage1b(*next_a.pop(0))
                ssm_stage2(ti_cur, *b_res.pop(k))
                for _ in range(per_unit):
                    if tasks:
                        tasks.pop(0)()
            while tasks:
                tasks.pop(0)()
            post = post_tasks(t, gT, gT_b, szs, szs_b)
        for tk_ in post:
            tk_()
        ctx.barrier()

    with ExitStack() as sB:
        S_ring = Ring([(ps(sB, "psSc%d" % i, [128, 512]), Buf("psSc%d" % i)) for i in range(3)])
        O_ring = Ring([(ps(sB, "psO%d" % i, [128, 512]), Buf("psO%d" % i)) for i in range(2)])
        psG = ps(sB, "psG", [128, 512]); psG_b = Buf("psG")
        psM = ps(sB, "psM", [128, 512]); psM_v = psM[:].bitcast(BF16); psM_b = Buf("psM")
        psBc = ps(sB, "psBc", [128, 512]); psBc_b = Buf("psBc")

        PMASK = sb(sB, "PMASK", [128, NCH, 32]); PAST = sb(sB, "PAST", [128, NCH, 32]); cm_b = Buf("cmask")
        ctx.dma(PMASK[:].rearrange("p a b -> p (a b)"), PMASK_d, w=[cm_b])
        ctx.dma(PAST[:].rearrange("p a b -> p (a b)"), PAST_d, w=[cm_b])
        b31 = sb(sB, "b31", [128, 8])
        ctx.dma(b31[:], b31_d, w=[cm_b])
        ones_f = sb(sB, "ones_f", [128, 64], BF16)
        ctx.op("dve", lambda: dve.memset(ones_f[:], 1.0), w=[cm_b])

        Htiles = sb(sB, "Htiles", [128, 8, 3, 128], BF16); H_b = [[Buf("H%d_%d" % (h_, j_)) for j_ in range(3)] for h_ in range(8)]
        KA = [sb(sB, "KA%d" % i, [96, S], BF16) for i in range(2)]
        QA = [sb(sB, "QA%d" % i, [96, S], BF16) for i in range(2)]
        VE = [sb(sB, "VE%d" % i, [128, NKT, 65], BF16) for i in range(2)]
        KA_b = [Buf("KA%d" % i) for i in range(2)]
        QA_b = [Buf("QA%d" % i) for i in range(2)]
        QM_b = [[Buf("QM%d_%d" % (i, t)) for t in range(NT)] for i in range(2)]
        VE_b = [Buf("VE%d" % i) for i in range(2)]

        def load_head(h):
            i = h % 2
            pr, hh = h // 2, h % 2
            ctx.dma(KA[i][0:64, :], KT_d[pr, hh * 64:(hh + 1) * 64, :], w=[KA_b[i]])
            ctx.dma(QA[i][0:64, :], QT_d[pr, hh * 64:(hh + 1) * 64, :], w=[QA_b[i]])
            ctx.dma(VE[i][:, :, 0:64], V_d[h], w=[VE_b[i]])

        NHB = 0 if stop in ('setup', 'A') else (NH if stop != 'B1' else 1)
        ctx.dma(KA[0][64:96, :], EOH_d, w=[KA_b[0]])
        ctx.op("pool", lambda: pool.memset(VE[0][:, :, 64:65], 1.0), w=[VE_b[0]])
        if NHB:
            load_head(0)
        ctx.dma(KA[1][64:96, :], EOH_d, w=[KA_b[1]])
        ctx.op("pool", lambda: pool.memset(VE[1][:, :, 64:65], 1.0), w=[VE_b[1]])
        if True:
            sb0 = sB
            relb_f = sb(sb0, "relb_f", [32, 8]); relb_h = sb(sb0, "relb_h", [32, 8], BF16)
            OHt = sb(sb0, "OHt", [32, 384], BF16); negm = sb(sb0, "negm", [8, 384])
            Ff = sb(sb0, "Ff", [8, 384]); Fh = sb(sb0, "Fh", [8, 2, 384], BF16)
            Pb = Buf("biasprep")
            Pb1, Pb2, Pb3 = Buf("bp1"), Buf("bp2"), Buf("bp3")
            ctx.dma(relb_f[:], relb_d, w=[Pb1]); ctx.dma(OHt[:], OH_d, w=[Pb2]); ctx.dma(negm[:], NEGM_d, w=[Pb3])
            ctx.op("dve", lambda: dve.tensor_copy(out=relb_h[:], in_=relb_f[:]), r=[Pb1, Pb2, Pb3], w=[Pb])
            ctx.op("pe", lambda: pe.matmul(psG[0:8, 0:384], lhsT=relb_h[:, :], rhs=OHt[:, :], start=True, stop=True), r=[Pb], w=[psG_b])
            ctx.op("dve", lambda: dve.tensor_tensor(out=Ff[:], in0=psG[0:8, 0:384], in1=negm[:], op=ALU.add), r=[psG_b, Pb], w=[Pb])
            ctx.op("dve", lambda: dve.tensor_copy(out=Fh[:, 0, :], in_=Ff[:]), r=[Pb], w=[Pb])
            ctx.op("dve", lambda: dve.tensor_scalar(out=Fh[:, 1, :], in0=Ff[:], scalar1=Ff[:, 380:381], scalar2=None, op0=ALU.subtract), r=[Pb], w=[Pb])
            ctx.dma(F_d[0:8, :], Fh[:, 0, :], r=[Pb], w=[Pb])
            ctx.dma(F_d[8:16, :], Fh[:, 1, :], r=[Pb], w=[Pb])
            for h in range(NH):
                for j, (row, off) in enumerate(((h, 0), (h, 128), (8 + h, 128))):
                    src = bass.AP(tensor=F_d.tensor, offset=row * 384 + off, ap=[[1, 128], [1, 128]])
                    ctx.dma(Htiles[:, h, j, :], src, r=[Pb], w=[H_b[h][j]])
        ksum = sb(sB, "ksum", [64, 32]); kmT = [sb(sB, "kmT%d" % i, [64, 32], BF16) for i in range(2)]
        km_b = [Buf("km%d" % i) for i in range(2)]
        gm = sb(sB, "gm", [128, 4, 32]); ns_ = sb(sB, "ns", [128, 4, 32]); thr8 = sb(sB, "thr8", [128, 4, 8]); gm_b = Buf("gm")
        Mq_ring = Ring([(sb(sB, "Mq%d" % i, [128, 4, 96], BF16), Buf("Mq%d" % i)) for i in range(2)])
        for mq, mqb in Mq_ring.items:
            ctx.op("pool", lambda: pool.memset(mq[:], 0.0), w=[mqb])
        PT_ring = Ring([(sb(sB, "PT%d" % i, [128, 512], BF16), Buf("PT%d" % i)) for i in range(4)])
        osb_ring = Ring([(sb(sB, "osb%d" % i, [65, 512]), Buf("osb%d" % i)) for i in range(2)])
        rec_ring = Ring([(sb(sB, "rec%d" % i, [65, 512], BF16), Buf("rec%d" % i)) for i in range(2)])
        oo_ring = Ring([(sb(sB, "oo%d" % i, [64, 512], BF16), Buf("oo%d" % i)) for i in range(2)])

        def prep_head(h):
            i = h % 2
            Kh = KA[i]
            ctx.op("dve", lambda: dve.tensor_reduce(out=ksum[:, 0:NB], in_=Kh[0:64, :].rearrange("p (a b) -> p a b", b=BLK), axis=AX.X, op=ALU.add),
                   r=[KA_b[i]], w=[km_b[i]])
            if NB < 32:
                ctx.op("dve", lambda: dve.memset(ksum[:, NB:32], 0.0), r=[], w=[km_b[i]])
            ctx.op("dve", lambda: dve.tensor_scalar(out=kmT[i][:], in0=ksum[:], scalar1=1.0 / BLK, scalar2=None, op0=ALU.mult), r=[km_b[i]], w=[km_b[i]])

        def gate1(h, T):
            i = h % 2
            Qh = QA[i]
            c0 = 4 * T
            q0 = T * TT
            for ci in range(4):
                ctx.op("pe", lambda: pe.matmul(psG[:, ci * 32:(ci + 1) * 32], lhsT=Qh[0:64, q0 + ci * 128:q0 + (ci + 1) * 128], rhs=kmT[i][:, :], start=True, stop=True),
                       r=[QA_b[i], km_b[i]], w=[psG_b], sig=(ci == 3))
            ctx.op("dve", lambda: dve.tensor_tensor(out=gm[:], in0=psG[:, 0:128].rearrange("p (a b) -> p a b", a=4), in1=PMASK[:, c0:c0 + 4, :], op=ALU.add),
                   r=[psG_b, cm_b], w=[gm_b])
            for ci in range(4):
                ctx.op("dve", lambda: dve.max(out=thr8[:, ci, :], in_=gm[:, ci, :]), r=[gm_b], w=[gm_b])
            ctx.op("dve", lambda: dve.tensor_tensor(out=ns_[:], in0=gm[:], in1=thr8[:, :, 2:3].to_broadcast([128, 4, 32]), op=ALU.is_lt), r=[gm_b], w=[gm_b])
            ctx.op("dve", lambda: dve.tensor_scalar(out=ns_[:], in0=ns_[:], scalar1=NEG, scalar2=b31[:, h:h + 1], op0=ALU.mult, op1=ALU.add), r=[gm_b, cm_b], w=[gm_b])
            Mq, Mq_b = Mq_ring.next()
            ctx.op("dve", lambda: dve.tensor_tensor(out=Mq[:, :, 64:96], in0=ns_[:], in1=PAST[:, c0:c0 + 4, :], op=ALU.mult), r=[gm_b, cm_b], w=[Mq_b])
            return Mq, Mq_b

        def gate2(h, T, Mq, Mq_b):
            i = h % 2
            q0 = T * TT
            for ci in range(4):
                ctx.op("pe", lambda: pe.transpose(psM_v[0:96, ci * 128:(ci + 1) * 128], Mq[:, ci, :], ident[:]), r=[Mq_b, ident_b], w=[psM_b], sig=(ci == 3))
            ctx.op("dve", lambda: dve.tensor_copy(out=QA[i][64:96, q0:q0 + TT], in_=psM_v[64:96, 0:512]), r=[psM_b], w=[QM_b[i][T]])

        def emit_norm(h_, T_, Ops_, Ops_b_):
            osb, osb_b = osb_ring.next()
            ctx.op("act", lambda: act.copy(out=osb[:], in_=Ops_[0:65, :]), r=[Ops_b_], w=[osb_b])
            rec, rec_b = rec_ring.next()
            ctx.op("act", lambda: act.activation(out=osb[64:65, :], in_=osb[64:65, :], func=AF.Ln), r=[osb_b], w=[osb_b])
            ctx.op("act", lambda: act.activation(out=rec[64:65, :], in_=osb[64:65, :], func=AF.Exp, scale=-1.0), r=[osb_b], w=[rec_b])
            ctx.op("pe", lambda: pe.matmul(psBc[0:64, :], lhsT=ones_f[64:65, 0:64], rhs=rec[64:65, :], start=True, stop=True), r=[rec_b, cm_b], w=[psBc_b])
            oo, oo_b = oo_ring.next()
            ctx.op("dve", lambda: dve.tensor_tensor(out=oo[:], in0=osb[0:64, :], in1=psBc[0:64, :], op=ALU.mult), r=[osb_b, psBc_b], w=[oo_b])
            ctx.dma(OA_d[h_, :, T_ * TT:(T_ + 1) * TT], oo[:], r=[oo_b])

        pending_norm = None
        work = [(h, T) for h in range(NHB) for T in range(NT)]
        LA = 2
        if work:
            prep_head(0)
            g = gate1(0, 0)
            gate2(0, 0, *g)
        for wi, (h, T) in enumerate(work):
            i = h % 2
            if T == 0 and h + 1 < NHB:
                load_head(h + 1)
            Kh, Qh, Vh = KA[i], QA[i], VE[i]
            q0 = T * TT
            nxt_w = work[wi + 1] if wi + 1 < len(work) else None
            g = None
            if nxt_w is not None:
                if nxt_w[1] == 0:
                    prep_head(nxt_w[0])
                g = gate1(*nxt_w)
            Ops, Ops_b = O_ring.next()
            nkt = 4 * T + 4

            def emit_S(kt):
                cl = max(0, kt - 4 * T)
                cs = slice(cl * 128, 512)
                Sps, Sps_b = S_ring.next()
                adds = []
                ci0 = kt - 4 * T
                if 0 <= ci0 <= 3:
                    adds.append((ci0, 0))
                ci1 = kt + 1 - 4 * T
                if 0 <= ci1 <= 3:
                    adds.append((ci1, 1 if kt % 2 == 0 else 2))
                ctx.op("pe", lambda: pe.matmul(Sps[:, cs], lhsT=Kh[0:96, kt * 128:(kt + 1) * 128], rhs=Qh[0:96, q0 + cl * 128:q0 + 512], start=True, stop=(len(adds) == 0)),
                       r=[KA_b[i], QA_b[i], QM_b[i][T]], w=[Sps_b], sig=(len(adds) == 0))
                for ai, (cidx, j) in enumerate(adds):
                    last = ai == len(adds) - 1
                    ctx.op("pe", lambda: pe.matmul(Sps[:, cidx * 128:(cidx + 1) * 128], lhsT=Jm[:, :], rhs=Htiles[:, h, j, :], start=False, stop=last),
                           r=[Jm_b, H_b[h][j]], w=[Sps_b], sig=last)
                return kt, cs, Sps, Sps_b

            def emit_PV(kt, cs, Sps, Sps_b):
                PT, PT_b = PT_ring.next()
                ctx.op("act", lambda: act.activation(out=PT[:, cs], in_=Sps[:, cs], func=AF.Exp), r=[Sps_b], w=[PT_b])
                ctx.op("pe", lambda: pe.matmul(Ops[0:65, cs], lhsT=Vh[:, kt, 0:65], rhs=PT[:, cs], start=(kt == 0), stop=(kt == nkt - 1)),
                       r=[VE_b[i], PT_b], w=[Ops_b], sig=(kt == nkt - 1))

            pend = []
            for kt in range(nkt):
                pend.append(emit_S(kt))
                if kt == LA and pending_norm is not None:
                    emit_norm(*pending_norm)
                    pending_norm = None
                if len(pend) > LA:
                    emit_PV(*pend.pop(0))
            while pend:
                emit_PV(*pend.pop(0))
            if g is not None:
                gate2(nxt_w[0], nxt_w[1], *g)
            pending_norm = (h, T, Ops, Ops_b)
        if pending_norm is not None:
            emit_norm(*pending_norm)
        ctx.barrier()

    with ExitStack() as sC:
        psT_v = [ps(sC, "psTc%d" % i, [128, 512])[:].bitcast(BF16) for i in range(2)]
        psT_b = [Buf("psTc0"), Buf("psTc1")]
        mm_ring = Ring([(ps(sC, "mc%d" % i, [128, 512]), Buf("mc%d" % i)) for i in range(6)])
        tk_ring = mm_ring

        Wz = sb(sC, "Wz", [128, 8, 2560], BF16); Wz_b = Buf("Wz")
        Wap = sb(sC, "Wap", [128, 4, 1024], BF16); Wap_b = Buf("Wap")
        Wout = sb(sC, "Wout", [128, 8, 1024], BF16); Wout_b = Buf("Wout")
        Wpg = sb(sC, "Wpg", [128, 8, 1024], BF16); Wpg_b = Buf("Wpg")
        Wpp = sb(sC, "Wpp", [128, 2, 1024], BF16); Wpp_b = Buf("Wpp")
        lng = sb(sC, "lng", [128, D]); lnb = sb(sC, "lnb", [128, D]); ln_b = Buf("ln")
        ctx.dma(lng[:], lng_d, w=[ln_b]); ctx.dma(lnb[:], lnb_d, w=[ln_b])
        NS = TC // 128
        x_ring = Ring([(sb(sC, "xc%d" % i, [128, D]), Buf("xc%d" % i)) for i in range(2 * NS)])
        xb = sb(sC, "xbC", [128, NS, D], BF16); xb_b = [Buf("xbC%d" % i) for i in range(NS)]
        xTc_ring = Ring([(sb(sC, "xTC%d" % i, [128, 8, TC], BF16), Buf("xTC%d" % i)) for i in range(2)])
        p_ring = Ring([(sb(sC, "pc%d" % i, [128, 256]), Buf("pc%d" % i)) for i in range(NS)])
        pbf = sb(sC, "pbf", [128, NS, 256], BF16); pbf_b = [Buf("pbf%d" % i) for i in range(NS)]
        pTc_ring = Ring([(sb(sC, "pT%d" % i, [128, 2, TC], BF16), Buf("pT%d" % i)) for i in range(2)])
        sza = sb(sC, "sza", [128, 4, TC], BF16); sza_b = Buf("sza")
        sga = sb(sC, "sga", [128, 8, TC], BF16); sga_b = Buf("sga")
        sgs = sb(sC, "sgs", [128, 8, TC], BF16); sgs_b = Buf("sgs")
        oa_ring = Ring([(sb(sC, "oat%d" % i, [128, 4, TC], BF16), Buf("oat%d" % i)) for i in range(2)])
        oz = sb(sC, "oz", [128, 4, TC], BF16); oz_b = Buf("oz")
        ys_ring = Ring([(sb(sC, "yst%d" % i, [128, 8, TC], BF16), Buf("yst%d" % i)) for i in range(2)])
        mg_ring = Ring([(sb(sC, "merge%d" % i, [128, 8, TC], BF16), Buf("merge%d" % i)) for i in range(2)])
        ta_ring = Ring([(sb(sC, "tma%d" % i, [128, TC]), Buf("tma%d" % i)) for i in range(2)])
        tb_ring = Ring([(sb(sC, "tmb%d" % i, [128, TC]), Buf("tmb%d" % i)) for i in range(2)])
        s_ring = Ring([(sb(sC, "srow%d" % i, [128, D]), Buf("srow%d" % i)) for i in range(2)])
        o_ring = Ring([(sb(sC, "orow%d" % i, [128, D]), Buf("orow%d" % i)) for i in range(2)])
        st_ring = Ring([((sb(sC, "bst%d" % i, [128, 2, 6]), sb(sC, "bmv%d" % i, [128, 2]), sb(sC, "brs%d" % i, [128, 2])), Buf("bst%d" % i)) for i in range(2)])

        def issue_xC(t_):
            tk0 = t_ * TC
            xts_ = []
            for sub in range(NS):
                xt, xtb = x_ring.next()
                xts_.append((xt, xtb))
                ctx.dma(xt[:], x_d[tk0 + sub * 128:tk0 + (sub + 1) * 128, :], w=[xtb])
                ctx.op("pool", lambda: pool.tensor_copy(out=xb[:, sub, :], in_=xt[:]), r=[xtb], w=[xb_b[sub]])
                pt_, ptb = p_ring.next()
                ctx.dma(pt_[:], p_d[tk0 + sub * 128:tk0 + (sub + 1) * 128, :], w=[ptb])
                ctx.op("pool", lambda: pool.tensor_copy(out=pbf[:, sub, :], in_=pt_[:]), r=[ptb], w=[pbf_b[sub]])
            oat_, oat_b_ = oa_ring.next()
            ctx.dma(oat_[:], OA_d[:, :, tk0:tk0 + TC].rearrange("(pr hh) d t -> (hh d) pr t", hh=2), w=[oat_b_])
            yst_, yst_b_ = ys_ring.next()
            ctx.dma(yst_[:], YS_d[:, :, tk0:tk0 + TC].rearrange("c p t -> p c t"), w=[yst_b_])
            return xts_, (oat_, oat_b_), (yst_, yst_b_)

        NTCC = 0 if stop in ('setup', 'A', 'B', 'B1') else NTC
        if NTCC:
            nxt = issue_xC(0)
        stage_ring = Ring([(sb(sC, "wstc%d" % i, [128, 2048]), Buf("wstc%d" % i)) for i in range(2)])
        load_weight(Wz, Wz_b, w_in_d, 1536, 512, 8, stage_ring, dcol0=0)
        load_weight(Wz, Wz_b, w_in_d, 3072, 2048, 8, stage_ring, dcol0=512)
        load_weight(Wap, Wap_b, w_ap_d, 0, 1024, 4, stage_ring)
        load_weight(Wpg, Wpg_b, w_pg_d, 0, 1024, 8, stage_ring)
        load_weight(Wpp, Wpp_b, w_pp_d, 0, 1024, 2, stage_ring)
        load_weight(Wout, Wout_b, w_out_d, 0, 1024, 8, stage_ring)

        for t in range(NTCC):
            tok0 = t * TC
            xts, (oat, oat_b), (yst, yst_b) = nxt
            xT, xT_b = xTc_ring.next()
            pT, pT_b = pTc_ring.next()
            merge, merge_b = mg_ring.next()
            for dk in range(8):
                hf = dk % 2
                tp = psT_v[hf][:, 0:TC]
                for sub in range(NS):
                    ctx.op("pe", lambda: pe.transpose(tp[:, sub * 128:(sub + 1) * 128], xb[:, sub, dk * 128:(dk + 1) * 128], ident[:]),
                           r=[xb_b[sub], ident_b], w=[psT_b[hf]], sig=(sub == NS - 1))
                ctx.op("act", lambda: act.copy(out=xT[:, dk, :], in_=tp), r=[psT_b[hf]], w=[xT_b])
            for kc in range(2):
                hf = kc % 2
                tp = psT_v[hf][:, 0:TC]
                for sub in range(NS):
                    ctx.op("pe", lambda: pe.transpose(tp[:, sub * 128:(sub + 1) * 128], pbf[:, sub, kc * 128:(kc + 1) * 128], ident[:]),
                           r=[pbf_b[sub], ident_b], w=[psT_b[hf]], sig=(sub == NS - 1))
                ctx.op("act", lambda: act.copy(out=pT[:, kc, :], in_=tp), r=[psT_b[hf]], w=[pT_b])
            if t + 1 < NTCC:
                nxt = issue_xC(t + 1)
            for cc in range(20):
                pt, pb = mm_ring.next()
                mm_group(pt[:, 0:TC], pb, [(Wz[:, dk, cc * 128:(cc + 1) * 128], xT[:, dk, :]) for dk in range(8)], r=[Wz_b, xT_b])
                if cc < 4:
                    ctx.op("act", lambda: act.activation(out=sza[:, cc, :], in_=pt[:, 0:TC], func=AF.Silu), r=[pb], w=[sza_b])
                elif cc < 12:
                    ctx.op("act", lambda: act.activation(out=sga[:, cc - 4, :], in_=pt[:, 0:TC], func=AF.Sigmoid), r=[pb], w=[sga_b])
                else:
                    ctx.op("act", lambda: act.activation(out=sgs[:, cc - 12, :], in_=pt[:, 0:TC], func=AF.Sigmoid), r=[pb], w=[sgs_b])
            ctx.op("pool", lambda: pool.tensor_tensor(out=oz[:], in0=oat[:], in1=sza[:], op=ALU.mult), r=[oat_b, sza_b], w=[oz_b])
            for oc in range(8):
                pt, pb = mm_ring.next()
                mm_group(pt[:, 0:TC], pb, [(Wap[:, kc, oc * 128:(oc + 1) * 128], oz[:, kc, :]) for kc in range(4)], r=[Wap_b, oz_b])
                ta, ta_b = ta_ring.next()
                tb, tb_b = tb_ring.next()
                ctx.op("dve", lambda: dve.tensor_tensor(out=ta[:], in0=pt[:, 0:TC], in1=sga[:, oc, :], op=ALU.mult), r=[pb, sga_b], w=[ta_b])
                ctx.op("pool", lambda: pool.tensor_tensor(out=tb[:], in0=yst[:, oc, :], in1=sgs[:, oc, :], op=ALU.mult), r=[yst_b, sgs_b], w=[tb_b])
                ctx.op("dve", lambda: dve.tensor_tensor(out=merge[:, oc, :], in0=ta[:], in1=tb[:], op=ALU.add), r=[ta_b, tb_b], w=[merge_b])
            for sub in range(NS):
                xt, xtb = xts[sub]
                srow, srow_b = s_ring.next()
                tsl = slice(sub * 128, (sub + 1) * 128)
                (bst, bmv, brs), bst_b = st_ring.next()
                for hc in range(2):
                    csl = slice(hc * 512, (hc + 1) * 512)
                    pg, pg_b = tk_ring.next()
                    mm_group(pg[:, :], pg_b, [(xT[:, dk, tsl], Wpg[:, dk, csl]) for dk in range(8)], r=[Wpg_b, xT_b])
                    pp, pp_b = tk_ring.next()
                    mm_group(pp[:, :], pp_b, [(pT[:, kc, tsl], Wpp[:, kc, csl]) for kc in range(2)], r=[Wpp_b, pT_b])
                    mx, mx_b = tk_ring.next()
                    mm_group(mx[:, :], mx_b, [(merge[:, kc, tsl], Wout[:, kc, csl]) for kc in range(8)], r=[Wout_b, merge_b])
                    ctx.op("act", lambda: act.activation(out=srow[:, csl], in_=pg[:, :], func=AF.Sigmoid), r=[pg_b], w=[srow_b])
                    ctx.op("dve", lambda: dve.tensor_tensor(out=srow[:, csl], in0=srow[:, csl], in1=pp[:, :], op=ALU.mult), r=[srow_b, pp_b], w=[srow_b])
                    ctx.op("dve", lambda: dve.tensor_tensor(out=srow[:, csl], in0=srow[:, csl], in1=mx[:, :], op=ALU.add), r=[srow_b, mx_b], w=[srow_b])
                    ctx.op("dve", lambda: dve.scalar_tensor_tensor(out=srow[:, csl], in0=xt[:, csl], scalar=ALPHA, in1=srow[:, csl], op0=ALU.mult, op1=ALU.add),
                           r=[srow_b, xtb], w=[srow_b])
                    ctx.op("dve", lambda: dve.bn_stats(out=bst[:, hc, :], in_=srow[:, csl]), r=[srow_b], w=[bst_b])
                ctx.op("dve", lambda: dve.bn_aggr(out=bmv[:], in_=bst[:].rearrange("p a b -> p (a b)")), r=[bst_b], w=[bst_b])
                ctx.op("dve", lambda: dve.tensor_scalar(out=brs[:, 0:1], in0=bmv[:, 1:2], scalar1=LN_EPS, scalar2=None, op0=ALU.add), r=[bst_b], w=[bst_b])
                ctx.op("act", lambda: act.activation(out=brs[:, 0:1], in_=brs[:, 0:1], func=AF.Sqrt), r=[bst_b], w=[bst_b])
                ctx.op("dve", lambda: dve.reciprocal(out=brs[:, 0:1], in_=brs[:, 0:1]), r=[bst_b], w=[bst_b])
                ctx.op("dve", lambda: dve.scalar_tensor_tensor(out=brs[:, 1:2], in0=bmv[:, 0:1], scalar=-1.0, in1=brs[:, 0:1], op0=ALU.mult, op1=ALU.mult),
                       r=[bst_b], w=[bst_b])
                orow, orow_b = o_ring.next()
                ctx.op("dve", lambda: dve.tensor_scalar(out=orow[:], in0=srow[:], scalar1=brs[:, 0:1], scalar2=brs[:, 1:2], op0=ALU.mult, op1=ALU.add), r=[srow_b, bst_b], w=[orow_b])
                ctx.op("dve", lambda: dve.tensor_tensor(out=orow[:], in0=orow[:], in1=lng[:], op=ALU.mult), r=[orow_b, ln_b], w=[orow_b])
                ctx.op("pool", lambda: pool.tensor_tensor(out=orow[:], in0=orow[:], in1=lnb[:], op=ALU.add), r=[orow_b, ln_b], w=[orow_b])
                ctx.dma(out_d[tok0 + sub * 128:tok0 + (sub + 1) * 128, :], orow[:], r=[orow_b])
        ctx.barrier(engines=["sp"])
    es.close()
    return nc


def host_consts(S):
    bf = ml_dtypes.bfloat16
    NCH = S // 128
    c = {}
    c["ident"] = np.eye(128, dtype=np.float32).astype(bf)
    c["Jm"] = np.ascontiguousarray(np.eye(128, dtype=np.float32)[::-1]).astype(bf)
    i = np.arange(384)
    dist = i - 127
    bk = t5_bucket_np(np.maximum(dist, 0))
    OH = np.zeros((32, 384), np.float32)
    valid = dist >= 0
    OH[bk[valid], i[valid]] = 1.0
    c["OH"] = OH.astype(bf)
    NEGM = np.zeros((8, 384), np.float32)
    NEGM[:, ~valid] = NEG
    c["NEGM"] = NEGM
    keys = np.arange(S)
    EOH = (keys[None, :] // BLK == np.arange(32)[:, None]).astype(np.float32)
    c["EOH"] = EOH.astype(bf)
    ch = np.arange(NCH)
    blk = ch // 2
    n = np.arange(32)
    past = (n[None, :] < blk[:, None])
    PM = np.where(past, 0.0, -1e30).astype(np.float32)
    c["PMASK"] = np.ascontiguousarray(np.broadcast_to(PM.reshape(1, -1), (128, NCH * 32)))
    c["PAST01"] = np.ascontiguousarray(np.broadcast_to(past.astype(np.float32).reshape(1, -1), (128, NCH * 32)))
    c["jidx"] = np.ascontiguousarray(np.broadcast_to(np.arange(129, dtype=np.float32)[None, :], (128, 129)))
    return c


def host_params(inp):
    o = {}
    a_re = inp["ssm_a_re"][0]; a_im = inp["ssm_a_im"][0]; ldt = inp["ssm_log_dt"][0]
    b_re = inp["ssm_b_re"][0]; b_im = inp["ssm_b_im"][0]; c_re = inp["ssm_c_re"][0]; c_im = inp["ssm_c_im"][0]
    pidx = np.arange(128); g2 = pidx // 64; n = pidx % 64
    pr = np.arange(16)
    G = 2 * pr[None, :] + g2[:, None]
    o["are_l"] = np.ascontiguousarray(a_re[G, n[:, None]])
    o["aim_l"] = np.ascontiguousarray(a_im[G, n[:, None]])
    o["ldt_l"] = np.ascontiguousarray(ldt[G])
    Gc = 2 * pr[:, None] + g2[None, :]
    are_pc = a_re[Gc, n[None, :]]
    aim_pc = a_im[Gc, n[None, :]]
    ldt_pc = ldt[Gc]
    for name, v in (("are_rep", are_pc), ("aim_rep", aim_pc), ("ldt_rep", ldt_pc)):
        o[name] = np.ascontiguousarray(np.broadcast_to(v.reshape(1, 2048), (128, 2048))).astype(np.float32)
    brT = np.zeros((128, 16, 128), np.float32); biT = np.zeros((128, 16, 128), np.float32)
    cre = np.zeros((128, 16, 128), np.float32); cim = np.zeros((128, 16, 128), np.float32)
    for p_ in range(16):
        r0 = (p_ % 4) * 32
        for gg in range(2):
            g = 2 * p_ + gg
            brT[r0 + gg * 16:r0 + gg * 16 + 16, p_, gg * 64:(gg + 1) * 64] = b_re[g].T
            biT[r0 + gg * 16:r0 + gg * 16 + 16, p_, gg * 64:(gg + 1) * 64] = b_im[g].T
            cre[gg * 64:(gg + 1) * 64, p_, r0 + gg * 16:r0 + gg * 16 + 16] = c_re[g].T
            cim[gg * 64:(gg + 1) * 64, p_, r0 + gg * 16:r0 + gg * 16 + 16] = c_im[g].T
    o["brT"] = brT.reshape(128, 2048); o["biT"] = biT.reshape(128, 2048)
    o["cre_pad"] = cre.reshape(128, 2048); o["cim_pad"] = cim.reshape(128, 2048)
    o["d_l"] = np.ascontiguousarray(inp["ssm_d"][0].reshape(4, 128).T)
    o["lng"] = np.ascontiguousarray(np.broadcast_to(inp["ln_g"][0][None, :], (128, D)))
    o["lnb"] = np.ascontiguousarray(np.broadcast_to(inp["ln_b"][0][None, :], (128, D)))
    o["relb"] = np.ascontiguousarray(inp["rel_bias"])
    o["b31rep"] = np.ascontiguousarray(np.broadcast_to(inp["rel_bias"][31][None, :], (128, 8)))
    o["w_in"] = np.ascontiguousarray(inp["w_in"][0]); o["w_ap"] = np.ascontiguousarray(inp["w_attn_proj"][0])
    o["w_sp"] = np.ascontiguousarray(inp["w_ssm_proj"][0]); o["w_out"] = np.ascontiguousarray(inp["w_out"][0])
    o["w_glu"] = np.ascontiguousarray(inp["w_glu"][0]); o["w_pg"] = np.ascontiguousarray(inp["w_ple_gate"][0])
    o["w_pp"] = np.ascontiguousarray(inp["w_ple_proj"][0])
    return {k: np.asarray(v, np.float32) for k, v in o.items()}


_NC_CACHE = {}


def run(inputs, S, n_cores, **bk):
    inp = {k: np.asarray(v) for k, v in inputs.items()}
    if S not in _NC_CACHE:
        _NC_CACHE[S] = build_nc(S, **bk)
    nc = _NC_CACHE[S]
    shared = host_params(inp)
    shared.update(host_consts(S))
    in_maps = []
    for b in range(n_cores):
        m = dict(shared)
        m["x"] = np.ascontiguousarray(inp["x"][b], dtype=np.float32)
        m["p"] = np.ascontiguousarray(inp["p"][0, b], dtype=np.float32)
        in_maps.append(m)
    res = run_bass_kernel_spmd(nc, in_maps, core_ids=list(range(n_cores)))
    return np.stack([np.asarray(r["out"]) for r in res.results], axis=0).astype(np.float32)


def kernel(**inputs):
    return run(inputs, 8192, 8)
```

```python
import math
from contextlib import ExitStack

import numpy as np
import ml_dtypes

import concourse.bass as bass
import concourse.mybir as mybir
from concourse.bass_utils import run_bass_kernel_spmd

F32 = mybir.dt.float32
BF16 = mybir.dt.bfloat16
I32 = mybir.dt.int32
AF = mybir.ActivationFunctionType
ALU = mybir.AluOpType
AX = mybir.AxisListType

D = 1024
NH = 8
HD = 64
BLK = 256
NEG = -30000.0
TWO_PI = 2.0 * math.pi
ALPHA = 2.0 ** 0.25
LN_EPS = 1e-5


class Buf:
    __slots__ = ("w", "r", "name")

    def __init__(self, name=""):
        self.w = None
        self.r = {}
        self.name = name


class Ring:
    def __init__(self, items):
        self.items = items
        self.i = 0

    def next(self):
        it = self.items[self.i % len(self.items)]
        self.i += 1
        return it


class Ctx:
    def __init__(self, nc, es, n_dma_sems=24):
        self.nc = nc
        self.engs = {"pe": nc.tensor, "act": nc.scalar, "dve": nc.vector, "pool": nc.gpsimd, "sp": nc.sync}
        self.sem = {}
        self.cnt = {}
        self.seen = {e: {} for e in self.engs}
        self.pend = {e: ([], []) for e in self.engs}
        for e in self.engs:
            self.sem[e] = es.enter_context(nc.semaphore("s_" + e))
            self.cnt[e] = 0
        self.dq = []
        for i in range(n_dma_sems):
            k = "d%d" % i
            self.sem[k] = es.enter_context(nc.semaphore("s_" + k))
            self.cnt[k] = 0
            self.dq.append(k)
        self.dqi = 0

    def _wait(self, e, tok):
        if tok is None:
            return
        k, v = tok
        if v <= 0 or self.seen[e].get(k, 0) >= v:
            return
        self.engs[e].wait_ge(self.sem[k], v)
        self.seen[e][k] = v

    def _deps(self, e, r, w):
        for b in r:
            if b.w is not None and not (e == "pe" and b.w[0] == "pe"):
                self._wait(e, b.w)
        for b in w:
            if b.w is not None and b.w[0] != e:
                self._wait(e, b.w)
            for k, v in b.r.items():
                if k != e:
                    self._wait(e, (k, v))

    def _reg(self, tok, r, w):
        k, v = tok
        for b in r:
            if b.r.get(k, 0) < v:
                b.r[k] = v
        for b in w:
            b.w = tok
            b.r = {}

    def op(self, e, fn, r=(), w=(), sig=True):
        self._deps(e, r, w)
        inst = fn()
        pr, pw = self.pend[e]
        if not sig:
            pr.extend(r)
            pw.extend(w)
            return None
        self.cnt[e] += 1
        inst.then_inc(self.sem[e], 1)
        tok = (e, self.cnt[e])
        self._reg(tok, list(r) + pr, list(w) + pw)
        self.pend[e] = ([], [])
        return tok

    def dma(self, out, in_, r=(), w=(), q="sp"):
        k = self.dq[self.dqi % len(self.dq)]
        self.dqi += 1
        self._wait(q, (k, self.cnt[k]))
        self._deps(q, r, w)
        inst = self.engs[q].dma_start(out=out, in_=in_)
        self.cnt[k] += 16
        inst.then_inc(self.sem[k], 16)
        tok = (k, self.cnt[k])
        self._reg(tok, r, w)
        return tok

    def barrier(self, engines=None):
        toks = [(k, v) for k, v in self.cnt.items() if v > 0]
        for e in (engines or self.engs):
            for t in toks:
                if t[0] != e:
                    self._wait(e, t)


def t5_bucket_np(dist):
    dist = np.asarray(dist, np.int64)
    d = np.maximum(dist, 1).astype(np.float32)
    large = 16 + (np.log(d / np.float32(16)) / np.float32(math.log(128 / 16)) * np.float32(16)).astype(np.int32)
    large = np.minimum(large, 31)
    return np.where(dist < 16, dist, large)


def build_nc(S, TT=512, TC=256, stop=None, nt_lim=None):
    NT = S // TT
    NTC = S // TC
    NKT = S // 128
    NB = S // BLK
    NCH = S // 128
    assert NB <= 32
    nc = bass.Bass("TRN2", target_bir_lowering=False)
    es = ExitStack()
    ctx = Ctx(nc, es)
    pe, act, dve, pool = nc.tensor, nc.scalar, nc.vector, nc.gpsimd

    def din(name, shape, dt=F32):
        return nc.dram_tensor(name, list(shape), dt, kind="ExternalInput").ap()

    def dscr(name, shape, dt=BF16):
        return nc.dram_tensor(name, list(shape), dt, kind="Internal").ap()

    x_d = din("x", [S, D])
    p_d = din("p", [S, 256])
    w_in_d = din("w_in", [D, 5120])
    w_ap_d = din("w_ap", [512, D])
    w_sp_d = din("w_sp", [512, D])
    w_out_d = din("w_out", [D, D])
    w_glu_d = din("w_glu", [512, D])
    w_pg_d = din("w_pg", [D, D])
    w_pp_d = din("w_pp", [256, D])
    are_rep_d = din("are_rep", [128, 2048])
    aim_rep_d = din("aim_rep", [128, 2048])
    ldt_rep_d = din("ldt_rep", [128, 2048])
    brT_d = din("brT", [128, 2048])
    biT_d = din("biT", [128, 2048])
    cre_d = din("cre_pad", [128, 2048])
    cim_d = din("cim_pad", [128, 2048])
    are_l_d = din("are_l", [128, 16])
    aim_l_d = din("aim_l", [128, 16])
    ldt_l_d = din("ldt_l", [128, 16])
    d_l_d = din("d_l", [128, 4])
    lng_d = din("lng", [128, D])
    lnb_d = din("lnb", [128, D])
    relb_d = din("relb", [32, 8])
    b31_d = din("b31rep", [128, 8])
    ident_d = din("ident", [128, 128], BF16)
    J_d = din("Jm", [128, 128], BF16)
    OH_d = din("OH", [32, 384], BF16)
    NEGM_d = din("NEGM", [8, 384])
    EOH_d = din("EOH", [32, S], BF16)
    PMASK_d = din("PMASK", [128, NCH * 32])
    PAST_d = din("PAST01", [128, NCH * 32])
    jidx_d = din("jidx", [128, 129])
    out_d = nc.dram_tensor("out", [S, D], F32, kind="ExternalOutput").ap()

    QT_d = dscr("QT", [4, 128, S])
    KT_d = dscr("KT", [4, 128, S])
    V_d = dscr("Vs", [8, 128, S // 128, 64])
    OA_d = dscr("OA", [8, 64, S])
    YS_d = dscr("YS", [8, 128, S])
    F_d = dscr("Fd", [16, 384])

    def sb(stack, name, shape, dt=F32):
        return stack.enter_context(nc.sbuf_tensor("sb_" + name, list(shape), dt))

    def ps(stack, name, shape, dt=F32):
        return stack.enter_context(nc.psum_tensor("ps_" + name, list(shape), dt))

    ident = sb(es, "ident", [128, 128], BF16)
    ident_b = Buf("ident")
    Jm = sb(es, "Jm", [128, 128], BF16)
    Jm_b = Buf("J")
    ctx.dma(ident[:], ident_d, w=[ident_b])
    ctx.dma(Jm[:], J_d, w=[Jm_b])

    cast_rr = Ring(["act", "dve", "pool"])

    def cast_copy(e, out, in_):
        if e == "act":
            return lambda: act.copy(out=out, in_=in_)
        if e == "dve":
            return lambda: dve.tensor_copy(out=out, in_=in_)
        return lambda: pool.tensor_copy(out=out, in_=in_)

    def load_weight(dst, dst_b, src, col0, ncols, KC, stage_ring, dcol0=0):
        cw = 2048 // KC
        for c0 in range(0, ncols, cw):
            w_ = min(cw, ncols - c0)
            st, stb = stage_ring.next()
            stv = st[:, 0:KC * w_].rearrange("p (kc c) -> p kc c", kc=KC)
            ctx.dma(stv, src[:, col0 + c0:col0 + c0 + w_].rearrange("(kc p) c -> p kc c", p=128), w=[stb])
            e = cast_rr.next()
            ctx.op(e, cast_copy(e, dst[:, :, dcol0 + c0:dcol0 + c0 + w_], stv), r=[stb], w=[dst_b])

    def mm_group(out_ap, out_b, pairs, r):
        n = len(pairs)
        for i, (l, rh) in enumerate(pairs):
            ctx.op("pe", lambda: pe.matmul(out_ap, lhsT=l, rhs=rh, start=(i == 0), stop=(i == n - 1)),
                   r=r, w=[out_b], sig=(i == n - 1))

    with ExitStack() as sA:
        _psT = ps(sA, "psT0", [128, 512])[:].bitcast(BF16)
        psT_v = [_psT[:, 0:512], _psT[:, 512:1024]]
        _psT_b = Buf("psT")
        psT_b = [_psT_b, _psT_b]
        mm_ring = Ring([(ps(sA, "mm%d" % i, [128, 512]), Buf("mm%d" % i)) for i in range(2)])
        psS_ring = Ring([[(ps(sA, "psS%d_%d" % (j, i), [128, 512]), Buf("psS%d_%d" % (j, i))) for i in range(2)] for j in range(2)])
        psY = ps(sA, "psY", [128, 512])
        _psY_b = Buf("psY")
        psY_b = [_psY_b] * 4

        Wqk = sb(sA, "Wqk", [128, 8, 1024], BF16); Wqk_b = Buf("Wqk")
        Wv = sb(sA, "Wv", [128, 8, 512], BF16); Wv_b = Buf("Wv")
        Wuz = sb(sA, "Wuz", [128, 8, 1024], BF16); Wuz_b = Buf("Wuz")
        Wglu = sb(sA, "Wglu", [128, 4, 1024], BF16); Wglu_b = Buf("Wglu")
        Wsp = sb(sA, "Wsp", [128, 4, 1024], BF16); Wsp_b = Buf("Wsp")
        Bre = sb(sA, "Bre", [128, 16, 128], BF16); Bim = sb(sA, "Bim", [128, 16, 128], BF16)
        Cre = sb(sA, "Cre", [128, 16, 128], BF16); Cim = sb(sA, "Cim", [128, 16, 128], BF16)
        BC_b = Buf("BC")
        T_b = Buf("T")
        rtab = sb(sA, "rtab", [128, 16, 128])
        Dg = sb(sA, "Dg", [128, 2, 4, 128], BF16)
        Ec_t = sb(sA, "Ec_t", [128, 16]); Es_t = sb(sA, "Es_t", [128, 16])
        Tcb = sb(sA, "Tcb", [128, 16, 128], BF16); Tsb = sb(sA, "Tsb", [128, 16, 128], BF16)
        mag_l = sb(sA, "mag_l", [128, 16]); mag_b = Buf("mag")
        d_l = sb(sA, "d_l", [128, 4]); d_b = Buf("d")
        car_r = sb(sA, "car_r", [128, 16]); car_i = sb(sA, "car_i", [128, 16])
        car_b = [Buf("car%d" % q) for q in range(2)]
        ctx.dma(d_l[:], d_l_d, w=[d_b])

        with ExitStack() as s0:
            Tc = sb(s0, "Tc", [128, 16, 128]); Ts = sb(s0, "Ts", [128, 16, 128])
            dhi = sb(s0, "dhi", [128, 4], BF16); dlo = sb(s0, "dlo", [128, 4])
            A_ = sb(s0, "pA", [128, 2048]); Bm = sb(s0, "pB", [128, 2048]); L_ = sb(s0, "pL", [128, 2048])
            t0 = sb(s0, "pt0", [128, 2048]); t1 = sb(s0, "pt1", [128, 2048]); t2 = sb(s0, "pt2", [128, 2048])
            t3 = sb(s0, "pt3", [128, 2048]); t4 = sb(s0, "pt4", [128, 2048]); t5 = sb(s0, "pt5", [128, 2048])
            ti = sb(s0, "pti", [128, 2048], I32)
            bR = sb(s0, "pbR", [128, 2048]); bI = sb(s0, "pbI", [128, 2048])
            al = sb(s0, "al", [128, 16]); bl = sb(s0, "bl", [128, 16]); ll = sb(s0, "ll", [128, 16])
            th = sb(s0, "th", [128, 16]); jx = sb(s0, "jx", [128, 129])
            th128 = sb(s0, "th128", [128, 16]); sc16 = sb(s0, "sc16", [128, 16])
            P = Buf("setup")

            Pl = []
            for t_, d_ in ((A_, are_rep_d), (Bm, aim_rep_d), (L_, ldt_rep_d), (bR, brT_d), (bI, biT_d),
                           (t0, cre_d), (t1, cim_d), (al, are_l_d), (bl, aim_l_d), (ll, ldt_l_d), (jx, jidx_d)):
                pb_ = Buf("pl")
                Pl.append(pb_)
                ctx.dma(t_[:], d_, w=[pb_])
            ctx.op("dve", lambda: dve.memset(th[:], 0.0), r=Pl, w=[P])

            def V(fn):
                ctx.op("dve", fn, r=[P], w=[P])

            def A(fn):
                ctx.op("act", fn, r=[P], w=[P])

            V(lambda: dve.tensor_copy(out=Cre[:].rearrange("p a b -> p (a b)"), in_=t0[:]))
            V(lambda: dve.tensor_scalar(out=Cim[:].rearrange("p a b -> p (a b)"), in0=t1[:], scalar1=-1.0, scalar2=None, op0=ALU.mult))

            def emit_sin(out, x, n, shift, xs):
                xi = ti[:, 0:n]
                V(lambda: dve.tensor_scalar(out=xs, in0=x, scalar1=shift, scalar2=None, op0=ALU.add))
                V(lambda: dve.tensor_scalar(out=xi, in0=xs, scalar1=1.0 / TWO_PI, scalar2=None, op0=ALU.mult))
                V(lambda: dve.tensor_copy(out=out, in_=xi))
                V(lambda: dve.scalar_tensor_tensor(out=out, in0=out, scalar=-TWO_PI, in1=xs, op0=ALU.mult, op1=ALU.add))
                V(lambda: dve.tensor_scalar(out=xs, in0=out, scalar1=math.pi, scalar2=-TWO_PI, op0=ALU.is_gt, op1=ALU.mult))
                V(lambda: dve.tensor_tensor(out=out, in0=out, in1=xs, op=ALU.add))
                V(lambda: dve.tensor_scalar(out=xs, in0=out, scalar1=-math.pi, scalar2=TWO_PI, op0=ALU.is_lt, op1=ALU.mult))
                V(lambda: dve.tensor_tensor(out=out, in0=out, in1=xs, op=ALU.add))
                V(lambda: dve.tensor_scalar(out=out, in0=out, scalar1=math.pi, scalar2=-math.pi, op0=ALU.min, op1=ALU.max))
                A(lambda: act.activation(out=out, in_=out, func=AF.Sin))

            A(lambda: act.activation(out=L_[:], in_=L_[:], func=AF.Exp))
            V(lambda: dve.tensor_tensor(out=t0[:], in0=L_[:], in1=A_[:], op=ALU.mult))
            A(lambda: act.activation(out=t0[:], in_=t0[:], func=AF.Exp))
            V(lambda: dve.tensor_tensor(out=t1[:], in0=L_[:], in1=Bm[:], op=ALU.mult))
            emit_sin(t2[:], t1[:], 2048, 0.0, t4[:])
            emit_sin(t3[:], t1[:], 2048, math.pi / 2, t4[:])
            V(lambda: dve.tensor_tensor(out=t3[:], in0=t3[:], in1=t0[:], op=ALU.mult))
            V(lambda: dve.tensor_tensor(out=t2[:], in0=t2[:], in1=t0[:], op=ALU.mult))
            V(lambda: dve.tensor_scalar(out=t3[:], in0=t3[:], scalar1=-1.0, scalar2=None, op0=ALU.add))
            V(lambda: dve.tensor_tensor(out=t0[:], in0=A_[:], in1=A_[:], op=ALU.mult))
            V(lambda: dve.tensor_tensor(out=t1[:], in0=Bm[:], in1=Bm[:], op=ALU.mult))
            V(lambda: dve.tensor_tensor(out=t0[:], in0=t0[:], in1=t1[:], op=ALU.add))
            V(lambda: dve.reciprocal(out=t0[:], in_=t0[:]))
            V(lambda: dve.tensor_tensor(out=t4[:], in0=t3[:], in1=A_[:], op=ALU.mult))
            V(lambda: dve.tensor_tensor(out=t1[:], in0=t2[:], in1=Bm[:], op=ALU.mult))
            V(lambda: dve.tensor_tensor(out=t4[:], in0=t4[:], in1=t1[:], op=ALU.add))
            V(lambda: dve.tensor_tensor(out=t4[:], in0=t4[:], in1=t0[:], op=ALU.mult))
            V(lambda: dve.tensor_tensor(out=t5[:], in0=t2[:], in1=A_[:], op=ALU.mult))
            V(lambda: dve.tensor_tensor(out=t1[:], in0=t3[:], in1=Bm[:], op=ALU.mult))
            V(lambda: dve.tensor_tensor(out=t5[:], in0=t5[:], in1=t1[:], op=ALU.subtract))
            V(lambda: dve.tensor_tensor(out=t5[:], in0=t5[:], in1=t0[:], op=ALU.mult))
            V(lambda: dve.tensor_tensor(out=t0[:], in0=t4[:], in1=bR[:], op=ALU.mult))
            V(lambda: dve.tensor_tensor(out=t1[:], in0=t5[:], in1=bI[:], op=ALU.mult))
            V(lambda: dve.tensor_tensor(out=Bre[:].rearrange("p a b -> p (a b)"), in0=t0[:], in1=t1[:], op=ALU.subtract))
            V(lambda: dve.tensor_tensor(out=t0[:], in0=t4[:], in1=bI[:], op=ALU.mult))
            V(lambda: dve.tensor_tensor(out=t1[:], in0=t5[:], in1=bR[:], op=ALU.mult))
            V(lambda: dve.tensor_tensor(out=Bim[:].rearrange("p a b -> p (a b)"), in0=t0[:], in1=t1[:], op=ALU.add))
            A(lambda: act.activation(out=ll[:], in_=ll[:], func=AF.Exp))
            V(lambda: dve.tensor_tensor(out=al[:], in0=ll[:], in1=al[:], op=ALU.mult))
            A(lambda: act.activation(out=mag_l[:], in_=al[:], func=AF.Exp))
            V(lambda: dve.tensor_tensor(out=th[:], in0=ll[:], in1=bl[:], op=ALU.mult))
            V(lambda: dve.tensor_tensor(out=t0[:].rearrange("p (a b) -> p a b", a=16),
                                        in0=th[:].rearrange("p (a o) -> p a o", o=1).to_broadcast([128, 16, 128]),
                                        in1=jx[:, 0:128].rearrange("p (o b) -> p o b", o=1).to_broadcast([128, 16, 128]), op=ALU.mult))
            emit_sin(Ts[:].rearrange("p a b -> p (a b)"), t0[:], 2048, 0.0, t1[:])
            emit_sin(Tc[:].rearrange("p a b -> p (a b)"), t0[:], 2048, math.pi / 2, t1[:])
            V(lambda: dve.tensor_scalar(out=th128[:], in0=th[:], scalar1=128.0, scalar2=None, op0=ALU.mult))
            emit_sin(Es_t[:], th128[:], 16, 0.0, sc16[:])
            emit_sin(Ec_t[:], th128[:], 16, math.pi / 2, sc16[:])
            V(lambda: dve.tensor_copy(out=Tcb[:], in_=Tc[:]))
            V(lambda: dve.tensor_copy(out=Tsb[:], in_=Ts[:]))
            V(lambda: dve.tensor_copy(out=rtab[:], in_=mag_l[:].rearrange("p (a o) -> p a o", o=1).to_broadcast([128, 16, 128])))
            V(lambda: dve.memset(rtab[:, :, 0:1], 0.0))
            ctx.op("dve", lambda: dve.tensor_copy(out=dhi[:], in_=d_l[:]), r=[P, d_b], w=[P])
            V(lambda: dve.tensor_tensor(out=dlo[:], in0=d_l[:], in1=dhi[:], op=ALU.subtract))
            for q_ in range(4):
                ctx.op("dve", lambda: dve.tensor_scalar(out=Dg[:, 0, q_, :], in0=ident[:], scalar1=dhi[:, q_:q_ + 1], scalar2=None, op0=ALU.mult), r=[P, ident_b], w=[P])
                ctx.op("dve", lambda: dve.tensor_scalar(out=Dg[:, 1, q_, :], in0=ident[:], scalar1=dlo[:, q_:q_ + 1], scalar2=None, op0=ALU.mult), r=[P, ident_b], w=[P])
            V(lambda: dve.memset(car_r[:], 0.0))
            V(lambda: dve.memset(car_i[:], 0.0))
            ctx.op("dve", lambda: dve.memset(t0[:, 0:1], 0.0), r=[P], w=[P, BC_b, T_b, mag_b] + car_b)

            ctx.barrier()
        with ExitStack() as s0:
            stage_ring = Ring([(sb(s0, "wst%d" % i, [128, 2048]), Buf("wst%d" % i)) for i in range(2)])
            load_weight(Wqk, Wqk_b, w_in_d, 0, 1024, 8, stage_ring)
            load_weight(Wv, Wv_b, w_in_d, 1024, 512, 8, stage_ring)
            load_weight(Wuz, Wuz_b, w_in_d, 2048, 1024, 8, stage_ring)
            load_weight(Wglu, Wglu_b, w_glu_d, 0, 1024, 4, stage_ring)
            load_weight(Wsp, Wsp_b, w_sp_d, 0, 1024, 4, stage_ring)
            ctx.barrier()

        x_ring = Ring([(sb(sA, "xa%d" % i, [128, D]), Buf("xa%d" % i)) for i in range(3)])
        xb = sb(sA, "xbA", [128, 4, D], BF16)
        xb_b = [Buf("xb%d" % i) for i in range(4)]
        xT_ring = Ring([(sb(sA, "xTA%d" % i, [128, 8, TT], BF16), Buf("xTA%d" % i)) for i in range(2)])
        uT_ring = Ring([(sb(sA, "uT%d" % i, [128, 4, TT], BF16), [Buf("uT%d_%d" % (i, q)) for q in range(4)]) for i in range(2)])
        szs_ring = Ring([(sb(sA, "szs%d" % i, [128, 4, TT], BF16), Buf("szs%d" % i)) for i in range(2)])
        qk_ring = Ring([(sb(sA, "qkst%d" % i, [128, TT], BF16), Buf("qkst%d" % i)) for i in range(2)])
        v_ring = Ring([(sb(sA, "vst%d" % i, [128, 512], BF16), Buf("vst%d" % i)) for i in range(2)])
        ys_ring = Ring([(sb(sA, "ysst%d" % i, [128, TT], BF16), Buf("ysst%d" % i)) for i in range(2)])
        bpr = sb(sA, "bpr", [128, 8, 128], BF16); bpi = sb(sA, "bpi", [128, 8, 128], BF16); tm2 = sb(sA, "tm2", [128, 8, 128], BF16)
        bp_b = Buf("bp")
        Sb_ring = Ring([((sb(sA, "Sbr%d" % i, [128, 8, 128], BF16), sb(sA, "Sbi%d" % i, [128, 8, 128], BF16)), Buf("Sb%d" % i)) for i in range(2)])
        g_ring = Ring([((sb(sA, "gr%d" % i, [128, 8, 128], BF16), sb(sA, "gi%d" % i, [128, 8, 128], BF16)), Buf("g%d" % i)) for i in range(2)])
        tp3 = sb(sA, "tp3", [128, 8, 128], BF16); tp4 = sb(sA, "tp4", [128, 8, 128], BF16)
        tp5 = sb(sA, "tp5", [128, 8, 128], BF16); tp6 = sb(sA, "tp6", [128, 8, 128], BF16); tpd_b = Buf("tpd")
        h_ring = Ring([((sb(sA, "hr%d" % i, [128, 8, 128], BF16), sb(sA, "mhi%d" % i, [128, 8, 128], BF16)), (Buf("hr%d" % i), Buf("hi%d" % i))) for i in range(2)])
        rc_r = sb(sA, "rc_r", [128, 8]); rc_i = sb(sA, "rc_i", [128, 8]); ctm = sb(sA, "ctm", [128, 8]); ctm2 = sb(sA, "ctm2", [128, 8])
        gT_ring = Ring([(sb(sA, "gT%d" % i, [128, 4, TT], BF16), Buf("gT%d" % i)) for i in range(2)])
        sigb_ring = Ring([(sb(sA, "sigb%d" % i, [128, TT]), Buf("sigb%d" % i)) for i in range(2)])
        yg = sb(sA, "yg", [128, 4, TT], BF16); yg_b = Buf("yg")

        def issue_xA(t_):
            for sub in range(4):
                xt, xtb = x_ring.next()
                ctx.dma(xt[:], x_d[t_ * TT + sub * 128:t_ * TT + (sub + 1) * 128, :], w=[xtb])
                ctx.op("act", lambda: act.copy(out=xb[:, sub, :], in_=xt[:]), r=[xtb], w=[xb_b[sub]])

        NTA = 0 if stop == 'setup' else (nt_lim or NT)

        def proj_tasks(t_):
            tk0 = t_ * TT
            xT_, xT_b_ = xT_ring.next()
            uT_, uT_b_ = uT_ring.next()
            szs_, szs_b_ = szs_ring.next()
            tasks_ = []

            def t_tr(dk):
                hf = dk % 2
                tp = psT_v[hf][:, 0:512]
                for sub in range(4):
                    ctx.op("pe", lambda: pe.transpose(tp[:, sub * 128:(sub + 1) * 128], xb[:, sub, dk * 128:(dk + 1) * 128], ident[:]),
                           r=[xb_b[sub], ident_b], w=[psT_b[hf]], sig=(sub == 3))
                ctx.op("act", lambda: act.copy(out=xT_[:, dk, :], in_=tp), r=[psT_b[hf]], w=[xT_b_])
                if dk == 7 and t_ + 1 < NTA:
                    issue_xA(t_ + 1)

            def t_qk(cc):
                pt, pb = mm_ring.next()
                mm_group(pt[:, 0:TT], pb, [(Wqk[:, dk, cc * 128:(cc + 1) * 128], xT_[:, dk, :]) for dk in range(8)], r=[Wqk_b, xT_b_])
                stg, stb = qk_ring.next()
                if cc < 4:
                    ctx.op("act", lambda: act.activation(out=stg[:], in_=pt[:, 0:TT], func=AF.Copy, scale=0.125), r=[pb], w=[stb])
                    ctx.dma(QT_d[cc, :, tk0:tk0 + TT], stg[:], r=[stb])
                else:
                    ctx.op("act", lambda: act.copy(out=stg[:], in_=pt[:, 0:TT]), r=[pb], w=[stb])
                    ctx.dma(KT_d[cc - 4, :, tk0:tk0 + TT], stg[:], r=[stb])

            def t_v(sub):
                pt, pb = mm_ring.next()
                mm_group(pt[:, 0:512], pb, [(xT_[:, dk, sub * 128:(sub + 1) * 128], Wv[:, dk, :]) for dk in range(8)], r=[Wv_b, xT_b_])
                stg, stb = v_ring.next()
                ctx.op("act", lambda: act.copy(out=stg[:], in_=pt[:, 0:512]), r=[pb], w=[stb])
                ctx.dma(V_d[:, :, t_ * 4 + sub, :].rearrange("h p d -> p h d"), stg[:].rearrange("p (h d) -> p h d", h=8), r=[stb])

            def t_uz(cc):
                pt, pb = mm_ring.next()
                mm_group(pt[:, 0:TT], pb, [(Wuz[:, dk, cc * 128:(cc + 1) * 128], xT_[:, dk, :]) for dk in range(8)], r=[Wuz_b, xT_b_])
                if cc < 4:
                    ctx.op("act", lambda: act.copy(out=uT_[:, cc, :], in_=pt[:, 0:TT]), r=[pb], w=[uT_b_[cc]])
                else:
                    ctx.op("act", lambda: act.activation(out=szs_[:, cc - 4, :], in_=pt[:, 0:TT], func=AF.Silu), r=[pb], w=[szs_b_])

            for dk in range(8):
                tasks_.append(lambda dk=dk: t_tr(dk))
            for cc in range(8):
                tasks_.append(lambda cc=cc: t_uz(cc))
            for cc in range(8):
                tasks_.append(lambda cc=cc: t_qk(cc))
            for sub in range(4):
                tasks_.append(lambda sub=sub: t_v(sub))
            return tasks_, (uT_, uT_b_, szs_, szs_b_)

        def post_tasks(t_, gT_, gT_b_, szs_, szs_b_):
            tk0 = t_ * TT
            tasks_ = []

            def t_glu(oc):
                pa, pab = mm_ring.next()
                mm_group(pa[:, 0:TT], pab, [(Wglu[:, kc, oc * 128:(oc + 1) * 128], gT_[:, kc, :]) for kc in range(4)], r=[Wglu_b, gT_b_])
                pbk, pbb = mm_ring.next()
                mm_group(pbk[:, 0:TT], pbb, [(Wglu[:, kc, (oc + 4) * 128:(oc + 5) * 128], gT_[:, kc, :]) for kc in range(4)], r=[Wglu_b, gT_b_])
                sg, sgb = sigb_ring.next()
                ctx.op("act", lambda: act.activation(out=sg[:], in_=pbk[:, 0:TT], func=AF.Sigmoid), r=[pbb], w=[sgb])
                ctx.op("dve", lambda: dve.tensor_tensor(out=sg[:], in0=sg[:], in1=pa[:, 0:TT], op=ALU.mult), r=[sgb, pab], w=[sgb])
                ctx.op("dve", lambda: dve.tensor_tensor(out=yg[:, oc, :], in0=sg[:], in1=szs_[:, oc, :], op=ALU.mult), r=[sgb, szs_b_], w=[yg_b])

            def t_sp(oc):
                pt, pb = mm_ring.next()
                mm_group(pt[:, 0:TT], pb, [(Wsp[:, kc, oc * 128:(oc + 1) * 128], yg[:, kc, :]) for kc in range(4)], r=[Wsp_b, yg_b])
                stg, stb = ys_ring.next()
                ctx.op("act", lambda: act.copy(out=stg[:], in_=pt[:, 0:TT]), r=[pb], w=[stb])
                ctx.dma(YS_d[oc, :, tk0:tk0 + TT], stg[:], r=[stb])

            for oc in range(4):
                tasks_.append(lambda oc=oc: t_glu(oc))
            for oc in range(8):
                tasks_.append(lambda oc=oc: t_sp(oc))
            return tasks_

        post = []
        cur_tile = None
        if NTA:
            issue_xA(0)
            tasks, cur_tile = proj_tasks(0)
            for tk_ in tasks:
                tk_()
        for t in range(NTA):
            tok0 = t * TT
            uT, uT_b, szs, szs_b = cur_tile
            gT, gT_b = gT_ring.next()
            tasks = list(post)
            post_n = list(post)
            if t + 1 < NTA:
                ptk, cur_tile = proj_tasks(t + 1)
                tasks += ptk
            per_unit = (len(tasks) + 7) // 8
            def ssm_stage1a(ti_, s_, hf):
                tsl = slice(s_ * 128, (s_ + 1) * 128)
                (Sbr, Sbi), Sb_b = Sb_ring.next()
                for qi in range(2):
                    q = 2 * hf + qi
                    (Sre, Sre_b), (Sim, Sim_b) = psS_ring.next()
                    for i in range(4):
                        pr = 4 * q + i
                        ctx.op("pe", lambda: pe.matmul(Sre[:, i * 128:(i + 1) * 128], lhsT=Bre[:, pr, :], rhs=ti_[0][:, q, tsl], start=True, stop=True),
                               r=[BC_b, ti_[1][q]], w=[Sre_b], sig=(i == 3))
                    for i in range(4):
                        pr = 4 * q + i
                        ctx.op("pe", lambda: pe.matmul(Sim[:, i * 128:(i + 1) * 128], lhsT=Bim[:, pr, :], rhs=ti_[0][:, q, tsl], start=True, stop=True),
                               r=[BC_b, ti_[1][q]], w=[Sim_b], sig=(i == 3))
                    ctx.op("act", lambda: act.copy(out=Sbr[:, qi * 4:(qi + 1) * 4, :].rearrange("p a b -> p (a b)"), in_=Sre[:, 0:512]), r=[Sre_b], w=[Sb_b])
                    ctx.op("act", lambda: act.copy(out=Sbi[:, qi * 4:(qi + 1) * 4, :].rearrange("p a b -> p (a b)"), in_=Sim[:, 0:512]), r=[Sim_b], w=[Sb_b])
                return s_, hf, Sbr, Sbi, Sb_b

            def ssm_stage1b(s_, hf, Sbr, Sbi, Sb_b):
                p0 = 8 * hf
                Tcq = Tcb[:, p0:p0 + 8, :]
                Tsq = Tsb[:, p0:p0 + 8, :]
                cb = car_b[hf]
                ctx.op("dve", lambda: dve.tensor_tensor(out=bpr[:], in0=Sbr[:], in1=Tcq, op=ALU.mult), r=[Sb_b, T_b], w=[bp_b])
                ctx.op("dve", lambda: dve.tensor_tensor(out=tm2[:], in0=Sbi[:], in1=Tsq, op=ALU.mult), r=[Sb_b, T_b], w=[bp_b])
                ctx.op("dve", lambda: dve.tensor_tensor(out=bpr[:], in0=bpr[:], in1=tm2[:], op=ALU.add), r=[bp_b], w=[bp_b])
                ctx.op("dve", lambda: dve.tensor_tensor(out=bpi[:], in0=Sbi[:], in1=Tcq, op=ALU.mult), r=[Sb_b, T_b, bp_b], w=[bp_b])
                ctx.op("dve", lambda: dve.tensor_tensor(out=tm2[:], in0=Sbr[:], in1=Tsq, op=ALU.mult), r=[Sb_b, T_b, bp_b], w=[bp_b])
                ctx.op("dve", lambda: dve.tensor_tensor(out=bpi[:], in0=bpi[:], in1=tm2[:], op=ALU.subtract), r=[bp_b], w=[bp_b])
                ctx.op("dve", lambda: dve.tensor_tensor(out=rc_r[:], in0=mag_l[:, p0:p0 + 8], in1=car_r[:, p0:p0 + 8], op=ALU.mult), r=[mag_b, cb], w=[cb])
                ctx.op("dve", lambda: dve.tensor_tensor(out=rc_i[:], in0=mag_l[:, p0:p0 + 8], in1=car_i[:, p0:p0 + 8], op=ALU.mult), r=[mag_b, cb], w=[cb])
                ctx.op("dve", lambda: dve.tensor_tensor(out=bpr[:, :, 0], in0=bpr[:, :, 0], in1=rc_r[:], op=ALU.add), r=[bp_b, cb], w=[bp_b])
                ctx.op("dve", lambda: dve.tensor_tensor(out=bpi[:, :, 0], in0=bpi[:, :, 0], in1=rc_i[:], op=ALU.add), r=[bp_b, cb], w=[bp_b])
                (gr, gi), g_b = g_ring.next()
                rt = rtab[:, p0:p0 + 8, :].rearrange("p a b -> p (a b)")
                ctx.op("dve", lambda: dve.tensor_tensor_scan(out=gr[:].rearrange("p a b -> p (a b)"), data0=rt, data1=bpr[:].rearrange("p a b -> p (a b)"),
                                                             initial=0.0, op0=ALU.mult, op1=ALU.add), r=[bp_b, mag_b], w=[g_b])
                ctx.op("dve", lambda: dve.tensor_tensor_scan(out=gi[:].rearrange("p a b -> p (a b)"), data0=rt, data1=bpi[:].rearrange("p a b -> p (a b)"),
                                                             initial=0.0, op0=ALU.mult, op1=ALU.add), r=[bp_b, mag_b], w=[g_b])
                Ec = Ec_t[:, p0:p0 + 8]
                Es = Es_t[:, p0:p0 + 8]
                ctx.op("dve", lambda: dve.tensor_tensor(out=ctm[:], in0=Ec, in1=gr[:, :, 127], op=ALU.mult), r=[T_b, g_b, cb], w=[cb])
                ctx.op("dve", lambda: dve.tensor_tensor(out=ctm2[:], in0=Es, in1=gi[:, :, 127], op=ALU.mult), r=[T_b, g_b, cb], w=[cb])
                ctx.op("dve", lambda: dve.tensor_tensor(out=car_r[:, p0:p0 + 8], in0=ctm[:], in1=ctm2[:], op=ALU.subtract), r=[cb], w=[cb])
                ctx.op("dve", lambda: dve.tensor_tensor(out=ctm[:], in0=Ec, in1=gi[:, :, 127], op=ALU.mult), r=[T_b, g_b, cb], w=[cb])
                ctx.op("dve", lambda: dve.tensor_tensor(out=ctm2[:], in0=Es, in1=gr[:, :, 127], op=ALU.mult), r=[T_b, g_b, cb], w=[cb])
                ctx.op("dve", lambda: dve.tensor_tensor(out=car_i[:, p0:p0 + 8], in0=ctm[:], in1=ctm2[:], op=ALU.add), r=[cb], w=[cb])
                return s_, hf, gr, gi, g_b

            def ssm_stage2(ti_, s_, hf, gr, gi, g_b):
                tsl = slice(s_ * 128, (s_ + 1) * 128)
                p0 = 8 * hf
                Tcq = Tcb[:, p0:p0 + 8, :]
                Tsq = Tsb[:, p0:p0 + 8, :]
                (hr, mhi), (h_b, hi_b) = h_ring.next()
                ctx.op("dve", lambda: dve.tensor_tensor(out=tp3[:], in0=gr[:], in1=Tcq, op=ALU.mult), r=[g_b, T_b], w=[tpd_b])
                ctx.op("dve", lambda: dve.tensor_tensor(out=tp4[:], in0=gi[:], in1=Tsq, op=ALU.mult), r=[g_b, T_b], w=[tpd_b])
                ctx.op("dve", lambda: dve.tensor_tensor(out=hr[:], in0=tp3[:], in1=tp4[:], op=ALU.subtract), r=[tpd_b], w=[h_b])
                ctx.op("dve", lambda: dve.tensor_tensor(out=tp5[:], in0=gi[:], in1=Tcq, op=ALU.mult), r=[g_b, T_b], w=[tpd_b])
                ctx.op("dve", lambda: dve.tensor_tensor(out=tp6[:], in0=gr[:], in1=Tsq, op=ALU.mult), r=[g_b, T_b], w=[tpd_b])
                ctx.op("dve", lambda: dve.tensor_tensor(out=mhi[:], in0=tp5[:], in1=tp6[:], op=ALU.add), r=[tpd_b], w=[hi_b])
                for qi in range(2):
                    q = 2 * hf + qi
                    yreg = psY[:, qi * 128:(qi + 1) * 128]
                    ctx.op("pe", lambda: pe.matmul(yreg, lhsT=Dg[:, 0, q, :], rhs=ti_[0][:, q, tsl], start=True, stop=False), r=[BC_b, ti_[1][q]], w=[_psY_b], sig=False)
                    ctx.op("pe", lambda: pe.matmul(yreg, lhsT=Dg[:, 1, q, :], rhs=ti_[0][:, q, tsl], start=False, stop=False), r=[BC_b, ti_[1][q]], w=[_psY_b], sig=False)
                    for i in range(4):
                        pr = 4 * q + i
                        ctx.op("pe", lambda: pe.matmul(yreg, lhsT=Cre[:, pr, :], rhs=hr[:, qi * 4 + i, :], start=False, stop=False),
                               r=[BC_b, h_b], w=[_psY_b], sig=False)
                        ctx.op("pe", lambda: pe.matmul(yreg, lhsT=Cim[:, pr, :], rhs=mhi[:, qi * 4 + i, :], start=False, stop=(i == 3)),
                               r=[BC_b, hi_b], w=[_psY_b], sig=(i == 3 and qi == 1))
                ctx.op("act", lambda: act.activation(out=ti_[2][:, 2 * hf:2 * hf + 2, tsl], in_=psY[:, 0:256].rearrange("p (a b) -> p a b", a=2), func=AF.Gelu),
                       r=[_psY_b], w=[ti_[3]])

            units = [(s_, hf) for s_ in range(TT // 128) for hf in range(2)]
            nu = len(units)
            ti_cur = (uT, uT_b, gT, gT_b)
            ti_nxt = (cur_tile[0], cur_tile[1]) if t + 1 < NTA else None
            if t + 1 < NTA:
                assert len(post_n) + 16 <= 6 * per_unit
            if t == 0:
                a_res = {0: ssm_stage1a(ti_cur, *units[0]), 1: ssm_stage1a(ti_cur, *units[1])}
                b_res = {0: ssm_stage1b(*a_res.pop(0))}
            else:
                a_res, b_res = next_a, next_b
            next_a, next_b = {}, {}
            for k in range(nu):
                if k + 2 < nu:
                    a_res[k + 2] = ssm_stage1a(ti_cur, *units[k + 2])
                elif ti_nxt is not None:
                    next_a[k + 2 - nu] = ssm_stage1a(ti_nxt, *units[k + 2 - nu])
                if k + 1 < nu:
                    b_res[k + 1] = ssm_stage1b(*a_res.pop(k + 1))
                elif ti_nxt is not None:
                    next_b[0] = ssm_stage1b(*next_a.pop(0))
                ssm_stage2(ti_cur, *b_res.pop(k))
                for _ in range(per_unit):
                    if tasks:
                        tasks.pop(0)()
            while tasks:
                tasks.pop(0)()
            post = post_tasks(t, gT, gT_b, szs, szs_b)
        for tk_ in post:
            tk_()
        ctx.barrier()

    with ExitStack() as sB:
        S_ring = Ring([(ps(sB, "psSc%d" % i, [128, 512]), Buf("psSc%d" % i)) for i in range(3)])
        O_ring = Ring([(ps(sB, "psO%d" % i, [128, 512]), Buf("psO%d" % i)) for i in range(2)])
        psG = ps(sB, "psG", [128, 512]); psG_b = Buf("psG")
        psM = ps(sB, "psM", [128, 512]); psM_v = psM[:].bitcast(BF16); psM_b = Buf("psM")
        psBc = ps(sB, "psBc", [128, 512]); psBc_b = Buf("psBc")

        PMASK = sb(sB, "PMASK", [128, NCH, 32]); PAST = sb(sB, "PAST", [128, NCH, 32]); cm_b = Buf("cmask")
        ctx.dma(PMASK[:].rearrange("p a b -> p (a b)"), PMASK_d, w=[cm_b])
        ctx.dma(PAST[:].rearrange("p a b -> p (a b)"), PAST_d, w=[cm_b])
        b31 = sb(sB, "b31", [128, 8])
        ctx.dma(b31[:], b31_d, w=[cm_b])
        ones_f = sb(sB, "ones_f", [128, 64], BF16)
        ctx.op("dve", lambda: dve.memset(ones_f[:], 1.0), w=[cm_b])

        Htiles = sb(sB, "Htiles", [128, 8, 3, 128], BF16); H_b = [[Buf("H%d_%d" % (h_, j_)) for j_ in range(3)] for h_ in range(8)]
        KA = [sb(sB, "KA%d" % i, [96, S], BF16) for i in range(2)]
        QA = [sb(sB, "QA%d" % i, [96, S], BF16) for i in range(2)]
        VE = [sb(sB, "VE%d" % i, [128, NKT, 65], BF16) for i in range(2)]
        KA_b = [Buf("KA%d" % i) for i in range(2)]
        QA_b = [Buf("QA%d" % i) for i in range(2)]
        QM_b = [[Buf("QM%d_%d" % (i, t)) for t in range(NT)] for i in range(2)]
        VE_b = [Buf("VE%d" % i) for i in range(2)]

        def load_head(h):
            i = h % 2
            pr, hh = h // 2, h % 2
            ctx.dma(KA[i][0:64, :], KT_d[pr, hh * 64:(hh + 1) * 64, :], w=[KA_b[i]])
            ctx.dma(QA[i][0:64, :], QT_d[pr, hh * 64:(hh + 1) * 64, :], w=[QA_b[i]])
            ctx.dma(VE[i][:, :, 0:64], V_d[h], w=[VE_b[i]])

        NHB = 0 if stop in ('setup', 'A') else (NH if stop != 'B1' else 1)
        ctx.dma(KA[0][64:96, :], EOH_d, w=[KA_b[0]])
        ctx.op("pool", lambda: pool.memset(VE[0][:, :, 64:65], 1.0), w=[VE_b[0]])
        if NHB:
            load_head(0)
        ctx.dma(KA[1][64:96, :], EOH_d, w=[KA_b[1]])
        ctx.op("pool", lambda: pool.memset(VE[1][:, :, 64:65], 1.0), w=[VE_b[1]])
        if True:
            sb0 = sB
            relb_f = sb(sb0, "relb_f", [32, 8]); relb_h = sb(sb0, "relb_h", [32, 8], BF16)
            OHt = sb(sb0, "OHt", [32, 384], BF16); negm = sb(sb0, "negm", [8, 384])
            Ff = sb(sb0, "Ff", [8, 384]); Fh = sb(sb0, "Fh", [8, 2, 384], BF16)
            Pb = Buf("biasprep")
            Pb1, Pb2, Pb3 = Buf("bp1"), Buf("bp2"), Buf("bp3")
            ctx.dma(relb_f[:], relb_d, w=[Pb1]); ctx.dma(OHt[:], OH_d, w=[Pb2]); ctx.dma(negm[:], NEGM_d, w=[Pb3])
            ctx.op("dve", lambda: dve.tensor_copy(out=relb_h[:], in_=relb_f[:]), r=[Pb1, Pb2, Pb3], w=[Pb])
            ctx.op("pe", lambda: pe.matmul(psG[0:8, 0:384], lhsT=relb_h[:, :], rhs=OHt[:, :], start=True, stop=True), r=[Pb], w=[psG_b])
            ctx.op("dve", lambda: dve.tensor_tensor(out=Ff[:], in0=psG[0:8, 0:384], in1=negm[:], op=ALU.add), r=[psG_b, Pb], w=[Pb])
            ctx.op("dve", lambda: dve.tensor_copy(out=Fh[:, 0, :], in_=Ff[:]), r=[Pb], w=[Pb])
            ctx.op("dve", lambda: dve.tensor_scalar(out=Fh[:, 1, :], in0=Ff[:], scalar1=Ff[:, 380:381], scalar2=None, op0=ALU.subtract), r=[Pb], w=[Pb])
            ctx.dma(F_d[0:8, :], Fh[:, 0, :], r=[Pb], w=[Pb])
            ctx.dma(F_d[8:16, :], Fh[:, 1, :], r=[Pb], w=[Pb])
            for h in range(NH):
                for j, (row, off) in enumerate(((h, 0), (h, 128), (8 + h, 128))):
                    src = bass.AP(tensor=F_d.tensor, offset=row * 384 + off, ap=[[1, 128], [1, 128]])
                    ctx.dma(Htiles[:, h, j, :], src, r=[Pb], w=[H_b[h][j]])
        ksum = sb(sB, "ksum", [64, 32]); kmT = [sb(sB, "kmT%d" % i, [64, 32], BF16) for i in range(2)]
        km_b = [Buf("km%d" % i) for i in range(2)]
        gm = sb(sB, "gm", [128, 4, 32]); ns_ = sb(sB, "ns", [128, 4, 32]); thr8 = sb(sB, "thr8", [128, 4, 8]); gm_b = Buf("gm")
        Mq_ring = Ring([(sb(sB, "Mq%d" % i, [128, 4, 96], BF16), Buf("Mq%d" % i)) for i in range(2)])
        for mq, mqb in Mq_ring.items:
            ctx.op("pool", lambda: pool.memset(mq[:], 0.0), w=[mqb])
        PT_ring = Ring([(sb(sB, "PT%d" % i, [128, 512], BF16), Buf("PT%d" % i)) for i in range(4)])
        osb_ring = Ring([(sb(sB, "osb%d" % i, [65, 512]), Buf("osb%d" % i)) for i in range(2)])
        rec_ring = Ring([(sb(sB, "rec%d" % i, [65, 512], BF16), Buf("rec%d" % i)) for i in range(2)])
        oo_ring = Ring([(sb(sB, "oo%d" % i, [64, 512], BF16), Buf("oo%d" % i)) for i in range(2)])

        def prep_head(h):
            i = h % 2
            Kh = KA[i]
            ctx.op("dve", lambda: dve.tensor_reduce(out=ksum[:, 0:NB], in_=Kh[0:64, :].rearrange("p (a b) -> p a b", b=BLK), axis=AX.X, op=ALU.add),
                   r=[KA_b[i]], w=[km_b[i]])
            if NB < 32:
                ctx.op("dve", lambda: dve.memset(ksum[:, NB:32], 0.0), r=[], w=[km_b[i]])
            ctx.op("dve", lambda: dve.tensor_scalar(out=kmT[i][:], in0=ksum[:], scalar1=1.0 / BLK, scalar2=None, op0=ALU.mult), r=[km_b[i]], w=[km_b[i]])

        def gate1(h, T):
            i = h % 2
            Qh = QA[i]
            c0 = 4 * T
            q0 = T * TT
            for ci in range(4):
                ctx.op("pe", lambda: pe.matmul(psG[:, ci * 32:(ci + 1) * 32], lhsT=Qh[0:64, q0 + ci * 128:q0 + (ci + 1) * 128], rhs=kmT[i][:, :], start=True, stop=True),
                       r=[QA_b[i], km_b[i]], w=[psG_b], sig=(ci == 3))
            ctx.op("dve", lambda: dve.tensor_tensor(out=gm[:], in0=psG[:, 0:128].rearrange("p (a b) -> p a b", a=4), in1=PMASK[:, c0:c0 + 4, :], op=ALU.add),
                   r=[psG_b, cm_b], w=[gm_b])
            for ci in range(4):
                ctx.op("dve", lambda: dve.max(out=thr8[:, ci, :], in_=gm[:, ci, :]), r=[gm_b], w=[gm_b])
            ctx.op("dve", lambda: dve.tensor_tensor(out=ns_[:], in0=gm[:], in1=thr8[:, :, 2:3].to_broadcast([128, 4, 32]), op=ALU.is_lt), r=[gm_b], w=[gm_b])
            ctx.op("dve", lambda: dve.tensor_scalar(out=ns_[:], in0=ns_[:], scalar1=NEG, scalar2=b31[:, h:h + 1], op0=ALU.mult, op1=ALU.add), r=[gm_b, cm_b], w=[gm_b])
            Mq, Mq_b = Mq_ring.next()
            ctx.op("dve", lambda: dve.tensor_tensor(out=Mq[:, :, 64:96], in0=ns_[:], in1=PAST[:, c0:c0 + 4, :], op=ALU.mult), r=[gm_b, cm_b], w=[Mq_b])
            return Mq, Mq_b

        def gate2(h, T, Mq, Mq_b):
            i = h % 2
            q0 = T * TT
            for ci in range(4):
                ctx.op("pe", lambda: pe.transpose(psM_v[0:96, ci * 128:(ci + 1) * 128], Mq[:, ci, :], ident[:]), r=[Mq_b, ident_b], w=[psM_b], sig=(ci == 3))
            ctx.op("dve", lambda: dve.tensor_copy(out=QA[i][64:96, q0:q0 + TT], in_=psM_v[64:96, 0:512]), r=[psM_b], w=[QM_b[i][T]])

        def emit_norm(h_, T_, Ops_, Ops_b_):
            osb, osb_b = osb_ring.next()
            ctx.op("act", lambda: act.copy(out=osb[:], in_=Ops_[0:65, :]), r=[Ops_b_], w=[osb_b])
            rec, rec_b = rec_ring.next()
            ctx.op("act", lambda: act.activation(out=osb[64:65, :], in_=osb[64:65, :], func=AF.Ln), r=[osb_b], w=[osb_b])
            ctx.op("act", lambda: act.activation(out=rec[64:65, :], in_=osb[64:65, :], func=AF.Exp, scale=-1.0), r=[osb_b], w=[rec_b])
            ctx.op("pe", lambda: pe.matmul(psBc[0:64, :], lhsT=ones_f[64:65, 0:64], rhs=rec[64:65, :], start=True, stop=True), r=[rec_b, cm_b], w=[psBc_b])
            oo, oo_b = oo_ring.next()
            ctx.op("dve", lambda: dve.tensor_tensor(out=oo[:], in0=osb[0:64, :], in1=psBc[0:64, :], op=ALU.mult), r=[osb_b, psBc_b], w=[oo_b])
            ctx.dma(OA_d[h_, :, T_ * TT:(T_ + 1) * TT], oo[:], r=[oo_b])

        pending_norm = None
        work = [(h, T) for h in range(NHB) for T in range(NT)]
        LA = 2
        if work:
            prep_head(0)
            g = gate1(0, 0)
            gate2(0, 0, *g)
        for wi, (h, T) in enumerate(work):
            i = h % 2
            if T == 0 and h + 1 < NHB:
                load_head(h + 1)
            Kh, Qh, Vh = KA[i], QA[i], VE[i]
            q0 = T * TT
            nxt_w = work[wi + 1] if wi + 1 < len(work) else None
            g = None
            if nxt_w is not None:
                if nxt_w[1] == 0:
                    prep_head(nxt_w[0])
                g = gate1(*nxt_w)
            Ops, Ops_b = O_ring.next()
            nkt = 4 * T + 4

            def emit_S(kt):
                cl = max(0, kt - 4 * T)
                cs = slice(cl * 128, 512)
                Sps, Sps_b = S_ring.next()
                adds = []
                ci0 = kt - 4 * T
                if 0 <= ci0 <= 3:
                    adds.append((ci0, 0))
                ci1 = kt + 1 - 4 * T
                if 0 <= ci1 <= 3:
                    adds.append((ci1, 1 if kt % 2 == 0 else 2))
                ctx.op("pe", lambda: pe.matmul(Sps[:, cs], lhsT=Kh[0:96, kt * 128:(kt + 1) * 128], rhs=Qh[0:96, q0 + cl * 128:q0 + 512], start=True, stop=(len(adds) == 0)),
                       r=[KA_b[i], QA_b[i], QM_b[i][T]], w=[Sps_b], sig=(len(adds) == 0))
                for ai, (cidx, j) in enumerate(adds):
                    last = ai == len(adds) - 1
                    ctx.op("pe", lambda: pe.matmul(Sps[:, cidx * 128:(cidx + 1) * 128], lhsT=Jm[:, :], rhs=Htiles[:, h, j, :], start=False, stop=last),
                           r=[Jm_b, H_b[h][j]], w=[Sps_b], sig=last)
                return kt, cs, Sps, Sps_b

            def emit_PV(kt, cs, Sps, Sps_b):
                PT, PT_b = PT_ring.next()
                ctx.op("act", lambda: act.activation(out=PT[:, cs], in_=Sps[:, cs], func=AF.Exp), r=[Sps_b], w=[PT_b])
                ctx.op("pe", lambda: pe.matmul(Ops[0:65, cs], lhsT=Vh[:, kt, 0:65], rhs=PT[:, cs], start=(kt == 0), stop=(kt == nkt - 1)),
                       r=[VE_b[i], PT_b], w=[Ops_b], sig=(kt == nkt - 1))

            pend = []
            for kt in range(nkt):
                pend.append(emit_S(kt))
                if kt == LA and pending_norm is not None:
                    emit_norm(*pending_norm)
                    pending_norm = None
                if len(pend) > LA:
                    emit_PV(*pend.pop(0))
            while pend:
                emit_PV(*pend.pop(0))
            if g is not None:
                gate2(nxt_w[0], nxt_w[1], *g)
            pending_norm = (h, T, Ops, Ops_b)
        if pending_norm is not None:
            emit_norm(*pending_norm)
        ctx.barrier()

    with ExitStack() as sC:
        psT_v = [ps(sC, "psTc%d" % i, [128, 512])[:].bitcast(BF16) for i in range(2)]
        psT_b = [Buf("psTc0"), Buf("psTc1")]
        mm_ring = Ring([(ps(sC, "mc%d" % i, [128, 512]), Buf("mc%d" % i)) for i in range(6)])
        tk_ring = mm_ring

        Wz = sb(sC, "Wz", [128, 8, 2560], BF16); Wz_b = Buf("Wz")
        Wap = sb(sC, "Wap", [128, 4, 1024], BF16); Wap_b = Buf("Wap")
        Wout = sb(sC, "Wout", [128, 8, 1024], BF16); Wout_b = Buf("Wout")
        Wpg = sb(sC, "Wpg", [128, 8, 1024], BF16); Wpg_b = Buf("Wpg")
        Wpp = sb(sC, "Wpp", [128, 2, 1024], BF16); Wpp_b = Buf("Wpp")
        lng = sb(sC, "lng", [128, D]); lnb = sb(sC, "lnb", [128, D]); ln_b = Buf("ln")
        ctx.dma(lng[:], lng_d, w=[ln_b]); ctx.dma(lnb[:], lnb_d, w=[ln_b])
        NS = TC // 128
        x_ring = Ring([(sb(sC, "xc%d" % i, [128, D]), Buf("xc%d" % i)) for i in range(2 * NS)])
        xb = sb(sC, "xbC", [128, NS, D], BF16); xb_b = [Buf("xbC%d" % i) for i in range(NS)]
        xTc_ring = Ring([(sb(sC, "xTC%d" % i, [128, 8, TC], BF16), Buf("xTC%d" % i)) for i in range(2)])
        p_ring = Ring([(sb(sC, "pc%d" % i, [128, 256]), Buf("pc%d" % i)) for i in range(NS)])
        pbf = sb(sC, "pbf", [128, NS, 256], BF16); pbf_b = [Buf("pbf%d" % i) for i in range(NS)]
        pTc_ring = Ring([(sb(sC, "pT%d" % i, [128, 2, TC], BF16), Buf("pT%d" % i)) for i in range(2)])
        sza = sb(sC, "sza", [128, 4, TC], BF16); sza_b = Buf("sza")
        sga = sb(sC, "sga", [128, 8, TC], BF16); sga_b = Buf("sga")
        sgs = sb(sC, "sgs", [128, 8, TC], BF16); sgs_b = Buf("sgs")
        oa_ring = Ring([(sb(sC, "oat%d" % i, [128, 4, TC], BF16), Buf("oat%d" % i)) for i in range(2)])
        oz = sb(sC, "oz", [128, 4, TC], BF16); oz_b = Buf("oz")
        ys_ring = Ring([(sb(sC, "yst%d" % i, [128, 8, TC], BF16), Buf("yst%d" % i)) for i in range(2)])
        mg_ring = Ring([(sb(sC, "merge%d" % i, [128, 8, TC], BF16), Buf("merge%d" % i)) for i in range(2)])
        ta_ring = Ring([(sb(sC, "tma%d" % i, [128, TC]), Buf("tma%d" % i)) for i in range(2)])
        tb_ring = Ring([(sb(sC, "tmb%d" % i, [128, TC]), Buf("tmb%d" % i)) for i in range(2)])
        s_ring = Ring([(sb(sC, "srow%d" % i, [128, D]), Buf("srow%d" % i)) for i in range(2)])
        o_ring = Ring([(sb(sC, "orow%d" % i, [128, D]), Buf("orow%d" % i)) for i in range(2)])
        st_ring = Ring([((sb(sC, "bst%d" % i, [128, 2, 6]), sb(sC, "bmv%d" % i, [128, 2]), sb(sC, "brs%d" % i, [128, 2])), Buf("bst%d" % i)) for i in range(2)])

        def issue_xC(t_):
            tk0 = t_ * TC
            xts_ = []
            for sub in range(NS):
                xt, xtb = x_ring.next()
                xts_.append((xt, xtb))
                ctx.dma(xt[:], x_d[tk0 + sub * 128:tk0 + (sub + 1) * 128, :], w=[xtb])
                ctx.op("pool", lambda: pool.tensor_copy(out=xb[:, sub, :], in_=xt[:]), r=[xtb], w=[xb_b[sub]])
                pt_, ptb = p_ring.next()
                ctx.dma(pt_[:], p_d[tk0 + sub * 128:tk0 + (sub + 1) * 128, :], w=[ptb])
                ctx.op("pool", lambda: pool.tensor_copy(out=pbf[:, sub, :], in_=pt_[:]), r=[ptb], w=[pbf_b[sub]])
            oat_, oat_b_ = oa_ring.next()
            ctx.dma(oat_[:], OA_d[:, :, tk0:tk0 + TC].rearrange("(pr hh) d t -> (hh d) pr t", hh=2), w=[oat_b_])
            yst_, yst_b_ = ys_ring.next()
            ctx.dma(yst_[:], YS_d[:, :, tk0:tk0 + TC].rearrange("c p t -> p c t"), w=[yst_b_])
            return xts_, (oat_, oat_b_), (yst_, yst_b_)

        NTCC = 0 if stop in ('setup', 'A', 'B', 'B1') else NTC
        def emit_front(t_, nxt_):
            xT_, xT_b_ = xTc_ring.next()
            pT_, pT_b_ = pTc_ring.next()
            for dk in range(8):
                hf = dk % 2
                tp = psT_v[hf][:, 0:TC]
                for sub in range(NS):
                    ctx.op("pe", lambda: pe.transpose(tp[:, sub * 128:(sub + 1) * 128], xb[:, sub, dk * 128:(dk + 1) * 128], ident[:]),
                           r=[xb_b[sub], ident_b], w=[psT_b[hf]], sig=(sub == NS - 1))
                ctx.op("act", lambda: act.copy(out=xT_[:, dk, :], in_=tp), r=[psT_b[hf]], w=[xT_b_])
            for kc in range(2):
                hf = kc % 2
                tp = psT_v[hf][:, 0:TC]
                for sub in range(NS):
                    ctx.op("pe", lambda: pe.transpose(tp[:, sub * 128:(sub + 1) * 128], pbf[:, sub, kc * 128:(kc + 1) * 128], ident[:]),
                           r=[pbf_b[sub], ident_b], w=[psT_b[hf]], sig=(sub == NS - 1))
                ctx.op("act", lambda: act.copy(out=pT_[:, kc, :], in_=tp), r=[psT_b[hf]], w=[pT_b_])
            res = nxt_ + ((xT_, xT_b_, pT_, pT_b_),)
            nn = issue_xC(t_ + 1) if t_ + 1 < NTCC else None
            return res, nn

        if NTCC:
            nxt = issue_xC(0)
        stage_ring = Ring([(sb(sC, "wstc%d" % i, [128, 2048]), Buf("wstc%d" % i)) for i in range(2)])
        load_weight(Wz, Wz_b, w_in_d, 1536, 512, 8, stage_ring, dcol0=0)
        load_weight(Wz, Wz_b, w_in_d, 3072, 2048, 8, stage_ring, dcol0=512)
        load_weight(Wap, Wap_b, w_ap_d, 0, 1024, 4, stage_ring)
        load_weight(Wpg, Wpg_b, w_pg_d, 0, 1024, 8, stage_ring)
        load_weight(Wpp, Wpp_b, w_pp_d, 0, 1024, 2, stage_ring)
        load_weight(Wout, Wout_b, w_out_d, 0, 1024, 8, stage_ring)

        if NTCC:
            cur_front, nxt = emit_front(0, nxt)
        for t in range(NTCC):
            tok0 = t * TC
            xts, (oat, oat_b), (yst, yst_b), (xT, xT_b, pT, pT_b) = cur_front
            merge, merge_b = mg_ring.next()
            front_done = False
            for cc in range(20):
                pt, pb = mm_ring.next()
                mm_group(pt[:, 0:TC], pb, [(Wz[:, dk, cc * 128:(cc + 1) * 128], xT[:, dk, :]) for dk in range(8)], r=[Wz_b, xT_b])
                if cc < 4:
                    ctx.op("act", lambda: act.activation(out=sza[:, cc, :], in_=pt[:, 0:TC], func=AF.Silu), r=[pb], w=[sza_b])
                elif cc < 12:
                    ctx.op("act", lambda: act.activation(out=sga[:, cc - 4, :], in_=pt[:, 0:TC], func=AF.Sigmoid), r=[pb], w=[sga_b])
                else:
                    ctx.op("act", lambda: act.activation(out=sgs[:, cc - 12, :], in_=pt[:, 0:TC], func=AF.Sigmoid), r=[pb], w=[sgs_b])
            ctx.op("pool", lambda: pool.tensor_tensor(out=oz[:], in0=oat[:], in1=sza[:], op=ALU.mult), r=[oat_b, sza_b], w=[oz_b])
            for oc in range(8):
                pt, pb = mm_ring.next()
                mm_group(pt[:, 0:TC], pb, [(Wap[:, kc, oc * 128:(oc + 1) * 128], oz[:, kc, :]) for kc in range(4)], r=[Wap_b, oz_b])
                ta, ta_b = ta_ring.next()
                tb, tb_b = tb_ring.next()
                ctx.op("dve", lambda: dve.tensor_tensor(out=ta[:], in0=pt[:, 0:TC], in1=sga[:, oc, :], op=ALU.mult), r=[pb, sga_b], w=[ta_b])
                ctx.op("pool", lambda: pool.tensor_tensor(out=tb[:], in0=yst[:, oc, :], in1=sgs[:, oc, :], op=ALU.mult), r=[yst_b, sgs_b], w=[tb_b])
                ctx.op("dve", lambda: dve.tensor_tensor(out=merge[:, oc, :], in0=ta[:], in1=tb[:], op=ALU.add), r=[ta_b, tb_b], w=[merge_b])
            for sub in range(NS):
                xt, xtb = xts[sub]
                srow, srow_b = s_ring.next()
                tsl = slice(sub * 128, (sub + 1) * 128)
                (bst, bmv, brs), bst_b = st_ring.next()
                for hc in range(2):
                    csl = slice(hc * 512, (hc + 1) * 512)
                    pg, pg_b = tk_ring.next()
                    mm_group(pg[:, :], pg_b, [(xT[:, dk, tsl], Wpg[:, dk, csl]) for dk in range(8)], r=[Wpg_b, xT_b])
                    pp, pp_b = tk_ring.next()
                    mm_group(pp[:, :], pp_b, [(pT[:, kc, tsl], Wpp[:, kc, csl]) for kc in range(2)], r=[Wpp_b, pT_b])
                    mx, mx_b = tk_ring.next()
                    mm_group(mx[:, :], mx_b, [(merge[:, kc, tsl], Wout[:, kc, csl]) for kc in range(8)], r=[Wout_b, merge_b])
                    ctx.op("act", lambda: act.activation(out=srow[:, csl], in_=pg[:, :], func=AF.Sigmoid), r=[pg_b], w=[srow_b])
                    ctx.op("dve", lambda: dve.tensor_tensor(out=srow[:, csl], in0=srow[:, csl], in1=pp[:, :], op=ALU.mult), r=[srow_b, pp_b], w=[srow_b])
                    ctx.op("dve", lambda: dve.tensor_tensor(out=srow[:, csl], in0=srow[:, csl], in1=mx[:, :], op=ALU.add), r=[srow_b, mx_b], w=[srow_b])
                    ctx.op("dve", lambda: dve.scalar_tensor_tensor(out=srow[:, csl], in0=xt[:, csl], scalar=ALPHA, in1=srow[:, csl], op0=ALU.mult, op1=ALU.add),
                           r=[srow_b, xtb], w=[srow_b])
                    ctx.op("dve", lambda: dve.bn_stats(out=bst[:, hc, :], in_=srow[:, csl]), r=[srow_b], w=[bst_b])
                if sub == NS - 1 and t + 1 < NTCC:
                    cur_front, nxt = emit_front(t + 1, nxt)
                ctx.op("dve", lambda: dve.bn_aggr(out=bmv[:], in_=bst[:].rearrange("p a b -> p (a b)")), r=[bst_b], w=[bst_b])
                ctx.op("dve", lambda: dve.tensor_scalar(out=brs[:, 0:1], in0=bmv[:, 1:2], scalar1=LN_EPS, scalar2=None, op0=ALU.add), r=[bst_b], w=[bst_b])
                ctx.op("act", lambda: act.activation(out=brs[:, 0:1], in_=brs[:, 0:1], func=AF.Sqrt), r=[bst_b], w=[bst_b])
                ctx.op("dve", lambda: dve.reciprocal(out=brs[:, 0:1], in_=brs[:, 0:1]), r=[bst_b], w=[bst_b])
                ctx.op("dve", lambda: dve.scalar_tensor_tensor(out=brs[:, 1:2], in0=bmv[:, 0:1], scalar=-1.0, in1=brs[:, 0:1], op0=ALU.mult, op1=ALU.mult),
                       r=[bst_b], w=[bst_b])
                orow, orow_b = o_ring.next()
                ctx.op("dve", lambda: dve.tensor_scalar(out=orow[:], in0=srow[:], scalar1=brs[:, 0:1], scalar2=brs[:, 1:2], op0=ALU.mult, op1=ALU.add), r=[srow_b, bst_b], w=[orow_b])
                ctx.op("dve", lambda: dve.tensor_tensor(out=orow[:], in0=orow[:], in1=lng[:], op=ALU.mult), r=[orow_b, ln_b], w=[orow_b])
                ctx.op("pool", lambda: pool.tensor_tensor(out=orow[:], in0=orow[:], in1=lnb[:], op=ALU.add), r=[orow_b, ln_b], w=[orow_b])
                ctx.dma(out_d[tok0 + sub * 128:tok0 + (sub + 1) * 128, :], orow[:], r=[orow_b])
        ctx.barrier(engines=["sp"])
    es.close()
    return nc


def host_consts(S):
    bf = ml_dtypes.bfloat16
    NCH = S // 128
    c = {}
    c["ident"] = np.eye(128, dtype=np.float32).astype(bf)
    c["Jm"] = np.ascontiguousarray(np.eye(128, dtype=np.float32)[::-1]).astype(bf)
    i = np.arange(384)
    dist = i - 127
    bk = t5_bucket_np(np.maximum(dist, 0))
    OH = np.zeros((32, 384), np.float32)
    valid = dist >= 0
    OH[bk[valid], i[valid]] = 1.0
    c["OH"] = OH.astype(bf)
    NEGM = np.zeros((8, 384), np.float32)
    NEGM[:, ~valid] = NEG
    c["NEGM"] = NEGM
    keys = np.arange(S)
    EOH = (keys[None, :] // BLK == np.arange(32)[:, None]).astype(np.float32)
    c["EOH"] = EOH.astype(bf)
    ch = np.arange(NCH)
    blk = ch // 2
    n = np.arange(32)
    past = (n[None, :] < blk[:, None])
    PM = np.where(past, 0.0, -1e30).astype(np.float32)
    c["PMASK"] = np.ascontiguousarray(np.broadcast_to(PM.reshape(1, -1), (128, NCH * 32)))
    c["PAST01"] = np.ascontiguousarray(np.broadcast_to(past.astype(np.float32).reshape(1, -1), (128, NCH * 32)))
    c["jidx"] = np.ascontiguousarray(np.broadcast_to(np.arange(129, dtype=np.float32)[None, :], (128, 129)))
    return c


def host_params(inp):
    o = {}
    a_re = inp["ssm_a_re"][0]; a_im = inp["ssm_a_im"][0]; ldt = inp["ssm_log_dt"][0]
    b_re = inp["ssm_b_re"][0]; b_im = inp["ssm_b_im"][0]; c_re = inp["ssm_c_re"][0]; c_im = inp["ssm_c_im"][0]
    pidx = np.arange(128); g2 = pidx // 64; n = pidx % 64
    pr = np.arange(16)
    G = 2 * pr[None, :] + g2[:, None]
    o["are_l"] = np.ascontiguousarray(a_re[G, n[:, None]])
    o["aim_l"] = np.ascontiguousarray(a_im[G, n[:, None]])
    o["ldt_l"] = np.ascontiguousarray(ldt[G])
    Gc = 2 * pr[:, None] + g2[None, :]
    are_pc = a_re[Gc, n[None, :]]
    aim_pc = a_im[Gc, n[None, :]]
    ldt_pc = ldt[Gc]
    for name, v in (("are_rep", are_pc), ("aim_rep", aim_pc), ("ldt_rep", ldt_pc)):
        o[name] = np.ascontiguousarray(np.broadcast_to(v.reshape(1, 2048), (128, 2048))).astype(np.float32)
    brT = np.zeros((128, 16, 128), np.float32); biT = np.zeros((128, 16, 128), np.float32)
    cre = np.zeros((128, 16, 128), np.float32); cim = np.zeros((128, 16, 128), np.float32)
    for p_ in range(16):
        r0 = (p_ % 4) * 32
        for gg in range(2):
            g = 2 * p_ + gg
            brT[r0 + gg * 16:r0 + gg * 16 + 16, p_, gg * 64:(gg + 1) * 64] = b_re[g].T
            biT[r0 + gg * 16:r0 + gg * 16 + 16, p_, gg * 64:(gg + 1) * 64] = b_im[g].T
            cre[gg * 64:(gg + 1) * 64, p_, r0 + gg * 16:r0 + gg * 16 + 16] = c_re[g].T
            cim[gg * 64:(gg + 1) * 64, p_, r0 + gg * 16:r0 + gg * 16 + 16] = c_im[g].T
    o["brT"] = brT.reshape(128, 2048); o["biT"] = biT.reshape(128, 2048)
    o["cre_pad"] = cre.reshape(128, 2048); o["cim_pad"] = cim.reshape(128, 2048)
    o["d_l"] = np.ascontiguousarray(inp["ssm_d"][0].reshape(4, 128).T)
    o["lng"] = np.ascontiguousarray(np.broadcast_to(inp["ln_g"][0][None, :], (128, D)))
    o["lnb"] = np.ascontiguousarray(np.broadcast_to(inp["ln_b"][0][None, :], (128, D)))
    o["relb"] = np.ascontiguousarray(inp["rel_bias"])
    o["b31rep"] = np.ascontiguousarray(np.broadcast_to(inp["rel_bias"][31][None, :], (128, 8)))
    o["w_in"] = np.ascontiguousarray(inp["w_in"][0]); o["w_ap"] = np.ascontiguousarray(inp["w_attn_proj"][0])
    o["w_sp"] = np.ascontiguousarray(inp["w_ssm_proj"][0]); o["w_out"] = np.ascontiguousarray(inp["w_out"][0])
    o["w_glu"] = np.ascontiguousarray(inp["w_glu"][0]); o["w_pg"] = np.ascontiguousarray(inp["w_ple_gate"][0])
    o["w_pp"] = np.ascontiguousarray(inp["w_ple_proj"][0])
    return {k: np.asarray(v, np.float32) for k, v in o.items()}


_NC_CACHE = {}


def run(inputs, S, n_cores, **bk):
    inp = {k: np.asarray(v) for k, v in inputs.items()}
    if S not in _NC_CACHE:
        _NC_CACHE[S] = build_nc(S, **bk)
    nc = _NC_CACHE[S]
    shared = host_params(inp)
    shared.update(host_consts(S))
    in_maps = []
    for b in range(n_cores):
        m = dict(shared)
        m["x"] = np.ascontiguousarray(inp["x"][b], dtype=np.float32)
        m["p"] = np.ascontiguousarray(inp["p"][0, b], dtype=np.float32)
        in_maps.append(m)
    res = run_bass_kernel_spmd(nc, in_maps, core_ids=list(range(n_cores)))
    return np.stack([np.asarray(r["out"]) for r in res.results], axis=0).astype(np.float32)


def kernel(**inputs):
    return run(inputs, 8192, 8)
```

```python
import math
from contextlib import ExitStack

import numpy as np
import ml_dtypes

import concourse.bass as bass
import concourse.mybir as mybir
from concourse.bass_utils import run_bass_kernel_spmd

F32 = mybir.dt.float32
BF16 = mybir.dt.bfloat16
I32 = mybir.dt.int32
AF = mybir.ActivationFunctionType
ALU = mybir.AluOpType
AX = mybir.AxisListType

D = 1024
NH = 8
HD = 64
BLK = 256
NEG = -30000.0
TWO_PI = 2.0 * math.pi
ALPHA = 2.0 ** 0.25
LN_EPS = 1e-5


class Buf:
    __slots__ = ("w", "r", "name")

    def __init__(self, name=""):
        self.w = None
        self.r = {}
        self.name = name


class Ring:
    def __init__(self, items):
        self.items = items
        self.i = 0

    def next(self):
        it = self.items[self.i % len(self.items)]
        self.i += 1
        return it


class Ctx:
    def __init__(self, nc, es, n_dma_sems=24):
        self.nc = nc
        self.engs = {"pe": nc.tensor, "act": nc.scalar, "dve": nc.vector, "pool": nc.gpsimd, "sp": nc.sync}
        self.sem = {}
        self.cnt = {}
        self.seen = {e: {} for e in self.engs}
        self.pend = {e: ([], []) for e in self.engs}
        for e in self.engs:
            self.sem[e] = es.enter_context(nc.semaphore("s_" + e))
            self.cnt[e] = 0
        self.dq = []
        for i in range(n_dma_sems):
            k = "d%d" % i
            self.sem[k] = es.enter_context(nc.semaphore("s_" + k))
            self.cnt[k] = 0
            self.dq.append(k)
        self.dqi = 0

    def _wait(self, e, tok):
        if tok is None:
            return
        k, v = tok
        if v <= 0 or self.seen[e].get(k, 0) >= v:
            return
        self.engs[e].wait_ge(self.sem[k], v)
        self.seen[e][k] = v

    def _deps(self, e, r, w):
        for b in r:
            if b.w is not None and not (e == "pe" and b.w[0] == "pe"):
                self._wait(e, b.w)
        for b in w:
            if b.w is not None and b.w[0] != e:
                self._wait(e, b.w)
            for k, v in b.r.items():
                if k != e:
                    self._wait(e, (k, v))

    def _reg(self, tok, r, w):
        k, v = tok
        for b in r:
            if b.r.get(k, 0) < v:
                b.r[k] = v
        for b in w:
            b.w = tok
            b.r = {}

    def op(self, e, fn, r=(), w=(), sig=True):
        self._deps(e, r, w)
        inst = fn()
        pr, pw = self.pend[e]
        if not sig:
            pr.extend(r)
            pw.extend(w)
            return None
        self.cnt[e] += 1
        inst.then_inc(self.sem[e], 1)
        tok = (e, self.cnt[e])
        self._reg(tok, list(r) + pr, list(w) + pw)
        self.pend[e] = ([], [])
        return tok

    def dma(self, out, in_, r=(), w=(), q="sp"):
        k = self.dq[self.dqi % len(self.dq)]
        self.dqi += 1
        self._wait(q, (k, self.cnt[k]))
        self._deps(q, r, w)
        inst = self.engs[q].dma_start(out=out, in_=in_)
        self.cnt[k] += 16
        inst.then_inc(self.sem[k], 16)
        tok = (k, self.cnt[k])
        self._reg(tok, r, w)
        return tok

    def barrier(self, engines=None):
        toks = [(k, v) for k, v in self.cnt.items() if v > 0]
        for e in (engines or self.engs):
            for t in toks:
                if t[0] != e:
                    self._wait(e, t)


def t5_bucket_np(dist):
    dist = np.asarray(dist, np.int64)
    d = np.maximum(dist, 1).astype(np.float32)
    large = 16 + (np.log(d / np.float32(16)) / np.float32(math.log(128 / 16)) * np.float32(16)).astype(np.int32)
    large = np.minimum(large, 31)
    return np.where(dist < 16, dist, large)


def build_nc(S, TT=512, TC=256, stop=None, nt_lim=None):
    NT = S // TT
    NTC = S // TC
    NKT = S // 128
    NB = S // BLK
    NCH = S // 128
    assert NB <= 32
    nc = bass.Bass("TRN2", target_bir_lowering=False)
    es = ExitStack()
    ctx = Ctx(nc, es)
    pe, act, dve, pool = nc.tensor, nc.scalar, nc.vector, nc.gpsimd

    def din(name, shape, dt=F32):
        return nc.dram_tensor(name, list(shape), dt, kind="ExternalInput").ap()

    def dscr(name, shape, dt=BF16):
        return nc.dram_tensor(name, list(shape), dt, kind="Internal").ap()

    x_d = din("x", [S, D])
    p_d = din("p", [S, 256])
    w_in_d = din("w_in", [D, 5120])
    w_ap_d = din("w_ap", [512, D])
    w_sp_d = din("w_sp", [512, D])
    w_out_d = din("w_out", [D, D])
    w_glu_d = din("w_glu", [512, D])
    w_pg_d = din("w_pg", [D, D])
    w_pp_d = din("w_pp", [256, D])
    are_rep_d = din("are_rep", [128, 2048])
    aim_rep_d = din("aim_rep", [128, 2048])
    ldt_rep_d = din("ldt_rep", [128, 2048])
    brT_d = din("brT", [128, 2048])
    biT_d = din("biT", [128, 2048])
    cre_d = din("cre_pad", [128, 2048])
    cim_d = din("cim_pad", [128, 2048])
    are_l_d = din("are_l", [128, 16])
    aim_l_d = din("aim_l", [128, 16])
    ldt_l_d = din("ldt_l", [128, 16])
    d_l_d = din("d_l", [128, 4])
    lng_d = din("lng", [128, D])
    lnb_d = din("lnb", [128, D])
    relb_d = din("relb", [32, 8])
    b31_d = din("b31rep", [128, 8])
    ident_d = din("ident", [128, 128], BF16)
    J_d = din("Jm", [128, 128], BF16)
    OH_d = din("OH", [32, 384], BF16)
    NEGM_d = din("NEGM", [8, 384])
    EOH_d = din("EOH", [32, S], BF16)
    PMASK_d = din("PMASK", [128, NCH * 32])
    PAST_d = din("PAST01", [128, NCH * 32])
    jidx_d = din("jidx", [128, 129])
    out_d = nc.dram_tensor("out", [S, D], F32, kind="ExternalOutput").ap()

    QT_d = dscr("QT", [4, 128, S])
    KT_d = dscr("KT", [4, 128, S])
    V_d = dscr("Vs", [8, 128, S // 128, 64])
    OA_d = dscr("OA", [8, 64, S])
    YS_d = dscr("YS", [8, 128, S])
    F_d = dscr("Fd", [16, 384])

    def sb(stack, name, shape, dt=F32):
        return stack.enter_context(nc.sbuf_tensor("sb_" + name, list(shape), dt))

    def ps(stack, name, shape, dt=F32):
        return stack.enter_context(nc.psum_tensor("ps_" + name, list(shape), dt))

    ident = sb(es, "ident", [128, 128], BF16)
    ident_b = Buf("ident")
    Jm = sb(es, "Jm", [128, 128], BF16)
    Jm_b = Buf("J")
    ctx.dma(ident[:], ident_d, w=[ident_b])
    ctx.dma(Jm[:], J_d, w=[Jm_b])

    cast_rr = Ring(["act", "dve", "pool"])

    def cast_copy(e, out, in_):
        if e == "act":
            return lambda: act.copy(out=out, in_=in_)
        if e == "dve":
            return lambda: dve.tensor_copy(out=out, in_=in_)
        return lambda: pool.tensor_copy(out=out, in_=in_)

    def load_weight(dst, dst_b, src, col0, ncols, KC, stage_ring, dcol0=0):
        cw = 2048 // KC
        for c0 in range(0, ncols, cw):
            w_ = min(cw, ncols - c0)
            st, stb = stage_ring.next()
            stv = st[:, 0:KC * w_].rearrange("p (kc c) -> p kc c", kc=KC)
            ctx.dma(stv, src[:, col0 + c0:col0 + c0 + w_].rearrange("(kc p) c -> p kc c", p=128), w=[stb])
            e = cast_rr.next()
            ctx.op(e, cast_copy(e, dst[:, :, dcol0 + c0:dcol0 + c0 + w_], stv), r=[stb], w=[dst_b])

    def mm_group(out_ap, out_b, pairs, r):
        n = len(pairs)
        for i, (l, rh) in enumerate(pairs):
            ctx.op("pe", lambda: pe.matmul(out_ap, lhsT=l, rhs=rh, start=(i == 0), stop=(i == n - 1)),
                   r=r, w=[out_b], sig=(i == n - 1))

    with ExitStack() as sA:
        _psT = ps(sA, "psT0", [128, 512])[:].bitcast(BF16)
        psT_v = [_psT[:, 0:512], _psT[:, 512:1024]]
        _psT_b = Buf("psT")
        psT_b = [_psT_b, _psT_b]
        mm_ring = Ring([(ps(sA, "mm%d" % i, [128, 512]), Buf("mm%d" % i)) for i in range(2)])
        psS_ring = Ring([[(ps(sA, "psS%d_%d" % (j, i), [128, 512]), Buf("psS%d_%d" % (j, i))) for i in range(2)] for j in range(2)])
        psY = ps(sA, "psY", [128, 512])
        _psY_b = Buf("psY")
        psY_b = [_psY_b] * 4

        Wqk = sb(sA, "Wqk", [128, 8, 1024], BF16); Wqk_b = Buf("Wqk")
        Wv = sb(sA, "Wv", [128, 8, 512], BF16); Wv_b = Buf("Wv")
        Wuz = sb(sA, "Wuz", [128, 8, 1024], BF16); Wuz_b = Buf("Wuz")
        Wglu = sb(sA, "Wglu", [128, 4, 1024], BF16); Wglu_b = Buf("Wglu")
        Wsp = sb(sA, "Wsp", [128, 4, 1024], BF16); Wsp_b = Buf("Wsp")
        Bre = sb(sA, "Bre", [128, 16, 128], BF16); Bim = sb(sA, "Bim", [128, 16, 128], BF16)
        Cre = sb(sA, "Cre", [128, 16, 128], BF16); Cim = sb(sA, "Cim", [128, 16, 128], BF16)
        BC_b = Buf("BC")
        T_b = Buf("T")
        rtab = sb(sA, "rtab", [128, 16, 128])
        Dg = sb(sA, "Dg", [128, 2, 4, 128], BF16)
        Ec_t = sb(sA, "Ec_t", [128, 16]); Es_t = sb(sA, "Es_t", [128, 16])
        Tcb = sb(sA, "Tcb", [128, 16, 128], BF16); Tsb = sb(sA, "Tsb", [128, 16, 128], BF16)
        mag_l = sb(sA, "mag_l", [128, 16]); mag_b = Buf("mag")
        d_l = sb(sA, "d_l", [128, 4]); d_b = Buf("d")
        car_r = sb(sA, "car_r", [128, 16]); car_i = sb(sA, "car_i", [128, 16])
        car_b = [Buf("car%d" % q) for q in range(2)]
        ctx.dma(d_l[:], d_l_d, w=[d_b])

        with ExitStack() as s0:
            Tc = sb(s0, "Tc", [128, 16, 128]); Ts = sb(s0, "Ts", [128, 16, 128])
            dhi = sb(s0, "dhi", [128, 4], BF16); dlo = sb(s0, "dlo", [128, 4])
            A_ = sb(s0, "pA", [128, 2048]); Bm = sb(s0, "pB", [128, 2048]); L_ = sb(s0, "pL", [128, 2048])
            t0 = sb(s0, "pt0", [128, 2048]); t1 = sb(s0, "pt1", [128, 2048]); t2 = sb(s0, "pt2", [128, 2048])
            t3 = sb(s0, "pt3", [128, 2048]); t4 = sb(s0, "pt4", [128, 2048]); t5 = sb(s0, "pt5", [128, 2048])
            ti = sb(s0, "pti", [128, 2048], I32)
            bR = sb(s0, "pbR", [128, 2048]); bI = sb(s0, "pbI", [128, 2048])
            al = sb(s0, "al", [128, 16]); bl = sb(s0, "bl", [128, 16]); ll = sb(s0, "ll", [128, 16])
            th = sb(s0, "th", [128, 16]); jx = sb(s0, "jx", [128, 129])
            th128 = sb(s0, "th128", [128, 16]); sc16 = sb(s0, "sc16", [128, 16])
            P = Buf("setup")

            Pl = []
            for t_, d_ in ((A_, are_rep_d), (Bm, aim_rep_d), (L_, ldt_rep_d), (bR, brT_d), (bI, biT_d),
                           (t0, cre_d), (t1, cim_d), (al, are_l_d), (bl, aim_l_d), (ll, ldt_l_d), (jx, jidx_d)):
                pb_ = Buf("pl")
                Pl.append(pb_)
                ctx.dma(t_[:], d_, w=[pb_])
            ctx.op("dve", lambda: dve.memset(th[:], 0.0), r=Pl, w=[P])

            def V(fn):
                ctx.op("dve", fn, r=[P], w=[P])

            def A(fn):
                ctx.op("act", fn, r=[P], w=[P])

            V(lambda: dve.tensor_copy(out=Cre[:].rearrange("p a b -> p (a b)"), in_=t0[:]))
            V(lambda: dve.tensor_scalar(out=Cim[:].rearrange("p a b -> p (a b)"), in0=t1[:], scalar1=-1.0, scalar2=None, op0=ALU.mult))

            def emit_sin(out, x, n, shift, xs):
                xi = ti[:, 0:n]
                V(lambda: dve.tensor_scalar(out=xs, in0=x, scalar1=shift, scalar2=None, op0=ALU.add))
                V(lambda: dve.tensor_scalar(out=xi, in0=xs, scalar1=1.0 / TWO_PI, scalar2=None, op0=ALU.mult))
                V(lambda: dve.tensor_copy(out=out, in_=xi))
                V(lambda: dve.scalar_tensor_tensor(out=out, in0=out, scalar=-TWO_PI, in1=xs, op0=ALU.mult, op1=ALU.add))
                V(lambda: dve.tensor_scalar(out=xs, in0=out, scalar1=math.pi, scalar2=-TWO_PI, op0=ALU.is_gt, op1=ALU.mult))
                V(lambda: dve.tensor_tensor(out=out, in0=out, in1=xs, op=ALU.add))
                V(lambda: dve.tensor_scalar(out=xs, in0=out, scalar1=-math.pi, scalar2=TWO_PI, op0=ALU.is_lt, op1=ALU.mult))
                V(lambda: dve.tensor_tensor(out=out, in0=out, in1=xs, op=ALU.add))
                V(lambda: dve.tensor_scalar(out=out, in0=out, scalar1=math.pi, scalar2=-math.pi, op0=ALU.min, op1=ALU.max))
                A(lambda: act.activation(out=out, in_=out, func=AF.Sin))

            A(lambda: act.activation(out=L_[:], in_=L_[:], func=AF.Exp))
            V(lambda: dve.tensor_tensor(out=t0[:], in0=L_[:], in1=A_[:], op=ALU.mult))
            A(lambda: act.activation(out=t0[:], in_=t0[:], func=AF.Exp))
            V(lambda: dve.tensor_tensor(out=t1[:], in0=L_[:], in1=Bm[:], op=ALU.mult))
            emit_sin(t2[:], t1[:], 2048, 0.0, t4[:])
            emit_sin(t3[:], t1[:], 2048, math.pi / 2, t4[:])
            V(lambda: dve.tensor_tensor(out=t3[:], in0=t3[:], in1=t0[:], op=ALU.mult))
            V(lambda: dve.tensor_tensor(out=t2[:], in0=t2[:], in1=t0[:], op=ALU.mult))
            V(lambda: dve.tensor_scalar(out=t3[:], in0=t3[:], scalar1=-1.0, scalar2=None, op0=ALU.add))
            V(lambda: dve.tensor_tensor(out=t0[:], in0=A_[:], in1=A_[:], op=ALU.mult))
            V(lambda: dve.tensor_tensor(out=t1[:], in0=Bm[:], in1=Bm[:], op=ALU.mult))
            V(lambda: dve.tensor_tensor(out=t0[:], in0=t0[:], in1=t1[:], op=ALU.add))
            V(lambda: dve.reciprocal(out=t0[:], in_=t0[:]))
            V(lambda: dve.tensor_tensor(out=t4[:], in0=t3[:], in1=A_[:], op=ALU.mult))
            V(lambda: dve.tensor_tensor(out=t1[:], in0=t2[:], in1=Bm[:], op=ALU.mult))
            V(lambda: dve.tensor_tensor(out=t4[:], in0=t4[:], in1=t1[:], op=ALU.add))
            V(lambda: dve.tensor_tensor(out=t4[:], in0=t4[:], in1=t0[:], op=ALU.mult))
            V(lambda: dve.tensor_tensor(out=t5[:], in0=t2[:], in1=A_[:], op=ALU.mult))
            V(lambda: dve.tensor_tensor(out=t1[:], in0=t3[:], in1=Bm[:], op=ALU.mult))
            V(lambda: dve.tensor_tensor(out=t5[:], in0=t5[:], in1=t1[:], op=ALU.subtract))
            V(lambda: dve.tensor_tensor(out=t5[:], in0=t5[:], in1=t0[:], op=ALU.mult))
            V(lambda: dve.tensor_tensor(out=t0[:], in0=t4[:], in1=bR[:], op=ALU.mult))
            V(lambda: dve.tensor_tensor(out=t1[:], in0=t5[:], in1=bI[:], op=ALU.mult))
            V(lambda: dve.tensor_tensor(out=Bre[:].rearrange("p a b -> p (a b)"), in0=t0[:], in1=t1[:], op=ALU.subtract))
            V(lambda: dve.tensor_tensor(out=t0[:], in0=t4[:], in1=bI[:], op=ALU.mult))
            V(lambda: dve.tensor_tensor(out=t1[:], in0=t5[:], in1=bR[:], op=ALU.mult))
            V(lambda: dve.tensor_tensor(out=Bim[:].rearrange("p a b -> p (a b)"), in0=t0[:], in1=t1[:], op=ALU.add))
            A(lambda: act.activation(out=ll[:], in_=ll[:], func=AF.Exp))
            V(lambda: dve.tensor_tensor(out=al[:], in0=ll[:], in1=al[:], op=ALU.mult))
            A(lambda: act.activation(out=mag_l[:], in_=al[:], func=AF.Exp))
            V(lambda: dve.tensor_tensor(out=th[:], in0=ll[:], in1=bl[:], op=ALU.mult))
            V(lambda: dve.tensor_tensor(out=t0[:].rearrange("p (a b) -> p a b", a=16),
                                        in0=th[:].rearrange("p (a o) -> p a o", o=1).to_broadcast([128, 16, 128]),
                                        in1=jx[:, 0:128].rearrange("p (o b) -> p o b", o=1).to_broadcast([128, 16, 128]), op=ALU.mult))
            emit_sin(Ts[:].rearrange("p a b -> p (a b)"), t0[:], 2048, 0.0, t1[:])
            emit_sin(Tc[:].rearrange("p a b -> p (a b)"), t0[:], 2048, math.pi / 2, t1[:])
            V(lambda: dve.tensor_scalar(out=th128[:], in0=th[:], scalar1=128.0, scalar2=None, op0=ALU.mult))
            emit_sin(Es_t[:], th128[:], 16, 0.0, sc16[:])
            emit_sin(Ec_t[:], th128[:], 16, math.pi / 2, sc16[:])
            V(lambda: dve.tensor_tensor(out=Ec_t[:], in0=Ec_t[:], in1=mag_l[:], op=ALU.mult))
            V(lambda: dve.tensor_tensor(out=Es_t[:], in0=Es_t[:], in1=mag_l[:], op=ALU.mult))
            V(lambda: dve.tensor_copy(out=Tcb[:], in_=Tc[:]))
            V(lambda: dve.tensor_copy(out=Tsb[:], in_=Ts[:]))
            V(lambda: dve.tensor_copy(out=rtab[:], in_=mag_l[:].rearrange("p (a o) -> p a o", o=1).to_broadcast([128, 16, 128])))
            V(lambda: dve.memset(rtab[:, :, 0:1], 0.0))
            ctx.op("dve", lambda: dve.tensor_copy(out=dhi[:], in_=d_l[:]), r=[P, d_b], w=[P])
            V(lambda: dve.tensor_tensor(out=dlo[:], in0=d_l[:], in1=dhi[:], op=ALU.subtract))
            for q_ in range(4):
                ctx.op("dve", lambda: dve.tensor_scalar(out=Dg[:, 0, q_, :], in0=ident[:], scalar1=dhi[:, q_:q_ + 1], scalar2=None, op0=ALU.mult), r=[P, ident_b], w=[P])
                ctx.op("dve", lambda: dve.tensor_scalar(out=Dg[:, 1, q_, :], in0=ident[:], scalar1=dlo[:, q_:q_ + 1], scalar2=None, op0=ALU.mult), r=[P, ident_b], w=[P])
            V(lambda: dve.memset(car_r[:], 0.0))
            V(lambda: dve.memset(car_i[:], 0.0))
            ctx.op("dve", lambda: dve.memset(t0[:, 0:1], 0.0), r=[P], w=[P, BC_b, T_b, mag_b] + car_b)

            ctx.barrier()
        with ExitStack() as s0:
            stage_ring = Ring([(sb(s0, "wst%d" % i, [128, 2048]), Buf("wst%d" % i)) for i in range(2)])
            load_weight(Wqk, Wqk_b, w_in_d, 0, 1024, 8, stage_ring)
            load_weight(Wv, Wv_b, w_in_d, 1024, 512, 8, stage_ring)
            load_weight(Wuz, Wuz_b, w_in_d, 2048, 1024, 8, stage_ring)
            load_weight(Wglu, Wglu_b, w_glu_d, 0, 1024, 4, stage_ring)
            load_weight(Wsp, Wsp_b, w_sp_d, 0, 1024, 4, stage_ring)
            ctx.barrier()

        x_ring = Ring([(sb(sA, "xa%d" % i, [128, D]), Buf("xa%d" % i)) for i in range(3)])
        xb = sb(sA, "xbA", [128, 4, D], BF16)
        xb_b = [Buf("xb%d" % i) for i in range(4)]
        xT_ring = Ring([(sb(sA, "xTA%d" % i, [128, 8, TT], BF16), Buf("xTA%d" % i)) for i in range(2)])
        uT_ring = Ring([(sb(sA, "uT%d" % i, [128, 4, TT], BF16), [Buf("uT%d_%d" % (i, q)) for q in range(4)]) for i in range(2)])
        szs_ring = Ring([(sb(sA, "szs%d" % i, [128, 4, TT], BF16), Buf("szs%d" % i)) for i in range(2)])
        qk_ring = Ring([(sb(sA, "qkst%d" % i, [128, TT], BF16), Buf("qkst%d" % i)) for i in range(2)])
        v_ring = Ring([(sb(sA, "vst%d" % i, [128, 512], BF16), Buf("vst%d" % i)) for i in range(2)])
        ys_ring = Ring([(sb(sA, "ysst%d" % i, [128, TT], BF16), Buf("ysst%d" % i)) for i in range(2)])
        bpr = sb(sA, "bpr", [128, 8, 128], BF16); bpi = sb(sA, "bpi", [128, 8, 128], BF16); tm2 = sb(sA, "tm2", [128, 8, 128], BF16)
        bp_b = Buf("bp")
        Sb_ring = Ring([((sb(sA, "Sbr%d" % i, [128, 8, 128], BF16), sb(sA, "Sbi%d" % i, [128, 8, 128], BF16)), Buf("Sb%d" % i)) for i in range(2)])
        g_ring = Ring([((sb(sA, "gr%d" % i, [128, 8, 128], BF16), sb(sA, "gi%d" % i, [128, 8, 128], BF16)), Buf("g%d" % i)) for i in range(2)])
        tp3 = sb(sA, "tp3", [128, 8, 128], BF16); tp4 = sb(sA, "tp4", [128, 8, 128], BF16)
        tp5 = sb(sA, "tp5", [128, 8, 128], BF16); tp6 = sb(sA, "tp6", [128, 8, 128], BF16); tpd_b = Buf("tpd")
        h_ring = Ring([((sb(sA, "hr%d" % i, [128, 8, 128], BF16), sb(sA, "mhi%d" % i, [128, 8, 128], BF16)), (Buf("hr%d" % i), Buf("hi%d" % i))) for i in range(2)])
        rc_r = sb(sA, "rc_r", [128, 8]); rc_i = sb(sA, "rc_i", [128, 8]); ctm = sb(sA, "ctm", [128, 8]); ctm2 = sb(sA, "ctm2", [128, 8])
        gT_ring = Ring([(sb(sA, "gT%d" % i, [128, 4, TT], BF16), Buf("gT%d" % i)) for i in range(2)])
        sigb_ring = Ring([(sb(sA, "sigb%d" % i, [128, TT]), Buf("sigb%d" % i)) for i in range(2)])
        yg = sb(sA, "yg", [128, 4, TT], BF16); yg_b = Buf("yg")

        def issue_xA(t_):
            for sub in range(4):
                xt, xtb = x_ring.next()
                ctx.dma(xt[:], x_d[t_ * TT + sub * 128:t_ * TT + (sub + 1) * 128, :], w=[xtb])
                ctx.op("act", lambda: act.copy(out=xb[:, sub, :], in_=xt[:]), r=[xtb], w=[xb_b[sub]])

        NTA = 0 if stop == 'setup' else (nt_lim or NT)

        def proj_tasks(t_):
            tk0 = t_ * TT
            xT_, xT_b_ = xT_ring.next()
            uT_, uT_b_ = uT_ring.next()
            szs_, szs_b_ = szs_ring.next()
            tasks_ = []

            def t_tr(dk):
                hf = dk % 2
                tp = psT_v[hf][:, 0:512]
                for sub in range(4):
                    ctx.op("pe", lambda: pe.transpose(tp[:, sub * 128:(sub + 1) * 128], xb[:, sub, dk * 128:(dk + 1) * 128], ident[:]),
                           r=[xb_b[sub], ident_b], w=[psT_b[hf]], sig=(sub == 3))
                ctx.op("act", lambda: act.copy(out=xT_[:, dk, :], in_=tp), r=[psT_b[hf]], w=[xT_b_])
                if dk == 7 and t_ + 1 < NTA:
                    issue_xA(t_ + 1)

            def t_qk(cc):
                pt, pb = mm_ring.next()
                mm_group(pt[:, 0:TT], pb, [(Wqk[:, dk, cc * 128:(cc + 1) * 128], xT_[:, dk, :]) for dk in range(8)], r=[Wqk_b, xT_b_])
                stg, stb = qk_ring.next()
                if cc < 4:
                    ctx.op("act", lambda: act.activation(out=stg[:], in_=pt[:, 0:TT], func=AF.Copy, scale=0.125), r=[pb], w=[stb])
                    ctx.dma(QT_d[cc, :, tk0:tk0 + TT], stg[:], r=[stb])
                else:
                    ctx.op("act", lambda: act.copy(out=stg[:], in_=pt[:, 0:TT]), r=[pb], w=[stb])
                    ctx.dma(KT_d[cc - 4, :, tk0:tk0 + TT], stg[:], r=[stb])

            def t_v(sub):
                pt, pb = mm_ring.next()
                mm_group(pt[:, 0:512], pb, [(xT_[:, dk, sub * 128:(sub + 1) * 128], Wv[:, dk, :]) for dk in range(8)], r=[Wv_b, xT_b_])
                stg, stb = v_ring.next()
                ctx.op("act", lambda: act.copy(out=stg[:], in_=pt[:, 0:512]), r=[pb], w=[stb])
                ctx.dma(V_d[:, :, t_ * 4 + sub, :].rearrange("h p d -> p h d"), stg[:].rearrange("p (h d) -> p h d", h=8), r=[stb])

            def t_uz(cc):
                pt, pb = mm_ring.next()
                mm_group(pt[:, 0:TT], pb, [(Wuz[:, dk, cc * 128:(cc + 1) * 128], xT_[:, dk, :]) for dk in range(8)], r=[Wuz_b, xT_b_])
                if cc < 4:
                    ctx.op("act", lambda: act.copy(out=uT_[:, cc, :], in_=pt[:, 0:TT]), r=[pb], w=[uT_b_[cc]])
                else:
                    ctx.op("act", lambda: act.activation(out=szs_[:, cc - 4, :], in_=pt[:, 0:TT], func=AF.Silu), r=[pb], w=[szs_b_])

            for dk in range(8):
                tasks_.append(lambda dk=dk: t_tr(dk))
            for cc in range(8):
                tasks_.append(lambda cc=cc: t_uz(cc))
            for cc in range(8):
                tasks_.append(lambda cc=cc: t_qk(cc))
            for sub in range(4):
                tasks_.append(lambda sub=sub: t_v(sub))
            return tasks_, (uT_, uT_b_, szs_, szs_b_)

        def post_tasks(t_, gT_, gT_b_, szs_, szs_b_):
            tk0 = t_ * TT
            tasks_ = []

            def t_glu(oc):
                pa, pab = mm_ring.next()
                mm_group(pa[:, 0:TT], pab, [(Wglu[:, kc, oc * 128:(oc + 1) * 128], gT_[:, kc, :]) for kc in range(4)], r=[Wglu_b, gT_b_])
                pbk, pbb = mm_ring.next()
                mm_group(pbk[:, 0:TT], pbb, [(Wglu[:, kc, (oc + 4) * 128:(oc + 5) * 128], gT_[:, kc, :]) for kc in range(4)], r=[Wglu_b, gT_b_])
                sg, sgb = sigb_ring.next()
                ctx.op("act", lambda: act.activation(out=sg[:], in_=pbk[:, 0:TT], func=AF.Sigmoid), r=[pbb], w=[sgb])
                ctx.op("dve", lambda: dve.tensor_tensor(out=sg[:], in0=sg[:], in1=pa[:, 0:TT], op=ALU.mult), r=[sgb, pab], w=[sgb])
                ctx.op("dve", lambda: dve.tensor_tensor(out=yg[:, oc, :], in0=sg[:], in1=szs_[:, oc, :], op=ALU.mult), r=[sgb, szs_b_], w=[yg_b])

            def t_sp(oc):
                pt, pb = mm_ring.next()
                mm_group(pt[:, 0:TT], pb, [(Wsp[:, kc, oc * 128:(oc + 1) * 128], yg[:, kc, :]) for kc in range(4)], r=[Wsp_b, yg_b])
                stg, stb = ys_ring.next()
                ctx.op("act", lambda: act.copy(out=stg[:], in_=pt[:, 0:TT]), r=[pb], w=[stb])
                ctx.dma(YS_d[oc, :, tk0:tk0 + TT], stg[:], r=[stb])

            for oc in range(4):
                tasks_.append(lambda oc=oc: t_glu(oc))
            for oc in range(8):
                tasks_.append(lambda oc=oc: t_sp(oc))
            return tasks_

        post = []
        cur_tile = None
        if NTA:
            issue_xA(0)
            tasks, cur_tile = proj_tasks(0)
            for tk_ in tasks:
                tk_()
        for t in range(NTA):
            tok0 = t * TT
            uT, uT_b, szs, szs_b = cur_tile
            gT, gT_b = gT_ring.next()
            tasks = list(post)
            post_n = list(post)
            if t + 1 < NTA:
                ptk, cur_tile = proj_tasks(t + 1)
                tasks += ptk
            per_unit = (len(tasks) + 7) // 8
            def ssm_stage1a(ti_, s_, hf):
                tsl = slice(s_ * 128, (s_ + 1) * 128)
                (Sbr, Sbi), Sb_b = Sb_ring.next()
                for qi in range(2):
                    q = 2 * hf + qi
                    (Sre, Sre_b), (Sim, Sim_b) = psS_ring.next()
                    for i in range(4):
                        pr = 4 * q + i
                        ctx.op("pe", lambda: pe.matmul(Sre[:, i * 128:(i + 1) * 128], lhsT=Bre[:, pr, :], rhs=ti_[0][:, q, tsl], start=True, stop=True),
                               r=[BC_b, ti_[1][q]], w=[Sre_b], sig=(i == 3))
                    for i in range(4):
                        pr = 4 * q + i
                        ctx.op("pe", lambda: pe.matmul(Sim[:, i * 128:(i + 1) * 128], lhsT=Bim[:, pr, :], rhs=ti_[0][:, q, tsl], start=True, stop=True),
                               r=[BC_b, ti_[1][q]], w=[Sim_b], sig=(i == 3))
                    ctx.op("act", lambda: act.copy(out=Sbr[:, qi * 4:(qi + 1) * 4, :].rearrange("p a b -> p (a b)"), in_=Sre[:, 0:512]), r=[Sre_b], w=[Sb_b])
                    ctx.op("act", lambda: act.copy(out=Sbi[:, qi * 4:(qi + 1) * 4, :].rearrange("p a b -> p (a b)"), in_=Sim[:, 0:512]), r=[Sim_b], w=[Sb_b])
                return s_, hf, Sbr, Sbi, Sb_b

            def ssm_stage1b(s_, hf, Sbr, Sbi, Sb_b):
                p0 = 8 * hf
                Tcq = Tcb[:, p0:p0 + 8, :]
                Tsq = Tsb[:, p0:p0 + 8, :]
                cb = car_b[hf]
                ctx.op("dve", lambda: dve.tensor_tensor(out=bpr[:], in0=Sbr[:], in1=Tcq, op=ALU.mult), r=[Sb_b, T_b], w=[bp_b])
                ctx.op("dve", lambda: dve.tensor_tensor(out=tm2[:], in0=Sbi[:], in1=Tsq, op=ALU.mult), r=[Sb_b, T_b], w=[bp_b])
                ctx.op("dve", lambda: dve.tensor_tensor(out=bpr[:], in0=bpr[:], in1=tm2[:], op=ALU.add), r=[bp_b], w=[bp_b])
                ctx.op("dve", lambda: dve.tensor_tensor(out=bpi[:], in0=Sbi[:], in1=Tcq, op=ALU.mult), r=[Sb_b, T_b, bp_b], w=[bp_b])
                ctx.op("dve", lambda: dve.tensor_tensor(out=tm2[:], in0=Sbr[:], in1=Tsq, op=ALU.mult), r=[Sb_b, T_b, bp_b], w=[bp_b])
                ctx.op("dve", lambda: dve.tensor_tensor(out=bpi[:], in0=bpi[:], in1=tm2[:], op=ALU.subtract), r=[bp_b], w=[bp_b])
                ctx.op("dve", lambda: dve.tensor_tensor(out=bpr[:, :, 0], in0=bpr[:, :, 0], in1=car_r[:, p0:p0 + 8], op=ALU.add), r=[bp_b, cb], w=[bp_b])
                ctx.op("dve", lambda: dve.tensor_tensor(out=bpi[:, :, 0], in0=bpi[:, :, 0], in1=car_i[:, p0:p0 + 8], op=ALU.add), r=[bp_b, cb], w=[bp_b])
                (gr, gi), g_b = g_ring.next()
                rt = rtab[:, p0:p0 + 8, :].rearrange("p a b -> p (a b)")
                ctx.op("dve", lambda: dve.tensor_tensor_scan(out=gr[:].rearrange("p a b -> p (a b)"), data0=rt, data1=bpr[:].rearrange("p a b -> p (a b)"),
                                                             initial=0.0, op0=ALU.mult, op1=ALU.add), r=[bp_b, mag_b], w=[g_b])
                ctx.op("dve", lambda: dve.tensor_tensor_scan(out=gi[:].rearrange("p a b -> p (a b)"), data0=rt, data1=bpi[:].rearrange("p a b -> p (a b)"),
                                                             initial=0.0, op0=ALU.mult, op1=ALU.add), r=[bp_b, mag_b], w=[g_b])
                Ec = Ec_t[:, p0:p0 + 8]
                Es = Es_t[:, p0:p0 + 8]
                ctx.op("dve", lambda: dve.tensor_tensor(out=ctm[:], in0=Ec, in1=gr[:, :, 127], op=ALU.mult), r=[T_b, g_b, cb], w=[cb])
                ctx.op("dve", lambda: dve.tensor_tensor(out=ctm2[:], in0=Es, in1=gi[:, :, 127], op=ALU.mult), r=[T_b, g_b, cb], w=[cb])
                ctx.op("dve", lambda: dve.tensor_tensor(out=car_r[:, p0:p0 + 8], in0=ctm[:], in1=ctm2[:], op=ALU.subtract), r=[cb], w=[cb])
                ctx.op("dve", lambda: dve.tensor_tensor(out=ctm[:], in0=Ec, in1=gi[:, :, 127], op=ALU.mult), r=[T_b, g_b, cb], w=[cb])
                ctx.op("dve", lambda: dve.tensor_tensor(out=ctm2[:], in0=Es, in1=gr[:, :, 127], op=ALU.mult), r=[T_b, g_b, cb], w=[cb])
                ctx.op("dve", lambda: dve.tensor_tensor(out=car_i[:, p0:p0 + 8], in0=ctm[:], in1=ctm2[:], op=ALU.add), r=[cb], w=[cb])
                return s_, hf, gr, gi, g_b

            def ssm_stage2(ti_, s_, hf, gr, gi, g_b):
                tsl = slice(s_ * 128, (s_ + 1) * 128)
                p0 = 8 * hf
                Tcq = Tcb[:, p0:p0 + 8, :]
                Tsq = Tsb[:, p0:p0 + 8, :]
                (hr, mhi), (h_b, hi_b) = h_ring.next()
                ctx.op("dve", lambda: dve.tensor_tensor(out=tp3[:], in0=gr[:], in1=Tcq, op=ALU.mult), r=[g_b, T_b], w=[tpd_b])
                ctx.op("dve", lambda: dve.tensor_tensor(out=tp4[:], in0=gi[:], in1=Tsq, op=ALU.mult), r=[g_b, T_b], w=[tpd_b])
                ctx.op("dve", lambda: dve.tensor_tensor(out=hr[:], in0=tp3[:], in1=tp4[:], op=ALU.subtract), r=[tpd_b], w=[h_b])
                ctx.op("dve", lambda: dve.tensor_tensor(out=tp5[:], in0=gi[:], in1=Tcq, op=ALU.mult), r=[g_b, T_b], w=[tpd_b])
                ctx.op("dve", lambda: dve.tensor_tensor(out=tp6[:], in0=gr[:], in1=Tsq, op=ALU.mult), r=[g_b, T_b], w=[tpd_b])
                ctx.op("dve", lambda: dve.tensor_tensor(out=mhi[:], in0=tp5[:], in1=tp6[:], op=ALU.add), r=[tpd_b], w=[hi_b])
                for qi in range(2):
                    q = 2 * hf + qi
                    yreg = psY[:, qi * 128:(qi + 1) * 128]
                    ctx.op("pe", lambda: pe.matmul(yreg, lhsT=Dg[:, 0, q, :], rhs=ti_[0][:, q, tsl], start=True, stop=False), r=[BC_b, ti_[1][q]], w=[_psY_b], sig=False)
                    ctx.op("pe", lambda: pe.matmul(yreg, lhsT=Dg[:, 1, q, :], rhs=ti_[0][:, q, tsl], start=False, stop=False), r=[BC_b, ti_[1][q]], w=[_psY_b], sig=False)
                    for i in range(4):
                        pr = 4 * q + i
                        ctx.op("pe", lambda: pe.matmul(yreg, lhsT=Cre[:, pr, :], rhs=hr[:, qi * 4 + i, :], start=False, stop=False),
                               r=[BC_b, h_b], w=[_psY_b], sig=False)
                        ctx.op("pe", lambda: pe.matmul(yreg, lhsT=Cim[:, pr, :], rhs=mhi[:, qi * 4 + i, :], start=False, stop=(i == 3)),
                               r=[BC_b, hi_b], w=[_psY_b], sig=(i == 3 and qi == 1))
                ctx.op("act", lambda: act.activation(out=ti_[2][:, 2 * hf:2 * hf + 2, tsl], in_=psY[:, 0:256].rearrange("p (a b) -> p a b", a=2), func=AF.Gelu),
                       r=[_psY_b], w=[ti_[3]])

            units = [(s_, hf) for s_ in range(TT // 128) for hf in range(2)]
            nu = len(units)
            ti_cur = (uT, uT_b, gT, gT_b)
            ti_nxt = (cur_tile[0], cur_tile[1]) if t + 1 < NTA else None
            if t + 1 < NTA:
                assert len(post_n) + 16 <= 6 * per_unit
            if t == 0:
                a_res = {0: ssm_stage1a(ti_cur, *units[0]), 1: ssm_stage1a(ti_cur, *units[1])}
                b_res = {0: ssm_stage1b(*a_res.pop(0))}
            else:
                a_res, b_res = next_a, next_b
            next_a, next_b = {}, {}
            for k in range(nu):
                if k + 2 < nu:
                    a_res[k + 2] = ssm_stage1a(ti_cur, *units[k + 2])
                elif ti_nxt is not None:
                    next_a[k + 2 - nu] = ssm_stage1a(ti_nxt, *units[k + 2 - nu])
                if k + 1 < nu:
                    b_res[k + 1] = ssm_stage1b(*a_res.pop(k + 1))
                elif ti_nxt is not None:
                    next_b[0] = ssm_stage1b(*next_a.pop(0))
                ssm_stage2(ti_cur, *b_res.pop(k))
                for _ in range(per_unit):
                    if tasks:
                        tasks.pop(0)()
            while tasks:
                tasks.pop(0)()
            post = post_tasks(t, gT, gT_b, szs, szs_b)
        for tk_ in post:
            tk_()
        ctx.barrier()

    with ExitStack() as sB:
        S_ring = Ring([(ps(sB, "psSc%d" % i, [128, 512]), Buf("psSc%d" % i)) for i in range(3)])
        O_ring = Ring([(ps(sB, "psO%d" % i, [128, 512]), Buf("psO%d" % i)) for i in range(2)])
        psG = ps(sB, "psG", [128, 512]); psG_b = Buf("psG")
        psM = ps(sB, "psM", [128, 512]); psM_v = psM[:].bitcast(BF16); psM_b = Buf("psM")
        psBc = ps(sB, "psBc", [128, 512]); psBc_b = Buf("psBc")

        PMASK = sb(sB, "PMASK", [128, NCH, 32]); PAST = sb(sB, "PAST", [128, NCH, 32]); cm_b = Buf("cmask")
        ctx.dma(PMASK[:].rearrange("p a b -> p (a b)"), PMASK_d, w=[cm_b])
        ctx.dma(PAST[:].rearrange("p a b -> p (a b)"), PAST_d, w=[cm_b])
        b31 = sb(sB, "b31", [128, 8])
        ctx.dma(b31[:], b31_d, w=[cm_b])
        ones_f = sb(sB, "ones_f", [128, 64], BF16)
        ctx.op("dve", lambda: dve.memset(ones_f[:], 1.0), w=[cm_b])

        Htiles = sb(sB, "Htiles", [128, 8, 3, 128], BF16); H_b = [[Buf("H%d_%d" % (h_, j_)) for j_ in range(3)] for h_ in range(8)]
        KA = [sb(sB, "KA%d" % i, [96, S], BF16) for i in range(2)]
        QA = [sb(sB, "QA%d" % i, [96, S], BF16) for i in range(2)]
        VE = [sb(sB, "VE%d" % i, [128, NKT, 65], BF16) for i in range(2)]
        KA_b = [Buf("KA%d" % i) for i in range(2)]
        QA_b = [Buf("QA%d" % i) for i in range(2)]
        QM_b = [[Buf("QM%d_%d" % (i, t)) for t in range(NT)] for i in range(2)]
        VE_b = [Buf("VE%d" % i) for i in range(2)]

        def load_head(h):
            i = h % 2
            pr, hh = h // 2, h % 2
            ctx.dma(KA[i][0:64, :], KT_d[pr, hh * 64:(hh + 1) * 64, :], w=[KA_b[i]])
            ctx.dma(QA[i][0:64, :], QT_d[pr, hh * 64:(hh + 1) * 64, :], w=[QA_b[i]])
            ctx.dma(VE[i][:, :, 0:64], V_d[h], w=[VE_b[i]])

        NHB = 0 if stop in ('setup', 'A') else (NH if stop != 'B1' else 1)
        ctx.dma(KA[0][64:96, :], EOH_d, w=[KA_b[0]])
        ctx.op("pool", lambda: pool.memset(VE[0][:, :, 64:65], 1.0), w=[VE_b[0]])
        if NHB:
            load_head(0)
        ctx.dma(KA[1][64:96, :], EOH_d, w=[KA_b[1]])
        ctx.op("pool", lambda: pool.memset(VE[1][:, :, 64:65], 1.0), w=[VE_b[1]])
        if True:
            sb0 = sB
            relb_f = sb(sb0, "relb_f", [32, 8]); relb_h = sb(sb0, "relb_h", [32, 8], BF16)
            OHt = sb(sb0, "OHt", [32, 384], BF16); negm = sb(sb0, "negm", [8, 384])
            Ff = sb(sb0, "Ff", [8, 384]); Fh = sb(sb0, "Fh", [8, 2, 384], BF16)
            Pb = Buf("biasprep")
            Pb1, Pb2, Pb3 = Buf("bp1"), Buf("bp2"), Buf("bp3")
            ctx.dma(relb_f[:], relb_d, w=[Pb1]); ctx.dma(OHt[:], OH_d, w=[Pb2]); ctx.dma(negm[:], NEGM_d, w=[Pb3])
            ctx.op("dve", lambda: dve.tensor_copy(out=relb_h[:], in_=relb_f[:]), r=[Pb1, Pb2, Pb3], w=[Pb])
            ctx.op("pe", lambda: pe.matmul(psG[0:8, 0:384], lhsT=relb_h[:, :], rhs=OHt[:, :], start=True, stop=True), r=[Pb], w=[psG_b])
            ctx.op("dve", lambda: dve.tensor_tensor(out=Ff[:], in0=psG[0:8, 0:384], in1=negm[:], op=ALU.add), r=[psG_b, Pb], w=[Pb])
            ctx.op("dve", lambda: dve.tensor_copy(out=Fh[:, 0, :], in_=Ff[:]), r=[Pb], w=[Pb])
            ctx.op("dve", lambda: dve.tensor_scalar(out=Fh[:, 1, :], in0=Ff[:], scalar1=Ff[:, 380:381], scalar2=None, op0=ALU.subtract), r=[Pb], w=[Pb])
            ctx.dma(F_d[0:8, :], Fh[:, 0, :], r=[Pb], w=[Pb])
            ctx.dma(F_d[8:16, :], Fh[:, 1, :], r=[Pb], w=[Pb])
            for h in range(NH):
                for j, (row, off) in enumerate(((h, 0), (h, 128), (8 + h, 128))):
                    src = bass.AP(tensor=F_d.tensor, offset=row * 384 + off, ap=[[1, 128], [1, 128]])
                    ctx.dma(Htiles[:, h, j, :], src, r=[Pb], w=[H_b[h][j]])
        ksum = sb(sB, "ksum", [64, 32]); kmT = [sb(sB, "kmT%d" % i, [64, 32], BF16) for i in range(2)]
        km_b = [Buf("km%d" % i) for i in range(2)]
        gm = sb(sB, "gm", [128, 4, 32]); ns_ = sb(sB, "ns", [128, 4, 32]); thr8 = sb(sB, "thr8", [128, 4, 8]); gm_b = Buf("gm")
        Mq_ring = Ring([(sb(sB, "Mq%d" % i, [128, 4, 96], BF16), Buf("Mq%d" % i)) for i in range(2)])
        for mq, mqb in Mq_ring.items:
            ctx.op("pool", lambda: pool.memset(mq[:], 0.0), w=[mqb])
        PT_ring = Ring([(sb(sB, "PT%d" % i, [128, 512], BF16), Buf("PT%d" % i)) for i in range(4)])
        osb_ring = Ring([(sb(sB, "osb%d" % i, [65, 512]), Buf("osb%d" % i)) for i in range(2)])
        rec_ring = Ring([(sb(sB, "rec%d" % i, [65, 512], BF16), Buf("rec%d" % i)) for i in range(2)])
        oo_ring = Ring([(sb(sB, "oo%d" % i, [64, 512], BF16), Buf("oo%d" % i)) for i in range(2)])

        def prep_head(h):
            i = h % 2
            Kh = KA[i]
            ctx.op("dve", lambda: dve.tensor_reduce(out=ksum[:, 0:NB], in_=Kh[0:64, :].rearrange("p (a b) -> p a b", b=BLK), axis=AX.X, op=ALU.add),
                   r=[KA_b[i]], w=[km_b[i]])
            if NB < 32:
                ctx.op("dve", lambda: dve.memset(ksum[:, NB:32], 0.0), r=[], w=[km_b[i]])
            ctx.op("dve", lambda: dve.tensor_scalar(out=kmT[i][:], in0=ksum[:], scalar1=1.0 / BLK, scalar2=None, op0=ALU.mult), r=[km_b[i]], w=[km_b[i]])

        def gate1(h, T):
            i = h % 2
            Qh = QA[i]
            c0 = 4 * T
            q0 = T * TT
            for ci in range(4):
                ctx.op("pe", lambda: pe.matmul(psG[:, ci * 32:(ci + 1) * 32], lhsT=Qh[0:64, q0 + ci * 128:q0 + (ci + 1) * 128], rhs=kmT[i][:, :], start=True, stop=True),
                       r=[QA_b[i], km_b[i]], w=[psG_b], sig=(ci == 3))
            ctx.op("dve", lambda: dve.tensor_tensor(out=gm[:], in0=psG[:, 0:128].rearrange("p (a b) -> p a b", a=4), in1=PMASK[:, c0:c0 + 4, :], op=ALU.add),
                   r=[psG_b, cm_b], w=[gm_b])
            for ci in range(4):
                ctx.op("dve", lambda: dve.max(out=thr8[:, ci, :], in_=gm[:, ci, :]), r=[gm_b], w=[gm_b])
            ctx.op("dve", lambda: dve.tensor_tensor(out=ns_[:], in0=gm[:], in1=thr8[:, :, 2:3].to_broadcast([128, 4, 32]), op=ALU.is_lt), r=[gm_b], w=[gm_b])
            ctx.op("dve", lambda: dve.tensor_scalar(out=ns_[:], in0=ns_[:], scalar1=NEG, scalar2=b31[:, h:h + 1], op0=ALU.mult, op1=ALU.add), r=[gm_b, cm_b], w=[gm_b])
            Mq, Mq_b = Mq_ring.next()
            ctx.op("dve", lambda: dve.tensor_tensor(out=Mq[:, :, 64:96], in0=ns_[:], in1=PAST[:, c0:c0 + 4, :], op=ALU.mult), r=[gm_b, cm_b], w=[Mq_b])
            return Mq, Mq_b

        def gate2(h, T, Mq, Mq_b):
            i = h % 2
            q0 = T * TT
            for ci in range(4):
                ctx.op("pe", lambda: pe.transpose(psM_v[0:96, ci * 128:(ci + 1) * 128], Mq[:, ci, :], ident[:]), r=[Mq_b, ident_b], w=[psM_b], sig=(ci == 3))
            ctx.op("dve", lambda: dve.tensor_copy(out=QA[i][64:96, q0:q0 + TT], in_=psM_v[64:96, 0:512]), r=[psM_b], w=[QM_b[i][T]])

        def emit_norm(h_, T_, Ops_, Ops_b_):
            osb, osb_b = osb_ring.next()
            ctx.op("act", lambda: act.copy(out=osb[:], in_=Ops_[0:65, :]), r=[Ops_b_], w=[osb_b])
            rec, rec_b = rec_ring.next()
            ctx.op("act", lambda: act.activation(out=osb[64:65, :], in_=osb[64:65, :], func=AF.Ln), r=[osb_b], w=[osb_b])
            ctx.op("act", lambda: act.activation(out=rec[64:65, :], in_=osb[64:65, :], func=AF.Exp, scale=-1.0), r=[osb_b], w=[rec_b])
            ctx.op("pe", lambda: pe.matmul(psBc[0:64, :], lhsT=ones_f[64:65, 0:64], rhs=rec[64:65, :], start=True, stop=True), r=[rec_b, cm_b], w=[psBc_b])
            oo, oo_b = oo_ring.next()
            ctx.op("dve", lambda: dve.tensor_tensor(out=oo[:], in0=osb[0:64, :], in1=psBc[0:64, :], op=ALU.mult), r=[osb_b, psBc_b], w=[oo_b])
            ctx.dma(OA_d[h_, :, T_ * TT:(T_ + 1) * TT], oo[:], r=[oo_b])

        pending_norm = None
        work = [(h, T) for h in range(NHB) for T in range(NT)]
        LA = 2
        if work:
            prep_head(0)
            g = gate1(0, 0)
            gate2(0, 0, *g)
        for wi, (h, T) in enumerate(work):
            i = h % 2
            if T == 0 and h + 1 < NHB:
                load_head(h + 1)
            Kh, Qh, Vh = KA[i], QA[i], VE[i]
            q0 = T * TT
            nxt_w = work[wi + 1] if wi + 1 < len(work) else None
            g = None
            if nxt_w is not None:
                if nxt_w[1] == 0:
                    prep_head(nxt_w[0])
                g = gate1(*nxt_w)
            Ops, Ops_b = O_ring.next()
            nkt = 4 * T + 4

            def emit_S(kt):
                cl = max(0, kt - 4 * T)
                cs = slice(cl * 128, 512)
                Sps, Sps_b = S_ring.next()
                adds = []
                ci0 = kt - 4 * T
                if 0 <= ci0 <= 3:
                    adds.append((ci0, 0))
                ci1 = kt + 1 - 4 * T
                if 0 <= ci1 <= 3:
                    adds.append((ci1, 1 if kt % 2 == 0 else 2))
                ctx.op("pe", lambda: pe.matmul(Sps[:, cs], lhsT=Kh[0:96, kt * 128:(kt + 1) * 128], rhs=Qh[0:96, q0 + cl * 128:q0 + 512], start=True, stop=(len(adds) == 0)),
                       r=[KA_b[i], QA_b[i], QM_b[i][T]], w=[Sps_b], sig=(len(adds) == 0))
                for ai, (cidx, j) in enumerate(adds):
                    last = ai == len(adds) - 1
                    ctx.op("pe", lambda: pe.matmul(Sps[:, cidx * 128:(cidx + 1) * 128], lhsT=Jm[:, :], rhs=Htiles[:, h, j, :], start=False, stop=last),
                           r=[Jm_b, H_b[h][j]], w=[Sps_b], sig=last)
                return kt, cs, Sps, Sps_b

            def emit_PV(kt, cs, Sps, Sps_b):
                PT, PT_b = PT_ring.next()
                ctx.op("act", lambda: act.activation(out=PT[:, cs], in_=Sps[:, cs], func=AF.Exp), r=[Sps_b], w=[PT_b])
                ctx.op("pe", lambda: pe.matmul(Ops[0:65, cs], lhsT=Vh[:, kt, 0:65], rhs=PT[:, cs], start=(kt == 0), stop=(kt == nkt - 1)),
                       r=[VE_b[i], PT_b], w=[Ops_b], sig=(kt == nkt - 1))

            pend = []
            for kt in range(nkt):
                pend.append(emit_S(kt))
                if kt == LA and pending_norm is not None:
                    emit_norm(*pending_norm)
                    pending_norm = None
                if len(pend) > LA:
                    emit_PV(*pend.pop(0))
            while pend:
                emit_PV(*pend.pop(0))
            if g is not None:
                gate2(nxt_w[0], nxt_w[1], *g)
            pending_norm = (h, T, Ops, Ops_b)
        if pending_norm is not None:
            emit_norm(*pending_norm)
        ctx.barrier()

    with ExitStack() as sC:
        psT_v = [ps(sC, "psTc%d" % i, [128, 512])[:].bitcast(BF16) for i in range(2)]
        psT_b = [Buf("psTc0"), Buf("psTc1")]
        mm_ring = Ring([(ps(sC, "mc%d" % i, [128, 512]), Buf("mc%d" % i)) for i in range(6)])
        tk_ring = mm_ring

        Wz = sb(sC, "Wz", [128, 8, 2560], BF16); Wz_b = Buf("Wz")
        Wap = sb(sC, "Wap", [128, 4, 1024], BF16); Wap_b = Buf("Wap")
        Wout = sb(sC, "Wout", [128, 8, 1024], BF16); Wout_b = Buf("Wout")
        Wpg = sb(sC, "Wpg", [128, 8, 1024], BF16); Wpg_b = Buf("Wpg")
        Wpp = sb(sC, "Wpp", [128, 2, 1024], BF16); Wpp_b = Buf("Wpp")
        lng = sb(sC, "lng", [128, D]); lnb = sb(sC, "lnb", [128, D]); ln_b = Buf("ln")
        ctx.dma(lng[:], lng_d, w=[ln_b]); ctx.dma(lnb[:], lnb_d, w=[ln_b])
        NS = TC // 128
        x_ring = Ring([(sb(sC, "xc%d" % i, [128, D]), Buf("xc%d" % i)) for i in range(2 * NS)])
        xb = sb(sC, "xbC", [128, NS, D], BF16); xb_b = [Buf("xbC%d" % i) for i in range(NS)]
        xTc_ring = Ring([(sb(sC, "xTC%d" % i, [128, 8, TC], BF16), Buf("xTC%d" % i)) for i in range(2)])
        p_ring = Ring([(sb(sC, "pc%d" % i, [128, 256]), Buf("pc%d" % i)) for i in range(NS)])
        pbf = sb(sC, "pbf", [128, NS, 256], BF16); pbf_b = [Buf("pbf%d" % i) for i in range(NS)]
        pTc_ring = Ring([(sb(sC, "pT%d" % i, [128, 2, TC], BF16), Buf("pT%d" % i)) for i in range(2)])
        sza = sb(sC, "sza", [128, 4, TC], BF16); sza_b = Buf("sza")
        sga = sb(sC, "sga", [128, 8, TC], BF16); sga_b = Buf("sga")
        sgs = sb(sC, "sgs", [128, 8, TC], BF16); sgs_b = Buf("sgs")
        oa_ring = Ring([(sb(sC, "oat%d" % i, [128, 4, TC], BF16), Buf("oat%d" % i)) for i in range(2)])
        oz = sb(sC, "oz", [128, 4, TC], BF16); oz_b = Buf("oz")
        ys_ring = Ring([(sb(sC, "yst%d" % i, [128, 8, TC], BF16), Buf("yst%d" % i)) for i in range(2)])
        mg_ring = Ring([(sb(sC, "merge%d" % i, [128, 8, TC], BF16), Buf("merge%d" % i)) for i in range(2)])
        ta_ring = Ring([(sb(sC, "tma%d" % i, [128, TC]), Buf("tma%d" % i)) for i in range(2)])
        tb_ring = Ring([(sb(sC, "tmb%d" % i, [128, TC]), Buf("tmb%d" % i)) for i in range(2)])
        s_ring = Ring([(sb(sC, "srow%d" % i, [128, D]), Buf("srow%d" % i)) for i in range(2)])
        o_ring = Ring([(sb(sC, "orow%d" % i, [128, D]), Buf("orow%d" % i)) for i in range(2)])
        st_ring = Ring([((sb(sC, "bst%d" % i, [128, 2, 6]), sb(sC, "bmv%d" % i, [128, 2]), sb(sC, "brs%d" % i, [128, 2])), Buf("bst%d" % i)) for i in range(2)])

        def issue_xC(t_):
            tk0 = t_ * TC
            xts_ = []
            for sub in range(NS):
                xt, xtb = x_ring.next()
                xts_.append((xt, xtb))
                ctx.dma(xt[:], x_d[tk0 + sub * 128:tk0 + (sub + 1) * 128, :], w=[xtb])
                ctx.op("pool", lambda: pool.tensor_copy(out=xb[:, sub, :], in_=xt[:]), r=[xtb], w=[xb_b[sub]])
                pt_, ptb = p_ring.next()
                ctx.dma(pt_[:], p_d[tk0 + sub * 128:tk0 + (sub + 1) * 128, :], w=[ptb])
                ctx.op("pool", lambda: pool.tensor_copy(out=pbf[:, sub, :], in_=pt_[:]), r=[ptb], w=[pbf_b[sub]])
            oat_, oat_b_ = oa_ring.next()
            ctx.dma(oat_[:], OA_d[:, :, tk0:tk0 + TC].rearrange("(pr hh) d t -> (hh d) pr t", hh=2), w=[oat_b_])
            yst_, yst_b_ = ys_ring.next()
            ctx.dma(yst_[:], YS_d[:, :, tk0:tk0 + TC].rearrange("c p t -> p c t"), w=[yst_b_])
            return xts_, (oat_, oat_b_), (yst_, yst_b_)

        NTCC = 0 if stop in ('setup', 'A', 'B', 'B1') else NTC
        def emit_front(t_, nxt_):
            xT_, xT_b_ = xTc_ring.next()
            pT_, pT_b_ = pTc_ring.next()
            for dk in range(8):
                hf = dk % 2
                tp = psT_v[hf][:, 0:TC]
                for sub in range(NS):
                    ctx.op("pe", lambda: pe.transpose(tp[:, sub * 128:(sub + 1) * 128], xb[:, sub, dk * 128:(dk + 1) * 128], ident[:]),
                           r=[xb_b[sub], ident_b], w=[psT_b[hf]], sig=(sub == NS - 1))
                ctx.op("act", lambda: act.copy(out=xT_[:, dk, :], in_=tp), r=[psT_b[hf]], w=[xT_b_])
            for kc in range(2):
                hf = kc % 2
                tp = psT_v[hf][:, 0:TC]
                for sub in range(NS):
                    ctx.op("pe", lambda: pe.transpose(tp[:, sub * 128:(sub + 1) * 128], pbf[:, sub, kc * 128:(kc + 1) * 128], ident[:]),
                           r=[pbf_b[sub], ident_b], w=[psT_b[hf]], sig=(sub == NS - 1))
                ctx.op("act", lambda: act.copy(out=pT_[:, kc, :], in_=tp), r=[psT_b[hf]], w=[pT_b_])
            res = nxt_ + ((xT_, xT_b_, pT_, pT_b_),)
            nn = issue_xC(t_ + 1) if t_ + 1 < NTCC else None
            return res, nn

        if NTCC:
            nxt = issue_xC(0)
        stage_ring = Ring([(sb(sC, "wstc%d" % i, [128, 2048]), Buf("wstc%d" % i)) for i in range(2)])
        load_weight(Wz, Wz_b, w_in_d, 1536, 512, 8, stage_ring, dcol0=0)
        load_weight(Wz, Wz_b, w_in_d, 3072, 2048, 8, stage_ring, dcol0=512)
        load_weight(Wap, Wap_b, w_ap_d, 0, 1024, 4, stage_ring)
        load_weight(Wpg, Wpg_b, w_pg_d, 0, 1024, 8, stage_ring)
        load_weight(Wpp, Wpp_b, w_pp_d, 0, 1024, 2, stage_ring)
        load_weight(Wout, Wout_b, w_out_d, 0, 1024, 8, stage_ring)

        if NTCC:
            cur_front, nxt = emit_front(0, nxt)
        for t in range(NTCC):
            tok0 = t * TC
            xts, (oat, oat_b), (yst, yst_b), (xT, xT_b, pT, pT_b) = cur_front
            merge, merge_b = mg_ring.next()
            front_done = False
            for cc in range(20):
                pt, pb = mm_ring.next()
                mm_group(pt[:, 0:TC], pb, [(Wz[:, dk, cc * 128:(cc + 1) * 128], xT[:, dk, :]) for dk in range(8)], r=[Wz_b, xT_b])
                if cc < 4:
                    ctx.op("act", lambda: act.activation(out=sza[:, cc, :], in_=pt[:, 0:TC], func=AF.Silu), r=[pb], w=[sza_b])
                elif cc < 12:
                    ctx.op("act", lambda: act.activation(out=sga[:, cc - 4, :], in_=pt[:, 0:TC], func=AF.Sigmoid), r=[pb], w=[sga_b])
                else:
                    ctx.op("act", lambda: act.activation(out=sgs[:, cc - 12, :], in_=pt[:, 0:TC], func=AF.Sigmoid), r=[pb], w=[sgs_b])
            ctx.op("pool", lambda: pool.tensor_tensor(out=oz[:], in0=oat[:], in1=sza[:], op=ALU.mult), r=[oat_b, sza_b], w=[oz_b])
            for oc in range(8):
                pt, pb = mm_ring.next()
                mm_group(pt[:, 0:TC], pb, [(Wap[:, kc, oc * 128:(oc + 1) * 128], oz[:, kc, :]) for kc in range(4)], r=[Wap_b, oz_b])
                ta, ta_b = ta_ring.next()
                tb, tb_b = tb_ring.next()
                ctx.op("dve", lambda: dve.tensor_tensor(out=ta[:], in0=pt[:, 0:TC], in1=sga[:, oc, :], op=ALU.mult), r=[pb, sga_b], w=[ta_b])
                ctx.op("pool", lambda: pool.tensor_tensor(out=tb[:], in0=yst[:, oc, :], in1=sgs[:, oc, :], op=ALU.mult), r=[yst_b, sgs_b], w=[tb_b])
                ctx.op("dve", lambda: dve.tensor_tensor(out=merge[:, oc, :], in0=ta[:], in1=tb[:], op=ALU.add), r=[ta_b, tb_b], w=[merge_b])
            for sub in range(NS):
                xt, xtb = xts[sub]
                srow, srow_b = s_ring.next()
                tsl = slice(sub * 128, (sub + 1) * 128)
                (bst, bmv, brs), bst_b = st_ring.next()
                for hc in range(2):
                    csl = slice(hc * 512, (hc + 1) * 512)
                    pg, pg_b = tk_ring.next()
                    mm_group(pg[:, :], pg_b, [(xT[:, dk, tsl], Wpg[:, dk, csl]) for dk in range(8)], r=[Wpg_b, xT_b])
                    pp, pp_b = tk_ring.next()
                    mm_group(pp[:, :], pp_b, [(pT[:, kc, tsl], Wpp[:, kc, csl]) for kc in range(2)], r=[Wpp_b, pT_b])
                    mx, mx_b = tk_ring.next()
                    mm_group(mx[:, :], mx_b, [(merge[:, kc, tsl], Wout[:, kc, csl]) for kc in range(8)], r=[Wout_b, merge_b])
                    ctx.op("act", lambda: act.activation(out=srow[:, csl], in_=pg[:, :], func=AF.Sigmoid), r=[pg_b], w=[srow_b])
                    ctx.op("dve", lambda: dve.tensor_tensor(out=srow[:, csl], in0=srow[:, csl], in1=pp[:, :], op=ALU.mult), r=[srow_b, pp_b], w=[srow_b])
                    ctx.op("dve", lambda: dve.tensor_tensor(out=srow[:, csl], in0=srow[:, csl], in1=mx[:, :], op=ALU.add), r=[srow_b, mx_b], w=[srow_b])
                    ctx.op("dve", lambda: dve.scalar_tensor_tensor(out=srow[:, csl], in0=xt[:, csl], scalar=ALPHA, in1=srow[:, csl], op0=ALU.mult, op1=ALU.add),
                           r=[srow_b, xtb], w=[srow_b])
                    ctx.op("dve", lambda: dve.bn_stats(out=bst[:, hc, :], in_=srow[:, csl]), r=[srow_b], w=[bst_b])
                if sub == NS - 1 and t + 1 < NTCC:
                    cur_front, nxt = emit_front(t + 1, nxt)
                ctx.op("dve", lambda: dve.bn_aggr(out=bmv[:], in_=bst[:].rearrange("p a b -> p (a b)")), r=[bst_b], w=[bst_b])
                ctx.op("dve", lambda: dve.tensor_scalar(out=brs[:, 0:1], in0=bmv[:, 1:2], scalar1=LN_EPS, scalar2=None, op0=ALU.add), r=[bst_b], w=[bst_b])
                ctx.op("act", lambda: act.activation(out=brs[:, 0:1], in_=brs[:, 0:1], func=AF.Sqrt), r=[bst_b], w=[bst_b])
                ctx.op("dve", lambda: dve.reciprocal(out=brs[:, 0:1], in_=brs[:, 0:1]), r=[bst_b], w=[bst_b])
                ctx.op("dve", lambda: dve.scalar_tensor_tensor(out=brs[:, 1:2], in0=bmv[:, 0:1], scalar=-1.0, in1=brs[:, 0:1], op0=ALU.mult, op1=ALU.mult),
                       r=[bst_b], w=[bst_b])
                orow, orow_b = o_ring.next()
                ctx.op("dve", lambda: dve.tensor_scalar(out=orow[:], in0=srow[:], scalar1=brs[:, 0:1], scalar2=brs[:, 1:2], op0=ALU.mult, op1=ALU.add), r=[srow_b, bst_b], w=[orow_b])
                ctx.op("dve", lambda: dve.tensor_tensor(out=orow[:], in0=orow[:], in1=lng[:], op=ALU.mult), r=[orow_b, ln_b], w=[orow_b])
                ctx.op("pool", lambda: pool.tensor_tensor(out=orow[:], in0=orow[:], in1=lnb[:], op=ALU.add), r=[orow_b, ln_b], w=[orow_b])
                ctx.dma(out_d[tok0 + sub * 128:tok0 + (sub + 1) * 128, :], orow[:], r=[orow_b])
        ctx.barrier(engines=["sp"])
    es.close()
    return nc


def host_consts(S):
    bf = ml_dtypes.bfloat16
    NCH = S // 128
    c = {}
    c["ident"] = np.eye(128, dtype=np.float32).astype(bf)
    c["Jm"] = np.ascontiguousarray(np.eye(128, dtype=np.float32)[::-1]).astype(bf)
    i = np.arange(384)
    dist = i - 127
    bk = t5_bucket_np(np.maximum(dist, 0))
    OH = np.zeros((32, 384), np.float32)
    valid = dist >= 0
    OH[bk[valid], i[valid]] = 1.0
    c["OH"] = OH.astype(bf)
    NEGM = np.zeros((8, 384), np.float32)
    NEGM[:, ~valid] = NEG
    c["NEGM"] = NEGM
    keys = np.arange(S)
    EOH = (keys[None, :] // BLK == np.arange(32)[:, None]).astype(np.float32)
    c["EOH"] = EOH.astype(bf)
    ch = np.arange(NCH)
    blk = ch // 2
    n = np.arange(32)
    past = (n[None, :] < blk[:, None])
    PM = np.where(past, 0.0, -1e30).astype(np.float32)
    c["PMASK"] = np.ascontiguousarray(np.broadcast_to(PM.reshape(1, -1), (128, NCH * 32)))
    c["PAST01"] = np.ascontiguousarray(np.broadcast_to(past.astype(np.float32).reshape(1, -1), (128, NCH * 32)))
    c["jidx"] = np.ascontiguousarray(np.broadcast_to(np.arange(129, dtype=np.float32)[None, :], (128, 129)))
    return c


def host_params(inp):
    o = {}
    a_re = inp["ssm_a_re"][0]; a_im = inp["ssm_a_im"][0]; ldt = inp["ssm_log_dt"][0]
    b_re = inp["ssm_b_re"][0]; b_im = inp["ssm_b_im"][0]; c_re = inp["ssm_c_re"][0]; c_im = inp["ssm_c_im"][0]
    pidx = np.arange(128); g2 = pidx // 64; n = pidx % 64
    pr = np.arange(16)
    G = 2 * pr[None, :] + g2[:, None]
    o["are_l"] = np.ascontiguousarray(a_re[G, n[:, None]])
    o["aim_l"] = np.ascontiguousarray(a_im[G, n[:, None]])
    o["ldt_l"] = np.ascontiguousarray(ldt[G])
    Gc = 2 * pr[:, None] + g2[None, :]
    are_pc = a_re[Gc, n[None, :]]
    aim_pc = a_im[Gc, n[None, :]]
    ldt_pc = ldt[Gc]
    for name, v in (("are_rep", are_pc), ("aim_rep", aim_pc), ("ldt_rep", ldt_pc)):
        o[name] = np.ascontiguousarray(np.broadcast_to(v.reshape(1, 2048), (128, 2048))).astype(np.float32)
    brT = np.zeros((128, 16, 128), np.float32); biT = np.zeros((128, 16, 128), np.float32)
    cre = np.zeros((128, 16, 128), np.float32); cim = np.zeros((128, 16, 128), np.float32)
    for p_ in range(16):
        r0 = (p_ % 4) * 32
        for gg in range(2):
            g = 2 * p_ + gg
            brT[r0 + gg * 16:r0 + gg * 16 + 16, p_, gg * 64:(gg + 1) * 64] = b_re[g].T
            biT[r0 + gg * 16:r0 + gg * 16 + 16, p_, gg * 64:(gg + 1) * 64] = b_im[g].T
            cre[gg * 64:(gg + 1) * 64, p_, r0 + gg * 16:r0 + gg * 16 + 16] = c_re[g].T
            cim[gg * 64:(gg + 1) * 64, p_, r0 + gg * 16:r0 + gg * 16 + 16] = c_im[g].T
    o["brT"] = brT.reshape(128, 2048); o["biT"] = biT.reshape(128, 2048)
    o["cre_pad"] = cre.reshape(128, 2048); o["cim_pad"] = cim.reshape(128, 2048)
    o["d_l"] = np.ascontiguousarray(inp["ssm_d"][0].reshape(4, 128).T)
    o["lng"] = np.ascontiguousarray(np.broadcast_to(inp["ln_g"][0][None, :], (128, D)))
    o["lnb"] = np.ascontiguousarray(np.broadcast_to(inp["ln_b"][0][None, :], (128, D)))
    o["relb"] = np.ascontiguousarray(inp["rel_bias"])
    o["b31rep"] = np.ascontiguousarray(np.broadcast_to(inp["rel_bias"][31][None, :], (128, 8)))
    o["w_in"] = np.ascontiguousarray(inp["w_in"][0]); o["w_ap"] = np.ascontiguousarray(inp["w_attn_proj"][0])
    o["w_sp"] = np.ascontiguousarray(inp["w_ssm_proj"][0]); o["w_out"] = np.ascontiguousarray(inp["w_out"][0])
    o["w_glu"] = np.ascontiguousarray(inp["w_glu"][0]); o["w_pg"] = np.ascontiguousarray(inp["w_ple_gate"][0])
    o["w_pp"] = np.ascontiguousarray(inp["w_ple_proj"][0])
    return {k: np.asarray(v, np.float32) for k, v in o.items()}


_NC_CACHE = {}


def run(inputs, S, n_cores, **bk):
    inp = {k: np.asarray(v) for k, v in inputs.items()}
    if S not in _NC_CACHE:
        _NC_CACHE[S] = build_nc(S, **bk)
    nc = _NC_CACHE[S]
    shared = host_params(inp)
    shared.update(host_consts(S))
    in_maps = []
    for b in range(n_cores):
        m = dict(shared)
        m["x"] = np.ascontiguousarray(inp["x"][b], dtype=np.float32)
        m["p"] = np.ascontiguousarray(inp["p"][0, b], dtype=np.float32)
        in_maps.append(m)
    res = run_bass_kernel_spmd(nc, in_maps, core_ids=list(range(n_cores)))
    return np.stack([np.asarray(r["out"]) for r in res.results], axis=0).astype(np.float32)


def kernel(**inputs):
    return run(inputs, 8192, 8)
```
